# Optimizing a Trainium2 kernel written in Bass

```python
import math
import jax, jax.numpy as jnp
from jax import lax
import numpy as np

D_MODEL = 1024
BATCH = 4
SEQ = 8192
DEPTH = 4
DEC_BATCH = 8
DEC_SEQ = 8192
PAST_LEN = 128

GRID_W = 64
N_EVEN = (DEPTH + 1) // 2
N_ODD = DEPTH // 2
D_RET = D_MODEL // 2
RET_HEADS = 4
RET_HEAD_DIM = D_RET // RET_HEADS
RET_CHUNK = 128
D_SSM = D_MODEL - D_RET
SSM_GROUP = 16
SSM_GROUPS = D_SSM // SSM_GROUP
SSM_STATE = 64
EVEN_IN = 4 * D_RET + 2 * D_SSM
D_NA = D_MODEL
NA_HEADS = 16
NA_HEAD_DIM = D_NA // NA_HEADS
NA_ROWS_MAX = 8
NA_COLS = 16
ODD_IN = 4 * D_NA
ROPE_BASE = 10000.0
EPS = 1e-6
NEG_INF = -1e30
DT_MIN = 1e-3
DT_MAX = 1e-1

kernel_name = "hybrid_retention_s5_natten_encoder"


def _rmsnorm(x, g):
    xf = x.astype(jnp.float32)
    y = xf * lax.rsqrt(jnp.mean(xf * xf, axis=-1, keepdims=True) + EPS)
    return (y * g.astype(jnp.float32)).astype(x.dtype)


def _head_norm(o):
    of = o.astype(jnp.float32)
    mu = jnp.mean(of, axis=-1, keepdims=True)
    var = jnp.mean(jnp.square(of - mu), axis=-1, keepdims=True)
    return ((of - mu) * lax.rsqrt(var + EPS)).astype(o.dtype)


def _rotary(x):
    L, dh = x.shape[1], x.shape[-1]
    inv = ROPE_BASE ** (-jnp.arange(0, dh, 2, dtype=jnp.float32) / dh)
    ang = jnp.arange(L, dtype=jnp.float32)[:, None] * inv[None, :]
    cos = jnp.cos(ang)[None, :, None, :].astype(x.dtype)
    sin = jnp.sin(ang)[None, :, None, :].astype(x.dtype)
    x1, x2 = x[..., : dh // 2], x[..., dh // 2:]
    return jnp.concatenate([x1 * cos - x2 * sin, x1 * sin + x2 * cos], axis=-1)


def _retention(q, k, v):
    B, L, H, dk = q.shape
    dv = v.shape[-1]
    dt = q.dtype
    cs = RET_CHUNK
    n = L // cs
    log_g = jnp.log1p(-jnp.exp2(-5.0 - jnp.arange(H, dtype=jnp.float32)))
    pos = jnp.arange(cs, dtype=jnp.float32)
    intra = jnp.exp(jnp.abs(pos[:, None] - pos[None, :])[None] * log_g[:, None, None]).astype(dt)
    q_fwd = jnp.exp(pos[:, None] * log_g[None]).astype(dt)[:, :, None]
    q_bwd = jnp.exp((cs - 1.0 - pos)[:, None] * log_g[None]).astype(dt)[:, :, None]
    k_fwd = jnp.exp((cs - pos)[:, None] * log_g[None]).astype(dt)[:, :, None]
    k_bwd = jnp.exp((pos + 1.0)[:, None] * log_g[None]).astype(dt)[:, :, None]
    chunk_decay = jnp.exp(cs * log_g).astype(dt)[None, :, None, None]

    qc = q.reshape(B, n, cs, H, dk)
    kc = k.reshape(B, n, cs, H, dk)
    vc = v.reshape(B, n, cs, H, dv)
    s = jnp.einsum('bnihd,bnjhd->bnhij', qc, kc) * intra
    o = jnp.einsum('bnhij,bnjhe->bnihe', s, vc)

    kv_f = jnp.einsum('bnjhd,bnjhe->nbhde', kc * k_fwd, vc)
    kv_b = jnp.einsum('bnjhd,bnjhe->nbhde', kc * k_bwd, vc)

    def step(carry, kv):
        return chunk_decay * carry + kv, carry

    init = jnp.zeros((B, H, dk, dv), dt)
    _, st_f = lax.scan(step, init, kv_f)
    _, st_b = lax.scan(step, init, kv_b, reverse=True)
    o = (o + jnp.einsum('bnihd,nbhde->bnihe', qc * q_fwd, st_f)
         + jnp.einsum('bnihd,nbhde->bnihe', qc * q_bwd, st_b))
    return o.reshape(B, L, H, dv)


def _cplx_combine(e_i, e_j):
    ar_i, ai_i, br_i, bi_i = e_i
    ar_j, ai_j, br_j, bi_j = e_j
    ar = ar_j * ar_i - ai_j * ai_i
    ai = ar_j * ai_i + ai_j * ar_i
    br = ar_j * br_i - ai_j * bi_i + br_j
    bi = ar_j * bi_i + ai_j * br_i + bi_j
    return ar, ai, br, bi


def _s5_scan(u, a_re, a_im, log_step, b_re, b_im, c_re, c_im, reverse):
    f32 = jnp.float32
    a_re = a_re.astype(f32)
    a_im = a_im.astype(f32)
    delta = jnp.exp(log_step.astype(f32))[:, None]
    z_re, z_im = a_re * delta, a_im * delta
    mag = jnp.exp(z_re)
    abar_re, abar_im = mag * jnp.cos(z_im), mag * jnp.sin(z_im)
    den = a_re * a_re + a_im * a_im
    n_re, n_im = abar_re - 1.0, abar_im
    f_re = (n_re * a_re + n_im * a_im) / den
    f_im = (n_im * a_re - n_re * a_im) / den
    b_re = b_re.astype(f32)
    b_im = b_im.astype(f32)
    bb_re = f_re[..., None] * b_re - f_im[..., None] * b_im
    bb_im = f_re[..., None] * b_im + f_im[..., None] * b_re
    bu_re = jnp.einsum('blgi,gpi->blgp', u, bb_re)
    bu_im = jnp.einsum('blgi,gpi->blgp', u, bb_im)
    shape = (1, u.shape[1]) + abar_re.shape
    elems = (jnp.broadcast_to(abar_re[None, None], shape),
             jnp.broadcast_to(abar_im[None, None], shape), bu_re, bu_im)
    _, _, x_re, x_im = lax.associative_scan(_cplx_combine, elems, reverse=reverse, axis=1)
    return (jnp.einsum('blgp,gip->blgi', x_re, c_re.astype(f32))
            - jnp.einsum('blgp,gip->blgi', x_im, c_im.astype(f32)))


def _s5(u, a_re, a_im, log_step, b_re, b_im, c_re, c_im, d_skip, w_glu):
    B, L, _ = u.shape
    dt = u.dtype
    uf = u.astype(jnp.float32)
    ug = uf.reshape(B, L, SSM_GROUPS, SSM_GROUP)
    y = (_s5_scan(ug, a_re[0], a_im[0], log_step[0], b_re[0], b_im[0], c_re[0], c_im[0], False)
         + _s5_scan(ug, a_re[1], a_im[1], log_step[1], b_re[1], b_im[1], c_re[1], c_im[1], True))
    y = y.reshape(B, L, D_SSM) + d_skip.astype(jnp.float32) * uf
    y = jax.nn.gelu(y)
    y = y * jax.nn.sigmoid(y @ w_glu.astype(jnp.float32))
    return y.astype(dt)


def _neighbourhood_attention(q, k, v, rel_bias):
    B, L, H, dh = q.shape
    rows = L // GRID_W
    kr = min(NA_ROWS_MAX, rows)
    q = q.reshape(B, rows, GRID_W, H, dh)
    k = k.reshape(B, rows, GRID_W, H, dh)
    v = v.reshape(B, rows, GRID_W, H, dh)
    r_idx = jnp.arange(rows)
    row_start = jnp.clip(r_idx - kr // 2, 0, rows - kr)
    c_idx = jnp.arange(GRID_W)
    col_start = jnp.clip(c_idx - NA_COLS // 2, 0, GRID_W - NA_COLS)
    col_valid = ((c_idx[None, :] >= col_start[:, None])
                 & (c_idx[None, :] < col_start[:, None] + NA_COLS))
    dc_idx = jnp.clip(c_idx[None, :] - c_idx[:, None] + NA_COLS - 1, 0, 2 * NA_COLS - 2)
    col_bias = jnp.take(rel_bias.astype(jnp.float32), dc_idx, axis=2)
    scale = dh ** -0.5

    def one_row(r):
        rows_k = row_start[r] + jnp.arange(kr)
        kb = jnp.take(k, rows_k, axis=1)
        vb = jnp.take(v, rows_k, axis=1)
        qr = lax.dynamic_index_in_dim(q, r, axis=1, keepdims=False)
        s = jnp.einsum('bqhd,brkhd->bhqrk', qr, kb).astype(jnp.float32) * scale
        bias = jnp.take(col_bias, rows_k - r + NA_ROWS_MAX - 1, axis=1)
        s = s + jnp.transpose(bias, (0, 2, 1, 3))[None]
        s = jnp.where(col_valid[:, None, :], s, NEG_INF)
        p = jax.nn.softmax(s.reshape(B, H, GRID_W, kr * GRID_W), axis=-1)
        p = p.reshape(B, H, GRID_W, kr, GRID_W).astype(v.dtype)
        return jnp.einsum('bhqrk,brkhd->bqhd', p, vb)

    out = lax.map(one_row, r_idx)
    return jnp.moveaxis(out, 0, 1).reshape(B, L, H * dh)


def _even_mixer(h, w_in, w_out, a_re, a_im, log_step, b_re, b_im, c_re, c_im, d_skip, w_glu):
    B, L, _ = h.shape
    z = h @ w_in
    q, k, v, ga, ub, gb = jnp.split(
        z, [D_RET, 2 * D_RET, 3 * D_RET, 4 * D_RET, 4 * D_RET + D_SSM], axis=-1)
    hs = (B, L, RET_HEADS, RET_HEAD_DIM)
    q = _rotary(q.reshape(hs))
    k = _rotary(k.reshape(hs)) * (RET_HEAD_DIM ** -0.5)
    o_a = _head_norm(_retention(q, k, v.reshape(hs))).reshape(B, L, D_RET) * jax.nn.silu(ga)
    o_b = _s5(ub, a_re, a_im, log_step, b_re, b_im, c_re, c_im, d_skip, w_glu) * jax.nn.silu(gb)
    return jnp.concatenate([o_a, o_b], axis=-1) @ w_out


def _odd_mixer(h, w_in, w_out, rel_bias):
    B, L, _ = h.shape
    q, k, v, g = jnp.split(h @ w_in, 4, axis=-1)
    hs = (B, L, NA_HEADS, NA_HEAD_DIM)
    o = _neighbourhood_attention(q.reshape(hs), k.reshape(hs), v.reshape(hs), rel_bias)
    return (o * jax.nn.silu(g)) @ w_out


def _trunk(x, c, norm_pre, norm_post, w_mod, b_mod, w_in_ab, w_out_ab,
           ssm_a_re, ssm_a_im, ssm_log_step, ssm_b_re, ssm_b_im, ssm_c_re, ssm_c_im,
           ssm_d, ssm_w_glu, w_in_c, w_out_c, na_rel_bias):
    for i in range(DEPTH):
        mod = jax.nn.silu(c) @ w_mod[i] + b_mod[i]
        shift, scale, gate = jnp.split(mod[:, None, :], 3, axis=-1)
        h = _rmsnorm(x, norm_pre[i]) * (1.0 + scale) + shift
        j = i // 2
        if i % 2 == 0:
            y = _even_mixer(h, w_in_ab[j], w_out_ab[j], ssm_a_re[j], ssm_a_im[j],
                            ssm_log_step[j], ssm_b_re[j], ssm_b_im[j], ssm_c_re[j],
                            ssm_c_im[j], ssm_d[j], ssm_w_glu[j])
        else:
            y = _odd_mixer(h, w_in_c[j], w_out_c[j], na_rel_bias[j])
        x = x + gate * _rmsnorm(y, norm_post[i])
    return x


def setup_inputs(seed: int = 0) -> dict:
    key = jax.random.key(seed)
    ks = jax.random.split(key, 24)
    f32 = jnp.float32
    nrm = lambda k, s, sc: jax.random.normal(k, s, f32) * sc
    G, P, Gi = SSM_GROUPS, SSM_STATE, SSM_GROUP
    return {
        "x_prompt": nrm(ks[0], (BATCH, SEQ, D_MODEL), 1.0),
        "x_sample": nrm(ks[1], (DEC_BATCH, DEC_SEQ, D_MODEL), 1.0),
        "c_prompt": nrm(ks[2], (BATCH, D_MODEL), 1.0),
        "c_sample": nrm(ks[3], (DEC_BATCH, D_MODEL), 1.0),
        "norm_pre": 1.0 + nrm(ks[4], (DEPTH, D_MODEL), 0.05),
        "norm_post": 1.0 + nrm(ks[5], (DEPTH, D_MODEL), 0.05),
        "w_mod": nrm(ks[6], (DEPTH, D_MODEL, 3 * D_MODEL), 0.5 * D_MODEL ** -0.5),
        "b_mod": nrm(ks[7], (DEPTH, 3 * D_MODEL), 0.02),
        "w_in_ab": nrm(ks[8], (N_EVEN, D_MODEL, EVEN_IN), D_MODEL ** -0.5),
        "w_out_ab": nrm(ks[9], (N_EVEN, D_RET + D_SSM, D_MODEL), (D_RET + D_SSM) ** -0.5),
        "ssm_a_re": -0.5 + nrm(ks[10], (N_EVEN, 2, G, P), 0.01),
        "ssm_a_im": math.pi * jnp.arange(P, dtype=f32) + nrm(ks[11], (N_EVEN, 2, G, P), 0.01),
        "ssm_log_step": jax.random.uniform(ks[12], (N_EVEN, 2, G), f32,
                                           math.log(DT_MIN), math.log(DT_MAX)),
        "ssm_b_re": nrm(ks[13], (N_EVEN, 2, G, P, Gi), (2 * Gi) ** -0.5),
        "ssm_b_im": nrm(ks[14], (N_EVEN, 2, G, P, Gi), (2 * Gi) ** -0.5),
        "ssm_c_re": nrm(ks[15], (N_EVEN, 2, G, Gi, P), P ** -0.5),
        "ssm_c_im": nrm(ks[16], (N_EVEN, 2, G, Gi, P), P ** -0.5),
        "ssm_d": nrm(ks[17], (N_EVEN, D_SSM), 0.5),
        "ssm_w_glu": nrm(ks[18], (N_EVEN, D_SSM, D_SSM), D_SSM ** -0.5),
        "w_in_c": nrm(ks[19], (N_ODD, D_MODEL, ODD_IN), D_MODEL ** -0.5),
        "w_out_c": nrm(ks[20], (N_ODD, D_NA, D_MODEL), D_NA ** -0.5),
        "na_rel_bias": nrm(ks[21], (N_ODD, NA_HEADS, 2 * NA_ROWS_MAX - 1, 2 * NA_COLS - 1), 0.1),
    }


def reference(x_prompt, x_sample, c_prompt, c_sample, norm_pre, norm_post, w_mod, b_mod,
              w_in_ab, w_out_ab, ssm_a_re, ssm_a_im, ssm_log_step, ssm_b_re, ssm_b_im,
              ssm_c_re, ssm_c_im, ssm_d, ssm_w_glu, w_in_c, w_out_c, na_rel_bias):
    y_prompt = _trunk(x_prompt, c_prompt, norm_pre, norm_post, w_mod, b_mod, w_in_ab, w_out_ab,
                      ssm_a_re, ssm_a_im, ssm_log_step, ssm_b_re, ssm_b_im, ssm_c_re, ssm_c_im,
                      ssm_d, ssm_w_glu, w_in_c, w_out_c, na_rel_bias)
    y_sample = _trunk(x_sample, c_sample, norm_pre, norm_post, w_mod, b_mod, w_in_ab, w_out_ab,
                      ssm_a_re, ssm_a_im, ssm_log_step, ssm_b_re, ssm_b_im, ssm_c_re, ssm_c_im,
                      ssm_d, ssm_w_glu, w_in_c, w_out_c, na_rel_bias)
    return (y_prompt, y_sample)
```

```python
import contextlib
import math
import os
KSTOP = os.environ.get('KSTOP', 'all')
import numpy as np
import concourse.bass as bass
import concourse.mybir as mybir
from concourse.bass_utils import run_bass_kernel_spmd

F32 = mybir.dt.float32
BF16 = mybir.dt.bfloat16
I32 = mybir.dt.int32
ALU = mybir.AluOpType
AF = mybir.ActivationFunctionType

D = 1024
EPS = 1e-6
TWO_PI = 2.0 * math.pi


class Buf:
    __slots__ = ("name", "w", "r")

    def __init__(self, name):
        self.name = name
        self.w = []
        self.r = []


class Eng:
    def __init__(self, name):
        self.name = name
        self.ops = []
        self.known = {}
        self.sem = None
        self.cnt = 0
        self.pending = False
        self.dsems = []
        self.dvals = []
        self.dptr = 0
        self.own = set()


EPOCH = 30000
NDSEM = 10


class Tracker:
    def __init__(self, nc):
        self.nc = nc
        self.engs = {n: Eng(n) for n in ("pe", "act", "dve", "pool", "sp")}
        self.sems = []
        for e in self.engs.values():
            if e.name != "sp":
                e.sem = self._newsem(e.name)
                e.own.add(e.sem)
        self.n_ops = 0

    def _newsem(self, nm):
        h = self.nc.alloc_semaphore(name=f"{nm}_{len(self.sems)}")
        self.sems.append(h)
        return len(self.sems) - 1

    def _wait(self, e, ev):
        s, v = ev
        if e.known.get(s, 0) >= v:
            return
        if s in e.own:
            if e.name == "pe":
                return
            if s == e.sem and v > e.cnt:
                return
        e.known[s] = v
        sem = self.sems[s]
        e.ops.append(lambda q, sem=sem, v=v: q.wait_ge(sem, v))

    def _deps(self, e, reads, writes):
        for b in reads:
            for ev in b.w:
                self._wait(e, ev)
        for b in writes:
            for ev in b.w:
                self._wait(e, ev)
            for ev in b.r:
                self._wait(e, ev)

    def _commit(self, ev, reads, writes):
        for b in reads:
            for i, (s0, v0) in enumerate(b.r):
                if s0 == ev[0]:
                    b.r[i] = (s0, max(v0, ev[1]))
                    break
            else:
                b.r.append(ev)
        for b in writes:
            b.w = [ev]
            b.r = []

    def op(self, eng, fn, reads=(), writes=(), inc=True):
        e = self.engs[eng]
        self.n_ops += 1
        self._deps(e, reads, writes)
        if e.cnt >= EPOCH and inc and not e.pending:
            e.sem = self._newsem(e.name)
            e.cnt = 0
            e.own.add(e.sem)
        if inc:
            e.cnt += 1
            sem = self.sems[e.sem]
            e.ops.append(lambda q, fn=fn, sem=sem: fn(q).then_inc(sem, 1))
            ev = (e.sem, e.cnt)
            e.pending = False
        else:
            e.ops.append(lambda q, fn=fn: fn(q))
            ev = (e.sem, e.cnt + 1)
            e.pending = True
        self._commit(ev, reads, writes)

    def dma(self, eng, out, in_, reads=(), writes=()):
        e = self.engs[eng]
        self.n_ops += 1
        self._deps(e, reads, writes)
        if len(e.dsems) < NDSEM:
            e.dsems.append(self._newsem(e.name + "d"))
            e.dvals.append(0)
            k = len(e.dsems) - 1
        else:
            k = e.dptr
            e.dptr = (e.dptr + 1) % NDSEM
            self._wait(e, (e.dsems[k], e.dvals[k]))
            if e.dvals[k] >= EPOCH * 16:
                e.dsems[k] = self._newsem(e.name + "d")
                e.dvals[k] = 0
        e.dvals[k] += 16
        sem = self.sems[e.dsems[k]]
        e.ops.append(lambda q, out=out, in_=in_, sem=sem: q.dma_start(out=out, in_=in_).then_inc(sem, 16))
        ev = (e.dsems[k], e.dvals[k])
        self._commit(ev, reads, writes)
        return ev

    def finish(self):
        sp = self.engs["sp"]
        for e in self.engs.values():
            for k in range(len(e.dsems)):
                self._wait(sp, (e.dsems[k], e.dvals[k]))
            if e.sem is not None and e.cnt > 0:
                self._wait(sp, (e.sem, e.cnt))

    def replay(self):
        nc = self.nc
        E = self.engs
        with nc.Block() as block:
            @block.tensor
            def _(q):
                for f in E["pe"].ops:
                    f(q)

            @block.scalar
            def _(q):
                for f in E["act"].ops:
                    f(q)

            @block.vector
            def _(q):
                for f in E["dve"].ops:
                    f(q)

            @block.gpsimd
            def _(q):
                for f in E["pool"].ops:
                    f(q)

            @block.sync
            def _(q):
                for f in E["sp"].ops:
                    f(q)


RET_H = 4
NA_H = 16
GW = 64


def _ret_consts():
    f32 = np.float32
    h = np.arange(RET_H, dtype=f32)
    log_g = np.log1p(-np.exp2(-5.0 - h)).astype(f32)
    pos = np.arange(128, dtype=f32)
    dt = np.exp(np.abs(pos[:, None] - pos[None, :])[:, None, :] * log_g[None, :, None]).astype(f32)
    gam = np.zeros((128, 4, RET_H), f32)
    gam[:, 0, :] = np.exp(pos[:, None] * log_g[None])
    gam[:, 1, :] = np.exp((127.0 - pos)[:, None] * log_g[None])
    gam[:, 2, :] = np.exp((128.0 - pos)[:, None] * log_g[None])
    gam[:, 3, :] = np.exp((pos + 1.0)[:, None] * log_g[None])
    cdec = np.exp(128.0 * log_g).astype(f32)
    cd = np.broadcast_to(cdec[None, :], (128, RET_H)).copy()
    return dt, gam, cd


def _rot_tables(L):
    f32 = np.float32
    inv = (10000.0 ** (-np.arange(0, 128, 2, dtype=f32) / 128.0)).astype(f32)
    ang = (np.arange(L, dtype=f32)[:, None] * inv[None, :]).astype(f32)
    c = np.cos(ang).astype(f32)
    s = np.sin(ang).astype(f32)
    rq = np.zeros((L, 2, 128), f32)
    rq[:, 0, :64] = c
    rq[:, 0, 64:] = c
    rq[:, 1, :64] = -s
    rq[:, 1, 64:] = s
    rk = (rq * f32(128.0 ** -0.5)).astype(f32)
    return np.stack([rq, rk], axis=1).copy()


def _na_layout(relb):
    a = np.arange(2)[:, None, None, None]
    k = np.arange(64)[None, :, None, None]
    m = np.arange(-7, 9)[None, None, :, None]
    c = np.arange(64)[None, None, None, :]
    dr = np.clip(a - m + 7, 0, 14)
    dc = np.clip(k - c + 15, 0, 30)
    dr_b, dc_b = np.broadcast_arrays(dr, dc)
    z = relb[:, dr_b, dc_b]
    z = z.reshape(16, 128, 16 * 64).astype(np.float32)
    cs = np.clip(np.arange(64) - 8, 0, 48)
    kk = np.arange(64)[:, None]
    valid = (kk >= cs[None, :]) & (kk < cs[None, :] + 16)
    mask = np.where(valid, 0.0, -30000.0).astype(np.float32)
    mask = np.broadcast_to(mask[None, :, None, :], (2, 64, 16, 64)).reshape(128, 1024).copy()
    return z, mask


def _s5_layout(a_re, a_im, ls, b_re, b_im, c_re, c_im):
    f32 = np.float32
    def sm(a):
        return a.reshape(2, 16, 2, 64).transpose(0, 2, 3, 1).reshape(2, 128, 16).astype(f32)
    lsx = np.broadcast_to(ls[:, :, None], (2, 32, 64))
    sm3 = np.stack([sm(a_re), sm(a_im), sm(lsx)], axis=1).copy()
    rep3 = np.stack([a_re.reshape(2, 2048), a_im.reshape(2, 2048), lsx.reshape(2, 2048)], axis=1).astype(f32).copy()
    bl = np.zeros((2, 2, 128, 16, 2, 64), f32)
    cl = np.zeros((2, 2, 128, 16, 2, 16), f32)
    for s in range(16):
        for g1 in range(2):
            g = 2 * s + g1
            r0 = 32 * (s % 4) + 16 * g1
            for ri, b in enumerate((b_re, b_im)):
                bl[:, ri, r0:r0 + 16, s, g1, :] = b[:, g].transpose(0, 2, 1)
            for ri, c in enumerate((c_re, c_im)):
                cl[:, ri, 64 * g1:64 * g1 + 64, s, g1, :] = c[:, g].transpose(0, 2, 1)
    return sm3, rep3, bl.reshape(2, 2, 128, 16 * 128), cl.reshape(2, 2, 128, 16 * 32)


class Prog:
    def __init__(self, nseq, L, layers):
        self.nseq, self.L, self.layers = nseq, L, layers
        self.nch = L // 128
        nc = self.nc = bass.Bass("TRN2", target_bir_lowering=False)
        self.T = Tracker(nc)
        self.es = contextlib.ExitStack()
        self.dram = {}
        self.bufs = {}

    def sbt(self, name, shape, dt=F32):
        self._uid = getattr(self, '_uid', 0) + 1
        return self.nc.sbuf_tensor(f"{name}_u{self._uid}", list(shape), dt)

    def din(self, name, shape, dt=F32):
        t = self.nc.dram_tensor(name, list(shape), dt, kind="ExternalInput").ap()
        self.dram[name] = t
        return t

    def dout(self, name, shape, dt=F32):
        t = self.nc.dram_tensor(name, list(shape), dt, kind="ExternalOutput").ap()
        self.dram[name] = t
        return t

    def dscr(self, name, shape, dt=F32):
        t = self.nc.dram_tensor(name, list(shape), dt, kind="Internal").ap()
        self.dram[name] = t
        return t

    def sb(self, name, shape, dt=F32):
        t = self.es.enter_context(self.sbt(name, list(shape), dt))
        b = Buf(name)
        return t, b

    def ps(self, name):
        t = self.es.enter_context(self.nc.psum_tensor(name, [128, 512], F32))
        return t, Buf(name)

    def mm(self, out, lhsT, rhs, start, stop, reads, writes, inc=True):
        self.T.op("pe", lambda q: q.matmul(out, lhsT=lhsT, rhs=rhs, start=start, stop=stop),
                  reads=reads, writes=writes, inc=inc)

    def act(self, out, in_, func, reads, writes, scale=1.0, bias=None, accum=None):
        kw = {}
        if bias is not None:
            kw["bias"] = bias
        if accum is not None:
            kw["accum_out"] = accum
        self.T.op("act", lambda q: q.activation(out=out, in_=in_, func=func, scale=scale, **kw),
                  reads=reads, writes=writes)

    def tt(self, eng, out, in0, in1, op, reads, writes):
        self.T.op(eng, lambda q: q.tensor_tensor(out=out, in0=in0, in1=in1, op=op), reads=reads, writes=writes)

    def ts(self, eng, out, in0, s1, s2, op0, op1, reads, writes):
        if s2 is None:
            self.T.op(eng, lambda q: q.tensor_scalar(out=out, in0=in0, scalar1=s1, scalar2=None, op0=op0),
                      reads=reads, writes=writes)
        else:
            self.T.op(eng, lambda q: q.tensor_scalar(out=out, in0=in0, scalar1=s1, scalar2=s2, op0=op0, op1=op1),
                      reads=reads, writes=writes)

    def stt(self, eng, out, in0, scalar, in1, op0, op1, reads, writes):
        self.T.op(eng, lambda q: q.scalar_tensor_tensor(out=out, in0=in0, scalar=scalar, in1=in1, op0=op0, op1=op1),
                  reads=reads, writes=writes)

    def cp(self, eng, out, in_, reads, writes):
        self.T.op(eng, lambda q: q.tensor_copy(out=out, in_=in_), reads=reads, writes=writes)

    def memset(self, eng, ap, val, writes):
        self.T.op(eng, lambda q: q.memset(ap, val), reads=(), writes=writes)

    def dma(self, eng, out, in_, reads=(), writes=()):
        return self.T.dma(eng, out, in_, reads=reads, writes=writes)


def build(nseq, L, layers):
    P = Prog(nseq, L, layers)
    nc, T = P.nc, P.T
    nch = L // 128
    NL = 4
    x_in = P.din("x_in", [nseq, L, D])
    y_out = P.dout("y_out", [nseq, L, D])
    cT = P.din("cT", [128, 8, nseq])
    npreT = P.din("npreT", [NL, 128, 8])
    npostT = P.din("npostT", [NL, 128, 8])
    wmod = P.din("wmod", [NL, D, 3 * D])
    bmodT = P.din("bmodT", [NL, 128, 24])
    w_in_ab = P.din("w_in_ab", [2, D, 3072])
    w_out_ab = P.din("w_out_ab", [2, D, D])
    w_glu = P.din("w_glu", [2, 512, 512])
    ssm_d = P.din("ssm_d", [2, 512])
    s5_sm3 = P.din("s5_sm3", [2, 2, 3, 128, 16])
    s5_rep3 = P.din("s5_rep3", [2, 2, 3, 2048])
    s5_bl = P.din("s5_bl", [2, 2, 2, 128, 2048])
    s5_cl = P.din("s5_cl", [2, 2, 2, 128, 512])
    w_in_c = P.din("w_in_c", [2, D, 4096])
    w_out_c = P.din("w_out_c", [2, D, D])
    zg = P.din("zg", [2, 16, 128, 1024])
    zmask = P.din("zmask", [128, 1024])
    ident_d = P.din("ident", [128, 128])
    jmat_d = P.din("jmat", [128, 128])
    rot_d = P.din("rot", [L, 2, 2, 128])
    dtab_d = P.din("dtab", [128, 4, 128])
    gam_d = P.din("gam", [128, 4, 4])
    cdec_d = P.din("cdec", [128, 4])
    jt_d = P.din("jt", [128, 128])
    xs = [P.dscr("xsA", [nseq, L, D]), P.dscr("xsB", [nseq, L, D])]
    OPs = P.dscr("OPs", [L, 512])
    YPs = P.dscr("YPs", [L, 512])
    SGs = P.dscr("SGs", [L, 1024])
    QTs = P.dscr("QTs", [nch, 128, 512], BF16)
    KRs = P.dscr("KRs", [L, 512], BF16)
    VBs = P.dscr("VBs", [L, 512], BF16)
    UBs = P.dscr("UBs", [L, 512], BF16)
    b_OPs, b_YPs, b_SGs, b_QTs, b_KRs, b_VBs, b_UBs = (Buf(n) for n in "OP YP SG QT KR VB UB".split())
    b_xs = [[Buf(f"xs{i}_{s}") for s in range(nseq)] for i in range(2)]
    b_yout = [Buf(f"yout{s}") for s in range(nseq)]
    b_xin = Buf("xin")

    ident_bf, b_ident = P.sb("ident_bf", [128, 128], BF16)
    jmat_bf, b_jmat = P.sb("jmat_bf", [128, 128], BF16)
    identf, b_identf = P.sb("identf", [128, 128])
    epsT, b_eps = P.sb("epsT", [128, 1])
    ss, b_ss = P.sb("ss", [128, 4])
    sd, b_sd = P.sb("sd", [128, 4])
    rstd, b_rstd = P.sb("rstd", [128, 4])
    scT, b_scT = P.sb("scT", [128, 8, nseq])
    cTs, b_cTs = P.sb("cTs", [128, 8, nseq])
    modT, b_modT = P.sb("modT", [128, 24, nseq])
    gsT, b_gsT = P.sb("gsT", [128, 8, nseq])
    ggT, b_ggT = P.sb("ggT", [128, 8, nseq])
    ggrow = [P.sb(f"ggrow{s}", [128, 1024]) for s in range(nseq)]
    vecs, b_vecs = P.sb("vecs", [128, 40])
    psb = [P.ps(f"ps{i}") for i in range(8)]
    ps = [p[0] for p in psb]
    bps = [p[1] for p in psb]

    P.dma("sp", identf[:], ident_d, writes=[b_identf])
    P.dma("pool", ident_bf[:], ident_d, writes=[b_ident])
    P.dma("pool", jmat_bf[:], jmat_d, writes=[b_jmat])
    P.memset("pool", epsT[:], EPS, writes=[b_eps])
    P.dma("sp", cTs[:], cT, writes=[b_cTs])
    P.act(scT[:], cTs[:], AF.Sigmoid, reads=[b_cTs], writes=[b_scT])
    P.tt("dve", scT[:], scT[:], cTs[:], ALU.mult, reads=[b_scT, b_cTs], writes=[b_scT])

    def rstd_from_ss(col, n_feat):
        P.act(sd[:, col:col + 1], ss[:, col:col + 1], AF.Sqrt, reads=[b_ss, b_eps], writes=[b_sd],
              scale=1.0 / n_feat, bias=epsT[:, 0:1])
        P.T.op("dve", lambda q: q.reciprocal(out=rstd[:, col:col + 1], in_=sd[:, col:col + 1]),
               reads=[b_sd], writes=[b_rstd])

    def adaln(li, st_unused):
      with contextlib.ExitStack() as st:
        wm = [st.enter_context(P.sbt(f"wm{j}_{li}", [128, 8, 128], F32)) for j in range(2)]
        bwm = [Buf("wm0"), Buf("wm1")]
        gbl = st.enter_context(P.sbt(f"gbl_{li}", [128, 8, 128], F32))
        b_gbl = Buf("gbl")
        P.dma("sp", vecs[:, 0:8], npreT[li], writes=[b_vecs])
        P.dma("sp", vecs[:, 8:16], npostT[li], writes=[b_vecs])
        P.dma("sp", vecs[:, 16:40], bmodT[li], writes=[b_vecs])
        wsrc = wmod[li].rearrange("(k p) n -> p k n", p=128)
        for j in range(24):
            sl = j % 2
            P.dma("sp", wm[sl][:], wsrc[:, :, j * 128:(j + 1) * 128], writes=[bwm[sl]])
            for k in range(8):
                P.mm(ps[7][:, j * nseq:(j + 1) * nseq], wm[sl][:, k, :], scT[:, k, :], k == 0, k == 7,
                     reads=[bwm[sl], b_scT], writes=[bps[7]], inc=(k == 7))
        psv = ps[7][:, 0:24 * nseq].rearrange("p (j s) -> p j s", s=nseq)
        P.tt("dve", modT[:], psv, vecs[:, 16:40].unsqueeze(2).to_broadcast([128, 24, nseq]), ALU.add,
             reads=[bps[7], b_vecs], writes=[b_modT])
        P.ts("dve", gsT[:], modT[:, 8:16, :], 1.0, None, ALU.add, None, reads=[b_modT], writes=[b_gsT])
        P.tt("dve", gsT[:], gsT[:], vecs[:, 0:8].unsqueeze(2).to_broadcast([128, 8, nseq]), ALU.mult,
             reads=[b_gsT, b_vecs], writes=[b_gsT])
        P.tt("dve", ggT[:], modT[:, 16:24, :], vecs[:, 8:16].unsqueeze(2).to_broadcast([128, 8, nseq]), ALU.mult,
             reads=[b_modT, b_vecs], writes=[b_ggT])
        for s in range(nseq):
            P.cp("dve", gbl[:], ggT[:, :, s:s + 1].to_broadcast([128, 8, 128]), reads=[b_ggT], writes=[b_gbl])
            for c in range(8):
                bk = 5 + c // 4
                P.mm(ps[bk][:, (c % 4) * 128:(c % 4 + 1) * 128], gbl[:, c, :], identf[:], True, True,
                     reads=[b_gbl, b_identf], writes=[bps[bk]], inc=(c % 4 == 3))
            P.cp("dve", ggrow[s][0][:, 0:512], ps[5][:], reads=[bps[5]], writes=[ggrow[s][1]])
            P.act(ggrow[s][0][:, 512:1024], ps[6][:], AF.Copy, reads=[bps[6]], writes=[ggrow[s][1]])

    def make_common(st, tag):
        C = {}
        C["xt"] = [st.enter_context(P.sbt(f"xt{j}_{tag}", [128, 1024], F32)) for j in range(2)]
        C["bxt"] = [Buf("xt0"), Buf("xt1")]
        C["xn"] = st.enter_context(P.sbt(f"xn_{tag}", [128, 1024], BF16))
        C["bxn"] = Buf("xn")
        C["junk"] = st.enter_context(P.sbt(f"junk_{tag}", [128, 1024], BF16))
        C["bjunk"] = Buf("junk")
        C["yt"] = st.enter_context(P.sbt(f"yt_{tag}", [128, 1024], F32))
        C["byt"] = Buf("yt")
        C["cnt"] = 0
        return C

    def prenorm(C, s, src, bsrc, n, hT, bhT, col0):
        sl = C["cnt"] % 2
        C["cnt"] += 1
        xt, bxt = C["xt"][sl], C["bxt"][sl]
        P.dma("sp", xt[:], src[s, n * 128:(n + 1) * 128, :], reads=[bsrc], writes=[bxt])
        P.act(C["yt"][:], xt[:], AF.Square, reads=[bxt], writes=[C["byt"]])
        P.T.op("dve", lambda q, yt_=C["yt"]: q.reduce_sum(out=ss[:, 0:1], in_=yt_[:], axis=mybir.AxisListType.X),
               reads=[C["byt"]], writes=[b_ss])
        rstd_from_ss(0, D)
        P.act(C["xn"][:], xt[:], AF.Copy, reads=[bxt, b_rstd], writes=[C["bxn"]], scale=rstd[:, 0:1])
        for b in range(2):
            for j in range(4):
                k = 4 * b + j
                P.mm(ps[b][:, j * 128:(j + 1) * 128], C["xn"][:, k * 128:(k + 1) * 128], ident_bf[:], True, True,
                     reads=[C["bxn"], b_ident], writes=[bps[b]], inc=(j == 3))
            for j in range(4):
                k = 4 * b + j
                o = hT[:, k, col0:col0 + 128]
                i_ = ps[b][:, j * 128:(j + 1) * 128]
                if j % 2 == 0:
                    P.act(o, i_, AF.Identity, reads=[bps[b], b_gsT, b_modT], writes=[bhT],
                          scale=gsT[:, k, s:s + 1], bias=modT[:, k, s:s + 1])
                else:
                    P.ts("dve", o, i_, gsT[:, k, s:s + 1], modT[:, k, s:s + 1], ALU.mult, ALU.add,
                         reads=[bps[b], b_gsT, b_modT], writes=[bhT])
        return sl

    def post(C, s, ypb, xt, bxt, dst, bdst, n):
        yt, byt = C["yt"], C["byt"]
        for h in range(2):
            P.act(yt[:, h * 512:(h + 1) * 512], ps[ypb[h]][:], AF.Square, reads=[bps[ypb[h]]], writes=[byt])
        P.T.op("dve", lambda q, yt_=yt: q.reduce_sum(out=ss[:, 3:4], in_=yt_[:], axis=mybir.AxisListType.X),
               reads=[byt], writes=[b_ss])
        rstd_from_ss(3, D)
        for h in range(2):
            P.act(yt[:, h * 512:(h + 1) * 512], ps[ypb[h]][:], AF.Copy, reads=[bps[ypb[h]], b_rstd], writes=[byt],
                  scale=rstd[:, 3:4])
        P.tt("dve", yt[:], yt[:], ggrow[s][0][:], ALU.mult, reads=[byt, ggrow[s][1]], writes=[byt])
        P.tt("pool", yt[:], yt[:], xt[:], ALU.add, reads=[byt, bxt], writes=[byt])
        P.dma("pool", dst[s, n * 128:(n + 1) * 128, :], yt[:], reads=[byt], writes=[bdst])

    def sincos(st, tag, phi, bphi, shape, out_sin, out_cos, bout):
        tf = st.enter_context(P.sbt(f"sc_tf_{tag}", shape, F32))
        ti = st.enter_context(P.sbt(f"sc_ti_{tag}", shape, I32))
        btf, bti = Buf("tf"), Buf("ti")
        for shift, o in ((0.0, out_sin), (0.5 * math.pi, out_cos)):
            P.ts("dve", tf[:], phi, shift, 1.0 / TWO_PI, ALU.add, ALU.mult, reads=[bphi], writes=[btf])
            P.cp("dve", ti[:], tf[:], reads=[btf], writes=[bti])
            P.cp("dve", tf[:], ti[:], reads=[bti], writes=[btf])
            P.stt("dve", tf[:], tf[:], -TWO_PI, phi, ALU.mult, ALU.add, reads=[btf, bphi], writes=[btf])
            P.ts("dve", tf[:], tf[:], shift, 0.999999, ALU.add, ALU.mult, reads=[btf], writes=[btf])
            P.act(o, tf[:], AF.Sin, reads=[btf], writes=[bout])

    def even_layer(li, src, bsrc, dst, bdst):
        j = li // 2
        with contextlib.ExitStack() as st:
            def S(name, shape, dt=F32):
                return st.enter_context(P.sbt(f"{name}_e{li}", list(shape), dt)), Buf(name)
            adaln(li, st)
            C = make_common(st, f"e{li}")
            win, b_win = S("win", [128, 8, 3072], BF16)
            wout, b_wout = S("wout", [128, 8, 1024], BF16)
            wglu, b_wglu = S("wglu", [128, 4, 512], BF16)
            drow, b_drow = S("drow", [128, 512])
            dtab, b_dtab = S("dtab", [128, 4, 128])
            gam, b_gam = S("gam", [128, 4, 4])
            cdec, b_cdec = S("cdec", [128, 4])
            jt, b_jt = S("jt", [128, 128])
            wsrc = w_in_ab[j].rearrange("(k p) n -> p k n", p=128)
            for k in range(8):
                P.dma("pool", win[:, k, :], wsrc[:, k, :], writes=[b_win])
            P.dma("pool", wout[:], w_out_ab[j].rearrange("(k p) n -> p k n", p=128), writes=[b_wout])
            P.dma("pool", wglu[:], w_glu[j].rearrange("(k p) n -> p k n", p=128), writes=[b_wglu])
            P.dma("sp", drow[:], ssm_d[j].partition_broadcast(128), writes=[b_drow])
            P.dma("sp", dtab[:], dtab_d, writes=[b_dtab])
            P.dma("sp", gam[:], gam_d, writes=[b_gam])
            P.dma("sp", cdec[:], cdec_d, writes=[b_cdec])
            P.dma("sp", jt[:], jt_d, writes=[b_jt])
            WB = [S(f"WB{r}", [128, 2048], BF16) for r in range(2)]
            WC = [S(f"WC{r}", [128, 512], BF16) for r in range(3)]
            COS, b_COS = S("COS", [128, 16, 128])
            SIN, b_SIN = S("SIN", [128, 16, 128])
            RHO0, b_RHO0 = S("RHO0", [128, 16, 128])
            Gc, b_Gc = S("Gc", [128, 2, 16])
            cin = [S(f"cin{i}", [128, 2, 16]) for i in range(2)]
            hT, b_hT = S("hT", [128, 8, 128], BF16)
            rot, b_rot = S("rot", [128, 2, 2, 128])
            tA, b_tA = S("tA", [128, 512])
            tB, b_tB = S("tB", [128, 512])
            tC, b_tC = S("tC", [128, 512])
            tD, b_tD = S("tD", [128, 512])
            qr, b_qr = S("qr", [128, 512], BF16)
            kr, b_kr = S("kr", [128, 512], BF16)
            vb, b_vb = S("vb", [128, 512], BF16)
            kf, b_kf = S("kf", [128, 512], BF16)
            QT, b_QT = S("QT", [128, 512], BF16)
            KT, b_KT = S("KT", [128, 512], BF16)
            STb, b_STb = S("STb", [128, 512], BF16)
            stf, b_stf = S("stf", [128, 512])
            stbf, b_stbf = S("stbf", [128, 512], BF16)
            sg, b_sg = S("sg", [128, 1024])
            du, b_du = S("du", [128, 512])
            ub, b_ub = S("ub", [128, 512], BF16)
            uT, b_uT = S("uT", [128, 4, 128], BF16)
            wre, b_wre = S("wre", [128, 512])
            wim, b_wim = S("wim", [128, 512])
            sre, b_sre = S("sre", [128, 512])
            sim, b_sim = S("sim", [128, 512])
            Pk = [S(f"Pk{i}", [128, 512], BF16) for i in range(4)]
            lst, b_lst = S("lst", [128, 8, 4])
            opt, b_opt = S("opt", [128, 512])
            ypt, b_ypt = S("ypt", [128, 512])
            oab, b_oab = S("oab", [128, 1024], BF16)
            oT, b_oT = S("oT", [128, 8, 128], BF16)
            hn, b_hn = S("hn", [128, 16])
            ygb, b_ygb = S("ygb", [128, 512], BF16)

            def s5_setup(d):
                with contextlib.ExitStack() as s2:
                    def S2(name, shape, dt=F32):
                        return s2.enter_context(P.sbt(f"{name}_e{li}d{d}", list(shape), dt)), Buf(name)
                    sm, b_sm = S2("sm", [128, 3, 16])
                    P.dma("sp", sm[:], s5_sm3[j, d].rearrange("t p s -> p t s"), writes=[b_sm])
                    dl, b_dl = S2("dl", [128, 16])
                    zr, b_zr = S2("zr", [128, 16])
                    zi, b_zi = S2("zi", [128, 16])
                    rho, b_rho = S2("rho", [128, 16])
                    P.act(dl[:], sm[:, 2, :], AF.Exp, reads=[b_sm], writes=[b_dl])
                    P.tt("dve", zr[:], sm[:, 0, :], dl[:], ALU.mult, reads=[b_sm, b_dl], writes=[b_zr])
                    P.tt("dve", zi[:], sm[:, 1, :], dl[:], ALU.mult, reads=[b_sm, b_dl], writes=[b_zi])
                    P.act(rho[:], zr[:], AF.Exp, reads=[b_zr], writes=[b_rho])
                    for g4 in range(4):
                        with contextlib.ExitStack() as s4:
                            phi = s4.enter_context(P.sbt(f"phi_e{li}d{d}g{g4}", [128, 4, 128], F32))
                            b_phi = Buf("phi")
                            P.tt("dve", phi[:], zi[:, 4 * g4:4 * g4 + 4].unsqueeze(2).to_broadcast([128, 4, 128]),
                                 jt[:].unsqueeze(1).to_broadcast([128, 4, 128]), ALU.mult, reads=[b_zi, b_jt], writes=[b_phi])
                            sincos(s4, f"t{li}{d}{g4}", phi[:], b_phi, [128, 4, 128], SIN[:, 4 * g4:4 * g4 + 4, :],
                                   COS[:, 4 * g4:4 * g4 + 4, :], b_COS)
                            T_barrier()
                    P.cp("dve", RHO0[:], rho[:].unsqueeze(2).to_broadcast([128, 16, 128]), reads=[b_rho], writes=[b_RHO0])
                    P.memset("dve", RHO0[:, :, 0:1], 0.0, writes=[b_RHO0])
                    ph2, b_ph2 = S2("ph2", [128, 16])
                    sn2, b_sn2 = S2("sn2", [128, 2, 16])
                    P.ts("dve", ph2[:], zi[:], 128.0, None, ALU.mult, None, reads=[b_zi], writes=[b_ph2])
                    sincos(s2, f"g{li}{d}", ph2[:], b_ph2, [128, 16], sn2[:, 1, :], sn2[:, 0, :], b_sn2)
                    P.tt("dve", Gc[:], sn2[:], rho[:].unsqueeze(1).to_broadcast([128, 2, 16]), ALU.mult,
                         reads=[b_sn2, b_rho], writes=[b_Gc])
                    T_barrier()
                for cc in range(8):
                  with contextlib.ExitStack() as s3:
                    def S3(name, shape, dt=F32):
                        return s3.enter_context(P.sbt(f"{name}_e{li}d{d}c{cc}", list(shape), dt)), Buf(name)
                    rp, b_rp = S3("rp", [128, 3, 256])
                    P.dma("sp", rp[:], s5_rep3[j, d][:, cc * 256:(cc + 1) * 256].partition_broadcast(128), writes=[b_rp])
                    r_dl, b_r_dl = S3("r_dl", [128, 256])
                    r_zr, b_r_zr = S3("r_zr", [128, 256])
                    r_zi, b_r_zi = S3("r_zi", [128, 256])
                    r_rho, b_r_rho = S3("r_rho", [128, 256])
                    r_sn, b_r_sn = S3("r_sn", [128, 2, 256])
                    P.act(r_dl[:], rp[:, 2, :], AF.Exp, reads=[b_rp], writes=[b_r_dl])
                    P.tt("dve", r_zr[:], rp[:, 0, :], r_dl[:], ALU.mult, reads=[b_rp, b_r_dl], writes=[b_r_zr])
                    P.tt("dve", r_zi[:], rp[:, 1, :], r_dl[:], ALU.mult, reads=[b_rp, b_r_dl], writes=[b_r_zi])
                    P.act(r_rho[:], r_zr[:], AF.Exp, reads=[b_r_zr], writes=[b_r_rho])
                    sincos(s3, f"r{li}{d}{cc}", r_zi[:], b_r_zi, [128, 256], r_sn[:, 1, :], r_sn[:, 0, :], b_r_sn)
                    P.tt("dve", r_sn[:], r_sn[:], r_rho[:].unsqueeze(1).to_broadcast([128, 2, 256]), ALU.mult,
                         reads=[b_r_sn, b_r_rho], writes=[b_r_sn])
                    P.ts("dve", r_sn[:, 0, :], r_sn[:, 0, :], -1.0, None, ALU.add, None, reads=[b_r_sn], writes=[b_r_sn])
                    P.tt("dve", r_dl[:], rp[:, 0, :], rp[:, 0, :], ALU.mult, reads=[b_rp], writes=[b_r_dl])
                    P.tt("dve", r_zr[:], rp[:, 1, :], rp[:, 1, :], ALU.mult, reads=[b_rp], writes=[b_r_zr])
                    P.tt("dve", r_dl[:], r_dl[:], r_zr[:], ALU.add, reads=[b_r_dl, b_r_zr], writes=[b_r_dl])
                    P.T.op("dve", lambda q, r_dl=r_dl: q.reciprocal(out=r_dl[:], in_=r_dl[:]), reads=[b_r_dl], writes=[b_r_dl])
                    P.tt("dve", r_zr[:], r_sn[:, 0, :], rp[:, 0, :], ALU.mult, reads=[b_r_sn, b_rp], writes=[b_r_zr])
                    P.tt("dve", r_rho[:], r_sn[:, 1, :], rp[:, 1, :], ALU.mult, reads=[b_r_sn, b_rp], writes=[b_r_rho])
                    P.tt("dve", r_zr[:], r_zr[:], r_rho[:], ALU.add, reads=[b_r_zr, b_r_rho], writes=[b_r_zr])
                    P.tt("dve", r_zr[:], r_zr[:], r_dl[:], ALU.mult, reads=[b_r_zr, b_r_dl], writes=[b_r_zr])
                    P.tt("dve", r_zi[:], r_sn[:, 1, :], rp[:, 0, :], ALU.mult, reads=[b_r_sn, b_rp], writes=[b_r_zi])
                    P.tt("dve", r_rho[:], r_sn[:, 0, :], rp[:, 1, :], ALU.mult, reads=[b_r_sn, b_rp], writes=[b_r_rho])
                    P.tt("dve", r_zi[:], r_zi[:], r_rho[:], ALU.subtract, reads=[b_r_zi, b_r_rho], writes=[b_r_zi])
                    P.tt("dve", r_zi[:], r_zi[:], r_dl[:], ALU.mult, reads=[b_r_zi, b_r_dl], writes=[b_r_zi])
                    bl, b_bl = S3("bl", [128, 2, 256])
                    P.dma("sp", bl[:], s5_bl[j, d][:, :, cc * 256:(cc + 1) * 256].rearrange("r p n -> p r n"), writes=[b_bl])
                    P.tt("dve", r_dl[:], r_zr[:], bl[:, 0, :], ALU.mult, reads=[b_r_zr, b_bl], writes=[b_r_dl])
                    P.tt("dve", r_rho[:], r_zi[:], bl[:, 1, :], ALU.mult, reads=[b_r_zi, b_bl], writes=[b_r_rho])
                    P.tt("dve", WB[0][0][:, cc * 256:(cc + 1) * 256], r_dl[:], r_rho[:], ALU.subtract, reads=[b_r_dl, b_r_rho], writes=[WB[0][1]])
                    P.tt("dve", r_dl[:], r_zr[:], bl[:, 1, :], ALU.mult, reads=[b_r_zr, b_bl], writes=[b_r_dl])
                    P.tt("dve", r_rho[:], r_zi[:], bl[:, 0, :], ALU.mult, reads=[b_r_zi, b_bl], writes=[b_r_rho])
                    P.tt("dve", WB[1][0][:, cc * 256:(cc + 1) * 256], r_dl[:], r_rho[:], ALU.add, reads=[b_r_dl, b_r_rho], writes=[WB[1][1]])

                    T_barrier()
                with contextlib.ExitStack() as s2:
                    def S2(name, shape, dt=F32):
                        return s2.enter_context(P.sbt(f"{name}_e{li}d{d}x", list(shape), dt)), Buf(name)
                    cl, b_cl = S2("cl", [128, 2, 512])
                    P.dma("sp", cl[:], s5_cl[j, d].rearrange("r p n -> p r n"), writes=[b_cl])
                    P.cp("dve", WC[0][0][:], cl[:, 0, :], reads=[b_cl], writes=[WC[0][1]])
                    P.ts("dve", WC[1][0][:], cl[:, 0, :], -1.0, None, ALU.mult, None, reads=[b_cl], writes=[WC[1][1]])
                    P.ts("dve", WC[2][0][:], cl[:, 1, :], -1.0, None, ALU.mult, None, reads=[b_cl], writes=[WC[2][1]])
                    P.memset("dve", cin[0][0][:], 0.0, writes=[cin[0][1]])
                    T_barrier()

            def s5_chunk(ci, rev):
                cur, nxt = cin[ci % 2], cin[(ci + 1) % 2]
                for g in range(4):
                    for t in range(4):
                        s_ = 4 * g + t
                        P.mm(ps[3][:, t * 128:(t + 1) * 128], WB[0][0][:, s_ * 128:(s_ + 1) * 128], uT[:, g, :], True, True,
                             reads=[WB[0][1], b_uT], writes=[bps[3]], inc=False)
                        P.mm(ps[4][:, t * 128:(t + 1) * 128], WB[1][0][:, s_ * 128:(s_ + 1) * 128], uT[:, g, :], True, True,
                             reads=[WB[1][1], b_uT], writes=[bps[4]], inc=(t == 3))
                    b_both = [bps[3], bps[4]]
                    Cg = COS[:, 4 * g:4 * g + 4, :].rearrange("p a b -> p (a b)")
                    Sg = SIN[:, 4 * g:4 * g + 4, :].rearrange("p a b -> p (a b)")
                    Rg = RHO0[:, 4 * g:4 * g + 4, :].rearrange("p a b -> p (a b)")
                    P.tt("dve", tA[:], ps[3][:], Cg, ALU.mult, reads=[bps[3], b_COS], writes=[b_tA])
                    P.tt("dve", tB[:], ps[4][:], Sg, ALU.mult, reads=[bps[4], b_COS], writes=[b_tB])
                    P.tt("pool", wre[:], tA[:], tB[:], ALU.add, reads=[b_tA, b_tB], writes=[b_wre])
                    P.tt("dve", tC[:], ps[4][:], Cg, ALU.mult, reads=[bps[4], b_COS], writes=[b_tC])
                    P.tt("dve", tD[:], ps[3][:], Sg, ALU.mult, reads=[bps[3], b_COS], writes=[b_tD])
                    P.tt("pool", wim[:], tC[:], tD[:], ALU.subtract, reads=[b_tC, b_tD], writes=[b_wim])
                    w3 = wre[:].rearrange("p (a b) -> p a b", a=4)
                    P.tt("pool", w3[:, :, 0:1], w3[:, :, 0:1], cur[0][:, 0, 4 * g:4 * g + 4].unsqueeze(2), ALU.add,
                         reads=[b_wre, cur[1]], writes=[b_wre])
                    w3i = wim[:].rearrange("p (a b) -> p a b", a=4)
                    P.tt("pool", w3i[:, :, 0:1], w3i[:, :, 0:1], cur[0][:, 1, 4 * g:4 * g + 4].unsqueeze(2), ALU.add,
                         reads=[b_wim, cur[1]], writes=[b_wim])
                    P.T.op("dve", lambda q, Rg=Rg: q.tensor_tensor_scan(out=sre[:], data0=Rg, data1=wre[:], initial=0.0,
                                                                   op0=ALU.mult, op1=ALU.add),
                           reads=[b_RHO0, b_wre], writes=[b_sre])
                    P.T.op("dve", lambda q, Rg=Rg: q.tensor_tensor_scan(out=sim[:], data0=Rg, data1=wim[:], initial=0.0,
                                                                   op0=ALU.mult, op1=ALU.add),
                           reads=[b_RHO0, b_wim], writes=[b_sim])
                    s3 = sre[:].rearrange("p (a b) -> p a b", a=4)[:, :, 127:128]
                    s3i = sim[:].rearrange("p (a b) -> p a b", a=4)[:, :, 127:128]
                    Gr = Gc[:, 0, 4 * g:4 * g + 4].unsqueeze(2)
                    Gi = Gc[:, 1, 4 * g:4 * g + 4].unsqueeze(2)
                    l4 = lst[:]
                    P.tt("pool", l4[:, 0, :].unsqueeze(2), s3, Gr, ALU.mult, reads=[b_sre, b_Gc], writes=[b_lst])
                    P.tt("pool", l4[:, 1, :].unsqueeze(2), s3i, Gi, ALU.mult, reads=[b_sim, b_Gc], writes=[b_lst])
                    P.tt("pool", l4[:, 2, :].unsqueeze(2), s3i, Gr, ALU.mult, reads=[b_sim, b_Gc], writes=[b_lst])
                    P.tt("pool", l4[:, 3, :].unsqueeze(2), s3, Gi, ALU.mult, reads=[b_sre, b_Gc], writes=[b_lst])
                    P.tt("pool", nxt[0][:, 0, 4 * g:4 * g + 4], l4[:, 0, :], l4[:, 1, :], ALU.subtract,
                         reads=[b_lst], writes=[nxt[1]])
                    P.tt("pool", nxt[0][:, 1, 4 * g:4 * g + 4], l4[:, 2, :], l4[:, 3, :], ALU.add,
                         reads=[b_lst], writes=[nxt[1]])
                    def pv(ap):
                        v = ap.rearrange("p (a b) -> p a b", a=4)
                        return v[:, :, ::-1] if rev else v
                    C3 = COS[:, 4 * g:4 * g + 4, :]
                    S3 = SIN[:, 4 * g:4 * g + 4, :]
                    sr3 = sre[:].rearrange("p (a b) -> p a b", a=4)
                    si3 = sim[:].rearrange("p (a b) -> p a b", a=4)
                    P.tt("dve", pv(Pk[0][0][:]), sr3, C3, ALU.mult, reads=[b_sre, b_COS], writes=[Pk[0][1]])
                    P.tt("dve" if rev else "pool", pv(Pk[1][0][:]), si3, S3, ALU.mult, reads=[b_sim, b_COS], writes=[Pk[1][1]])
                    P.tt("dve", pv(Pk[2][0][:]), sr3, S3, ALU.mult, reads=[b_sre, b_COS], writes=[Pk[2][1]])
                    P.tt("dve" if rev else "pool", pv(Pk[3][0][:]), si3, C3, ALU.mult, reads=[b_sim, b_COS], writes=[Pk[3][1]])
                    for t in range(4):
                        s_ = 4 * g + t
                        o = ps[7][:, 32 * s_:32 * s_ + 32]
                        wsl = slice(32 * s_, 32 * s_ + 32)
                        tsl = slice(128 * t, 128 * t + 128)
                        P.mm(o, Pk[0][0][:, tsl], WC[0][0][:, wsl], True, False, reads=[Pk[0][1], WC[0][1]], writes=[bps[7]], inc=False)
                        P.mm(o, Pk[1][0][:, tsl], WC[1][0][:, wsl], False, False, reads=[Pk[1][1], WC[1][1]], writes=[bps[7]], inc=False)
                        P.mm(o, Pk[2][0][:, tsl], WC[2][0][:, wsl], False, False, reads=[Pk[2][1], WC[2][1]], writes=[bps[7]], inc=False)
                        P.mm(o, Pk[3][0][:, tsl], WC[2][0][:, wsl], False, True, reads=[Pk[3][1], WC[2][1]], writes=[bps[7]], inc=(t == 3))

            def ret_state_update(kbuf, b_kbuf, gcol):
                P.tt("pool", kf[:].rearrange("p (h e) -> p h e", h=4), kbuf[:].rearrange("p (h e) -> p h e", h=4),
                     gam[:, gcol, :].unsqueeze(2).to_broadcast([128, 4, 128]), ALU.mult,
                     reads=[b_kbuf, b_gam], writes=[b_kf])
                for h in range(4):
                    hs = slice(h * 128, (h + 1) * 128)
                    P.mm(ps[5][:, hs], kf[:, hs], vb[:, hs], True, True, reads=[b_kf, b_vb], writes=[bps[5]], inc=(h == 3))
                P.tt("pool", stf[:].rearrange("p (h e) -> p h e", h=4), stf[:].rearrange("p (h e) -> p h e", h=4),
                     cdec[:].unsqueeze(2).to_broadcast([128, 4, 128]), ALU.mult, reads=[b_stf, b_cdec], writes=[b_stf])
                P.tt("dve", stf[:], stf[:], ps[5][:], ALU.add, reads=[b_stf, bps[5]], writes=[b_stf])
                P.act(stbf[:], stf[:], AF.Copy, reads=[b_stf], writes=[b_stbf])

            for s in range(nseq):
                s5_setup(0)
                P.memset("dve", stf[:], 0.0, writes=[b_stf])
                P.memset("pool", stbf[:], 0.0, writes=[b_stbf])
                for n in range(nch if KSTOP not in ('setup',) else 0):
                    tsl = slice(n * 128, (n + 1) * 128)
                    prenorm(C, s, src, bsrc[s], n, hT, b_hT, 0)
                    P.dma("sp", rot[:], rot_d[tsl], writes=[b_rot])
                    for b in range(3):
                        for k in range(8):
                            P.mm(ps[2 + b][:], hT[:, k, :], win[:, k, b * 512:(b + 1) * 512], k == 0, k == 7,
                                 reads=[b_hT, b_win], writes=[bps[2 + b]], inc=(k == 7))
                    for (zb, tbl, outb, b_outb) in ((2, 0, qr, b_qr), (3, 1, kr, b_kr)):
                        z3 = ps[zb][:].rearrange("p (h e) -> p h e", h=4)
                        a3 = tA[:].rearrange("p (h e) -> p h e", h=4)
                        b3 = tB[:].rearrange("p (h e) -> p h e", h=4)
                        P.tt("dve", a3, z3, rot[:, tbl, 0, :].unsqueeze(1).to_broadcast([128, 4, 128]), ALU.mult,
                             reads=[bps[zb], b_rot], writes=[b_tA])
                        P.tt("dve", b3[:, :, 0:64], z3[:, :, 64:128], rot[:, tbl, 1, 0:64].unsqueeze(1).to_broadcast([128, 4, 64]),
                             ALU.mult, reads=[bps[zb], b_rot], writes=[b_tB])
                        P.tt("dve", b3[:, :, 64:128], z3[:, :, 0:64], rot[:, tbl, 1, 64:128].unsqueeze(1).to_broadcast([128, 4, 64]),
                             ALU.mult, reads=[bps[zb], b_rot], writes=[b_tB])
                        P.tt("pool", outb[:], tA[:], tB[:], ALU.add, reads=[b_tA, b_tB], writes=[b_outb])
                    P.act(vb[:], ps[4][:], AF.Copy, reads=[bps[4]], writes=[b_vb])
                    for h in range(4):
                        hs = slice(h * 128, (h + 1) * 128)
                        P.mm(ps[0][:, hs], qr[:, hs], ident_bf[:], True, True, reads=[b_qr, b_ident], writes=[bps[0]], inc=(h == 3))
                    for h in range(4):
                        hs = slice(h * 128, (h + 1) * 128)
                        P.mm(ps[1][:, hs], kr[:, hs], ident_bf[:], True, True, reads=[b_kr, b_ident], writes=[bps[1]], inc=(h == 3))
                    P.act(QT[:], ps[0][:], AF.Copy, reads=[bps[0]], writes=[b_QT])
                    P.cp("dve", KT[:], ps[1][:], reads=[bps[1]], writes=[b_KT])
                    for h in range(4):
                        hs = slice(h * 128, (h + 1) * 128)
                        P.mm(ps[2][:, hs], KT[:, hs], QT[:, hs], True, True, reads=[b_KT, b_QT], writes=[bps[2]], inc=(h == 3))
                    P.tt("dve", STb[:], ps[2][:], dtab[:].rearrange("p h e -> p (h e)"), ALU.mult,
                         reads=[bps[2], b_dtab], writes=[b_STb])
                    for h in range(4):
                        hs = slice(h * 128, (h + 1) * 128)
                        P.mm(ps[3][:, hs], STb[:, hs], vb[:, hs], True, True, reads=[b_STb, b_vb], writes=[bps[3]], inc=(h == 3))
                    for h in range(4):
                        hs = slice(h * 128, (h + 1) * 128)
                        P.mm(ps[6][:, hs], QT[:, hs], stbf[:, hs], True, True, reads=[b_QT, b_stbf], writes=[bps[6]], inc=(h == 3))
                    P.tt("dve", tC[:].rearrange("p (h e) -> p h e", h=4), ps[6][:].rearrange("p (h e) -> p h e", h=4),
                         gam[:, 0, :].unsqueeze(2).to_broadcast([128, 4, 128]), ALU.mult, reads=[bps[6], b_gam], writes=[b_tC])
                    P.tt("dve", opt[:], tC[:], ps[3][:], ALU.add, reads=[b_tC, bps[3]], writes=[b_opt])
                    P.dma("pool", OPs[tsl], opt[:], reads=[b_opt], writes=[b_OPs])
                    ret_state_update(kr, b_kr, 2)
                    P.dma("pool", QTs[n], QT[:], reads=[b_QT], writes=[b_QTs])
                    P.dma("pool", KRs[tsl], kr[:], reads=[b_kr], writes=[b_KRs])
                    P.dma("pool", VBs[tsl], vb[:], reads=[b_vb], writes=[b_VBs])
                    for b in range(3):
                        for k in range(8):
                            P.mm(ps[2 + b][:], hT[:, k, :], win[:, k, (3 + b) * 512:(4 + b) * 512], k == 0, k == 7,
                                 reads=[b_hT, b_win], writes=[bps[2 + b]], inc=(k == 7))
                    P.act(sg[:, 0:512], ps[2][:], AF.Silu, reads=[bps[2]], writes=[b_sg])
                    P.act(sg[:, 512:1024], ps[4][:], AF.Silu, reads=[bps[4]], writes=[b_sg])
                    P.dma("pool", SGs[tsl], sg[:], reads=[b_sg], writes=[b_SGs])
                    P.tt("dve", du[:], ps[3][:], drow[:], ALU.mult, reads=[bps[3], b_drow], writes=[b_du])
                    P.act(ub[:], ps[3][:], AF.Copy, reads=[bps[3]], writes=[b_ub])
                    P.dma("pool", UBs[tsl], ub[:], reads=[b_ub], writes=[b_UBs])
                    for q_ in range(4):
                        qs = slice(q_ * 128, (q_ + 1) * 128)
                        P.mm(ps[0][:, qs], ub[:, qs], ident_bf[:], True, True, reads=[b_ub, b_ident], writes=[bps[0]], inc=(q_ == 3))
                    P.act(uT[:].rearrange("p a b -> p (a b)"), ps[0][:], AF.Copy, reads=[bps[0]], writes=[b_uT])
                    s5_chunk(n, False)
                    P.tt("dve", ypt[:], ps[7][:], du[:], ALU.add, reads=[bps[7], b_du], writes=[b_ypt])
                    P.dma("pool", YPs[tsl], ypt[:], reads=[b_ypt], writes=[b_YPs])
                s5_setup(1)
                P.memset("dve", stf[:], 0.0, writes=[b_stf])
                P.memset("pool", stbf[:], 0.0, writes=[b_stbf])
                for ci, n in enumerate(range(nch - 1, -1, -1) if KSTOP == 'all' else []):
                    tsl = slice(n * 128, (n + 1) * 128)
                    sl = C["cnt"] % 2
                    C["cnt"] += 1
                    xt, bxt = C["xt"][sl], C["bxt"][sl]
                    P.dma("sp", xt[:], src[s, tsl, :], reads=[bsrc[s]], writes=[bxt])
                    P.dma("sp", opt[:], OPs[tsl], reads=[b_OPs], writes=[b_opt])
                    P.dma("sp", ypt[:], YPs[tsl], reads=[b_YPs], writes=[b_ypt])
                    P.dma("sp", sg[:], SGs[tsl], reads=[b_SGs], writes=[b_sg])
                    P.dma("sp", QT[:], QTs[n], reads=[b_QTs], writes=[b_QT])
                    P.dma("sp", kr[:], KRs[tsl], reads=[b_KRs], writes=[b_kr])
                    P.dma("sp", vb[:], VBs[tsl], reads=[b_VBs], writes=[b_vb])
                    P.dma("sp", ub[:], UBs[tsl], reads=[b_UBs], writes=[b_ub])
                    for h in range(4):
                        hs = slice(h * 128, (h + 1) * 128)
                        P.mm(ps[6][:, hs], QT[:, hs], stbf[:, hs], True, True, reads=[b_QT, b_stbf], writes=[bps[6]], inc=(h == 3))
                    P.tt("dve", tC[:].rearrange("p (h e) -> p h e", h=4), ps[6][:].rearrange("p (h e) -> p h e", h=4),
                         gam[:, 1, :].unsqueeze(2).to_broadcast([128, 4, 128]), ALU.mult, reads=[bps[6], b_gam], writes=[b_tC])
                    P.tt("pool", opt[:], opt[:], tC[:], ALU.add, reads=[b_opt, b_tC], writes=[b_opt])
                    ret_state_update(kr, b_kr, 3)
                    o3 = opt[:].rearrange("p (h e) -> p h e", h=4)
                    P.T.op("dve", lambda q, o3=o3: q.reduce_sum(out=hn[:, 0:4], in_=o3, axis=mybir.AxisListType.X),
                           reads=[b_opt], writes=[b_hn])
                    P.tt("pool", tA[:], opt[:], opt[:], ALU.mult, reads=[b_opt], writes=[b_tA])
                    P.T.op("dve", lambda q: q.reduce_sum(out=hn[:, 4:8], in_=tA[:].rearrange("p (h e) -> p h e", h=4),
                                                       axis=mybir.AxisListType.X), reads=[b_tA], writes=[b_hn])
                    P.ts("dve", hn[:, 0:8], hn[:, 0:8], 1.0 / 128.0, None, ALU.mult, None, reads=[b_hn], writes=[b_hn])
                    P.tt("dve", hn[:, 8:12], hn[:, 0:4], hn[:, 0:4], ALU.mult, reads=[b_hn], writes=[b_hn])
                    P.tt("dve", hn[:, 4:8], hn[:, 4:8], hn[:, 8:12], ALU.subtract, reads=[b_hn], writes=[b_hn])
                    P.act(hn[:, 8:12], hn[:, 4:8], AF.Sqrt, reads=[b_hn, b_eps], writes=[b_hn], bias=epsT[:, 0:1])
                    P.T.op("dve", lambda q: q.reciprocal(out=hn[:, 12:16], in_=hn[:, 8:12]), reads=[b_hn], writes=[b_hn])
                    a3 = tA[:].rearrange("p (h e) -> p h e", h=4)
                    P.tt("dve", a3, o3, hn[:, 0:4].unsqueeze(2).to_broadcast([128, 4, 128]), ALU.subtract,
                         reads=[b_opt, b_hn], writes=[b_tA])
                    P.tt("dve", a3, a3, hn[:, 12:16].unsqueeze(2).to_broadcast([128, 4, 128]), ALU.mult,
                         reads=[b_tA, b_hn], writes=[b_tA])
                    P.tt("pool", oab[:, 0:512], tA[:], sg[:, 0:512], ALU.mult, reads=[b_tA, b_sg], writes=[b_oab])
                    for q_ in range(4):
                        qs = slice(q_ * 128, (q_ + 1) * 128)
                        P.mm(ps[0][:, qs], ub[:, qs], jmat_bf[:], True, True, reads=[b_ub, b_jmat], writes=[bps[0]], inc=(q_ == 3))
                    P.act(uT[:].rearrange("p a b -> p (a b)"), ps[0][:], AF.Copy, reads=[bps[0]], writes=[b_uT])
                    s5_chunk(ci, True)
                    P.tt("dve", ypt[:], ypt[:], ps[7][:], ALU.add, reads=[b_ypt, bps[7]], writes=[b_ypt])
                    P.tt("pool", tB[:], ypt[:], ypt[:], ALU.mult, reads=[b_ypt], writes=[b_tB])
                    P.ts("dve", tB[:], tB[:], 0.044715, 1.0, ALU.mult, ALU.add, reads=[b_tB], writes=[b_tB])
                    P.tt("pool", tB[:], tB[:], ypt[:], ALU.mult, reads=[b_tB, b_ypt], writes=[b_tB])
                    P.act(tB[:], tB[:], AF.Sigmoid, reads=[b_tB], writes=[b_tB], scale=1.5957691216057308)
                    P.tt("dve", tD[:], ypt[:], tB[:], ALU.mult, reads=[b_ypt, b_tB], writes=[b_tD])
                    P.act(ygb[:], tD[:], AF.Copy, reads=[b_tD], writes=[b_ygb])
                    for q_ in range(4):
                        qs = slice(q_ * 128, (q_ + 1) * 128)
                        P.mm(ps[0][:, qs], ygb[:, qs], ident_bf[:], True, True, reads=[b_ygb, b_ident], writes=[bps[0]], inc=(q_ == 3))
                    P.act(uT[:].rearrange("p a b -> p (a b)"), ps[0][:], AF.Copy, reads=[bps[0]], writes=[b_uT])
                    for q_ in range(4):
                        P.mm(ps[1][:], uT[:, q_, :], wglu[:, q_, :], q_ == 0, q_ == 3, reads=[b_uT, b_wglu], writes=[bps[1]], inc=(q_ == 3))
                    P.act(tB[:], ps[1][:], AF.Sigmoid, reads=[bps[1]], writes=[b_tB])
                    P.tt("dve", tD[:], tD[:], tB[:], ALU.mult, reads=[b_tD, b_tB], writes=[b_tD])
                    P.tt("pool", oab[:, 512:1024], tD[:], sg[:, 512:1024], ALU.mult, reads=[b_tD, b_sg], writes=[b_oab])
                    for b in range(2):
                        for jj in range(4):
                            k = 4 * b + jj
                            P.mm(ps[b][:, jj * 128:(jj + 1) * 128], oab[:, k * 128:(k + 1) * 128], ident_bf[:], True, True,
                                 reads=[b_oab, b_ident], writes=[bps[b]], inc=(jj == 3))
                    P.act(oT[:, 0:4, :].rearrange("p a b -> p (a b)"), ps[0][:], AF.Copy, reads=[bps[0]], writes=[b_oT])
                    P.cp("dve", oT[:, 4:8, :].rearrange("p a b -> p (a b)"), ps[1][:], reads=[bps[1]], writes=[b_oT])
                    for hh in range(2):
                        for k in range(8):
                            P.mm(ps[2 + hh][:], oT[:, k, :], wout[:, k, hh * 512:(hh + 1) * 512], k == 0, k == 7,
                                 reads=[b_oT, b_wout], writes=[bps[2 + hh]], inc=(k == 7))
                    post(C, s, (2, 3), xt, bxt, dst, bdst[s], n)
            T_barrier()

    def T_barrier():
        evs = []
        for e in T.engs.values():
            for k in range(len(e.dsems)):
                evs.append((e.dsems[k], e.dvals[k]))
            if e.sem is not None and e.cnt > 0 and not e.pending:
                evs.append((e.sem, e.cnt))
        for e in T.engs.values():
            for ev in evs:
                T._wait(e, ev)

    def odd_layer(li, src, bsrc, dst, bdst):
        raise NotImplementedError

    P.odd_layer_hook = None
    cur, bcur = x_in, [b_xin] * nseq
    for idx, li in enumerate(layers):
        last = idx == len(layers) - 1
        dstt, bd = (y_out, b_yout) if last else (xs[idx % 2], b_xs[idx % 2])
        if li % 2 == 0:
            even_layer(li, cur, bcur, dstt, bd)
        else:
            ODD_IMPL(P, locals(), li, cur, bcur, dstt, bd)
        cur, bcur = dstt, bd
    T.finish()
    T.replay()
    P.es.close()
    return P


def ODD_IMPL(P, env, li, src, bsrc, dst, bdst):
    nc, T = P.nc, P.T
    E = env
    ps, bps = E["ps"], E["bps"]
    nseq, L = P.nseq, P.L
    ident_bf, b_ident = E["ident_bf"], E["b_ident"]
    j = li // 2
    rows = L // 64
    nblk = L // 256

    def rs(r):
        return min(max(r - 4, 0), rows - 8)

    with contextlib.ExitStack() as st:
        def S(name, shape, dt=F32):
            return st.enter_context(P.sbt(f"{name}_o{li}", list(shape), dt)), Buf(name)
        E["adaln"](li, None)
        C = E["make_common"](st, f"o{li}")
        winc, b_winc = S("winc", [128, 8, 4096], BF16)
        woutc, b_woutc = S("woutc", [128, 8, 1024], BF16)
        Z, b_Z = S("Z", [128, 16, 1024], BF16)
        hT, b_hT = S("hT", [128, 8, 256], BF16)
        KT = [S(f"KT{i}", [128, 8, 256], BF16) for i in range(3)]
        V = [S(f"V{i}", [128, 2, 8, 3, 64], BF16) for i in range(3)]
        QT = [S(f"QT{i}", [128, 8, 256], BF16) for i in range(2)]
        GT = [S(f"GT{i}", [128, 8, 256], BF16) for i in range(2)]
        pT = [S(f"pT{i}", [128, 256], BF16) for i in range(3)]
        og, b_og = S("og", [128, 8, 256], BF16)
        rd, b_rd = S("rd", [128, 256])
        rb, b_rb = S("rb", [128, 256])
        t2, b_t2 = S("t2", [128, 256])
        onesf, b_onesf = S("onesf", [128, 64])
        zer, b_zer = S("zer", [128, 256], BF16)
        wsrc = E["w_in_c"][j].rearrange("(k p) n -> p k n", p=128)
        for k in range(8):
            P.dma("pool", winc[:, k, :], wsrc[:, k, :], writes=[b_winc])
        P.dma("pool", woutc[:], E["w_out_c"][j].rearrange("(k p) n -> p k n", p=128), writes=[b_woutc])
        P.memset("dve", onesf[:], 1.0, writes=[b_onesf])
        P.memset("dve", zer[:], 0.0, writes=[b_zer])
        for i in range(3):
            P.memset("pool", V[i][0][:], 1.0, writes=[V[i][1]])
        with contextlib.ExitStack() as s2:
            zm = s2.enter_context(P.sbt(f"zm_o{li}", [128, 1024], F32)); b_zm = Buf("zm")
            zt = [s2.enter_context(P.sbt(f"zt{i}_o{li}", [128, 512], F32)) for i in range(2)]
            b_zt = [Buf("zt0"), Buf("zt1")]
            P.dma("sp", zm[:], E["zmask"], writes=[b_zm])
            for h in range(16):
                for hf in range(2):
                    P.dma("sp", zt[hf][:], E["zg"][j, h][:, hf * 512:(hf + 1) * 512], writes=[b_zt[hf]])
                    P.tt("dve", Z[:, h, hf * 512:(hf + 1) * 512], zt[hf][:], zm[:, hf * 512:(hf + 1) * 512], ALU.add,
                         reads=[b_zt[hf], b_zm], writes=[b_Z])
            E["T_barrier"]()

        def proj(s, b):
            ring = b % 3
            sl = b % 2
            for t in range(2):
                E["prenorm"](C, s, src, bsrc[s], 2 * b + t, hT, b_hT, t * 128)
            cnt = 0
            for (col0, kind) in ((0, "q"), (1024, "k"), (3072, "g")):
                for hp2 in range(4):
                    bank = 2 + cnt % 2
                    cnt += 1
                    for hh in range(2):
                        hp = 2 * hp2 + hh
                        for k in range(8):
                            P.mm(ps[bank][:, hh * 256:(hh + 1) * 256], winc[:, k, col0 + hp * 128:col0 + (hp + 1) * 128], hT[:, k, :],
                                 k == 0, k == 7, reads=[b_winc, b_hT], writes=[bps[bank]], inc=(k == 7 and hh == 1))
                    if kind == "q":
                        P.act(QT[sl][0][:, 2 * hp2:2 * hp2 + 2, :].rearrange("p a b -> p (a b)"), ps[bank][:], AF.Copy,
                              reads=[bps[bank]], writes=[QT[sl][1]], scale=0.125)
                    elif kind == "k":
                        P.cp("dve", KT[ring][0][:, 2 * hp2:2 * hp2 + 2, :].rearrange("p a b -> p (a b)"), ps[bank][:],
                             reads=[bps[bank]], writes=[KT[ring][1]])
                    else:
                        P.act(GT[sl][0][:, 2 * hp2:2 * hp2 + 2, :].rearrange("p a b -> p (a b)"), ps[bank][:], AF.Silu,
                              reads=[bps[bank]], writes=[GT[sl][1]])
            for t in range(2):
                for half in range(2):
                    bank = 2 + cnt % 2
                    cnt += 1
                    for k in range(8):
                        P.mm(ps[bank][:], hT[:, k, t * 128:(t + 1) * 128], winc[:, k, 2048 + half * 512:2048 + (half + 1) * 512],
                             k == 0, k == 7, reads=[b_hT, b_winc], writes=[bps[bank]], inc=(k == 7))
                    src4 = ps[bank][:].rearrange("p (a c d) -> p a c d", a=4, c=2)
                    P.cp("dve", V[ring][0][:, t, 4 * half:4 * half + 4, 0:3:2, :], src4, reads=[bps[bank]], writes=[V[ring][1]])

        def attn(s, b):
            R = 4 * b
            sl = b % 2
            lo = rs(R) & ~1
            hi = (rs(R + 3) + 7) & ~1
            tiles = []
            for r0 in range(lo, hi + 1, 2):
                qs = [r for r in range(R, R + 4) if (rs(r) <= r0 + 1 and r0 <= rs(r) + 7)]
                if not qs:
                    continue
                qa, qb = qs[0], qs[-1]
                partial = []
                for r in qs:
                    for a in range(2):
                        if not (rs(r) <= r0 + a <= rs(r) + 7):
                            partial.append((r, a))
                tiles.append((r0, qa, qb, partial))
            pcount = 0
            for h in range(16):
                hp, base = h // 2, 64 * (h % 2)
                P.mm(ps[6][:, 0:256], zer[:, 0:128], zer[:], True, False, reads=[b_zer], writes=[bps[6]], inc=False)
                for ti, (r0, qa, qb, partial) in enumerate(tiles):
                    kb = r0 // 4
                    tt_ = (r0 % 4) // 2
                    kring = kb % 3
                    c0, c1 = (qa - R) * 64, (qb - R + 1) * 64
                    z0, z1 = (qa - r0 + 7) * 64, (qb - r0 + 8) * 64
                    bank = 4 + pcount % 2
                    pt, b_pt = pT[pcount % 3]
                    pcount += 1
                    P.mm(ps[bank][:, c0:c1], KT[kring][0][base:base + 64, hp, tt_ * 128:(tt_ + 1) * 128],
                         QT[sl][0][base:base + 64, hp, c0:c1], True, False,
                         reads=[KT[kring][1], QT[sl][1]], writes=[bps[bank]], inc=False)
                    P.mm(ps[bank][:, c0:c1], ident_bf[:], Z[:, h, z0:z1], False, True,
                         reads=[b_ident, b_Z], writes=[bps[bank]], inc=True)
                    P.act(pt[:, c0:c1], ps[bank][:, c0:c1], AF.Exp, reads=[bps[bank]], writes=[b_pt])
                    for (r, a) in partial:
                        cc = (r - R) * 64
                        P.memset("pool", pt[64 * a:64 * a + 64, cc:cc + 64], 0.0, writes=[b_pt])
                    va = V[kring][0][:, tt_, hp, 0:2, :] if h % 2 == 0 else V[kring][0][:, tt_, hp, 1:3, :]
                    last = ti == len(tiles) - 1
                    P.mm(ps[6][:, c0:c1], va.rearrange("p a b -> p (a b)"), pt[:, c0:c1], False, last,
                         reads=[V[kring][1], b_pt], writes=[bps[6]], inc=last)
                if h % 2 == 0:
                    dr, orow = 64, 0
                else:
                    dr, orow = 0, 64
                P.T.op("dve", lambda q, dr=dr: q.reciprocal(out=rd[dr:dr + 1, :], in_=ps[6][dr:dr + 1, 0:256]),
                       reads=[bps[6]], writes=[b_rd])
                P.mm(ps[7][orow:orow + 64, 0:256], onesf[dr:dr + 1, 0:64], rd[dr:dr + 1, :], True, True,
                     reads=[b_onesf, b_rd], writes=[bps[7]])
                P.act(rb[orow:orow + 64, :], ps[7][orow:orow + 64, 0:256], AF.Copy, reads=[bps[7]], writes=[b_rb])
                P.tt("dve", t2[orow:orow + 64, :], ps[6][orow:orow + 64, 0:256], rb[orow:orow + 64, :], ALU.mult,
                     reads=[bps[6], b_rb], writes=[b_t2])
                P.tt("pool", og[orow:orow + 64, hp, :], t2[orow:orow + 64, :], GT[sl][0][orow:orow + 64, hp, :], ALU.mult,
                     reads=[b_t2, GT[sl][1]], writes=[b_og])
            for t in range(2):
                n = 2 * b + t
                sx = C["cnt"] % 2
                C["cnt"] += 1
                xt, bxt = C["xt"][sx], C["bxt"][sx]
                P.dma("sp", xt[:], src[s, n * 128:(n + 1) * 128, :], reads=[bsrc[s]], writes=[bxt])
                for hh in range(2):
                    for k in range(8):
                        P.mm(ps[2 + hh][:], og[:, k, t * 128:(t + 1) * 128], woutc[:, k, hh * 512:(hh + 1) * 512], k == 0, k == 7,
                             reads=[b_og, b_woutc], writes=[bps[2 + hh]], inc=(k == 7))
                E["post"](C, s, (2, 3), xt, bxt, dst, bdst[s], n)

        for s in range(nseq):
            proj(s, 0)
            for b in range(nblk):
                if b + 1 < nblk:
                    proj(s, b + 1)
                attn(s, b)
        E["T_barrier"]()


def _common_inputs(p, L):
    f32 = np.float32
    m = {}
    def T8(a):
        return np.ascontiguousarray(a.reshape(a.shape[0], -1, 128).transpose(0, 2, 1)).astype(f32)
    m["npreT"] = T8(p["norm_pre"])
    m["npostT"] = T8(p["norm_post"])
    m["wmod"] = np.ascontiguousarray(p["w_mod"], dtype=f32)
    m["bmodT"] = T8(p["b_mod"])
    m["w_in_ab"] = np.ascontiguousarray(p["w_in_ab"], dtype=f32)
    m["w_out_ab"] = np.ascontiguousarray(p["w_out_ab"], dtype=f32)
    m["w_glu"] = np.ascontiguousarray(p["ssm_w_glu"], dtype=f32)
    m["ssm_d"] = np.ascontiguousarray(p["ssm_d"], dtype=f32)
    sm3, rep3, bl, cl = [], [], [], []
    for j in range(2):
        a, b, c, d = _s5_layout(p["ssm_a_re"][j], p["ssm_a_im"][j], p["ssm_log_step"][j], p["ssm_b_re"][j],
                                p["ssm_b_im"][j], p["ssm_c_re"][j], p["ssm_c_im"][j])
        sm3.append(a); rep3.append(b); bl.append(c); cl.append(d)
    m["s5_sm3"] = np.stack(sm3).astype(f32)
    m["s5_rep3"] = np.stack(rep3).astype(f32)
    m["s5_bl"] = np.stack(bl).astype(f32)
    m["s5_cl"] = np.stack(cl).astype(f32)
    m["w_in_c"] = np.ascontiguousarray(p["w_in_c"], dtype=f32)
    m["w_out_c"] = np.ascontiguousarray(p["w_out_c"], dtype=f32)
    zs = []
    for j in range(2):
        z, mask = _na_layout(np.asarray(p["na_rel_bias"][j], dtype=f32))
        zs.append(z)
    m["zg"] = np.stack(zs).astype(f32)
    m["zmask"] = mask
    m["ident"] = np.eye(128, dtype=f32)
    m["jmat"] = np.eye(128, dtype=f32)[::-1].copy()
    m["rot"] = _rot_tables(L)
    dt, gam, cd = _ret_consts()
    m["dtab"] = dt
    m["gam"] = np.ascontiguousarray(gam.transpose(0, 1, 2))
    m["cdec"] = cd
    m["jt"] = np.broadcast_to(np.arange(128, dtype=f32)[None, :], (128, 128)).copy()
    return m


_PROG_CACHE = {}


def run_cores(xs_per_core, cs_per_core, params, layers):
    nseq, L, _ = xs_per_core[0].shape
    key = (nseq, L, tuple(layers))
    if key not in _PROG_CACHE:
        _PROG_CACHE[key] = build(nseq, L, list(layers))
    P = _PROG_CACHE[key]
    com = _common_inputs(params, L)
    in_maps = []
    for x, c in zip(xs_per_core, cs_per_core):
        m = dict(com)
        m["x_in"] = np.ascontiguousarray(x, dtype=np.float32)
        m["cT"] = np.ascontiguousarray(c.reshape(nseq, 8, 128).transpose(2, 1, 0), dtype=np.float32)
        in_maps.append(m)
    res = run_bass_kernel_spmd(P.nc, in_maps, core_ids=list(range(len(in_maps))))
    return [np.asarray(r["y_out"]) for r in res.results]


def kernel(**inputs):
    p = {k: np.asarray(v) for k, v in inputs.items()}
    xp, xsamp = p["x_prompt"], p["x_sample"]
    cp, cs = p["c_prompt"], p["c_sample"]
    seqs = [xp[i] for i in range(4)] + [xsamp[i] for i in range(8)]
    cvs = [cp[i] for i in range(4)] + [cs[i] for i in range(8)]
    slots = [(c, 8 + c if c < 4 else c) for c in range(8)]
    xs_pc = [np.stack([seqs[a], seqs[b]]) for a, b in slots]
    cs_pc = [np.stack([cvs[a], cvs[b]]) for a, b in slots]
    outs = run_cores(xs_pc, cs_pc, p, [0, 1, 2, 3])
    res = [None] * 12
    for c, (a, b) in enumerate(slots):
        res[a] = outs[c][0]
        if c < 4:
            res[b] = outs[c][1]
    y_prompt = np.stack(res[0:4]).astype(np.float32)
    y_sample = np.stack(res[4:12]).astype(np.float32)
    return (y_prompt, y_sample)
```

```python
import contextlib
import math
import os
KSTOP = os.environ.get('KSTOP', 'all')
import numpy as np
import concourse.bass as bass
import concourse.mybir as mybir
from concourse.bass_utils import run_bass_kernel_spmd

F32 = mybir.dt.float32
BF16 = mybir.dt.bfloat16
I32 = mybir.dt.int32
ALU = mybir.AluOpType
AF = mybir.ActivationFunctionType

D = 1024
EPS = 1e-6
TWO_PI = 2.0 * math.pi


class Buf:
    __slots__ = ("name", "w", "r")

    def __init__(self, name):
        self.name = name
        self.w = []
        self.r = []


class Eng:
    def __init__(self, name):
        self.name = name
        self.ops = []
        self.known = {}
        self.sem = None
        self.cnt = 0
        self.pending = False
        self.dsems = []
        self.dvals = []
        self.dptr = 0
        self.own = set()


EPOCH = 30000
NDSEM = 10


class Tracker:
    def __init__(self, nc):
        self.nc = nc
        self.engs = {n: Eng(n) for n in ("pe", "act", "dve", "pool", "sp")}
        self.sems = []
        for e in self.engs.values():
            if e.name != "sp":
                e.sem = self._newsem(e.name)
                e.own.add(e.sem)
        self.n_ops = 0

    def _newsem(self, nm):
        h = self.nc.alloc_semaphore(name=f"{nm}_{len(self.sems)}")
        self.sems.append(h)
        return len(self.sems) - 1

    def _wait(self, e, ev):
        s, v = ev
        if e.known.get(s, 0) >= v:
            return
        if s in e.own:
            if e.name == "pe":
                return
            if s == e.sem and v > e.cnt:
                return
        e.known[s] = v
        sem = self.sems[s]
        e.ops.append(lambda q, sem=sem, v=v: q.wait_ge(sem, v))

    def _deps(self, e, reads, writes):
        for b in reads:
            for ev in b.w:
                self._wait(e, ev)
        for b in writes:
            for ev in b.w:
                self._wait(e, ev)
            for ev in b.r:
                self._wait(e, ev)

    def _commit(self, ev, reads, writes):
        for b in reads:
            for i, (s0, v0) in enumerate(b.r):
                if s0 == ev[0]:
                    b.r[i] = (s0, max(v0, ev[1]))
                    break
            else:
                b.r.append(ev)
        for b in writes:
            b.w = [ev]
            b.r = []

    def op(self, eng, fn, reads=(), writes=(), inc=True):
        e = self.engs[eng]
        self.n_ops += 1
        self._deps(e, reads, writes)
        if e.cnt >= EPOCH and inc and not e.pending:
            e.sem = self._newsem(e.name)
            e.cnt = 0
            e.own.add(e.sem)
        if inc:
            e.cnt += 1
            sem = self.sems[e.sem]
            e.ops.append(lambda q, fn=fn, sem=sem: fn(q).then_inc(sem, 1))
            ev = (e.sem, e.cnt)
            e.pending = False
        else:
            e.ops.append(lambda q, fn=fn: fn(q))
            ev = (e.sem, e.cnt + 1)
            e.pending = True
        self._commit(ev, reads, writes)

    def dma(self, eng, out, in_, reads=(), writes=()):
        e = self.engs[eng]
        self.n_ops += 1
        self._deps(e, reads, writes)
        if len(e.dsems) < NDSEM:
            e.dsems.append(self._newsem(e.name + "d"))
            e.dvals.append(0)
            k = len(e.dsems) - 1
        else:
            k = e.dptr
            e.dptr = (e.dptr + 1) % NDSEM
            self._wait(e, (e.dsems[k], e.dvals[k]))
            if e.dvals[k] >= EPOCH * 16:
                e.dsems[k] = self._newsem(e.name + "d")
                e.dvals[k] = 0
        e.dvals[k] += 16
        sem = self.sems[e.dsems[k]]
        e.ops.append(lambda q, out=out, in_=in_, sem=sem: q.dma_start(out=out, in_=in_).then_inc(sem, 16))
        ev = (e.dsems[k], e.dvals[k])
        self._commit(ev, reads, writes)
        return ev

    def finish(self):
        sp = self.engs["sp"]
        for e in self.engs.values():
            for k in range(len(e.dsems)):
                self._wait(sp, (e.dsems[k], e.dvals[k]))
            if e.sem is not None and e.cnt > 0:
                self._wait(sp, (e.sem, e.cnt))

    def replay(self):
        nc = self.nc
        E = self.engs
        with nc.Block() as block:
            @block.tensor
            def _(q):
                for f in E["pe"].ops:
                    f(q)

            @block.scalar
            def _(q):
                for f in E["act"].ops:
                    f(q)

            @block.vector
            def _(q):
                for f in E["dve"].ops:
                    f(q)

            @block.gpsimd
            def _(q):
                for f in E["pool"].ops:
                    f(q)

            @block.sync
            def _(q):
                for f in E["sp"].ops:
                    f(q)


RET_H = 4
NA_H = 16
GW = 64


def _ret_consts():
    f32 = np.float32
    h = np.arange(RET_H, dtype=f32)
    log_g = np.log1p(-np.exp2(-5.0 - h)).astype(f32)
    pos = np.arange(128, dtype=f32)
    dt = np.exp(np.abs(pos[:, None] - pos[None, :])[:, None, :] * log_g[None, :, None]).astype(f32)
    gam = np.zeros((128, 4, RET_H), f32)
    gam[:, 0, :] = np.exp(pos[:, None] * log_g[None])
    gam[:, 1, :] = np.exp((127.0 - pos)[:, None] * log_g[None])
    gam[:, 2, :] = np.exp((128.0 - pos)[:, None] * log_g[None])
    gam[:, 3, :] = np.exp((pos + 1.0)[:, None] * log_g[None])
    cdec = np.exp(128.0 * log_g).astype(f32)
    cd = np.broadcast_to(cdec[None, :], (128, RET_H)).copy()
    return dt, gam, cd


def _rot_tables(L):
    f32 = np.float32
    inv = (10000.0 ** (-np.arange(0, 128, 2, dtype=f32) / 128.0)).astype(f32)
    ang = (np.arange(L, dtype=f32)[:, None] * inv[None, :]).astype(f32)
    c = np.cos(ang).astype(f32)
    s = np.sin(ang).astype(f32)
    rq = np.zeros((L, 2, 128), f32)
    rq[:, 0, :64] = c
    rq[:, 0, 64:] = c
    rq[:, 1, :64] = -s
    rq[:, 1, 64:] = s
    rk = (rq * f32(128.0 ** -0.5)).astype(f32)
    return np.stack([rq, rk], axis=1).copy()


def _na_layout(relb):
    a = np.arange(2)[:, None, None, None]
    k = np.arange(64)[None, :, None, None]
    m = np.arange(-7, 9)[None, None, :, None]
    c = np.arange(64)[None, None, None, :]
    dr = np.clip(a - m + 7, 0, 14)
    dc = np.clip(k - c + 15, 0, 30)
    dr_b, dc_b = np.broadcast_arrays(dr, dc)
    z = relb[:, dr_b, dc_b]
    z = z.reshape(16, 128, 16 * 64).astype(np.float32)
    cs = np.clip(np.arange(64) - 8, 0, 48)
    kk = np.arange(64)[:, None]
    valid = (kk >= cs[None, :]) & (kk < cs[None, :] + 16)
    mask = np.where(valid, 0.0, -30000.0).astype(np.float32)
    mask = np.broadcast_to(mask[None, :, None, :], (2, 64, 16, 64)).reshape(128, 1024).copy()
    return z, mask


def _s5_layout(a_re, a_im, ls, b_re, b_im, c_re, c_im):
    f32 = np.float32
    def sm(a):
        return a.reshape(2, 16, 2, 64).transpose(0, 2, 3, 1).reshape(2, 128, 16).astype(f32)
    lsx = np.broadcast_to(ls[:, :, None], (2, 32, 64))
    sm3 = np.stack([sm(a_re), sm(a_im), sm(lsx)], axis=1).copy()
    rep3 = np.stack([a_re.reshape(2, 2048), a_im.reshape(2, 2048), lsx.reshape(2, 2048)], axis=1).astype(f32).copy()
    bl = np.zeros((2, 2, 128, 16, 2, 64), f32)
    cl = np.zeros((2, 2, 128, 16, 2, 16), f32)
    for s in range(16):
        for g1 in range(2):
            g = 2 * s + g1
            r0 = 32 * (s % 4) + 16 * g1
            for ri, b in enumerate((b_re, b_im)):
                bl[:, ri, r0:r0 + 16, s, g1, :] = b[:, g].transpose(0, 2, 1)
            for ri, c in enumerate((c_re, c_im)):
                cl[:, ri, 64 * g1:64 * g1 + 64, s, g1, :] = c[:, g].transpose(0, 2, 1)
    return sm3, rep3, bl.reshape(2, 2, 128, 16 * 128), cl.reshape(2, 2, 128, 16 * 32)


class Prog:
    def __init__(self, nseq, L, layers):
        self.nseq, self.L, self.layers = nseq, L, layers
        self.nch = L // 128
        nc = self.nc = bass.Bass("TRN2", target_bir_lowering=False)
        self.T = Tracker(nc)
        self.es = contextlib.ExitStack()
        self.dram = {}
        self.bufs = {}

    def sbt(self, name, shape, dt=F32):
        self._uid = getattr(self, '_uid', 0) + 1
        return self.nc.sbuf_tensor(f"{name}_u{self._uid}", list(shape), dt)

    def din(self, name, shape, dt=F32):
        t = self.nc.dram_tensor(name, list(shape), dt, kind="ExternalInput").ap()
        self.dram[name] = t
        return t

    def dout(self, name, shape, dt=F32):
        t = self.nc.dram_tensor(name, list(shape), dt, kind="ExternalOutput").ap()
        self.dram[name] = t
        return t

    def dscr(self, name, shape, dt=F32):
        t = self.nc.dram_tensor(name, list(shape), dt, kind="Internal").ap()
        self.dram[name] = t
        return t

    def sb(self, name, shape, dt=F32):
        t = self.es.enter_context(self.sbt(name, list(shape), dt))
        b = Buf(name)
        return t, b

    def ps(self, name):
        t = self.es.enter_context(self.nc.psum_tensor(name, [128, 512], F32))
        return t, Buf(name)

    def mm(self, out, lhsT, rhs, start, stop, reads, writes, inc=True):
        self.T.op("pe", lambda q: q.matmul(out, lhsT=lhsT, rhs=rhs, start=start, stop=stop),
                  reads=reads, writes=writes, inc=inc)

    def act(self, out, in_, func, reads, writes, scale=1.0, bias=None, accum=None):
        kw = {}
        if bias is not None:
            kw["bias"] = bias
        if accum is not None:
            kw["accum_out"] = accum
        self.T.op("act", lambda q: q.activation(out=out, in_=in_, func=func, scale=scale, **kw),
                  reads=reads, writes=writes)

    def tt(self, eng, out, in0, in1, op, reads, writes):
        self.T.op(eng, lambda q: q.tensor_tensor(out=out, in0=in0, in1=in1, op=op), reads=reads, writes=writes)

    def ts(self, eng, out, in0, s1, s2, op0, op1, reads, writes):
        if s2 is None:
            self.T.op(eng, lambda q: q.tensor_scalar(out=out, in0=in0, scalar1=s1, scalar2=None, op0=op0),
                      reads=reads, writes=writes)
        else:
            self.T.op(eng, lambda q: q.tensor_scalar(out=out, in0=in0, scalar1=s1, scalar2=s2, op0=op0, op1=op1),
                      reads=reads, writes=writes)

    def stt(self, eng, out, in0, scalar, in1, op0, op1, reads, writes):
        self.T.op(eng, lambda q: q.scalar_tensor_tensor(out=out, in0=in0, scalar=scalar, in1=in1, op0=op0, op1=op1),
                  reads=reads, writes=writes)

    def cp(self, eng, out, in_, reads, writes):
        self.T.op(eng, lambda q: q.tensor_copy(out=out, in_=in_), reads=reads, writes=writes)

    def memset(self, eng, ap, val, writes):
        self.T.op(eng, lambda q: q.memset(ap, val), reads=(), writes=writes)

    def dma(self, eng, out, in_, reads=(), writes=()):
        return self.T.dma(eng, out, in_, reads=reads, writes=writes)


def build(nseq, L, layers):
    P = Prog(nseq, L, layers)
    nc, T = P.nc, P.T
    nch = L // 128
    NL = 4
    x_in = P.din("x_in", [nseq, L, D])
    y_out = P.dout("y_out", [nseq, L, D])
    cT = P.din("cT", [128, 8, nseq])
    npreT = P.din("npreT", [NL, 128, 8])
    npostT = P.din("npostT", [NL, 128, 8])
    wmod = P.din("wmod", [NL, D, 3 * D])
    bmodT = P.din("bmodT", [NL, 128, 24])
    w_in_ab = P.din("w_in_ab", [2, D, 3072])
    w_out_ab = P.din("w_out_ab", [2, D, D])
    w_glu = P.din("w_glu", [2, 512, 512])
    ssm_d = P.din("ssm_d", [2, 512])
    s5_sm3 = P.din("s5_sm3", [2, 2, 3, 128, 16])
    s5_rep3 = P.din("s5_rep3", [2, 2, 3, 2048])
    s5_bl = P.din("s5_bl", [2, 2, 2, 128, 2048])
    s5_cl = P.din("s5_cl", [2, 2, 2, 128, 512])
    w_in_c = P.din("w_in_c", [2, D, 4096])
    w_out_c = P.din("w_out_c", [2, D, D])
    zg = P.din("zg", [2, 16, 128, 1024])
    zmask = P.din("zmask", [128, 1024])
    ident_d = P.din("ident", [128, 128])
    jmat_d = P.din("jmat", [128, 128])
    rot_d = P.din("rot", [L, 2, 2, 128])
    dtab_d = P.din("dtab", [128, 4, 128])
    gam_d = P.din("gam", [128, 4, 4])
    cdec_d = P.din("cdec", [128, 4])
    jt_d = P.din("jt", [128, 128])
    xs = [P.dscr("xsA", [nseq, L, D]), P.dscr("xsB", [nseq, L, D])]
    OPs = P.dscr("OPs", [L, 512])
    YPs = P.dscr("YPs", [L, 512])
    SGs = P.dscr("SGs", [L, 1024])
    QTs = P.dscr("QTs", [nch, 128, 512], BF16)
    KRs = P.dscr("KRs", [L, 512], BF16)
    VBs = P.dscr("VBs", [L, 512], BF16)
    UBs = P.dscr("UBs", [L, 512], BF16)
    b_OPs, b_YPs, b_SGs, b_QTs, b_KRs, b_VBs, b_UBs = (Buf(n) for n in "OP YP SG QT KR VB UB".split())
    b_xs = [[Buf(f"xs{i}_{s}") for s in range(nseq)] for i in range(2)]
    b_yout = [Buf(f"yout{s}") for s in range(nseq)]
    b_xin = Buf("xin")

    ident_bf, b_ident = P.sb("ident_bf", [128, 128], BF16)
    jmat_bf, b_jmat = P.sb("jmat_bf", [128, 128], BF16)
    identf, b_identf = P.sb("identf", [128, 128])
    epsT, b_eps = P.sb("epsT", [128, 1])
    ss, b_ss = P.sb("ss", [128, 4])
    sd, b_sd = P.sb("sd", [128, 4])
    rstd, b_rstd = P.sb("rstd", [128, 4])
    scT, b_scT = P.sb("scT", [128, 8, nseq])
    cTs, b_cTs = P.sb("cTs", [128, 8, nseq])
    modT, b_modT = P.sb("modT", [128, 24, nseq])
    gsT, b_gsT = P.sb("gsT", [128, 8, nseq])
    ggT, b_ggT = P.sb("ggT", [128, 8, nseq])
    ggrow = [P.sb(f"ggrow{s}", [128, 1024]) for s in range(nseq)]
    vecs, b_vecs = P.sb("vecs", [128, 40])
    psb = [P.ps(f"ps{i}") for i in range(8)]
    ps = [p[0] for p in psb]
    bps = [p[1] for p in psb]

    P.dma("sp", identf[:], ident_d, writes=[b_identf])
    P.dma("pool", ident_bf[:], ident_d, writes=[b_ident])
    P.dma("pool", jmat_bf[:], jmat_d, writes=[b_jmat])
    P.memset("pool", epsT[:], EPS, writes=[b_eps])
    P.dma("sp", cTs[:], cT, writes=[b_cTs])
    P.act(scT[:], cTs[:], AF.Sigmoid, reads=[b_cTs], writes=[b_scT])
    P.tt("dve", scT[:], scT[:], cTs[:], ALU.mult, reads=[b_scT, b_cTs], writes=[b_scT])

    def rstd_from_ss(col, n_feat):
        P.act(sd[:, col:col + 1], ss[:, col:col + 1], AF.Sqrt, reads=[b_ss, b_eps], writes=[b_sd],
              scale=1.0 / n_feat, bias=epsT[:, 0:1])
        P.T.op("dve", lambda q: q.reciprocal(out=rstd[:, col:col + 1], in_=sd[:, col:col + 1]),
               reads=[b_sd], writes=[b_rstd])

    def adaln(li, st_unused):
      with contextlib.ExitStack() as st:
        wm = [st.enter_context(P.sbt(f"wm{j}_{li}", [128, 8, 128], F32)) for j in range(2)]
        bwm = [Buf("wm0"), Buf("wm1")]
        gbl = st.enter_context(P.sbt(f"gbl_{li}", [128, 8, 128], F32))
        b_gbl = Buf("gbl")
        P.dma("sp", vecs[:, 0:8], npreT[li], writes=[b_vecs])
        P.dma("sp", vecs[:, 8:16], npostT[li], writes=[b_vecs])
        P.dma("sp", vecs[:, 16:40], bmodT[li], writes=[b_vecs])
        wsrc = wmod[li].rearrange("(k p) n -> p k n", p=128)
        for j in range(24):
            sl = j % 2
            P.dma("sp", wm[sl][:], wsrc[:, :, j * 128:(j + 1) * 128], writes=[bwm[sl]])
            for k in range(8):
                P.mm(ps[7][:, j * nseq:(j + 1) * nseq], wm[sl][:, k, :], scT[:, k, :], k == 0, k == 7,
                     reads=[bwm[sl], b_scT], writes=[bps[7]], inc=(k == 7))
        psv = ps[7][:, 0:24 * nseq].rearrange("p (j s) -> p j s", s=nseq)
        P.tt("dve", modT[:], psv, vecs[:, 16:40].unsqueeze(2).to_broadcast([128, 24, nseq]), ALU.add,
             reads=[bps[7], b_vecs], writes=[b_modT])
        P.ts("dve", gsT[:], modT[:, 8:16, :], 1.0, None, ALU.add, None, reads=[b_modT], writes=[b_gsT])
        P.tt("dve", gsT[:], gsT[:], vecs[:, 0:8].unsqueeze(2).to_broadcast([128, 8, nseq]), ALU.mult,
             reads=[b_gsT, b_vecs], writes=[b_gsT])
        P.tt("dve", ggT[:], modT[:, 16:24, :], vecs[:, 8:16].unsqueeze(2).to_broadcast([128, 8, nseq]), ALU.mult,
             reads=[b_modT, b_vecs], writes=[b_ggT])
        for s in range(nseq):
            P.cp("dve", gbl[:], ggT[:, :, s:s + 1].to_broadcast([128, 8, 128]), reads=[b_ggT], writes=[b_gbl])
            for c in range(8):
                bk = 5 + c // 4
                P.mm(ps[bk][:, (c % 4) * 128:(c % 4 + 1) * 128], gbl[:, c, :], identf[:], True, True,
                     reads=[b_gbl, b_identf], writes=[bps[bk]], inc=(c % 4 == 3))
            P.cp("dve", ggrow[s][0][:, 0:512], ps[5][:], reads=[bps[5]], writes=[ggrow[s][1]])
            P.act(ggrow[s][0][:, 512:1024], ps[6][:], AF.Copy, reads=[bps[6]], writes=[ggrow[s][1]])

    def make_common(st, tag):
        C = {}
        C["xt"] = [st.enter_context(P.sbt(f"xt{j}_{tag}", [128, 1024], F32)) for j in range(2)]
        C["bxt"] = [Buf("xt0"), Buf("xt1")]
        C["xn"] = st.enter_context(P.sbt(f"xn_{tag}", [128, 1024], BF16))
        C["bxn"] = Buf("xn")
        C["junk"] = st.enter_context(P.sbt(f"junk_{tag}", [128, 1024], BF16))
        C["bjunk"] = Buf("junk")
        C["yt"] = st.enter_context(P.sbt(f"yt_{tag}", [128, 1024], F32))
        C["byt"] = Buf("yt")
        C["cnt"] = 0
        return C

    def prenorm_parts(C, s, src, bsrc, n, hT, bhT, col0):
        sl = C["cnt"] % 2
        C["cnt"] += 1
        xt, bxt = C["xt"][sl], C["bxt"][sl]

        def p1():
            P.dma("sp", xt[:], src[s, n * 128:(n + 1) * 128, :], reads=[bsrc], writes=[bxt])
            P.act(C["yt"][:], xt[:], AF.Square, reads=[bxt], writes=[C["byt"]])
            P.T.op("dve", lambda q, yt_=C["yt"]: q.reduce_sum(out=ss[:, 0:1], in_=yt_[:], axis=mybir.AxisListType.X),
                   reads=[C["byt"]], writes=[b_ss])
            P.act(sd[:, 0:1], ss[:, 0:1], AF.Sqrt, reads=[b_ss, b_eps], writes=[b_sd], scale=1.0 / D, bias=epsT[:, 0:1])

        def p2():
            P.T.op("dve", lambda q: q.reciprocal(out=rstd[:, 0:1], in_=sd[:, 0:1]), reads=[b_sd], writes=[b_rstd])
            P.act(C["xn"][:], xt[:], AF.Copy, reads=[bxt, b_rstd], writes=[C["bxn"]], scale=rstd[:, 0:1])
            for b in range(2):
                for j in range(4):
                    k = 4 * b + j
                    P.mm(ps[b][:, j * 128:(j + 1) * 128], C["xn"][:, k * 128:(k + 1) * 128], ident_bf[:], True, True,
                         reads=[C["bxn"], b_ident], writes=[bps[b]], inc=(j == 3))

        def p3():
            for b in range(2):
                for j in range(4):
                    k = 4 * b + j
                    o = hT[:, k, col0:col0 + 128]
                    i_ = ps[b][:, j * 128:(j + 1) * 128]
                    if j % 2 == 0:
                        P.act(o, i_, AF.Identity, reads=[bps[b], b_gsT, b_modT], writes=[bhT],
                              scale=gsT[:, k, s:s + 1], bias=modT[:, k, s:s + 1])
                    else:
                        P.ts("dve", o, i_, gsT[:, k, s:s + 1], modT[:, k, s:s + 1], ALU.mult, ALU.add,
                             reads=[bps[b], b_gsT, b_modT], writes=[bhT])
        return p1, p2, p3

    def prenorm(C, s, src, bsrc, n, hT, bhT, col0):
        p1, p2, p3 = prenorm_parts(C, s, src, bsrc, n, hT, bhT, col0)
        p1()
        p2()
        p3()

    def post(C, s, ypb, xt, bxt, dst, bdst, n):
        yt, byt = C["yt"], C["byt"]
        for h in range(2):
            P.act(yt[:, h * 512:(h + 1) * 512], ps[ypb[h]][:], AF.Square, reads=[bps[ypb[h]]], writes=[byt])
        P.T.op("dve", lambda q, yt_=yt: q.reduce_sum(out=ss[:, 3:4], in_=yt_[:], axis=mybir.AxisListType.X),
               reads=[byt], writes=[b_ss])
        rstd_from_ss(3, D)
        for h in range(2):
            P.act(yt[:, h * 512:(h + 1) * 512], ps[ypb[h]][:], AF.Copy, reads=[bps[ypb[h]], b_rstd], writes=[byt],
                  scale=rstd[:, 3:4])
        P.tt("dve", yt[:], yt[:], ggrow[s][0][:], ALU.mult, reads=[byt, ggrow[s][1]], writes=[byt])
        P.tt("pool", yt[:], yt[:], xt[:], ALU.add, reads=[byt, bxt], writes=[byt])
        P.dma("pool", dst[s, n * 128:(n + 1) * 128, :], yt[:], reads=[byt], writes=[bdst])

    def sincos(st, tag, phi, bphi, shape, out_sin, out_cos, bout):
        tf = st.enter_context(P.sbt(f"sc_tf_{tag}", shape, F32))
        ti = st.enter_context(P.sbt(f"sc_ti_{tag}", shape, I32))
        btf, bti = Buf("tf"), Buf("ti")
        for shift, o in ((0.0, out_sin), (0.5 * math.pi, out_cos)):
            P.ts("dve", tf[:], phi, shift, 1.0 / TWO_PI, ALU.add, ALU.mult, reads=[bphi], writes=[btf])
            P.cp("dve", ti[:], tf[:], reads=[btf], writes=[bti])
            P.cp("dve", tf[:], ti[:], reads=[bti], writes=[btf])
            P.stt("dve", tf[:], tf[:], -TWO_PI, phi, ALU.mult, ALU.add, reads=[btf, bphi], writes=[btf])
            P.ts("dve", tf[:], tf[:], shift, 0.999999, ALU.add, ALU.mult, reads=[btf], writes=[btf])
            P.act(o, tf[:], AF.Sin, reads=[btf], writes=[bout])

    def even_layer(li, src, bsrc, dst, bdst):
        j = li // 2
        with contextlib.ExitStack() as st:
            def S(name, shape, dt=F32):
                return st.enter_context(P.sbt(f"{name}_e{li}", list(shape), dt)), Buf(name)
            adaln(li, st)
            C = make_common(st, f"e{li}")
            win, b_win = S("win", [128, 8, 3072], BF16)
            wout, b_wout = S("wout", [128, 8, 1024], BF16)
            wglu, b_wglu = S("wglu", [128, 4, 512], BF16)
            drow, b_drow = S("drow", [128, 512])
            dtab, b_dtab = S("dtab", [128, 4, 128])
            gam, b_gam = S("gam", [128, 4, 4])
            cdec, b_cdec = S("cdec", [128, 4])
            jt, b_jt = S("jt", [128, 128])
            wsrc = w_in_ab[j].rearrange("(k p) n -> p k n", p=128)
            for k in range(8):
                P.dma("pool", win[:, k, :], wsrc[:, k, :], writes=[b_win])
            P.dma("pool", wout[:], w_out_ab[j].rearrange("(k p) n -> p k n", p=128), writes=[b_wout])
            P.dma("pool", wglu[:], w_glu[j].rearrange("(k p) n -> p k n", p=128), writes=[b_wglu])
            P.dma("sp", drow[:], ssm_d[j].partition_broadcast(128), writes=[b_drow])
            P.dma("sp", dtab[:], dtab_d, writes=[b_dtab])
            P.dma("sp", gam[:], gam_d, writes=[b_gam])
            P.dma("sp", cdec[:], cdec_d, writes=[b_cdec])
            P.dma("sp", jt[:], jt_d, writes=[b_jt])
            WB = [S(f"WB{r}", [128, 2048], BF16) for r in range(2)]
            WC = [S(f"WC{r}", [128, 512], BF16) for r in range(3)]
            COS, b_COS = S("COS", [128, 16, 128])
            SIN, b_SIN = S("SIN", [128, 16, 128])
            RHO0, b_RHO0 = S("RHO0", [128, 16, 128])
            Gc, b_Gc = S("Gc", [128, 2, 16])
            cin = [S(f"cin{i}", [128, 2, 16]) for i in range(2)]
            stf, b_stf = S("stf", [128, 512])
            stbf, b_stbf = S("stbf", [128, 512], BF16)
            hn, b_hn = S("hn", [128, 16])
            lst, b_lst = S("lst", [128, 8, 16])
            lastb = [S(f"lastb{i}", [128, 2, 16]) for i in range(2)]

            def s5_setup(d):
                with contextlib.ExitStack() as s2:
                    def S2(name, shape, dt=F32):
                        return s2.enter_context(P.sbt(f"{name}_e{li}d{d}", list(shape), dt)), Buf(name)
                    sm, b_sm = S2("sm", [128, 3, 16])
                    P.dma("sp", sm[:], s5_sm3[j, d].rearrange("t p s -> p t s"), writes=[b_sm])
                    dl, b_dl = S2("dl", [128, 16])
                    zr, b_zr = S2("zr", [128, 16])
                    zi, b_zi = S2("zi", [128, 16])
                    rho, b_rho = S2("rho", [128, 16])
                    P.act(dl[:], sm[:, 2, :], AF.Exp, reads=[b_sm], writes=[b_dl])
                    P.tt("dve", zr[:], sm[:, 0, :], dl[:], ALU.mult, reads=[b_sm, b_dl], writes=[b_zr])
                    P.tt("dve", zi[:], sm[:, 1, :], dl[:], ALU.mult, reads=[b_sm, b_dl], writes=[b_zi])
                    P.act(rho[:], zr[:], AF.Exp, reads=[b_zr], writes=[b_rho])
                    for g4 in range(4):
                        with contextlib.ExitStack() as s4:
                            phi = s4.enter_context(P.sbt(f"phi_e{li}d{d}g{g4}", [128, 4, 128], F32))
                            b_phi = Buf("phi")
                            P.tt("dve", phi[:], zi[:, 4 * g4:4 * g4 + 4].unsqueeze(2).to_broadcast([128, 4, 128]),
                                 jt[:].unsqueeze(1).to_broadcast([128, 4, 128]), ALU.mult, reads=[b_zi, b_jt], writes=[b_phi])
                            sincos(s4, f"t{li}{d}{g4}", phi[:], b_phi, [128, 4, 128], SIN[:, 4 * g4:4 * g4 + 4, :],
                                   COS[:, 4 * g4:4 * g4 + 4, :], b_COS)
                            T_barrier()
                    P.cp("dve", RHO0[:], rho[:].unsqueeze(2).to_broadcast([128, 16, 128]), reads=[b_rho], writes=[b_RHO0])
                    P.memset("dve", RHO0[:, :, 0:1], 0.0, writes=[b_RHO0])
                    ph2, b_ph2 = S2("ph2", [128, 16])
                    sn2, b_sn2 = S2("sn2", [128, 2, 16])
                    P.ts("dve", ph2[:], zi[:], 128.0, None, ALU.mult, None, reads=[b_zi], writes=[b_ph2])
                    sincos(s2, f"g{li}{d}", ph2[:], b_ph2, [128, 16], sn2[:, 1, :], sn2[:, 0, :], b_sn2)
                    P.tt("dve", Gc[:], sn2[:], rho[:].unsqueeze(1).to_broadcast([128, 2, 16]), ALU.mult,
                         reads=[b_sn2, b_rho], writes=[b_Gc])
                    T_barrier()
                for cc in range(8):
                  with contextlib.ExitStack() as s3:
                    def S3(name, shape, dt=F32):
                        return s3.enter_context(P.sbt(f"{name}_e{li}d{d}c{cc}", list(shape), dt)), Buf(name)
                    rp, b_rp = S3("rp", [128, 3, 256])
                    P.dma("sp", rp[:], s5_rep3[j, d][:, cc * 256:(cc + 1) * 256].partition_broadcast(128), writes=[b_rp])
                    r_dl, b_r_dl = S3("r_dl", [128, 256])
                    r_zr, b_r_zr = S3("r_zr", [128, 256])
                    r_zi, b_r_zi = S3("r_zi", [128, 256])
                    r_rho, b_r_rho = S3("r_rho", [128, 256])
                    r_sn, b_r_sn = S3("r_sn", [128, 2, 256])
                    P.act(r_dl[:], rp[:, 2, :], AF.Exp, reads=[b_rp], writes=[b_r_dl])
                    P.tt("dve", r_zr[:], rp[:, 0, :], r_dl[:], ALU.mult, reads=[b_rp, b_r_dl], writes=[b_r_zr])
                    P.tt("dve", r_zi[:], rp[:, 1, :], r_dl[:], ALU.mult, reads=[b_rp, b_r_dl], writes=[b_r_zi])
                    P.act(r_rho[:], r_zr[:], AF.Exp, reads=[b_r_zr], writes=[b_r_rho])
                    sincos(s3, f"r{li}{d}{cc}", r_zi[:], b_r_zi, [128, 256], r_sn[:, 1, :], r_sn[:, 0, :], b_r_sn)
                    P.tt("dve", r_sn[:], r_sn[:], r_rho[:].unsqueeze(1).to_broadcast([128, 2, 256]), ALU.mult,
                         reads=[b_r_sn, b_r_rho], writes=[b_r_sn])
                    P.ts("dve", r_sn[:, 0, :], r_sn[:, 0, :], -1.0, None, ALU.add, None, reads=[b_r_sn], writes=[b_r_sn])
                    P.tt("dve", r_dl[:], rp[:, 0, :], rp[:, 0, :], ALU.mult, reads=[b_rp], writes=[b_r_dl])
                    P.tt("dve", r_zr[:], rp[:, 1, :], rp[:, 1, :], ALU.mult, reads=[b_rp], writes=[b_r_zr])
                    P.tt("dve", r_dl[:], r_dl[:], r_zr[:], ALU.add, reads=[b_r_dl, b_r_zr], writes=[b_r_dl])
                    P.T.op("dve", lambda q, r_dl=r_dl: q.reciprocal(out=r_dl[:], in_=r_dl[:]), reads=[b_r_dl], writes=[b_r_dl])
                    P.tt("dve", r_zr[:], r_sn[:, 0, :], rp[:, 0, :], ALU.mult, reads=[b_r_sn, b_rp], writes=[b_r_zr])
                    P.tt("dve", r_rho[:], r_sn[:, 1, :], rp[:, 1, :], ALU.mult, reads=[b_r_sn, b_rp], writes=[b_r_rho])
                    P.tt("dve", r_zr[:], r_zr[:], r_rho[:], ALU.add, reads=[b_r_zr, b_r_rho], writes=[b_r_zr])
                    P.tt("dve", r_zr[:], r_zr[:], r_dl[:], ALU.mult, reads=[b_r_zr, b_r_dl], writes=[b_r_zr])
                    P.tt("dve", r_zi[:], r_sn[:, 1, :], rp[:, 0, :], ALU.mult, reads=[b_r_sn, b_rp], writes=[b_r_zi])
                    P.tt("dve", r_rho[:], r_sn[:, 0, :], rp[:, 1, :], ALU.mult, reads=[b_r_sn, b_rp], writes=[b_r_rho])
                    P.tt("dve", r_zi[:], r_zi[:], r_rho[:], ALU.subtract, reads=[b_r_zi, b_r_rho], writes=[b_r_zi])
                    P.tt("dve", r_zi[:], r_zi[:], r_dl[:], ALU.mult, reads=[b_r_zi, b_r_dl], writes=[b_r_zi])
                    bl, b_bl = S3("bl", [128, 2, 256])
                    P.dma("sp", bl[:], s5_bl[j, d][:, :, cc * 256:(cc + 1) * 256].rearrange("r p n -> p r n"), writes=[b_bl])
                    P.tt("dve", r_dl[:], r_zr[:], bl[:, 0, :], ALU.mult, reads=[b_r_zr, b_bl], writes=[b_r_dl])
                    P.tt("dve", r_rho[:], r_zi[:], bl[:, 1, :], ALU.mult, reads=[b_r_zi, b_bl], writes=[b_r_rho])
                    P.tt("dve", WB[0][0][:, cc * 256:(cc + 1) * 256], r_dl[:], r_rho[:], ALU.subtract, reads=[b_r_dl, b_r_rho], writes=[WB[0][1]])
                    P.tt("dve", r_dl[:], r_zr[:], bl[:, 1, :], ALU.mult, reads=[b_r_zr, b_bl], writes=[b_r_dl])
                    P.tt("dve", r_rho[:], r_zi[:], bl[:, 0, :], ALU.mult, reads=[b_r_zi, b_bl], writes=[b_r_rho])
                    P.tt("dve", WB[1][0][:, cc * 256:(cc + 1) * 256], r_dl[:], r_rho[:], ALU.add, reads=[b_r_dl, b_r_rho], writes=[WB[1][1]])

                    T_barrier()
                with contextlib.ExitStack() as s2:
                    def S2(name, shape, dt=F32):
                        return s2.enter_context(P.sbt(f"{name}_e{li}d{d}x", list(shape), dt)), Buf(name)
                    cl, b_cl = S2("cl", [128, 2, 512])
                    P.dma("sp", cl[:], s5_cl[j, d].rearrange("r p n -> p r n"), writes=[b_cl])
                    P.cp("dve", WC[0][0][:], cl[:, 0, :], reads=[b_cl], writes=[WC[0][1]])
                    P.ts("dve", WC[1][0][:], cl[:, 0, :], -1.0, None, ALU.mult, None, reads=[b_cl], writes=[WC[1][1]])
                    P.ts("dve", WC[2][0][:], cl[:, 1, :], -1.0, None, ALU.mult, None, reads=[b_cl], writes=[WC[2][1]])
                    P.memset("dve", cin[0][0][:], 0.0, writes=[cin[0][1]])
                    T_barrier()

            import types

            def alloc_work(wk, passB):
                def W(name, shape, dt=F32):
                    return wk.enter_context(P.sbt(f"{name}_e{li}", list(shape), dt)), Buf(name)
                w = types.SimpleNamespace()
                w.g = [types.SimpleNamespace() for _ in range(2)]
                tmps = {nm: W(f"{nm}m", [128, 512]) for nm in ("tA", "tB", "tC", "tD")}
                Pk1 = [W(f"Pk_{k}", [128, 512], BF16) for k in range(4)]
                for i, g in enumerate(w.g):
                    for nm in ("wre", "wim", "sre", "sim"):
                        setattr(g, nm, W(f"{nm}{i}", [128, 512]))
                    for nm in ("tA", "tB", "tC", "tD"):
                        setattr(g, nm, tmps[nm])
                    g.Pk = Pk1
                w.uT = W("uT", [128, 4, 128], BF16)
                w.kf = W("kf", [128, 512], BF16)
                w.tC = W("tCr", [128, 512])
                if not passB:
                    w.hT = [W(f"hT{i}", [128, 8, 128], BF16) for i in range(2)]
                    w.rot = [W(f"rot{i}", [128, 2, 2, 128]) for i in range(2)]
                    w.rA = W("rA", [128, 512])
                    w.rB = W("rB", [128, 512])
                    w.qr = W("qr", [128, 512], BF16)
                    w.kr = [W("kr", [128, 512], BF16)]
                    w.vb = [W("vb", [128, 512], BF16)]
                    w.QT = [W("QT", [128, 512], BF16)]
                    w.KT = W("KT", [128, 512], BF16)
                    w.STb = W("STb", [128, 512], BF16)
                    w.sg = [W("sg", [128, 1024])]
                    w.du = W("du", [128, 512])
                    w.ub = [W("ub", [128, 512], BF16)]
                    w.opt = [W("opt", [128, 512])]
                    w.ypt = [W("ypt", [128, 512])]
                else:
                    w.kr = [W(f"kr{i}", [128, 512], BF16) for i in range(2)]
                    w.vb = [W(f"vb{i}", [128, 512], BF16) for i in range(2)]
                    w.QT = [W(f"QT{i}", [128, 512], BF16) for i in range(2)]
                    w.sg = [W("sg0", [128, 1024])] * 2
                    w.ub = [W(f"ub{i}", [128, 512], BF16) for i in range(2)]
                    w.opt = [W(f"opt{i}", [128, 512]) for i in range(2)]
                    w.ypt = [W(f"ypt{i}", [128, 512]) for i in range(2)]
                    w.oab = W("oab", [128, 1024], BF16)
                    w.oT = W("oT", [128, 8, 128], BF16)
                    w.ygb = W("ygb", [128, 512], BF16)
                    w.tA = W("tAh", [128, 512])
                    w.tB = W("tBh", [128, 512])
                    w.tD = W("tDh", [128, 512])
                return w

            def s5_chunk(w, ci, rev, hooks=()):
                cur, nxt = cin[ci % 2], cin[(ci + 1) % 2]
                lb, b_lb = lastb[ci % 2]
                uT, b_uT = w.uT
                hooks = list(hooks)

                def hook():
                    if hooks:
                        hooks.pop(0)()

                def banks(g):
                    return (3, 4) if g % 2 == 0 else (5, 6)

                def stageApe(g):
                    br, bi = banks(g)
                    for t in range(4):
                        s_ = 4 * g + t
                        P.mm(ps[br][:, t * 128:(t + 1) * 128], WB[0][0][:, s_ * 128:(s_ + 1) * 128], uT[:, g, :], True, False,
                             reads=[WB[0][1], b_uT], writes=[bps[br]], inc=False)
                        P.mm(ps[bi][:, t * 128:(t + 1) * 128], WB[1][0][:, s_ * 128:(s_ + 1) * 128], uT[:, g, :], True, False,
                             reads=[WB[1][1], b_uT], writes=[bps[bi]], inc=False)
                    o_r = ps[br][:].rearrange("p (a b) -> p a b", a=4)[:, :, 0:1]
                    o_i = ps[bi][:].rearrange("p (a b) -> p a b", a=4)[:, :, 0:1]
                    P.mm(o_r, identf[:], cur[0][:, 0, 4 * g:4 * g + 4].unsqueeze(2), False, True,
                         reads=[b_identf, cur[1]], writes=[bps[br]], inc=False)
                    P.mm(o_i, identf[:], cur[0][:, 1, 4 * g:4 * g + 4].unsqueeze(2), False, True,
                         reads=[b_identf, cur[1]], writes=[bps[bi]], inc=True)

                def stageAve(g):
                    br, bi = banks(g)
                    G = w.g[g % 2]
                    Cg = COS[:, 4 * g:4 * g + 4, :].rearrange("p a b -> p (a b)")
                    Sg = SIN[:, 4 * g:4 * g + 4, :].rearrange("p a b -> p (a b)")
                    P.tt("dve", G.tA[0][:], ps[br][:], Cg, ALU.mult, reads=[bps[br], b_COS], writes=[G.tA[1]])
                    P.tt("dve", G.tB[0][:], ps[bi][:], Sg, ALU.mult, reads=[bps[bi], b_COS], writes=[G.tB[1]])
                    P.tt("pool", G.wre[0][:], G.tA[0][:], G.tB[0][:], ALU.add, reads=[G.tA[1], G.tB[1]], writes=[G.wre[1]])
                    P.tt("dve", G.tC[0][:], ps[bi][:], Cg, ALU.mult, reads=[bps[bi], b_COS], writes=[G.tC[1]])
                    P.stt("dve", G.tD[0][:], ps[br][:], -1.0, Sg, ALU.mult, ALU.mult, reads=[bps[br], b_COS], writes=[G.tD[1]])
                    P.tt("pool", G.wim[0][:], G.tC[0][:], G.tD[0][:], ALU.add, reads=[G.tC[1], G.tD[1]], writes=[G.wim[1]])

                def stageB(g):
                    G = w.g[g % 2]
                    Rg = RHO0[:, 4 * g:4 * g + 4, :].rearrange("p a b -> p (a b)")
                    P.T.op("dve", lambda q, Rg=Rg, o=G.sre[0], i=G.wre[0]: q.tensor_tensor_scan(
                        out=o[:], data0=Rg, data1=i[:], initial=0.0, op0=ALU.mult, op1=ALU.add),
                        reads=[b_RHO0, G.wre[1]], writes=[G.sre[1]])
                    P.T.op("dve", lambda q, Rg=Rg, o=G.sim[0], i=G.wim[0]: q.tensor_tensor_scan(
                        out=o[:], data0=Rg, data1=i[:], initial=0.0, op0=ALU.mult, op1=ALU.add),
                        reads=[b_RHO0, G.wim[1]], writes=[G.sim[1]])
                    sr3 = G.sre[0][:].rearrange("p (a b) -> p a b", a=4)
                    si3 = G.sim[0][:].rearrange("p (a b) -> p a b", a=4)
                    P.act(lb[:, 0, 4 * g:4 * g + 4].unsqueeze(2), sr3[:, :, 127:128], AF.Copy, reads=[G.sre[1]], writes=[b_lb])
                    P.act(lb[:, 1, 4 * g:4 * g + 4].unsqueeze(2), si3[:, :, 127:128], AF.Copy, reads=[G.sim[1]], writes=[b_lb])

                    def pv(ap):
                        v = ap.rearrange("p (a b) -> p a b", a=4)
                        return v[:, :, ::-1] if rev else v
                    C3 = COS[:, 4 * g:4 * g + 4, :]
                    S3 = SIN[:, 4 * g:4 * g + 4, :]
                    e2 = "dve" if rev else "pool"
                    P.tt("dve", pv(G.Pk[0][0][:]), sr3, C3, ALU.mult, reads=[G.sre[1], b_COS], writes=[G.Pk[0][1]])
                    P.tt(e2, pv(G.Pk[1][0][:]), si3, S3, ALU.mult, reads=[G.sim[1], b_COS], writes=[G.Pk[1][1]])
                    P.tt("dve", pv(G.Pk[2][0][:]), sr3, S3, ALU.mult, reads=[G.sre[1], b_COS], writes=[G.Pk[2][1]])
                    P.tt(e2, pv(G.Pk[3][0][:]), si3, C3, ALU.mult, reads=[G.sim[1], b_COS], writes=[G.Pk[3][1]])
                    for t in range(4):
                        s_ = 4 * g + t
                        o = ps[7][:, 32 * s_:32 * s_ + 32]
                        wsl = slice(32 * s_, 32 * s_ + 32)
                        tsl = slice(128 * t, 128 * t + 128)
                        Pk = G.Pk
                        P.mm(o, Pk[0][0][:, tsl], WC[0][0][:, wsl], True, False, reads=[Pk[0][1], WC[0][1]], writes=[bps[7]], inc=False)
                        P.mm(o, Pk[1][0][:, tsl], WC[1][0][:, wsl], False, False, reads=[Pk[1][1], WC[1][1]], writes=[bps[7]], inc=False)
                        P.mm(o, Pk[2][0][:, tsl], WC[2][0][:, wsl], False, False, reads=[Pk[2][1], WC[2][1]], writes=[bps[7]], inc=False)
                        P.mm(o, Pk[3][0][:, tsl], WC[2][0][:, wsl], False, True, reads=[Pk[3][1], WC[2][1]], writes=[bps[7]], inc=(t == 3))

                stageApe(0)
                stageAve(0)
                stageApe(1)
                hook()
                stageAve(1)
                hook()
                stageApe(2)
                stageB(0)
                hook()
                stageAve(2)
                hook()
                stageApe(3)
                stageB(1)
                hook()
                stageAve(3)
                hook()
                stageB(2)
                hook()
                stageB(3)
                hook()
                while hooks:
                    hook()
                l4 = lst[:]
                P.tt("pool", l4[:, 0, :], lb[:, 0, :], Gc[:, 0, :], ALU.mult, reads=[b_lb, b_Gc], writes=[b_lst])
                P.tt("pool", l4[:, 1, :], lb[:, 1, :], Gc[:, 1, :], ALU.mult, reads=[b_lb, b_Gc], writes=[b_lst])
                P.tt("pool", l4[:, 2, :], lb[:, 1, :], Gc[:, 0, :], ALU.mult, reads=[b_lb, b_Gc], writes=[b_lst])
                P.tt("pool", l4[:, 3, :], lb[:, 0, :], Gc[:, 1, :], ALU.mult, reads=[b_lb, b_Gc], writes=[b_lst])
                P.tt("pool", nxt[0][:, 0, :], l4[:, 0, :], l4[:, 1, :], ALU.subtract, reads=[b_lst], writes=[nxt[1]])
                P.tt("pool", nxt[0][:, 1, :], l4[:, 2, :], l4[:, 3, :], ALU.add, reads=[b_lst], writes=[nxt[1]])

            def ret_state_update(w, kbuf, b_kbuf, vbt, b_vbt, gcol):
                kf, b_kf = w.kf
                P.tt("pool", kf[:].rearrange("p (h e) -> p h e", h=4), kbuf[:].rearrange("p (h e) -> p h e", h=4),
                     gam[:, gcol, :].unsqueeze(2).to_broadcast([128, 4, 128]), ALU.mult,
                     reads=[b_kbuf, b_gam], writes=[b_kf])
                for h in range(4):
                    hs = slice(h * 128, (h + 1) * 128)
                    P.mm(ps[5][:, hs], kf[:, hs], vbt[:, hs], True, True, reads=[b_kf, b_vbt], writes=[bps[5]], inc=(h == 3))
                P.tt("pool", stf[:].rearrange("p (h e) -> p h e", h=4), stf[:].rearrange("p (h e) -> p h e", h=4),
                     cdec[:].unsqueeze(2).to_broadcast([128, 4, 128]), ALU.mult, reads=[b_stf, b_cdec], writes=[b_stf])
                P.tt("dve", stf[:], stf[:], ps[5][:], ALU.add, reads=[b_stf, bps[5]], writes=[b_stf])
                P.act(stbf[:], stf[:], AF.Copy, reads=[b_stf], writes=[b_stbf])

            def passA(s, w):
                P.memset("dve", stf[:], 0.0, writes=[b_stf])
                P.memset("pool", stbf[:], 0.0, writes=[b_stbf])
                nA = nch if KSTOP not in ('setup',) else 0
                if nA:
                    prenorm(C, s, src, bsrc[s], 0, w.hT[0][0], w.hT[0][1], 0)
                for n in range(nA):
                    tsl = slice(n * 128, (n + 1) * 128)
                    hT, b_hT = w.hT[n % 2]
                    rot, b_rot = w.rot[n % 2]
                    P.dma("sp", rot[:], rot_d[tsl], writes=[b_rot])
                    if n + 1 < nA:
                        prenorm(C, s, src, bsrc[s], n + 1, w.hT[(n + 1) % 2][0], w.hT[(n + 1) % 2][1], 0)
                    qr, b_qr = w.qr
                    kr, b_kr = w.kr[0]
                    vb, b_vb = w.vb[0]
                    QT, b_QT = w.QT[0]
                    KT, b_KT = w.KT
                    STb, b_STb = w.STb
                    sg, b_sg = w.sg[0]
                    du, b_du = w.du
                    ub, b_ub = w.ub[0]
                    opt, b_opt = w.opt[0]
                    ypt, b_ypt = w.ypt[0]
                    tC, b_tC = w.tC
                    uT, b_uT = w.uT
                    for b in range(3):
                        for k in range(8):
                            P.mm(ps[2 + b][:], hT[:, k, :], win[:, k, b * 512:(b + 1) * 512], k == 0, k == 7,
                                 reads=[b_hT, b_win], writes=[bps[2 + b]], inc=(k == 7))
                    for (zb, tbl, outb, b_outb) in ((2, 0, qr, b_qr), (3, 1, kr, b_kr)):
                        rA, b_rA = w.rA
                        rB, b_rB = w.rB
                        z3 = ps[zb][:].rearrange("p (h e) -> p h e", h=4)
                        a3 = rA[:].rearrange("p (h e) -> p h e", h=4)
                        b3 = rB[:].rearrange("p (h e) -> p h e", h=4)
                        P.tt("dve", a3, z3, rot[:, tbl, 0, :].unsqueeze(1).to_broadcast([128, 4, 128]), ALU.mult,
                             reads=[bps[zb], b_rot], writes=[b_rA])
                        P.tt("dve", b3[:, :, 0:64], z3[:, :, 64:128], rot[:, tbl, 1, 0:64].unsqueeze(1).to_broadcast([128, 4, 64]),
                             ALU.mult, reads=[bps[zb], b_rot], writes=[b_rB])
                        P.tt("dve", b3[:, :, 64:128], z3[:, :, 0:64], rot[:, tbl, 1, 64:128].unsqueeze(1).to_broadcast([128, 4, 64]),
                             ALU.mult, reads=[bps[zb], b_rot], writes=[b_rB])
                        P.tt("pool", outb[:], rA[:], rB[:], ALU.add, reads=[b_rA, b_rB], writes=[b_outb])
                    P.act(vb[:], ps[4][:], AF.Copy, reads=[bps[4]], writes=[b_vb])
                    for h in range(4):
                        hs = slice(h * 128, (h + 1) * 128)
                        P.mm(ps[0][:, hs], qr[:, hs], ident_bf[:], True, True, reads=[b_qr, b_ident], writes=[bps[0]], inc=(h == 3))
                    for h in range(4):
                        hs = slice(h * 128, (h + 1) * 128)
                        P.mm(ps[1][:, hs], kr[:, hs], ident_bf[:], True, True, reads=[b_kr, b_ident], writes=[bps[1]], inc=(h == 3))
                    P.act(QT[:], ps[0][:], AF.Copy, reads=[bps[0]], writes=[b_QT])
                    P.act(KT[:], ps[1][:], AF.Copy, reads=[bps[1]], writes=[b_KT])
                    for h in range(4):
                        hs = slice(h * 128, (h + 1) * 128)
                        P.mm(ps[2][:, hs], KT[:, hs], QT[:, hs], True, True, reads=[b_KT, b_QT], writes=[bps[2]], inc=(h == 3))
                    P.tt("dve", STb[:], ps[2][:], dtab[:].rearrange("p h e -> p (h e)"), ALU.mult,
                         reads=[bps[2], b_dtab], writes=[b_STb])
                    for h in range(4):
                        hs = slice(h * 128, (h + 1) * 128)
                        P.mm(ps[3][:, hs], STb[:, hs], vb[:, hs], True, True, reads=[b_STb, b_vb], writes=[bps[3]], inc=(h == 3))
                    for h in range(4):
                        hs = slice(h * 128, (h + 1) * 128)
                        P.mm(ps[6][:, hs], QT[:, hs], stbf[:, hs], True, True, reads=[b_QT, b_stbf], writes=[bps[6]], inc=(h == 3))
                    P.tt("dve", tC[:].rearrange("p (h e) -> p h e", h=4), ps[6][:].rearrange("p (h e) -> p h e", h=4),
                         gam[:, 0, :].unsqueeze(2).to_broadcast([128, 4, 128]), ALU.mult, reads=[bps[6], b_gam], writes=[b_tC])
                    P.tt("dve", opt[:], tC[:], ps[3][:], ALU.add, reads=[b_tC, bps[3]], writes=[b_opt])
                    P.dma("pool", OPs[tsl], opt[:], reads=[b_opt], writes=[b_OPs])
                    ret_state_update(w, kr, b_kr, vb, b_vb, 2)
                    P.dma("pool", QTs[n], QT[:], reads=[b_QT], writes=[b_QTs])
                    P.dma("pool", KRs[tsl], kr[:], reads=[b_kr], writes=[b_KRs])
                    P.dma("pool", VBs[tsl], vb[:], reads=[b_vb], writes=[b_VBs])
                    for b in range(3):
                        for k in range(8):
                            P.mm(ps[2 + b][:], hT[:, k, :], win[:, k, (3 + b) * 512:(4 + b) * 512], k == 0, k == 7,
                                 reads=[b_hT, b_win], writes=[bps[2 + b]], inc=(k == 7))
                    P.act(sg[:, 0:512], ps[2][:], AF.Silu, reads=[bps[2]], writes=[b_sg])
                    P.act(sg[:, 512:1024], ps[4][:], AF.Silu, reads=[bps[4]], writes=[b_sg])
                    P.dma("pool", SGs[tsl], sg[:], reads=[b_sg], writes=[b_SGs])
                    P.act(ub[:], ps[3][:], AF.Copy, reads=[bps[3]], writes=[b_ub])
                    P.tt("dve", du[:], ps[3][:], drow[:], ALU.mult, reads=[bps[3], b_drow], writes=[b_du])
                    P.dma("pool", UBs[tsl], ub[:], reads=[b_ub], writes=[b_UBs])
                    for q_ in range(4):
                        qs = slice(q_ * 128, (q_ + 1) * 128)
                        P.mm(ps[0][:, qs], ub[:, qs], ident_bf[:], True, True, reads=[b_ub, b_ident], writes=[bps[0]], inc=(q_ == 3))
                    P.act(uT[:].rearrange("p a b -> p (a b)"), ps[0][:], AF.Copy, reads=[bps[0]], writes=[b_uT])
                    s5_chunk(w, n, False)
                    P.tt("dve", ypt[:], ps[7][:], du[:], ALU.add, reads=[bps[7], b_du], writes=[b_ypt])
                    P.dma("pool", YPs[tsl], ypt[:], reads=[b_ypt], writes=[b_YPs])

            def passB(s, w):
                P.memset("dve", stf[:], 0.0, writes=[b_stf])
                P.memset("pool", stbf[:], 0.0, writes=[b_stbf])
                order = list(range(nch - 1, -1, -1)) if KSTOP == 'all' else []
                xts = {}

                def loads(ci):
                    n = order[ci]
                    tsl = slice(n * 128, (n + 1) * 128)
                    r = ci % 2
                    sl = C["cnt"] % 2
                    C["cnt"] += 1
                    xts[ci] = (C["xt"][sl], C["bxt"][sl])
                    P.dma("sp", C["xt"][sl][:], src[s, tsl, :], reads=[bsrc[s]], writes=[C["bxt"][sl]])
                    P.dma("sp", w.QT[r][0][:], QTs[n], reads=[b_QTs], writes=[w.QT[r][1]])
                    P.dma("sp", w.opt[r][0][:], OPs[tsl], reads=[b_OPs], writes=[w.opt[r][1]])
                    P.dma("sp", w.kr[r][0][:], KRs[tsl], reads=[b_KRs], writes=[w.kr[r][1]])
                    P.dma("sp", w.vb[r][0][:], VBs[tsl], reads=[b_VBs], writes=[w.vb[r][1]])
                    P.dma("sp", w.ub[r][0][:], UBs[tsl], reads=[b_UBs], writes=[w.ub[r][1]])
                    P.dma("sp", w.ypt[r][0][:], YPs[tsl], reads=[b_YPs], writes=[w.ypt[r][1]])

                if order:
                    loads(0)
                for ci, n in enumerate(order):
                    if ci + 1 < len(order):
                        loads(ci + 1)
                    r = ci % 2
                    xt, bxt = xts.pop(ci)
                    QT, b_QT = w.QT[r]
                    opt, b_opt = w.opt[r]
                    kr, b_kr = w.kr[r]
                    vb, b_vb = w.vb[r]
                    ub, b_ub = w.ub[r]
                    sg, b_sg = w.sg[r]
                    ypt, b_ypt = w.ypt[r]
                    tA, b_tA = w.tA
                    tB, b_tB = w.tB
                    tD, b_tD = w.tD
                    tC, b_tC = w.tC
                    uT, b_uT = w.uT
                    oab, b_oab = w.oab
                    oT, b_oT = w.oT
                    ygb, b_ygb = w.ygb
                    P.dma("sp", sg[:], SGs[n * 128:(n + 1) * 128], reads=[b_SGs], writes=[b_sg])
                    for q_ in range(4):
                        qs = slice(q_ * 128, (q_ + 1) * 128)
                        P.mm(ps[0][:, qs], ub[:, qs], jmat_bf[:], True, True, reads=[b_ub, b_jmat], writes=[bps[0]], inc=(q_ == 3))
                    P.act(uT[:].rearrange("p a b -> p (a b)"), ps[0][:], AF.Copy, reads=[bps[0]], writes=[b_uT])
                    for h in range(4):
                        hs = slice(h * 128, (h + 1) * 128)
                        P.mm(ps[6][:, hs], QT[:, hs], stbf[:, hs], True, True, reads=[b_QT, b_stbf], writes=[bps[6]], inc=(h == 3))
                    P.tt("dve", tC[:].rearrange("p (h e) -> p h e", h=4), ps[6][:].rearrange("p (h e) -> p h e", h=4),
                         gam[:, 1, :].unsqueeze(2).to_broadcast([128, 4, 128]), ALU.mult, reads=[bps[6], b_gam], writes=[b_tC])
                    P.tt("pool", opt[:], opt[:], tC[:], ALU.add, reads=[b_opt, b_tC], writes=[b_opt])
                    ret_state_update(w, kr, b_kr, vb, b_vb, 3)
                    o3 = opt[:].rearrange("p (h e) -> p h e", h=4)
                    P.T.op("dve", lambda q, o3=o3: q.reduce_sum(out=hn[:, 0:4], in_=o3, axis=mybir.AxisListType.X),
                           reads=[b_opt], writes=[b_hn])
                    P.tt("pool", tA[:], opt[:], opt[:], ALU.mult, reads=[b_opt], writes=[b_tA])
                    P.T.op("dve", lambda q, tA=tA: q.reduce_sum(out=hn[:, 4:8], in_=tA[:].rearrange("p (h e) -> p h e", h=4),
                                                              axis=mybir.AxisListType.X), reads=[b_tA], writes=[b_hn])
                    P.ts("dve", hn[:, 0:8], hn[:, 0:8], 1.0 / 128.0, None, ALU.mult, None, reads=[b_hn], writes=[b_hn])
                    P.tt("dve", hn[:, 8:12], hn[:, 0:4], hn[:, 0:4], ALU.mult, reads=[b_hn], writes=[b_hn])
                    P.tt("dve", hn[:, 4:8], hn[:, 4:8], hn[:, 8:12], ALU.subtract, reads=[b_hn], writes=[b_hn])
                    P.act(hn[:, 8:12], hn[:, 4:8], AF.Sqrt, reads=[b_hn, b_eps], writes=[b_hn], bias=epsT[:, 0:1])
                    P.T.op("dve", lambda q: q.reciprocal(out=hn[:, 12:16], in_=hn[:, 8:12]), reads=[b_hn], writes=[b_hn])
                    a3 = tA[:].rearrange("p (h e) -> p h e", h=4)
                    P.tt("pool", a3, o3, hn[:, 0:4].unsqueeze(2).to_broadcast([128, 4, 128]), ALU.subtract,
                         reads=[b_opt, b_hn], writes=[b_tA])
                    P.tt("pool", a3, a3, hn[:, 12:16].unsqueeze(2).to_broadcast([128, 4, 128]), ALU.mult,
                         reads=[b_tA, b_hn], writes=[b_tA])
                    P.tt("pool", oab[:, 0:512], tA[:], sg[:, 0:512], ALU.mult, reads=[b_tA, b_sg], writes=[b_oab])
                    s5_chunk(w, ci, True)
                    P.tt("dve", ypt[:], ypt[:], ps[7][:], ALU.add, reads=[b_ypt, bps[7]], writes=[b_ypt])
                    P.tt("pool", tB[:], ypt[:], ypt[:], ALU.mult, reads=[b_ypt], writes=[b_tB])
                    P.ts("dve", tB[:], tB[:], 0.044715, 1.0, ALU.mult, ALU.add, reads=[b_tB], writes=[b_tB])
                    P.tt("pool", tB[:], tB[:], ypt[:], ALU.mult, reads=[b_tB, b_ypt], writes=[b_tB])
                    P.act(tB[:], tB[:], AF.Sigmoid, reads=[b_tB], writes=[b_tB], scale=1.5957691216057308)
                    P.tt("dve", tD[:], ypt[:], tB[:], ALU.mult, reads=[b_ypt, b_tB], writes=[b_tD])
                    P.act(ygb[:], tD[:], AF.Copy, reads=[b_tD], writes=[b_ygb])
                    for q_ in range(4):
                        qs = slice(q_ * 128, (q_ + 1) * 128)
                        P.mm(ps[0][:, qs], ygb[:, qs], ident_bf[:], True, True, reads=[b_ygb, b_ident], writes=[bps[0]], inc=(q_ == 3))
                    P.act(uT[:].rearrange("p a b -> p (a b)"), ps[0][:], AF.Copy, reads=[bps[0]], writes=[b_uT])
                    for q_ in range(4):
                        P.mm(ps[1][:], uT[:, q_, :], wglu[:, q_, :], q_ == 0, q_ == 3, reads=[b_uT, b_wglu], writes=[bps[1]], inc=(q_ == 3))
                    P.act(tB[:], ps[1][:], AF.Sigmoid, reads=[bps[1]], writes=[b_tB])
                    P.tt("dve", tD[:], tD[:], tB[:], ALU.mult, reads=[b_tD, b_tB], writes=[b_tD])
                    P.tt("pool", oab[:, 512:1024], tD[:], sg[:, 512:1024], ALU.mult, reads=[b_tD, b_sg], writes=[b_oab])
                    for b in range(2):
                        for jj in range(4):
                            k = 4 * b + jj
                            P.mm(ps[b][:, jj * 128:(jj + 1) * 128], oab[:, k * 128:(k + 1) * 128], ident_bf[:], True, True,
                                 reads=[b_oab, b_ident], writes=[bps[b]], inc=(jj == 3))
                    P.act(oT[:, 0:4, :].rearrange("p a b -> p (a b)"), ps[0][:], AF.Copy, reads=[bps[0]], writes=[b_oT])
                    P.act(oT[:, 4:8, :].rearrange("p a b -> p (a b)"), ps[1][:], AF.Copy, reads=[bps[1]], writes=[b_oT])
                    for hh in range(2):
                        for k in range(8):
                            P.mm(ps[2 + hh][:], oT[:, k, :], wout[:, k, hh * 512:(hh + 1) * 512], k == 0, k == 7,
                                 reads=[b_oT, b_wout], writes=[bps[2 + hh]], inc=(k == 7))
                    post(C, s, (2, 3), xt, bxt, dst, bdst[s], n)

            for s in range(nseq):
                s5_setup(0)
                with contextlib.ExitStack() as wk:
                    w = alloc_work(wk, False)
                    passA(s, w)
                    T_barrier()
                s5_setup(1)
                with contextlib.ExitStack() as wk:
                    w = alloc_work(wk, True)
                    passB(s, w)
                    T_barrier()

    def T_barrier():
        evs = []
        for e in T.engs.values():
            for k in range(len(e.dsems)):
                evs.append((e.dsems[k], e.dvals[k]))
            if e.sem is not None and e.cnt > 0 and not e.pending:
                evs.append((e.sem, e.cnt))
        for e in T.engs.values():
            for ev in evs:
                T._wait(e, ev)

    def odd_layer(li, src, bsrc, dst, bdst):
        raise NotImplementedError

    P.odd_layer_hook = None
    cur, bcur = x_in, [b_xin] * nseq
    for idx, li in enumerate(layers):
        last = idx == len(layers) - 1
        dstt, bd = (y_out, b_yout) if last else (xs[idx % 2], b_xs[idx % 2])
        if li % 2 == 0:
            even_layer(li, cur, bcur, dstt, bd)
        else:
            ODD_IMPL(P, locals(), li, cur, bcur, dstt, bd)
        cur, bcur = dstt, bd
    T.finish()
    T.replay()
    P.es.close()
    return P


def ODD_IMPL(P, env, li, src, bsrc, dst, bdst):
    nc, T = P.nc, P.T
    E = env
    ps, bps = E["ps"], E["bps"]
    nseq, L = P.nseq, P.L
    ident_bf, b_ident = E["ident_bf"], E["b_ident"]
    j = li // 2
    rows = L // 64
    nblk = L // 256

    def rs(r):
        return min(max(r - 4, 0), rows - 8)

    with contextlib.ExitStack() as st:
        def S(name, shape, dt=F32):
            return st.enter_context(P.sbt(f"{name}_o{li}", list(shape), dt)), Buf(name)
        E["adaln"](li, None)
        C = E["make_common"](st, f"o{li}")
        winc, b_winc = S("winc", [128, 8, 4096], BF16)
        woutc, b_woutc = S("woutc", [128, 8, 1024], BF16)
        Z, b_Z = S("Z", [128, 16, 1024], BF16)
        hT, b_hT = S("hT", [128, 8, 256], BF16)
        KT = [S(f"KT{i}", [128, 8, 256], BF16) for i in range(3)]
        V = [S(f"V{i}", [128, 2, 8, 3, 64], BF16) for i in range(3)]
        QT = [S(f"QT{i}", [128, 8, 256], BF16) for i in range(2)]
        GT = [S(f"GT{i}", [128, 8, 256], BF16) for i in range(2)]
        pT = [S(f"pT{i}", [128, 256], BF16) for i in range(3)]
        og, b_og = S("og", [128, 8, 256], BF16)
        rd2 = [S(f"rd{i}", [128, 256]) for i in range(2)]
        rb2 = [S(f"rb{i}", [128, 256]) for i in range(2)]
        t22 = [S(f"t2{i}", [128, 256]) for i in range(2)]
        onesf, b_onesf = S("onesf", [128, 64])
        zer, b_zer = S("zer", [128, 256], BF16)
        wsrc = E["w_in_c"][j].rearrange("(k p) n -> p k n", p=128)
        for k in range(8):
            P.dma("pool", winc[:, k, :], wsrc[:, k, :], writes=[b_winc])
        P.dma("pool", woutc[:], E["w_out_c"][j].rearrange("(k p) n -> p k n", p=128), writes=[b_woutc])
        P.memset("dve", onesf[:], 1.0, writes=[b_onesf])
        P.memset("dve", zer[:], 0.0, writes=[b_zer])
        for i in range(3):
            P.memset("pool", V[i][0][:], 1.0, writes=[V[i][1]])
        with contextlib.ExitStack() as s2:
            zm = s2.enter_context(P.sbt(f"zm_o{li}", [128, 1024], F32)); b_zm = Buf("zm")
            zt0_ = s2.enter_context(P.sbt(f"zt0_o{li}", [128, 512], F32))
            zt = [zt0_, zt0_]
            b_zt0_ = Buf("zt0")
            b_zt = [b_zt0_, b_zt0_]
            P.dma("sp", zm[:], E["zmask"], writes=[b_zm])
            for h in range(16):
                for hf in range(2):
                    P.dma("sp", zt[hf][:], E["zg"][j, h][:, hf * 512:(hf + 1) * 512], writes=[b_zt[hf]])
                    P.tt("dve", Z[:, h, hf * 512:(hf + 1) * 512], zt[hf][:], zm[:, hf * 512:(hf + 1) * 512], ALU.add,
                         reads=[b_zt[hf], b_zm], writes=[b_Z])
            E["T_barrier"]()

        def proj(s, b):
            ring = b % 3
            sl = b % 2
            for t in range(2):
                E["prenorm"](C, s, src, bsrc[s], 2 * b + t, hT, b_hT, t * 128)
            cnt = 0
            for (col0, kind) in ((0, "q"), (1024, "k"), (3072, "g")):
                for hp2 in range(4):
                    bank = 2 + cnt % 2
                    cnt += 1
                    for hh in range(2):
                        hp = 2 * hp2 + hh
                        for k in range(8):
                            P.mm(ps[bank][:, hh * 256:(hh + 1) * 256], winc[:, k, col0 + hp * 128:col0 + (hp + 1) * 128], hT[:, k, :],
                                 k == 0, k == 7, reads=[b_winc, b_hT], writes=[bps[bank]], inc=(k == 7 and hh == 1))
                    if kind == "q":
                        P.act(QT[sl][0][:, 2 * hp2:2 * hp2 + 2, :].rearrange("p a b -> p (a b)"), ps[bank][:], AF.Copy,
                              reads=[bps[bank]], writes=[QT[sl][1]], scale=0.125)
                    elif kind == "k":
                        P.cp("dve", KT[ring][0][:, 2 * hp2:2 * hp2 + 2, :].rearrange("p a b -> p (a b)"), ps[bank][:],
                             reads=[bps[bank]], writes=[KT[ring][1]])
                    else:
                        P.act(GT[sl][0][:, 2 * hp2:2 * hp2 + 2, :].rearrange("p a b -> p (a b)"), ps[bank][:], AF.Silu,
                              reads=[bps[bank]], writes=[GT[sl][1]])
            for t in range(2):
                for half in range(2):
                    bank = 2 + cnt % 2
                    cnt += 1
                    for k in range(8):
                        P.mm(ps[bank][:], hT[:, k, t * 128:(t + 1) * 128], winc[:, k, 2048 + half * 512:2048 + (half + 1) * 512],
                             k == 0, k == 7, reads=[b_hT, b_winc], writes=[bps[bank]], inc=(k == 7))
                    src4 = ps[bank][:].rearrange("p (a c d) -> p a c d", a=4, c=2)
                    P.cp("dve", V[ring][0][:, t, 4 * half:4 * half + 4, 0:3:2, :], src4, reads=[bps[bank]], writes=[V[ring][1]])

        def attn(s, b):
            R = 4 * b
            sl = b % 2
            lo = rs(R) & ~1
            hi = (rs(R + 3) + 7) & ~1
            tiles = []
            for r0 in range(lo, hi + 1, 2):
                qs = [r for r in range(R, R + 4) if (rs(r) <= r0 + 1 and r0 <= rs(r) + 7)]
                if not qs:
                    continue
                qa, qb = qs[0], qs[-1]
                partial = []
                for r in qs:
                    for a in range(2):
                        if not (rs(r) <= r0 + a <= rs(r) + 7):
                            partial.append((r, a))
                tiles.append((r0, qa, qb, partial))
            pcount = [0]
            W = [(h, ti) for h in range(16) for ti in range(len(tiles))]
            info = {}
            deferred = []

            def emit_scores(h, ti):
                hp, base = h // 2, 64 * (h % 2)
                r0, qa, qb, partial = tiles[ti]
                kb = r0 // 4
                tt_ = (r0 % 4) // 2
                kring = kb % 3
                c0, c1 = (qa - R) * 64, (qb - R + 1) * 64
                z0, z1 = (qa - r0 + 7) * 64, (qb - r0 + 8) * 64
                bank = 4 + pcount[0] % 2
                pt, b_pt = pT[pcount[0] % 3]
                pcount[0] += 1
                P.mm(ps[bank][:, c0:c1], KT[kring][0][base:base + 64, hp, tt_ * 128:(tt_ + 1) * 128],
                     QT[sl][0][base:base + 64, hp, c0:c1], True, False,
                     reads=[KT[kring][1], QT[sl][1]], writes=[bps[bank]], inc=False)
                P.mm(ps[bank][:, c0:c1], ident_bf[:], Z[:, h, z0:z1], False, True,
                     reads=[b_ident, b_Z], writes=[bps[bank]], inc=True)
                P.act(pt[:, c0:c1], ps[bank][:, c0:c1], AF.Exp, reads=[bps[bank]], writes=[b_pt])
                for (r, a_) in partial:
                    cc = (r - R) * 64
                    P.memset("pool", pt[64 * a_:64 * a_ + 64, cc:cc + 64], 0.0, writes=[b_pt])
                info[(h, ti)] = (pt, b_pt, c0, c1, kring, tt_)

            def emit_pv(h, ti, idx):
                hp = h // 2
                pt, b_pt, c0, c1, kring, tt_ = info.pop((h, ti))
                ob = 6 + h % 2
                if ti == 0:
                    P.mm(ps[ob][:, 0:256], zer[:, 0:128], zer[:], True, False, reads=[b_zer], writes=[bps[ob]], inc=False)
                va = V[kring][0][:, tt_, hp, 0:2, :] if h % 2 == 0 else V[kring][0][:, tt_, hp, 1:3, :]
                last = ti == len(tiles) - 1
                P.mm(ps[ob][:, c0:c1], va.rearrange("p a b -> p (a b)"), pt[:, c0:c1], False, last,
                     reads=[V[kring][1], b_pt], writes=[bps[ob]], inc=last)
                if last:
                    par = h % 2
                    dr, orow = (64, 0) if par == 0 else (0, 64)
                    rdp, b_rdp = rd2[par]
                    rbp, b_rbp = rb2[par]
                    t2p, b_t2p = t22[par]
                    P.T.op("dve", lambda q, dr=dr, ob=ob, rdp=rdp: q.reciprocal(out=rdp[dr:dr + 1, :], in_=ps[ob][dr:dr + 1, 0:256]),
                           reads=[bps[ob]], writes=[b_rdp])

                    def part2(h=h, hp=hp, par=par, dr=dr, orow=orow, ob=ob, rdp=rdp, b_rdp=b_rdp, rbp=rbp, b_rbp=b_rbp, t2p=t2p, b_t2p=b_t2p):
                        P.mm(ps[par][orow:orow + 64, 0:256], onesf[dr:dr + 1, 0:64], rdp[dr:dr + 1, :], True, True,
                             reads=[b_onesf, b_rdp], writes=[bps[par]])
                        P.act(rbp[orow:orow + 64, :], ps[par][orow:orow + 64, 0:256], AF.Copy, reads=[bps[par]], writes=[b_rbp])
                        P.tt("dve", t2p[orow:orow + 64, :], ps[ob][orow:orow + 64, 0:256], rbp[orow:orow + 64, :], ALU.mult,
                             reads=[bps[ob], b_rbp], writes=[b_t2p])
                        P.tt("pool", og[orow:orow + 64, hp, :], t2p[orow:orow + 64, :], GT[sl][0][orow:orow + 64, hp, :], ALU.mult,
                             reads=[b_t2p, GT[sl][1]], writes=[b_og])
                    deferred.append((idx + 2, part2))

            for idx in range(len(W) + 1):
                if idx < len(W):
                    emit_scores(*W[idx])
                if idx >= 1:
                    emit_pv(W[idx - 1][0], W[idx - 1][1], idx)
                while deferred and deferred[0][0] <= idx:
                    deferred.pop(0)[1]()
            while deferred:
                deferred.pop(0)[1]()
            for t in range(2):
                n = 2 * b + t
                sx = C["cnt"] % 2
                C["cnt"] += 1
                xt, bxt = C["xt"][sx], C["bxt"][sx]
                P.dma("sp", xt[:], src[s, n * 128:(n + 1) * 128, :], reads=[bsrc[s]], writes=[bxt])
                for hh in range(2):
                    for k in range(8):
                        P.mm(ps[2 + hh][:], og[:, k, t * 128:(t + 1) * 128], woutc[:, k, hh * 512:(hh + 1) * 512], k == 0, k == 7,
                             reads=[b_og, b_woutc], writes=[bps[2 + hh]], inc=(k == 7))
                E["post"](C, s, (2, 3), xt, bxt, dst, bdst[s], n)

        for s in range(nseq):
            proj(s, 0)
            for b in range(nblk):
                if b + 1 < nblk:
                    proj(s, b + 1)
                attn(s, b)
        E["T_barrier"]()


def _common_inputs(p, L):
    f32 = np.float32
    m = {}
    def T8(a):
        return np.ascontiguousarray(a.reshape(a.shape[0], -1, 128).transpose(0, 2, 1)).astype(f32)
    m["npreT"] = T8(p["norm_pre"])
    m["npostT"] = T8(p["norm_post"])
    m["wmod"] = np.ascontiguousarray(p["w_mod"], dtype=f32)
    m["bmodT"] = T8(p["b_mod"])
    m["w_in_ab"] = np.ascontiguousarray(p["w_in_ab"], dtype=f32)
    m["w_out_ab"] = np.ascontiguousarray(p["w_out_ab"], dtype=f32)
    m["w_glu"] = np.ascontiguousarray(p["ssm_w_glu"], dtype=f32)
    m["ssm_d"] = np.ascontiguousarray(p["ssm_d"], dtype=f32)
    sm3, rep3, bl, cl = [], [], [], []
    for j in range(2):
        a, b, c, d = _s5_layout(p["ssm_a_re"][j], p["ssm_a_im"][j], p["ssm_log_step"][j], p["ssm_b_re"][j],
                                p["ssm_b_im"][j], p["ssm_c_re"][j], p["ssm_c_im"][j])
        sm3.append(a); rep3.append(b); bl.append(c); cl.append(d)
    m["s5_sm3"] = np.stack(sm3).astype(f32)
    m["s5_rep3"] = np.stack(rep3).astype(f32)
    m["s5_bl"] = np.stack(bl).astype(f32)
    m["s5_cl"] = np.stack(cl).astype(f32)
    m["w_in_c"] = np.ascontiguousarray(p["w_in_c"], dtype=f32)
    m["w_out_c"] = np.ascontiguousarray(p["w_out_c"], dtype=f32)
    zs = []
    for j in range(2):
        z, mask = _na_layout(np.asarray(p["na_rel_bias"][j], dtype=f32))
        zs.append(z)
    m["zg"] = np.stack(zs).astype(f32)
    m["zmask"] = mask
    m["ident"] = np.eye(128, dtype=f32)
    m["jmat"] = np.eye(128, dtype=f32)[::-1].copy()
    m["rot"] = _rot_tables(L)
    dt, gam, cd = _ret_consts()
    m["dtab"] = dt
    m["gam"] = np.ascontiguousarray(gam.transpose(0, 1, 2))
    m["cdec"] = cd
    m["jt"] = np.broadcast_to(np.arange(128, dtype=f32)[None, :], (128, 128)).copy()
    return m


_PROG_CACHE = {}


def run_cores(xs_per_core, cs_per_core, params, layers):
    nseq, L, _ = xs_per_core[0].shape
    key = (nseq, L, tuple(layers))
    if key not in _PROG_CACHE:
        _PROG_CACHE[key] = build(nseq, L, list(layers))
    P = _PROG_CACHE[key]
    com = _common_inputs(params, L)
    in_maps = []
    for x, c in zip(xs_per_core, cs_per_core):
        m = dict(com)
        m["x_in"] = np.ascontiguousarray(x, dtype=np.float32)
        m["cT"] = np.ascontiguousarray(c.reshape(nseq, 8, 128).transpose(2, 1, 0), dtype=np.float32)
        in_maps.append(m)
    res = run_bass_kernel_spmd(P.nc, in_maps, core_ids=list(range(len(in_maps))))
    return [np.asarray(r["y_out"]) for r in res.results]


def kernel(**inputs):
    p = {k: np.asarray(v) for k, v in inputs.items()}
    xp, xsamp = p["x_prompt"], p["x_sample"]
    cp, cs = p["c_prompt"], p["c_sample"]
    seqs = [xp[i] for i in range(4)] + [xsamp[i] for i in range(8)]
    cvs = [cp[i] for i in range(4)] + [cs[i] for i in range(8)]
    slots = [(c, 8 + c if c < 4 else c) for c in range(8)]
    xs_pc = [np.stack([seqs[a], seqs[b]]) for a, b in slots]
    cs_pc = [np.stack([cvs[a], cvs[b]]) for a, b in slots]
    outs = run_cores(xs_pc, cs_pc, p, [0, 1, 2, 3])
    res = [None] * 12
    for c, (a, b) in enumerate(slots):
        res[a] = outs[c][0]
        if c < 4:
            res[b] = outs[c][1]
    y_prompt = np.stack(res[0:4]).astype(np.float32)
    y_sample = np.stack(res[4:12]).astype(np.float32)
    return (y_prompt, y_sample)
```

```python
import contextlib
import math
import os
KSTOP = os.environ.get('KSTOP', 'all')
import numpy as np
import concourse.bass as bass
import concourse.mybir as mybir
from concourse.bass_utils import run_bass_kernel_spmd

F32 = mybir.dt.float32
BF16 = mybir.dt.bfloat16
I32 = mybir.dt.int32
ALU = mybir.AluOpType
AF = mybir.ActivationFunctionType

D = 1024
EPS = 1e-6
TWO_PI = 2.0 * math.pi


class Buf:
    __slots__ = ("name", "w", "r")

    def __init__(self, name):
        self.name = name
        self.w = []
        self.r = []


class Eng:
    def __init__(self, name):
        self.name = name
        self.ops = []
        self.known = {}
        self.sem = None
        self.cnt = 0
        self.pending = False
        self.dsems = []
        self.dvals = []
        self.dptr = 0
        self.own = set()


EPOCH = 30000
NDSEM = 10


class Tracker:
    def __init__(self, nc):
        self.nc = nc
        self.engs = {n: Eng(n) for n in ("pe", "act", "dve", "pool", "sp")}
        self.sems = []
        for e in self.engs.values():
            if e.name != "sp":
                e.sem = self._newsem(e.name)
                e.own.add(e.sem)
        self.n_ops = 0

    def _newsem(self, nm):
        h = self.nc.alloc_semaphore(name=f"{nm}_{len(self.sems)}")
        self.sems.append(h)
        return len(self.sems) - 1

    def _wait(self, e, ev):
        s, v = ev
        if e.known.get(s, 0) >= v:
            return
        if s in e.own:
            if e.name == "pe":
                return
            if s == e.sem and v > e.cnt:
                return
        e.known[s] = v
        sem = self.sems[s]
        e.ops.append(lambda q, sem=sem, v=v: q.wait_ge(sem, v))

    def _deps(self, e, reads, writes):
        for b in reads:
            for ev in b.w:
                self._wait(e, ev)
        for b in writes:
            for ev in b.w:
                self._wait(e, ev)
            for ev in b.r:
                self._wait(e, ev)

    def _commit(self, ev, reads, writes):
        for b in reads:
            for i, (s0, v0) in enumerate(b.r):
                if s0 == ev[0]:
                    b.r[i] = (s0, max(v0, ev[1]))
                    break
            else:
                b.r.append(ev)
        for b in writes:
            b.w = [ev]
            b.r = []

    def op(self, eng, fn, reads=(), writes=(), inc=True):
        e = self.engs[eng]
        self.n_ops += 1
        self._deps(e, reads, writes)
        if e.cnt >= EPOCH and inc and not e.pending:
            e.sem = self._newsem(e.name)
            e.cnt = 0
            e.own.add(e.sem)
        if inc:
            e.cnt += 1
            sem = self.sems[e.sem]
            e.ops.append(lambda q, fn=fn, sem=sem: fn(q).then_inc(sem, 1))
            ev = (e.sem, e.cnt)
            e.pending = False
        else:
            e.ops.append(lambda q, fn=fn: fn(q))
            ev = (e.sem, e.cnt + 1)
            e.pending = True
        self._commit(ev, reads, writes)

    def dma(self, eng, out, in_, reads=(), writes=()):
        e = self.engs[eng]
        self.n_ops += 1
        self._deps(e, reads, writes)
        if len(e.dsems) < NDSEM:
            e.dsems.append(self._newsem(e.name + "d"))
            e.dvals.append(0)
            k = len(e.dsems) - 1
        else:
            k = e.dptr
            e.dptr = (e.dptr + 1) % NDSEM
            self._wait(e, (e.dsems[k], e.dvals[k]))
            if e.dvals[k] >= EPOCH * 16:
                e.dsems[k] = self._newsem(e.name + "d")
                e.dvals[k] = 0
        e.dvals[k] += 16
        sem = self.sems[e.dsems[k]]
        e.ops.append(lambda q, out=out, in_=in_, sem=sem: q.dma_start(out=out, in_=in_).then_inc(sem, 16))
        ev = (e.dsems[k], e.dvals[k])
        self._commit(ev, reads, writes)
        return ev

    def finish(self):
        sp = self.engs["sp"]
        for e in self.engs.values():
            for k in range(len(e.dsems)):
                self._wait(sp, (e.dsems[k], e.dvals[k]))
            if e.sem is not None and e.cnt > 0:
                self._wait(sp, (e.sem, e.cnt))

    def replay(self):
        nc = self.nc
        E = self.engs
        with nc.Block() as block:
            @block.tensor
            def _(q):
                for f in E["pe"].ops:
                    f(q)

            @block.scalar
            def _(q):
                for f in E["act"].ops:
                    f(q)

            @block.vector
            def _(q):
                for f in E["dve"].ops:
                    f(q)

            @block.gpsimd
            def _(q):
                for f in E["pool"].ops:
                    f(q)

            @block.sync
            def _(q):
                for f in E["sp"].ops:
                    f(q)


RET_H = 4
NA_H = 16
GW = 64


def _ret_consts():
    f32 = np.float32
    h = np.arange(RET_H, dtype=f32)
    log_g = np.log1p(-np.exp2(-5.0 - h)).astype(f32)
    pos = np.arange(128, dtype=f32)
    dt = np.exp(np.abs(pos[:, None] - pos[None, :])[:, None, :] * log_g[None, :, None]).astype(f32)
    gam = np.zeros((128, 4, RET_H), f32)
    gam[:, 0, :] = np.exp(pos[:, None] * log_g[None])
    gam[:, 1, :] = np.exp((127.0 - pos)[:, None] * log_g[None])
    gam[:, 2, :] = np.exp((128.0 - pos)[:, None] * log_g[None])
    gam[:, 3, :] = np.exp((pos + 1.0)[:, None] * log_g[None])
    cdec = np.exp(128.0 * log_g).astype(f32)
    cd = np.broadcast_to(cdec[None, :], (128, RET_H)).copy()
    return dt, gam, cd


def _rot_tables(L):
    f32 = np.float32
    inv = (10000.0 ** (-np.arange(0, 128, 2, dtype=f32) / 128.0)).astype(f32)
    ang = (np.arange(L, dtype=f32)[:, None] * inv[None, :]).astype(f32)
    c = np.cos(ang).astype(f32)
    s = np.sin(ang).astype(f32)
    rq = np.zeros((L, 2, 128), f32)
    rq[:, 0, :64] = c
    rq[:, 0, 64:] = c
    rq[:, 1, :64] = -s
    rq[:, 1, 64:] = s
    rk = (rq * f32(128.0 ** -0.5)).astype(f32)
    return np.stack([rq, rk], axis=1).copy()


def _na_layout(relb):
    a = np.arange(2)[:, None, None, None]
    k = np.arange(64)[None, :, None, None]
    m = np.arange(-7, 9)[None, None, :, None]
    c = np.arange(64)[None, None, None, :]
    dr = np.clip(a - m + 7, 0, 14)
    dc = np.clip(k - c + 15, 0, 30)
    dr_b, dc_b = np.broadcast_arrays(dr, dc)
    z = relb[:, dr_b, dc_b]
    z = z.reshape(16, 128, 16 * 64).astype(np.float32)
    cs = np.clip(np.arange(64) - 8, 0, 48)
    kk = np.arange(64)[:, None]
    valid = (kk >= cs[None, :]) & (kk < cs[None, :] + 16)
    mask = np.where(valid, 0.0, -30000.0).astype(np.float32)
    mask = np.broadcast_to(mask[None, :, None, :], (2, 64, 16, 64)).reshape(128, 1024).copy()
    return z, mask


def _s5_layout(a_re, a_im, ls, b_re, b_im, c_re, c_im):
    f32 = np.float32
    def sm(a):
        return a.reshape(2, 16, 2, 64).transpose(0, 2, 3, 1).reshape(2, 128, 16).astype(f32)
    lsx = np.broadcast_to(ls[:, :, None], (2, 32, 64))
    sm3 = np.stack([sm(a_re), sm(a_im), sm(lsx)], axis=1).copy()
    rep3 = np.stack([a_re.reshape(2, 2048), a_im.reshape(2, 2048), lsx.reshape(2, 2048)], axis=1).astype(f32).copy()
    bl = np.zeros((2, 2, 128, 16, 2, 64), f32)
    cl = np.zeros((2, 2, 128, 16, 2, 16), f32)
    for s in range(16):
        for g1 in range(2):
            g = 2 * s + g1
            r0 = 32 * (s % 4) + 16 * g1
            for ri, b in enumerate((b_re, b_im)):
                bl[:, ri, r0:r0 + 16, s, g1, :] = b[:, g].transpose(0, 2, 1)
            for ri, c in enumerate((c_re, c_im)):
                cl[:, ri, 64 * g1:64 * g1 + 64, s, g1, :] = c[:, g].transpose(0, 2, 1)
    return sm3, rep3, bl.reshape(2, 2, 128, 16 * 128), cl.reshape(2, 2, 128, 16 * 32)


class Prog:
    def __init__(self, nseq, L, layers):
        self.nseq, self.L, self.layers = nseq, L, layers
        self.nch = L // 128
        nc = self.nc = bass.Bass("TRN2", target_bir_lowering=False)
        self.T = Tracker(nc)
        self.es = contextlib.ExitStack()
        self.dram = {}
        self.bufs = {}

    def sbt(self, name, shape, dt=F32):
        self._uid = getattr(self, '_uid', 0) + 1
        return self.nc.sbuf_tensor(f"{name}_u{self._uid}", list(shape), dt)

    def din(self, name, shape, dt=F32):
        t = self.nc.dram_tensor(name, list(shape), dt, kind="ExternalInput").ap()
        self.dram[name] = t
        return t

    def dout(self, name, shape, dt=F32):
        t = self.nc.dram_tensor(name, list(shape), dt, kind="ExternalOutput").ap()
        self.dram[name] = t
        return t

    def dscr(self, name, shape, dt=F32):
        t = self.nc.dram_tensor(name, list(shape), dt, kind="Internal").ap()
        self.dram[name] = t
        return t

    def sb(self, name, shape, dt=F32):
        t = self.es.enter_context(self.sbt(name, list(shape), dt))
        b = Buf(name)
        return t, b

    def ps(self, name):
        t = self.es.enter_context(self.nc.psum_tensor(name, [128, 512], F32))
        return t, Buf(name)

    def mm(self, out, lhsT, rhs, start, stop, reads, writes, inc=True):
        self.T.op("pe", lambda q: q.matmul(out, lhsT=lhsT, rhs=rhs, start=start, stop=stop),
                  reads=reads, writes=writes, inc=inc)

    def act(self, out, in_, func, reads, writes, scale=1.0, bias=None, accum=None):
        kw = {}
        if bias is not None:
            kw["bias"] = bias
        if accum is not None:
            kw["accum_out"] = accum
        self.T.op("act", lambda q: q.activation(out=out, in_=in_, func=func, scale=scale, **kw),
                  reads=reads, writes=writes)

    def tt(self, eng, out, in0, in1, op, reads, writes):
        self.T.op(eng, lambda q: q.tensor_tensor(out=out, in0=in0, in1=in1, op=op), reads=reads, writes=writes)

    def ts(self, eng, out, in0, s1, s2, op0, op1, reads, writes):
        if s2 is None:
            self.T.op(eng, lambda q: q.tensor_scalar(out=out, in0=in0, scalar1=s1, scalar2=None, op0=op0),
                      reads=reads, writes=writes)
        else:
            self.T.op(eng, lambda q: q.tensor_scalar(out=out, in0=in0, scalar1=s1, scalar2=s2, op0=op0, op1=op1),
                      reads=reads, writes=writes)

    def stt(self, eng, out, in0, scalar, in1, op0, op1, reads, writes):
        self.T.op(eng, lambda q: q.scalar_tensor_tensor(out=out, in0=in0, scalar=scalar, in1=in1, op0=op0, op1=op1),
                  reads=reads, writes=writes)

    def cp(self, eng, out, in_, reads, writes):
        self.T.op(eng, lambda q: q.tensor_copy(out=out, in_=in_), reads=reads, writes=writes)

    def memset(self, eng, ap, val, writes):
        self.T.op(eng, lambda q: q.memset(ap, val), reads=(), writes=writes)

    def dma(self, eng, out, in_, reads=(), writes=()):
        return self.T.dma(eng, out, in_, reads=reads, writes=writes)


def build(nseq, L, layers):
    P = Prog(nseq, L, layers)
    nc, T = P.nc, P.T
    nch = L // 128
    NL = 4
    x_in = P.din("x_in", [nseq, L, D])
    y_out = P.dout("y_out", [nseq, L, D])
    cT = P.din("cT", [128, 8, nseq])
    npreT = P.din("npreT", [NL, 128, 8])
    npostT = P.din("npostT", [NL, 128, 8])
    wmod = P.din("wmod", [NL, D, 3 * D])
    bmodT = P.din("bmodT", [NL, 128, 24])
    w_in_ab = P.din("w_in_ab", [2, D, 3072])
    w_out_ab = P.din("w_out_ab", [2, D, D])
    w_glu = P.din("w_glu", [2, 512, 512])
    ssm_d = P.din("ssm_d", [2, 512])
    s5_sm3 = P.din("s5_sm3", [2, 2, 3, 128, 16])
    s5_rep3 = P.din("s5_rep3", [2, 2, 3, 2048])
    s5_bl = P.din("s5_bl", [2, 2, 2, 128, 2048])
    s5_cl = P.din("s5_cl", [2, 2, 2, 128, 512])
    w_in_c = P.din("w_in_c", [2, D, 4096])
    w_out_c = P.din("w_out_c", [2, D, D])
    zg = P.din("zg", [2, 16, 128, 1024])
    zmask = P.din("zmask", [128, 1024])
    ident_d = P.din("ident", [128, 128])
    jmat_d = P.din("jmat", [128, 128])
    rot_d = P.din("rot", [L, 2, 2, 128])
    dtab_d = P.din("dtab", [128, 4, 128])
    gam_d = P.din("gam", [128, 4, 4])
    cdec_d = P.din("cdec", [128, 4])
    jt_d = P.din("jt", [128, 128])
    xs = [P.dscr("xsA", [nseq, L, D]), P.dscr("xsB", [nseq, L, D])]
    OPs_ = P.dscr("OPs", [nseq, L, 512])
    YPs_ = P.dscr("YPs", [nseq, L, 512])
    SGs_ = P.dscr("SGs", [nseq, L, 1024])
    QTs_ = P.dscr("QTs", [nseq, nch, 128, 512], BF16)
    KRs_ = P.dscr("KRs", [nseq, L, 512], BF16)
    VBs_ = P.dscr("VBs", [nseq, L, 512], BF16)
    UBs_ = P.dscr("UBs", [nseq, L, 512], BF16)
    SCRB = [[Buf(f"{n}{i}") for n in "OP YP SG QT KR VB UB".split()] for i in range(nseq)]
    b_xs = [[Buf(f"xs{i}_{s}") for s in range(nseq)] for i in range(2)]
    b_yout = [Buf(f"yout{s}") for s in range(nseq)]
    b_xin = Buf("xin")

    ident_bf, b_ident = P.sb("ident_bf", [128, 128], BF16)
    jmat_bf, b_jmat = P.sb("jmat_bf", [128, 128], BF16)
    identf, b_identf = P.sb("identf", [128, 128])
    epsT, b_eps = P.sb("epsT", [128, 1])
    ss, b_ss = P.sb("ss", [128, 4])
    sd, b_sd = P.sb("sd", [128, 4])
    rstd, b_rstd = P.sb("rstd", [128, 4])
    scT, b_scT = P.sb("scT", [128, 8, nseq])
    cTs, b_cTs = P.sb("cTs", [128, 8, nseq])
    modT, b_modT = P.sb("modT", [128, 24, nseq])
    gsT, b_gsT = P.sb("gsT", [128, 8, nseq])
    ggT, b_ggT = P.sb("ggT", [128, 8, nseq])
    ggrow = [P.sb(f"ggrow{s}", [128, 1024]) for s in range(nseq)]
    vecs, b_vecs = P.sb("vecs", [128, 40])
    psb = [P.ps(f"ps{i}") for i in range(8)]
    ps = [p[0] for p in psb]
    bps = [p[1] for p in psb]

    P.dma("sp", identf[:], ident_d, writes=[b_identf])
    P.dma("pool", ident_bf[:], ident_d, writes=[b_ident])
    P.dma("pool", jmat_bf[:], jmat_d, writes=[b_jmat])
    P.memset("pool", epsT[:], EPS, writes=[b_eps])
    P.dma("sp", cTs[:], cT, writes=[b_cTs])
    P.act(scT[:], cTs[:], AF.Sigmoid, reads=[b_cTs], writes=[b_scT])
    P.tt("dve", scT[:], scT[:], cTs[:], ALU.mult, reads=[b_scT, b_cTs], writes=[b_scT])

    def rstd_from_ss(col, n_feat):
        P.act(sd[:, col:col + 1], ss[:, col:col + 1], AF.Sqrt, reads=[b_ss, b_eps], writes=[b_sd],
              scale=1.0 / n_feat, bias=epsT[:, 0:1])
        P.T.op("dve", lambda q: q.reciprocal(out=rstd[:, col:col + 1], in_=sd[:, col:col + 1]),
               reads=[b_sd], writes=[b_rstd])

    def adaln(li, st_unused):
      with contextlib.ExitStack() as st:
        wm = [st.enter_context(P.sbt(f"wm{j}_{li}", [128, 8, 128], F32)) for j in range(2)]
        bwm = [Buf("wm0"), Buf("wm1")]
        gbl = st.enter_context(P.sbt(f"gbl_{li}", [128, 8, 128], F32))
        b_gbl = Buf("gbl")
        P.dma("sp", vecs[:, 0:8], npreT[li], writes=[b_vecs])
        P.dma("sp", vecs[:, 8:16], npostT[li], writes=[b_vecs])
        P.dma("sp", vecs[:, 16:40], bmodT[li], writes=[b_vecs])
        wsrc = wmod[li].rearrange("(k p) n -> p k n", p=128)
        for j in range(24):
            sl = j % 2
            P.dma("sp", wm[sl][:], wsrc[:, :, j * 128:(j + 1) * 128], writes=[bwm[sl]])
            for k in range(8):
                P.mm(ps[7][:, j * nseq:(j + 1) * nseq], wm[sl][:, k, :], scT[:, k, :], k == 0, k == 7,
                     reads=[bwm[sl], b_scT], writes=[bps[7]], inc=(k == 7))
        psv = ps[7][:, 0:24 * nseq].rearrange("p (j s) -> p j s", s=nseq)
        P.tt("dve", modT[:], psv, vecs[:, 16:40].unsqueeze(2).to_broadcast([128, 24, nseq]), ALU.add,
             reads=[bps[7], b_vecs], writes=[b_modT])
        P.ts("dve", gsT[:], modT[:, 8:16, :], 1.0, None, ALU.add, None, reads=[b_modT], writes=[b_gsT])
        P.tt("dve", gsT[:], gsT[:], vecs[:, 0:8].unsqueeze(2).to_broadcast([128, 8, nseq]), ALU.mult,
             reads=[b_gsT, b_vecs], writes=[b_gsT])
        P.tt("dve", ggT[:], modT[:, 16:24, :], vecs[:, 8:16].unsqueeze(2).to_broadcast([128, 8, nseq]), ALU.mult,
             reads=[b_modT, b_vecs], writes=[b_ggT])
        for s in range(nseq):
            P.cp("dve", gbl[:], ggT[:, :, s:s + 1].to_broadcast([128, 8, 128]), reads=[b_ggT], writes=[b_gbl])
            for c in range(8):
                bk = 5 + c // 4
                P.mm(ps[bk][:, (c % 4) * 128:(c % 4 + 1) * 128], gbl[:, c, :], identf[:], True, True,
                     reads=[b_gbl, b_identf], writes=[bps[bk]], inc=(c % 4 == 3))
            P.cp("dve", ggrow[s][0][:, 0:512], ps[5][:], reads=[bps[5]], writes=[ggrow[s][1]])
            P.act(ggrow[s][0][:, 512:1024], ps[6][:], AF.Copy, reads=[bps[6]], writes=[ggrow[s][1]])

    def make_common(st, tag):
        C = {}
        C["xt"] = [st.enter_context(P.sbt(f"xt{j}_{tag}", [128, 1024], F32)) for j in range(2)]
        C["bxt"] = [Buf("xt0"), Buf("xt1")]
        C["xn"] = st.enter_context(P.sbt(f"xn_{tag}", [128, 1024], BF16))
        C["bxn"] = Buf("xn")
        C["junk"] = st.enter_context(P.sbt(f"junk_{tag}", [128, 1024], BF16))
        C["bjunk"] = Buf("junk")
        C["yt"] = st.enter_context(P.sbt(f"yt_{tag}", [128, 1024], F32))
        C["byt"] = Buf("yt")
        C["cnt"] = 0
        return C

    def prenorm_parts(C, s, src, bsrc, n, hT, bhT, col0):
        sl = C["cnt"] % 2
        C["cnt"] += 1
        xt, bxt = C["xt"][sl], C["bxt"][sl]

        def p1():
            P.dma("sp", xt[:], src[s, n * 128:(n + 1) * 128, :], reads=[bsrc], writes=[bxt])
            P.act(C["yt"][:], xt[:], AF.Square, reads=[bxt], writes=[C["byt"]])
            P.T.op("dve", lambda q, yt_=C["yt"]: q.reduce_sum(out=ss[:, 0:1], in_=yt_[:], axis=mybir.AxisListType.X),
                   reads=[C["byt"]], writes=[b_ss])
            P.act(sd[:, 0:1], ss[:, 0:1], AF.Sqrt, reads=[b_ss, b_eps], writes=[b_sd], scale=1.0 / D, bias=epsT[:, 0:1])

        def p2():
            P.T.op("dve", lambda q: q.reciprocal(out=rstd[:, 0:1], in_=sd[:, 0:1]), reads=[b_sd], writes=[b_rstd])
            P.act(C["xn"][:], xt[:], AF.Copy, reads=[bxt, b_rstd], writes=[C["bxn"]], scale=rstd[:, 0:1])
            for b in range(2):
                for j in range(4):
                    k = 4 * b + j
                    P.mm(ps[b][:, j * 128:(j + 1) * 128], C["xn"][:, k * 128:(k + 1) * 128], ident_bf[:], True, True,
                         reads=[C["bxn"], b_ident], writes=[bps[b]], inc=(j == 3))

        def p3():
            for b in range(2):
                for j in range(4):
                    k = 4 * b + j
                    o = hT[:, k, col0:col0 + 128]
                    i_ = ps[b][:, j * 128:(j + 1) * 128]
                    if b == 0:
                        P.act(o, i_, AF.Identity, reads=[bps[b], b_gsT, b_modT], writes=[bhT],
                              scale=gsT[:, k, s:s + 1], bias=modT[:, k, s:s + 1])
                    else:
                        P.ts("dve", o, i_, gsT[:, k, s:s + 1], modT[:, k, s:s + 1], ALU.mult, ALU.add,
                             reads=[bps[b], b_gsT, b_modT], writes=[bhT])
        return p1, p2, p3

    def prenorm(C, s, src, bsrc, n, hT, bhT, col0):
        p1, p2, p3 = prenorm_parts(C, s, src, bsrc, n, hT, bhT, col0)
        p1()
        p2()
        p3()

    def post(C, s, ypb, xt, bxt, dst, bdst, n):
        yt, byt = C["yt"], C["byt"]
        for h in range(2):
            P.act(yt[:, h * 512:(h + 1) * 512], ps[ypb[h]][:], AF.Square, reads=[bps[ypb[h]]], writes=[byt])
        P.T.op("dve", lambda q, yt_=yt: q.reduce_sum(out=ss[:, 3:4], in_=yt_[:], axis=mybir.AxisListType.X),
               reads=[byt], writes=[b_ss])
        rstd_from_ss(3, D)
        for h in range(2):
            P.act(yt[:, h * 512:(h + 1) * 512], ps[ypb[h]][:], AF.Copy, reads=[bps[ypb[h]], b_rstd], writes=[byt],
                  scale=rstd[:, 3:4])
        P.tt("dve", yt[:], yt[:], ggrow[s][0][:], ALU.mult, reads=[byt, ggrow[s][1]], writes=[byt])
        P.tt("pool", yt[:], yt[:], xt[:], ALU.add, reads=[byt, bxt], writes=[byt])
        P.dma("pool", dst[s, n * 128:(n + 1) * 128, :], yt[:], reads=[byt], writes=[bdst])

    def sincos(st, tag, phi, bphi, shape, out_sin, out_cos, bout):
        tf = st.enter_context(P.sbt(f"sc_tf_{tag}", shape, F32))
        ti = st.enter_context(P.sbt(f"sc_ti_{tag}", shape, I32))
        btf, bti = Buf("tf"), Buf("ti")
        for shift, o in ((0.0, out_sin), (0.5 * math.pi, out_cos)):
            P.ts("dve", tf[:], phi, shift, 1.0 / TWO_PI, ALU.add, ALU.mult, reads=[bphi], writes=[btf])
            P.cp("dve", ti[:], tf[:], reads=[btf], writes=[bti])
            P.cp("dve", tf[:], ti[:], reads=[bti], writes=[btf])
            P.stt("dve", tf[:], tf[:], -TWO_PI, phi, ALU.mult, ALU.add, reads=[btf, bphi], writes=[btf])
            P.ts("dve", tf[:], tf[:], shift, 0.999999, ALU.add, ALU.mult, reads=[btf], writes=[btf])
            P.act(o, tf[:], AF.Sin, reads=[btf], writes=[bout])

    def even_layer(li, src, bsrc, dst, bdst):
        j = li // 2
        with contextlib.ExitStack() as st:
            def S(name, shape, dt=F32):
                return st.enter_context(P.sbt(f"{name}_e{li}", list(shape), dt)), Buf(name)
            adaln(li, st)
            C = make_common(st, f"e{li}")
            win, b_win = S("win", [128, 8, 3072], BF16)
            wout, b_wout = S("wout", [128, 8, 1024], BF16)
            wglu, b_wglu = S("wglu", [128, 4, 512], BF16)
            drow, b_drow = S("drow", [128, 512])
            dtab, b_dtab = S("dtab", [128, 4, 128])
            gam, b_gam = S("gam", [128, 4, 4])
            cdec, b_cdec = S("cdec", [128, 4])
            jt, b_jt = S("jt", [128, 128])
            wsrc = w_in_ab[j].rearrange("(k p) n -> p k n", p=128)
            for k in range(8):
                P.dma("pool", win[:, k, :], wsrc[:, k, :], writes=[b_win])
            P.dma("pool", wout[:], w_out_ab[j].rearrange("(k p) n -> p k n", p=128), writes=[b_wout])
            P.dma("pool", wglu[:], w_glu[j].rearrange("(k p) n -> p k n", p=128), writes=[b_wglu])
            P.dma("sp", drow[:], ssm_d[j].partition_broadcast(128), writes=[b_drow])
            P.dma("sp", dtab[:], dtab_d, writes=[b_dtab])
            P.dma("sp", gam[:], gam_d, writes=[b_gam])
            P.dma("sp", cdec[:], cdec_d, writes=[b_cdec])
            P.dma("sp", jt[:], jt_d, writes=[b_jt])
            WB = [S(f"WB{r}", [128, 2048], BF16) for r in range(2)]
            WC = [S(f"WC{r}", [128, 512], BF16) for r in range(3)]
            COS, b_COS = S("COS", [128, 16, 128])
            SIN, b_SIN = S("SIN", [128, 16, 128])
            RHO0, b_RHO0 = S("RHO0", [128, 16, 128])
            Gc, b_Gc = S("Gc", [128, 2, 16])
            cin = [S(f"cin{i}", [128, 2, 16]) for i in range(2)]
            stf, b_stf = S("stf", [128, 512])
            stbf, b_stbf = S("stbf", [128, 512], BF16)
            hn, b_hn = S("hn", [128, 16])
            lst, b_lst = S("lst", [128, 8, 16])
            lastb = [S(f"lastb{i}", [128, 2, 16]) for i in range(2)]

            def s5_setup(d):
                with contextlib.ExitStack() as s2:
                    def S2(name, shape, dt=F32):
                        return s2.enter_context(P.sbt(f"{name}_e{li}d{d}", list(shape), dt)), Buf(name)
                    sm, b_sm = S2("sm", [128, 3, 16])
                    P.dma("sp", sm[:], s5_sm3[j, d].rearrange("t p s -> p t s"), writes=[b_sm])
                    dl, b_dl = S2("dl", [128, 16])
                    zr, b_zr = S2("zr", [128, 16])
                    zi, b_zi = S2("zi", [128, 16])
                    rho, b_rho = S2("rho", [128, 16])
                    P.act(dl[:], sm[:, 2, :], AF.Exp, reads=[b_sm], writes=[b_dl])
                    P.tt("dve", zr[:], sm[:, 0, :], dl[:], ALU.mult, reads=[b_sm, b_dl], writes=[b_zr])
                    P.tt("dve", zi[:], sm[:, 1, :], dl[:], ALU.mult, reads=[b_sm, b_dl], writes=[b_zi])
                    P.act(rho[:], zr[:], AF.Exp, reads=[b_zr], writes=[b_rho])
                    for g4 in range(4):
                        with contextlib.ExitStack() as s4:
                            phi = s4.enter_context(P.sbt(f"phi_e{li}d{d}g{g4}", [128, 4, 128], F32))
                            b_phi = Buf("phi")
                            P.tt("dve", phi[:], zi[:, 4 * g4:4 * g4 + 4].unsqueeze(2).to_broadcast([128, 4, 128]),
                                 jt[:].unsqueeze(1).to_broadcast([128, 4, 128]), ALU.mult, reads=[b_zi, b_jt], writes=[b_phi])
                            sincos(s4, f"t{li}{d}{g4}", phi[:], b_phi, [128, 4, 128], SIN[:, 4 * g4:4 * g4 + 4, :],
                                   COS[:, 4 * g4:4 * g4 + 4, :], b_COS)
                            T_barrier()
                    P.cp("dve", RHO0[:], rho[:].unsqueeze(2).to_broadcast([128, 16, 128]), reads=[b_rho], writes=[b_RHO0])
                    P.memset("dve", RHO0[:, :, 0:1], 0.0, writes=[b_RHO0])
                    ph2, b_ph2 = S2("ph2", [128, 16])
                    sn2, b_sn2 = S2("sn2", [128, 2, 16])
                    P.ts("dve", ph2[:], zi[:], 128.0, None, ALU.mult, None, reads=[b_zi], writes=[b_ph2])
                    sincos(s2, f"g{li}{d}", ph2[:], b_ph2, [128, 16], sn2[:, 1, :], sn2[:, 0, :], b_sn2)
                    P.tt("dve", Gc[:], sn2[:], rho[:].unsqueeze(1).to_broadcast([128, 2, 16]), ALU.mult,
                         reads=[b_sn2, b_rho], writes=[b_Gc])
                    T_barrier()
                for cc in range(8):
                  with contextlib.ExitStack() as s3:
                    def S3(name, shape, dt=F32):
                        return s3.enter_context(P.sbt(f"{name}_e{li}d{d}c{cc}", list(shape), dt)), Buf(name)
                    rp, b_rp = S3("rp", [128, 3, 256])
                    P.dma("sp", rp[:], s5_rep3[j, d][:, cc * 256:(cc + 1) * 256].partition_broadcast(128), writes=[b_rp])
                    r_dl, b_r_dl = S3("r_dl", [128, 256])
                    r_zr, b_r_zr = S3("r_zr", [128, 256])
                    r_zi, b_r_zi = S3("r_zi", [128, 256])
                    r_rho, b_r_rho = S3("r_rho", [128, 256])
                    r_sn, b_r_sn = S3("r_sn", [128, 2, 256])
                    P.act(r_dl[:], rp[:, 2, :], AF.Exp, reads=[b_rp], writes=[b_r_dl])
                    P.tt("dve", r_zr[:], rp[:, 0, :], r_dl[:], ALU.mult, reads=[b_rp, b_r_dl], writes=[b_r_zr])
                    P.tt("dve", r_zi[:], rp[:, 1, :], r_dl[:], ALU.mult, reads=[b_rp, b_r_dl], writes=[b_r_zi])
                    P.act(r_rho[:], r_zr[:], AF.Exp, reads=[b_r_zr], writes=[b_r_rho])
                    sincos(s3, f"r{li}{d}{cc}", r_zi[:], b_r_zi, [128, 256], r_sn[:, 1, :], r_sn[:, 0, :], b_r_sn)
                    P.tt("dve", r_sn[:], r_sn[:], r_rho[:].unsqueeze(1).to_broadcast([128, 2, 256]), ALU.mult,
                         reads=[b_r_sn, b_r_rho], writes=[b_r_sn])
                    P.ts("dve", r_sn[:, 0, :], r_sn[:, 0, :], -1.0, None, ALU.add, None, reads=[b_r_sn], writes=[b_r_sn])
                    P.tt("dve", r_dl[:], rp[:, 0, :], rp[:, 0, :], ALU.mult, reads=[b_rp], writes=[b_r_dl])
                    P.tt("dve", r_zr[:], rp[:, 1, :], rp[:, 1, :], ALU.mult, reads=[b_rp], writes=[b_r_zr])
                    P.tt("dve", r_dl[:], r_dl[:], r_zr[:], ALU.add, reads=[b_r_dl, b_r_zr], writes=[b_r_dl])
                    P.T.op("dve", lambda q, r_dl=r_dl: q.reciprocal(out=r_dl[:], in_=r_dl[:]), reads=[b_r_dl], writes=[b_r_dl])
                    P.tt("dve", r_zr[:], r_sn[:, 0, :], rp[:, 0, :], ALU.mult, reads=[b_r_sn, b_rp], writes=[b_r_zr])
                    P.tt("dve", r_rho[:], r_sn[:, 1, :], rp[:, 1, :], ALU.mult, reads=[b_r_sn, b_rp], writes=[b_r_rho])
                    P.tt("dve", r_zr[:], r_zr[:], r_rho[:], ALU.add, reads=[b_r_zr, b_r_rho], writes=[b_r_zr])
                    P.tt("dve", r_zr[:], r_zr[:], r_dl[:], ALU.mult, reads=[b_r_zr, b_r_dl], writes=[b_r_zr])
                    P.tt("dve", r_zi[:], r_sn[:, 1, :], rp[:, 0, :], ALU.mult, reads=[b_r_sn, b_rp], writes=[b_r_zi])
                    P.tt("dve", r_rho[:], r_sn[:, 0, :], rp[:, 1, :], ALU.mult, reads=[b_r_sn, b_rp], writes=[b_r_rho])
                    P.tt("dve", r_zi[:], r_zi[:], r_rho[:], ALU.subtract, reads=[b_r_zi, b_r_rho], writes=[b_r_zi])
                    P.tt("dve", r_zi[:], r_zi[:], r_dl[:], ALU.mult, reads=[b_r_zi, b_r_dl], writes=[b_r_zi])
                    bl, b_bl = S3("bl", [128, 2, 256])
                    P.dma("sp", bl[:], s5_bl[j, d][:, :, cc * 256:(cc + 1) * 256].rearrange("r p n -> p r n"), writes=[b_bl])
                    P.tt("dve", r_dl[:], r_zr[:], bl[:, 0, :], ALU.mult, reads=[b_r_zr, b_bl], writes=[b_r_dl])
                    P.tt("dve", r_rho[:], r_zi[:], bl[:, 1, :], ALU.mult, reads=[b_r_zi, b_bl], writes=[b_r_rho])
                    P.tt("dve", WB[0][0][:, cc * 256:(cc + 1) * 256], r_dl[:], r_rho[:], ALU.subtract, reads=[b_r_dl, b_r_rho], writes=[WB[0][1]])
                    P.tt("dve", r_dl[:], r_zr[:], bl[:, 1, :], ALU.mult, reads=[b_r_zr, b_bl], writes=[b_r_dl])
                    P.tt("dve", r_rho[:], r_zi[:], bl[:, 0, :], ALU.mult, reads=[b_r_zi, b_bl], writes=[b_r_rho])
                    P.tt("dve", WB[1][0][:, cc * 256:(cc + 1) * 256], r_dl[:], r_rho[:], ALU.add, reads=[b_r_dl, b_r_rho], writes=[WB[1][1]])

                    T_barrier()
                with contextlib.ExitStack() as s2:
                    def S2(name, shape, dt=F32):
                        return s2.enter_context(P.sbt(f"{name}_e{li}d{d}x", list(shape), dt)), Buf(name)
                    cl, b_cl = S2("cl", [128, 2, 512])
                    P.dma("sp", cl[:], s5_cl[j, d].rearrange("r p n -> p r n"), writes=[b_cl])
                    P.cp("dve", WC[0][0][:], cl[:, 0, :], reads=[b_cl], writes=[WC[0][1]])
                    P.ts("dve", WC[1][0][:], cl[:, 0, :], -1.0, None, ALU.mult, None, reads=[b_cl], writes=[WC[1][1]])
                    P.ts("dve", WC[2][0][:], cl[:, 1, :], -1.0, None, ALU.mult, None, reads=[b_cl], writes=[WC[2][1]])
                    P.memset("dve", cin[0][0][:], 0.0, writes=[cin[0][1]])
                    T_barrier()

            import types

            def alloc_work(wk, passB):
                def W(name, shape, dt=F32):
                    return wk.enter_context(P.sbt(f"{name}_e{li}", list(shape), dt)), Buf(name)
                w = types.SimpleNamespace()
                w.g = [types.SimpleNamespace() for _ in range(2)]
                tmps = {nm: W(f"{nm}m", [128, 512]) for nm in ("tA", "tB", "tC", "tD")}
                Pk1 = [W(f"Pk_{k}", [128, 512], BF16) for k in range(4)]
                for i, g in enumerate(w.g):
                    for nm in ("wre", "wim", "sre", "sim"):
                        setattr(g, nm, W(f"{nm}{i}", [128, 512]))
                    for nm in ("tA", "tB", "tC", "tD"):
                        setattr(g, nm, tmps[nm])
                    g.Pk = Pk1
                w.uT = W("uT", [128, 4, 128], BF16)
                w.kf = W("kf", [128, 512], BF16)
                w.tC = W("tCr", [128, 512])
                if not passB:
                    w.hT = [W(f"hT{i}", [128, 8, 128], BF16) for i in range(2)]
                    w.rot = [W(f"rot{i}", [128, 2, 2, 128]) for i in range(2)]
                    w.rA = W("rA", [128, 512])
                    w.rB = W("rB", [128, 512])
                    w.qr = W("qr", [128, 512], BF16)
                    w.kr = [W("kr", [128, 512], BF16)]
                    w.vb = [W("vb", [128, 512], BF16)]
                    w.QT = [W("QT", [128, 512], BF16)]
                    w.KT = W("KT", [128, 512], BF16)
                    w.STb = W("STb", [128, 512], BF16)
                    w.sg = [W("sg", [128, 1024])]
                    w.du = W("du", [128, 512])
                    w.ub = [W("ub", [128, 512], BF16)]
                    w.opt = [W("opt", [128, 512])]
                    w.ypt = [W("ypt", [128, 512])]
                else:
                    w.kr = [W(f"kr{i}", [128, 512], BF16) for i in range(2)]
                    w.vb = [W(f"vb{i}", [128, 512], BF16) for i in range(2)]
                    w.QT = [W(f"QT{i}", [128, 512], BF16) for i in range(2)]
                    w.sg = [W("sg0", [128, 1024])] * 2
                    w.ub = [W(f"ub{i}", [128, 512], BF16) for i in range(2)]
                    w.opt = [W(f"opt{i}", [128, 512]) for i in range(2)]
                    w.ypt = [W(f"ypt{i}", [128, 512]) for i in range(2)]
                    w.oab = W("oab", [128, 1024], BF16)
                    w.oT = W("oT", [128, 8, 128], BF16)
                    w.ygb = W("ygb", [128, 512], BF16)
                    w.tA = W("tAh", [128, 512])
                    w.tB = W("tBh", [128, 512])
                    w.tD = W("tDh", [128, 512])
                return w

            def s5_chunk(w, ci, rev, hooks=()):
                cur, nxt = cin[ci % 2], cin[(ci + 1) % 2]
                lb, b_lb = lastb[ci % 2]
                uT, b_uT = w.uT
                hooks = list(hooks)

                def hook():
                    if hooks:
                        hooks.pop(0)()

                def banks(g):
                    return (3, 4) if g % 2 == 0 else (5, 6)

                def stageApe(g):
                    br, bi = banks(g)
                    for t in range(4):
                        s_ = 4 * g + t
                        P.mm(ps[br][:, t * 128:(t + 1) * 128], WB[0][0][:, s_ * 128:(s_ + 1) * 128], uT[:, g, :], True, False,
                             reads=[WB[0][1], b_uT], writes=[bps[br]], inc=False)
                        P.mm(ps[bi][:, t * 128:(t + 1) * 128], WB[1][0][:, s_ * 128:(s_ + 1) * 128], uT[:, g, :], True, False,
                             reads=[WB[1][1], b_uT], writes=[bps[bi]], inc=False)
                    o_r = ps[br][:].rearrange("p (a b) -> p a b", a=4)[:, :, 0:1]
                    o_i = ps[bi][:].rearrange("p (a b) -> p a b", a=4)[:, :, 0:1]
                    P.mm(o_r, identf[:], cur[0][:, 0, 4 * g:4 * g + 4].unsqueeze(2), False, True,
                         reads=[b_identf, cur[1]], writes=[bps[br]], inc=False)
                    P.mm(o_i, identf[:], cur[0][:, 1, 4 * g:4 * g + 4].unsqueeze(2), False, True,
                         reads=[b_identf, cur[1]], writes=[bps[bi]], inc=True)

                def stageAve(g):
                    br, bi = banks(g)
                    G = w.g[g % 2]
                    Cg = COS[:, 4 * g:4 * g + 4, :].rearrange("p a b -> p (a b)")
                    Sg = SIN[:, 4 * g:4 * g + 4, :].rearrange("p a b -> p (a b)")
                    P.tt("dve", G.tA[0][:], ps[br][:], Cg, ALU.mult, reads=[bps[br], b_COS], writes=[G.tA[1]])
                    P.tt("dve", G.tB[0][:], ps[bi][:], Sg, ALU.mult, reads=[bps[bi], b_COS], writes=[G.tB[1]])
                    P.tt("pool", G.wre[0][:], G.tA[0][:], G.tB[0][:], ALU.add, reads=[G.tA[1], G.tB[1]], writes=[G.wre[1]])
                    P.tt("dve", G.tC[0][:], ps[bi][:], Cg, ALU.mult, reads=[bps[bi], b_COS], writes=[G.tC[1]])
                    P.stt("dve", G.tD[0][:], ps[br][:], -1.0, Sg, ALU.mult, ALU.mult, reads=[bps[br], b_COS], writes=[G.tD[1]])
                    P.tt("pool", G.wim[0][:], G.tC[0][:], G.tD[0][:], ALU.add, reads=[G.tC[1], G.tD[1]], writes=[G.wim[1]])

                def stageB(g):
                    G = w.g[g % 2]
                    Rg = RHO0[:, 4 * g:4 * g + 4, :].rearrange("p a b -> p (a b)")
                    P.T.op("dve", lambda q, Rg=Rg, o=G.sre[0], i=G.wre[0]: q.tensor_tensor_scan(
                        out=o[:], data0=Rg, data1=i[:], initial=0.0, op0=ALU.mult, op1=ALU.add),
                        reads=[b_RHO0, G.wre[1]], writes=[G.sre[1]])
                    P.T.op("dve", lambda q, Rg=Rg, o=G.sim[0], i=G.wim[0]: q.tensor_tensor_scan(
                        out=o[:], data0=Rg, data1=i[:], initial=0.0, op0=ALU.mult, op1=ALU.add),
                        reads=[b_RHO0, G.wim[1]], writes=[G.sim[1]])
                    sr3 = G.sre[0][:].rearrange("p (a b) -> p a b", a=4)
                    si3 = G.sim[0][:].rearrange("p (a b) -> p a b", a=4)
                    P.act(lb[:, 0, 4 * g:4 * g + 4].unsqueeze(2), sr3[:, :, 127:128], AF.Copy, reads=[G.sre[1]], writes=[b_lb])
                    P.act(lb[:, 1, 4 * g:4 * g + 4].unsqueeze(2), si3[:, :, 127:128], AF.Copy, reads=[G.sim[1]], writes=[b_lb])

                    def pv(ap):
                        v = ap.rearrange("p (a b) -> p a b", a=4)
                        return v[:, :, ::-1] if rev else v
                    C3 = COS[:, 4 * g:4 * g + 4, :]
                    S3 = SIN[:, 4 * g:4 * g + 4, :]
                    e2 = "dve" if rev else "pool"
                    P.tt("dve", pv(G.Pk[0][0][:]), sr3, C3, ALU.mult, reads=[G.sre[1], b_COS], writes=[G.Pk[0][1]])
                    P.tt(e2, pv(G.Pk[1][0][:]), si3, S3, ALU.mult, reads=[G.sim[1], b_COS], writes=[G.Pk[1][1]])
                    P.tt("dve", pv(G.Pk[2][0][:]), sr3, S3, ALU.mult, reads=[G.sre[1], b_COS], writes=[G.Pk[2][1]])
                    P.tt(e2, pv(G.Pk[3][0][:]), si3, C3, ALU.mult, reads=[G.sim[1], b_COS], writes=[G.Pk[3][1]])
                    for t in range(4):
                        s_ = 4 * g + t
                        o = ps[7][:, 32 * s_:32 * s_ + 32]
                        wsl = slice(32 * s_, 32 * s_ + 32)
                        tsl = slice(128 * t, 128 * t + 128)
                        Pk = G.Pk
                        P.mm(o, Pk[0][0][:, tsl], WC[0][0][:, wsl], True, False, reads=[Pk[0][1], WC[0][1]], writes=[bps[7]], inc=False)
                        P.mm(o, Pk[1][0][:, tsl], WC[1][0][:, wsl], False, False, reads=[Pk[1][1], WC[1][1]], writes=[bps[7]], inc=False)
                        P.mm(o, Pk[2][0][:, tsl], WC[2][0][:, wsl], False, False, reads=[Pk[2][1], WC[2][1]], writes=[bps[7]], inc=False)
                        P.mm(o, Pk[3][0][:, tsl], WC[2][0][:, wsl], False, True, reads=[Pk[3][1], WC[2][1]], writes=[bps[7]], inc=(t == 3))

                stageApe(0)
                stageAve(0)
                stageApe(1)
                hook()
                stageAve(1)
                hook()
                stageApe(2)
                stageB(0)
                hook()
                stageAve(2)
                hook()
                stageApe(3)
                stageB(1)
                hook()
                stageAve(3)
                hook()
                stageB(2)
                hook()
                stageB(3)
                hook()
                while hooks:
                    hook()
                l4 = lst[:]
                P.tt("pool", l4[:, 0, :], lb[:, 0, :], Gc[:, 0, :], ALU.mult, reads=[b_lb, b_Gc], writes=[b_lst])
                P.tt("pool", l4[:, 1, :], lb[:, 1, :], Gc[:, 1, :], ALU.mult, reads=[b_lb, b_Gc], writes=[b_lst])
                P.tt("pool", l4[:, 2, :], lb[:, 1, :], Gc[:, 0, :], ALU.mult, reads=[b_lb, b_Gc], writes=[b_lst])
                P.tt("pool", l4[:, 3, :], lb[:, 0, :], Gc[:, 1, :], ALU.mult, reads=[b_lb, b_Gc], writes=[b_lst])
                P.tt("pool", nxt[0][:, 0, :], l4[:, 0, :], l4[:, 1, :], ALU.subtract, reads=[b_lst], writes=[nxt[1]])
                P.tt("pool", nxt[0][:, 1, :], l4[:, 2, :], l4[:, 3, :], ALU.add, reads=[b_lst], writes=[nxt[1]])

            def ret_state_update(w, kbuf, b_kbuf, vbt, b_vbt, gcol):
                kf, b_kf = w.kf
                P.tt("pool", kf[:].rearrange("p (h e) -> p h e", h=4), kbuf[:].rearrange("p (h e) -> p h e", h=4),
                     gam[:, gcol, :].unsqueeze(2).to_broadcast([128, 4, 128]), ALU.mult,
                     reads=[b_kbuf, b_gam], writes=[b_kf])
                for h in range(4):
                    hs = slice(h * 128, (h + 1) * 128)
                    P.mm(ps[5][:, hs], kf[:, hs], vbt[:, hs], True, True, reads=[b_kf, b_vbt], writes=[bps[5]], inc=(h == 3))
                P.tt("pool", stf[:].rearrange("p (h e) -> p h e", h=4), stf[:].rearrange("p (h e) -> p h e", h=4),
                     cdec[:].unsqueeze(2).to_broadcast([128, 4, 128]), ALU.mult, reads=[b_stf, b_cdec], writes=[b_stf])
                P.tt("dve", stf[:], stf[:], ps[5][:], ALU.add, reads=[b_stf, bps[5]], writes=[b_stf])
                P.act(stbf[:], stf[:], AF.Copy, reads=[b_stf], writes=[b_stbf])

            def passA(s, w):
                OPs, YPs, SGs, QTs, KRs, VBs, UBs = OPs_[s], YPs_[s], SGs_[s], QTs_[s], KRs_[s], VBs_[s], UBs_[s]
                b_OPs, b_YPs, b_SGs, b_QTs, b_KRs, b_VBs, b_UBs = SCRB[s]
                P.memset("dve", stf[:], 0.0, writes=[b_stf])
                P.memset("pool", stbf[:], 0.0, writes=[b_stbf])
                P.memset("dve", cin[0][0][:], 0.0, writes=[cin[0][1]])
                nA = nch if KSTOP not in ('setup',) else 0
                if nA:
                    prenorm(C, s, src, bsrc[s], 0, w.hT[0][0], w.hT[0][1], 0)
                for n in range(nA):
                    tsl = slice(n * 128, (n + 1) * 128)
                    hT, b_hT = w.hT[n % 2]
                    rot, b_rot = w.rot[n % 2]
                    P.dma("sp", rot[:], rot_d[tsl], writes=[b_rot])
                    if n + 1 < nA:
                        p1, p2, p3 = prenorm_parts(C, s, src, bsrc[s], n + 1, w.hT[(n + 1) % 2][0], w.hT[(n + 1) % 2][1], 0)
                    else:
                        p1 = p2 = p3 = (lambda: None)
                    qr, b_qr = w.qr
                    kr, b_kr = w.kr[0]
                    vb, b_vb = w.vb[0]
                    QT, b_QT = w.QT[0]
                    KT, b_KT = w.KT
                    STb, b_STb = w.STb
                    sg, b_sg = w.sg[0]
                    du, b_du = w.du
                    ub, b_ub = w.ub[0]
                    opt, b_opt = w.opt[0]
                    ypt, b_ypt = w.ypt[0]
                    tC, b_tC = w.tC
                    uT, b_uT = w.uT
                    def inproj(col, bank):
                        for k in range(8):
                            P.mm(ps[bank][:], hT[:, k, :], win[:, k, col * 512:(col + 1) * 512], k == 0, k == 7,
                                 reads=[b_hT, b_win], writes=[bps[bank]], inc=(k == 7))

                    def rotary(zb, tbl, outb, b_outb):
                        rA, b_rA = w.rA
                        rB, b_rB = w.rB
                        z3 = ps[zb][:].rearrange("p (h e) -> p h e", h=4)
                        a3 = rA[:].rearrange("p (h e) -> p h e", h=4)
                        b3 = rB[:].rearrange("p (h e) -> p h e", h=4)
                        P.tt("dve", a3, z3, rot[:, tbl, 0, :].unsqueeze(1).to_broadcast([128, 4, 128]), ALU.mult,
                             reads=[bps[zb], b_rot], writes=[b_rA])
                        P.tt("dve", b3[:, :, 0:64], z3[:, :, 64:128], rot[:, tbl, 1, 0:64].unsqueeze(1).to_broadcast([128, 4, 64]),
                             ALU.mult, reads=[bps[zb], b_rot], writes=[b_rB])
                        P.tt("dve", b3[:, :, 64:128], z3[:, :, 0:64], rot[:, tbl, 1, 64:128].unsqueeze(1).to_broadcast([128, 4, 64]),
                             ALU.mult, reads=[bps[zb], b_rot], writes=[b_rB])
                        P.tt("pool", outb[:], rA[:], rB[:], ALU.add, reads=[b_rA, b_rB], writes=[b_outb])

                    inproj(4, 2)
                    P.act(ub[:], ps[2][:], AF.Copy, reads=[bps[2]], writes=[b_ub])
                    P.tt("dve", du[:], ps[2][:], drow[:], ALU.mult, reads=[bps[2], b_drow, b_ub], writes=[b_du])
                    P.dma("pool", UBs[tsl], ub[:], reads=[b_ub], writes=[b_UBs])
                    for q_ in range(4):
                        qs = slice(q_ * 128, (q_ + 1) * 128)
                        P.mm(ps[0][:, qs], ub[:, qs], ident_bf[:], True, True, reads=[b_ub, b_ident], writes=[bps[0]], inc=(q_ == 3))
                    P.act(uT[:].rearrange("p a b -> p (a b)"), ps[0][:], AF.Copy, reads=[bps[0]], writes=[b_uT])

                    def H0():
                        inproj(0, 2)
                        inproj(1, 1)
                        rotary(2, 0, qr, b_qr)
                        rotary(1, 1, kr, b_kr)

                    def H1():
                        inproj(2, 2)
                        P.act(vb[:], ps[2][:], AF.Copy, reads=[bps[2]], writes=[b_vb])
                        inproj(3, 0)
                        P.act(sg[:, 0:512], ps[0][:], AF.Silu, reads=[bps[0]], writes=[b_sg])
                        inproj(5, 1)
                        P.act(sg[:, 512:1024], ps[1][:], AF.Silu, reads=[bps[1]], writes=[b_sg])
                        P.dma("pool", SGs[tsl], sg[:], reads=[b_sg], writes=[b_SGs])

                    def R1():
                        for h in range(4):
                            hs = slice(h * 128, (h + 1) * 128)
                            P.mm(ps[0][:, hs], qr[:, hs], ident_bf[:], True, True, reads=[b_qr, b_ident], writes=[bps[0]], inc=(h == 3))
                        for h in range(4):
                            hs = slice(h * 128, (h + 1) * 128)
                            P.mm(ps[1][:, hs], kr[:, hs], ident_bf[:], True, True, reads=[b_kr, b_ident], writes=[bps[1]], inc=(h == 3))
                        P.act(QT[:], ps[0][:], AF.Copy, reads=[bps[0]], writes=[b_QT])
                        P.act(KT[:], ps[1][:], AF.Copy, reads=[bps[1]], writes=[b_KT])
                        p1()
                    def R2():
                        for h in range(4):
                            hs = slice(h * 128, (h + 1) * 128)
                            P.mm(ps[2][:, hs], KT[:, hs], QT[:, hs], True, True, reads=[b_KT, b_QT], writes=[bps[2]], inc=(h == 3))
                        P.tt("dve", STb[:], ps[2][:], dtab[:].rearrange("p h e -> p (h e)"), ALU.mult,
                             reads=[bps[2], b_dtab], writes=[b_STb])
                    def R3():
                        for h in range(4):
                            hs = slice(h * 128, (h + 1) * 128)
                            P.mm(ps[2][:, hs], STb[:, hs], vb[:, hs], True, True, reads=[b_STb, b_vb], writes=[bps[2]], inc=(h == 3))
                        for h in range(4):
                            hs = slice(h * 128, (h + 1) * 128)
                            P.mm(ps[0][:, hs], QT[:, hs], stbf[:, hs], True, True, reads=[b_QT, b_stbf], writes=[bps[0]], inc=(h == 3))
                        P.tt("dve", tC[:].rearrange("p (h e) -> p h e", h=4), ps[0][:].rearrange("p (h e) -> p h e", h=4),
                             gam[:, 0, :].unsqueeze(2).to_broadcast([128, 4, 128]), ALU.mult, reads=[bps[0], b_gam], writes=[b_tC])
                        P.tt("dve", opt[:], tC[:], ps[2][:], ALU.add, reads=[b_tC, bps[2]], writes=[b_opt])
                        P.dma("pool", OPs[tsl], opt[:], reads=[b_opt], writes=[b_OPs])
                    def R4():
                        ret_state_update(w, kr, b_kr, vb, b_vb, 2)
                    def R5():
                        P.dma("pool", QTs[n], QT[:], reads=[b_QT], writes=[b_QTs])
                        P.dma("pool", KRs[tsl], kr[:], reads=[b_kr], writes=[b_KRs])
                        P.dma("pool", VBs[tsl], vb[:], reads=[b_vb], writes=[b_VBs])
                    def R45():
                        R4()
                        R5()

                    def P23():
                        p2()
                        p3()
                    s5_chunk(w, n, False, hooks=[H0, H1, R1, R2, R3, R45, P23])
                    P.tt("dve", ypt[:], ps[7][:], du[:], ALU.add, reads=[bps[7], b_du], writes=[b_ypt])
                    P.dma("pool", YPs[tsl], ypt[:], reads=[b_ypt], writes=[b_YPs])

            def passB(s, w):
                OPs, YPs, SGs, QTs, KRs, VBs, UBs = OPs_[s], YPs_[s], SGs_[s], QTs_[s], KRs_[s], VBs_[s], UBs_[s]
                b_OPs, b_YPs, b_SGs, b_QTs, b_KRs, b_VBs, b_UBs = SCRB[s]
                P.memset("dve", stf[:], 0.0, writes=[b_stf])
                P.memset("pool", stbf[:], 0.0, writes=[b_stbf])
                P.memset("dve", cin[0][0][:], 0.0, writes=[cin[0][1]])
                order = list(range(nch - 1, -1, -1)) if KSTOP == 'all' else []
                xts = {}

                def loads(ci):
                    n = order[ci]
                    tsl = slice(n * 128, (n + 1) * 128)
                    r = ci % 2
                    sl = C["cnt"] % 2
                    C["cnt"] += 1
                    xts[ci] = (C["xt"][sl], C["bxt"][sl])
                    P.dma("sp", C["xt"][sl][:], src[s, tsl, :], reads=[bsrc[s]], writes=[C["bxt"][sl]])
                    P.dma("sp", w.QT[r][0][:], QTs[n], reads=[b_QTs], writes=[w.QT[r][1]])
                    P.dma("sp", w.opt[r][0][:], OPs[tsl], reads=[b_OPs], writes=[w.opt[r][1]])
                    P.dma("sp", w.kr[r][0][:], KRs[tsl], reads=[b_KRs], writes=[w.kr[r][1]])
                    P.dma("sp", w.vb[r][0][:], VBs[tsl], reads=[b_VBs], writes=[w.vb[r][1]])
                    P.dma("sp", w.ub[r][0][:], UBs[tsl], reads=[b_UBs], writes=[w.ub[r][1]])
                    P.dma("sp", w.ypt[r][0][:], YPs[tsl], reads=[b_YPs], writes=[w.ypt[r][1]])

                if order:
                    loads(0)
                for ci, n in enumerate(order):
                    if ci + 1 < len(order):
                        loads(ci + 1)
                    r = ci % 2
                    xt, bxt = xts.pop(ci)
                    QT, b_QT = w.QT[r]
                    opt, b_opt = w.opt[r]
                    kr, b_kr = w.kr[r]
                    vb, b_vb = w.vb[r]
                    ub, b_ub = w.ub[r]
                    sg, b_sg = w.sg[r]
                    ypt, b_ypt = w.ypt[r]
                    tA, b_tA = w.tA
                    tB, b_tB = w.tB
                    tD, b_tD = w.tD
                    tC, b_tC = w.tC
                    uT, b_uT = w.uT
                    oab, b_oab = w.oab
                    oT, b_oT = w.oT
                    ygb, b_ygb = w.ygb
                    P.dma("sp", sg[:], SGs[n * 128:(n + 1) * 128], reads=[b_SGs], writes=[b_sg])
                    for q_ in range(4):
                        qs = slice(q_ * 128, (q_ + 1) * 128)
                        P.mm(ps[0][:, qs], ub[:, qs], jmat_bf[:], True, True, reads=[b_ub, b_jmat], writes=[bps[0]], inc=(q_ == 3))
                    P.act(uT[:].rearrange("p a b -> p (a b)"), ps[0][:], AF.Copy, reads=[bps[0]], writes=[b_uT])
                    for h in range(4):
                        hs = slice(h * 128, (h + 1) * 128)
                        P.mm(ps[6][:, hs], QT[:, hs], stbf[:, hs], True, True, reads=[b_QT, b_stbf], writes=[bps[6]], inc=(h == 3))
                    P.tt("dve", tC[:].rearrange("p (h e) -> p h e", h=4), ps[6][:].rearrange("p (h e) -> p h e", h=4),
                         gam[:, 1, :].unsqueeze(2).to_broadcast([128, 4, 128]), ALU.mult, reads=[bps[6], b_gam], writes=[b_tC])
                    P.tt("pool", opt[:], opt[:], tC[:], ALU.add, reads=[b_opt, b_tC], writes=[b_opt])
                    ret_state_update(w, kr, b_kr, vb, b_vb, 3)
                    o3 = opt[:].rearrange("p (h e) -> p h e", h=4)
                    P.T.op("dve", lambda q, o3=o3: q.reduce_sum(out=hn[:, 0:4], in_=o3, axis=mybir.AxisListType.X),
                           reads=[b_opt], writes=[b_hn])
                    P.tt("pool", tA[:], opt[:], opt[:], ALU.mult, reads=[b_opt], writes=[b_tA])
                    P.T.op("dve", lambda q, tA=tA: q.reduce_sum(out=hn[:, 4:8], in_=tA[:].rearrange("p (h e) -> p h e", h=4),
                                                              axis=mybir.AxisListType.X), reads=[b_tA], writes=[b_hn])
                    P.ts("dve", hn[:, 0:8], hn[:, 0:8], 1.0 / 128.0, None, ALU.mult, None, reads=[b_hn], writes=[b_hn])
                    P.tt("dve", hn[:, 8:12], hn[:, 0:4], hn[:, 0:4], ALU.mult, reads=[b_hn], writes=[b_hn])
                    P.tt("dve", hn[:, 4:8], hn[:, 4:8], hn[:, 8:12], ALU.subtract, reads=[b_hn], writes=[b_hn])
                    P.act(hn[:, 8:12], hn[:, 4:8], AF.Sqrt, reads=[b_hn, b_eps], writes=[b_hn], bias=epsT[:, 0:1])
                    P.T.op("dve", lambda q: q.reciprocal(out=hn[:, 12:16], in_=hn[:, 8:12]), reads=[b_hn], writes=[b_hn])
                    a3 = tA[:].rearrange("p (h e) -> p h e", h=4)
                    P.tt("pool", a3, o3, hn[:, 0:4].unsqueeze(2).to_broadcast([128, 4, 128]), ALU.subtract,
                         reads=[b_opt, b_hn], writes=[b_tA])
                    P.tt("pool", a3, a3, hn[:, 12:16].unsqueeze(2).to_broadcast([128, 4, 128]), ALU.mult,
                         reads=[b_tA, b_hn], writes=[b_tA])
                    P.tt("pool", oab[:, 0:512], tA[:], sg[:, 0:512], ALU.mult, reads=[b_tA, b_sg], writes=[b_oab])
                    s5_chunk(w, ci, True)
                    P.tt("dve", ypt[:], ypt[:], ps[7][:], ALU.add, reads=[b_ypt, bps[7]], writes=[b_ypt])
                    P.tt("pool", tB[:], ypt[:], ypt[:], ALU.mult, reads=[b_ypt], writes=[b_tB])
                    P.ts("dve", tB[:], tB[:], 0.044715, 1.0, ALU.mult, ALU.add, reads=[b_tB], writes=[b_tB])
                    P.tt("pool", tB[:], tB[:], ypt[:], ALU.mult, reads=[b_tB, b_ypt], writes=[b_tB])
                    P.act(tB[:], tB[:], AF.Sigmoid, reads=[b_tB], writes=[b_tB], scale=1.5957691216057308)
                    P.tt("dve", tD[:], ypt[:], tB[:], ALU.mult, reads=[b_ypt, b_tB], writes=[b_tD])
                    P.act(ygb[:], tD[:], AF.Copy, reads=[b_tD], writes=[b_ygb])
                    for q_ in range(4):
                        qs = slice(q_ * 128, (q_ + 1) * 128)
                        P.mm(ps[0][:, qs], ygb[:, qs], ident_bf[:], True, True, reads=[b_ygb, b_ident], writes=[bps[0]], inc=(q_ == 3))
                    P.act(uT[:].rearrange("p a b -> p (a b)"), ps[0][:], AF.Copy, reads=[bps[0]], writes=[b_uT])
                    for q_ in range(4):
                        P.mm(ps[1][:], uT[:, q_, :], wglu[:, q_, :], q_ == 0, q_ == 3, reads=[b_uT, b_wglu], writes=[bps[1]], inc=(q_ == 3))
                    P.act(tB[:], ps[1][:], AF.Sigmoid, reads=[bps[1]], writes=[b_tB])
                    P.tt("dve", tD[:], tD[:], tB[:], ALU.mult, reads=[b_tD, b_tB], writes=[b_tD])
                    P.tt("pool", oab[:, 512:1024], tD[:], sg[:, 512:1024], ALU.mult, reads=[b_tD, b_sg], writes=[b_oab])
                    for b in range(2):
                        for jj in range(4):
                            k = 4 * b + jj
                            P.mm(ps[b][:, jj * 128:(jj + 1) * 128], oab[:, k * 128:(k + 1) * 128], ident_bf[:], True, True,
                                 reads=[b_oab, b_ident], writes=[bps[b]], inc=(jj == 3))
                    P.act(oT[:, 0:4, :].rearrange("p a b -> p (a b)"), ps[0][:], AF.Copy, reads=[bps[0]], writes=[b_oT])
                    P.act(oT[:, 4:8, :].rearrange("p a b -> p (a b)"), ps[1][:], AF.Copy, reads=[bps[1]], writes=[b_oT])
                    for hh in range(2):
                        for k in range(8):
                            P.mm(ps[2 + hh][:], oT[:, k, :], wout[:, k, hh * 512:(hh + 1) * 512], k == 0, k == 7,
                                 reads=[b_oT, b_wout], writes=[bps[2 + hh]], inc=(k == 7))
                    post(C, s, (2, 3), xt, bxt, dst, bdst[s], n)

            s5_setup(0)
            with contextlib.ExitStack() as wk:
                w = alloc_work(wk, False)
                for s in range(nseq):
                    passA(s, w)
                T_barrier()
            s5_setup(1)
            with contextlib.ExitStack() as wk:
                w = alloc_work(wk, True)
                for s in range(nseq):
                    passB(s, w)
                T_barrier()

    def T_barrier():
        evs = []
        for e in T.engs.values():
            for k in range(len(e.dsems)):
                evs.append((e.dsems[k], e.dvals[k]))
            if e.sem is not None and e.cnt > 0 and not e.pending:
                evs.append((e.sem, e.cnt))
        for e in T.engs.values():
            for ev in evs:
                T._wait(e, ev)

    def odd_layer(li, src, bsrc, dst, bdst):
        raise NotImplementedError

    P.odd_layer_hook = None
    cur, bcur = x_in, [b_xin] * nseq
    for idx, li in enumerate(layers):
        last = idx == len(layers) - 1
        dstt, bd = (y_out, b_yout) if last else (xs[idx % 2], b_xs[idx % 2])
        if li % 2 == 0:
            even_layer(li, cur, bcur, dstt, bd)
        else:
            ODD_IMPL(P, locals(), li, cur, bcur, dstt, bd)
        cur, bcur = dstt, bd
    T.finish()
    T.replay()
    P.es.close()
    return P


def ODD_IMPL(P, env, li, src, bsrc, dst, bdst):
    nc, T = P.nc, P.T
    E = env
    ps, bps = E["ps"], E["bps"]
    nseq, L = P.nseq, P.L
    ident_bf, b_ident = E["ident_bf"], E["b_ident"]
    j = li // 2
    rows = L // 64
    nblk = L // 256

    def rs(r):
        return min(max(r - 4, 0), rows - 8)

    with contextlib.ExitStack() as st:
        def S(name, shape, dt=F32):
            return st.enter_context(P.sbt(f"{name}_o{li}", list(shape), dt)), Buf(name)
        E["adaln"](li, None)
        C = E["make_common"](st, f"o{li}")
        winc, b_winc = S("winc", [128, 8, 4096], BF16)
        woutc, b_woutc = S("woutc", [128, 8, 1024], BF16)
        Z, b_Z = S("Z", [128, 16, 1024], BF16)
        hT, b_hT = S("hT", [128, 8, 256], BF16)
        KT = [S(f"KT{i}", [128, 8, 256], BF16) for i in range(3)]
        V = [S(f"V{i}", [128, 2, 8, 3, 64], BF16) for i in range(3)]
        QT = [S(f"QT{i}", [128, 8, 256], BF16) for i in range(2)]
        GT = [S(f"GT{i}", [128, 8, 256], BF16) for i in range(2)]
        pT = [S(f"pT{i}", [128, 256], BF16) for i in range(3)]
        og, b_og = S("og", [128, 8, 256], BF16)
        rd2 = [S(f"rd{i}", [128, 256]) for i in range(2)]
        rb2 = [S(f"rb{i}", [128, 256]) for i in range(2)]
        t22 = [S(f"t2{i}", [128, 256]) for i in range(2)]
        onesf, b_onesf = S("onesf", [128, 64])
        zer, b_zer = S("zer", [128, 256], BF16)
        wsrc = E["w_in_c"][j].rearrange("(k p) n -> p k n", p=128)
        for k in range(8):
            P.dma("pool", winc[:, k, :], wsrc[:, k, :], writes=[b_winc])
        P.dma("pool", woutc[:], E["w_out_c"][j].rearrange("(k p) n -> p k n", p=128), writes=[b_woutc])
        P.memset("dve", onesf[:], 1.0, writes=[b_onesf])
        P.memset("dve", zer[:], 0.0, writes=[b_zer])
        for i in range(3):
            P.memset("pool", V[i][0][:], 1.0, writes=[V[i][1]])
        with contextlib.ExitStack() as s2:
            zm = s2.enter_context(P.sbt(f"zm_o{li}", [128, 1024], F32)); b_zm = Buf("zm")
            zt0_ = s2.enter_context(P.sbt(f"zt0_o{li}", [128, 512], F32))
            zt = [zt0_, zt0_]
            b_zt0_ = Buf("zt0")
            b_zt = [b_zt0_, b_zt0_]
            P.dma("sp", zm[:], E["zmask"], writes=[b_zm])
            for h in range(16):
                for hf in range(2):
                    P.dma("sp", zt[hf][:], E["zg"][j, h][:, hf * 512:(hf + 1) * 512], writes=[b_zt[hf]])
                    P.tt("dve", Z[:, h, hf * 512:(hf + 1) * 512], zt[hf][:], zm[:, hf * 512:(hf + 1) * 512], ALU.add,
                         reads=[b_zt[hf], b_zm], writes=[b_Z])
            E["T_barrier"]()

        def proj(s, b):
            ring = b % 3
            sl = b % 2
            for t in range(2):
                E["prenorm"](C, s, src, bsrc[s], 2 * b + t, hT, b_hT, t * 128)
            cnt = 0
            for (col0, kind) in ((0, "q"), (1024, "k"), (3072, "g")):
                for hp2 in range(4):
                    bank = 2 + cnt % 2
                    cnt += 1
                    for hh in range(2):
                        hp = 2 * hp2 + hh
                        for k in range(8):
                            P.mm(ps[bank][:, hh * 256:(hh + 1) * 256], winc[:, k, col0 + hp * 128:col0 + (hp + 1) * 128], hT[:, k, :],
                                 k == 0, k == 7, reads=[b_winc, b_hT], writes=[bps[bank]], inc=(k == 7 and hh == 1))
                    if kind == "q":
                        P.act(QT[sl][0][:, 2 * hp2:2 * hp2 + 2, :].rearrange("p a b -> p (a b)"), ps[bank][:], AF.Copy,
                              reads=[bps[bank]], writes=[QT[sl][1]], scale=0.125)
                    elif kind == "k":
                        P.cp("dve", KT[ring][0][:, 2 * hp2:2 * hp2 + 2, :].rearrange("p a b -> p (a b)"), ps[bank][:],
                             reads=[bps[bank]], writes=[KT[ring][1]])
                    else:
                        P.act(GT[sl][0][:, 2 * hp2:2 * hp2 + 2, :].rearrange("p a b -> p (a b)"), ps[bank][:], AF.Silu,
                              reads=[bps[bank]], writes=[GT[sl][1]])
            for t in range(2):
                for half in range(2):
                    bank = 2 + cnt % 2
                    cnt += 1
                    for k in range(8):
                        P.mm(ps[bank][:], hT[:, k, t * 128:(t + 1) * 128], winc[:, k, 2048 + half * 512:2048 + (half + 1) * 512],
                             k == 0, k == 7, reads=[b_hT, b_winc], writes=[bps[bank]], inc=(k == 7))
                    src4 = ps[bank][:].rearrange("p (a c d) -> p a c d", a=4, c=2)
                    P.cp("dve", V[ring][0][:, t, 4 * half:4 * half + 4, 0:3:2, :], src4, reads=[bps[bank]], writes=[V[ring][1]])

        def attn(s, b):
            R = 4 * b
            sl = b % 2
            lo = rs(R) & ~1
            hi = (rs(R + 3) + 7) & ~1
            tiles = []
            for r0 in range(lo, hi + 1, 2):
                qs = [r for r in range(R, R + 4) if (rs(r) <= r0 + 1 and r0 <= rs(r) + 7)]
                if not qs:
                    continue
                qa, qb = qs[0], qs[-1]
                partial = []
                for r in qs:
                    for a in range(2):
                        if not (rs(r) <= r0 + a <= rs(r) + 7):
                            partial.append((r, a))
                tiles.append((r0, qa, qb, partial))
            pcount = [0]
            W = [(h, ti) for h in range(16) for ti in range(len(tiles))]
            info = {}
            deferred = []

            def emit_scores(h, ti):
                hp, base = h // 2, 64 * (h % 2)
                r0, qa, qb, partial = tiles[ti]
                kb = r0 // 4
                tt_ = (r0 % 4) // 2
                kring = kb % 3
                c0, c1 = (qa - R) * 64, (qb - R + 1) * 64
                z0, z1 = (qa - r0 + 7) * 64, (qb - r0 + 8) * 64
                bank = 4 + pcount[0] % 2
                pt, b_pt = pT[pcount[0] % 3]
                pcount[0] += 1
                P.mm(ps[bank][:, c0:c1], KT[kring][0][base:base + 64, hp, tt_ * 128:(tt_ + 1) * 128],
                     QT[sl][0][base:base + 64, hp, c0:c1], True, False,
                     reads=[KT[kring][1], QT[sl][1]], writes=[bps[bank]], inc=False)
                P.mm(ps[bank][:, c0:c1], ident_bf[:], Z[:, h, z0:z1], False, True,
                     reads=[b_ident, b_Z], writes=[bps[bank]], inc=True)
                P.act(pt[:, c0:c1], ps[bank][:, c0:c1], AF.Exp, reads=[bps[bank]], writes=[b_pt])
                for (r, a_) in partial:
                    cc = (r - R) * 64
                    P.memset("pool", pt[64 * a_:64 * a_ + 64, cc:cc + 64], 0.0, writes=[b_pt])
                info[(h, ti)] = (pt, b_pt, c0, c1, kring, tt_)

            def emit_pv(h, ti, idx):
                hp = h // 2
                pt, b_pt, c0, c1, kring, tt_ = info.pop((h, ti))
                ob = 6 + h % 2
                if ti == 0:
                    P.mm(ps[ob][:, 0:256], zer[:, 0:128], zer[:], True, False, reads=[b_zer], writes=[bps[ob]], inc=False)
                va = V[kring][0][:, tt_, hp, 0:2, :] if h % 2 == 0 else V[kring][0][:, tt_, hp, 1:3, :]
                last = ti == len(tiles) - 1
                P.mm(ps[ob][:, c0:c1], va.rearrange("p a b -> p (a b)"), pt[:, c0:c1], False, last,
                     reads=[V[kring][1], b_pt], writes=[bps[ob]], inc=last)
                if last:
                    par = h % 2
                    dr, orow = (64, 0) if par == 0 else (0, 64)
                    rdp, b_rdp = rd2[par]
                    rbp, b_rbp = rb2[par]
                    t2p, b_t2p = t22[par]
                    P.T.op("dve", lambda q, dr=dr, ob=ob, rdp=rdp: q.reciprocal(out=rdp[dr:dr + 1, :], in_=ps[ob][dr:dr + 1, 0:256]),
                           reads=[bps[ob]], writes=[b_rdp])

                    def part2(h=h, hp=hp, par=par, dr=dr, orow=orow, ob=ob, rdp=rdp, b_rdp=b_rdp, rbp=rbp, b_rbp=b_rbp, t2p=t2p, b_t2p=b_t2p):
                        P.mm(ps[par][orow:orow + 64, 0:256], onesf[dr:dr + 1, 0:64], rdp[dr:dr + 1, :], True, True,
                             reads=[b_onesf, b_rdp], writes=[bps[par]])
                        P.act(rbp[orow:orow + 64, :], ps[par][orow:orow + 64, 0:256], AF.Copy, reads=[bps[par]], writes=[b_rbp])
                        P.tt("dve", t2p[orow:orow + 64, :], ps[ob][orow:orow + 64, 0:256], rbp[orow:orow + 64, :], ALU.mult,
                             reads=[bps[ob], b_rbp], writes=[b_t2p])
                        P.tt("pool", og[orow:orow + 64, hp, :], t2p[orow:orow + 64, :], GT[sl][0][orow:orow + 64, hp, :], ALU.mult,
                             reads=[b_t2p, GT[sl][1]], writes=[b_og])
                    deferred.append((idx + 2, part2))

            LA = 1
            for idx in range(len(W) + LA):
                if idx < len(W):
                    emit_scores(*W[idx])
                if idx >= LA:
                    emit_pv(W[idx - LA][0], W[idx - LA][1], idx)
                while deferred and deferred[0][0] <= idx:
                    deferred.pop(0)[1]()
            while deferred:
                deferred.pop(0)[1]()
            for t in range(2):
                n = 2 * b + t
                sx = C["cnt"] % 2
                C["cnt"] += 1
                xt, bxt = C["xt"][sx], C["bxt"][sx]
                P.dma("sp", xt[:], src[s, n * 128:(n + 1) * 128, :], reads=[bsrc[s]], writes=[bxt])
                for hh in range(2):
                    for k in range(8):
                        P.mm(ps[2 + hh][:], og[:, k, t * 128:(t + 1) * 128], woutc[:, k, hh * 512:(hh + 1) * 512], k == 0, k == 7,
                             reads=[b_og, b_woutc], writes=[bps[2 + hh]], inc=(k == 7))
                E["post"](C, s, (2, 3), xt, bxt, dst, bdst[s], n)

        for s in range(nseq):
            proj(s, 0)
            for b in range(nblk):
                if b + 1 < nblk:
                    proj(s, b + 1)
                attn(s, b)
        E["T_barrier"]()


def _common_inputs(p, L):
    f32 = np.float32
    m = {}
    def T8(a):
        return np.ascontiguousarray(a.reshape(a.shape[0], -1, 128).transpose(0, 2, 1)).astype(f32)
    m["npreT"] = T8(p["norm_pre"])
    m["npostT"] = T8(p["norm_post"])
    m["wmod"] = np.ascontiguousarray(p["w_mod"], dtype=f32)
    m["bmodT"] = T8(p["b_mod"])
    m["w_in_ab"] = np.ascontiguousarray(p["w_in_ab"], dtype=f32)
    m["w_out_ab"] = np.ascontiguousarray(p["w_out_ab"], dtype=f32)
    m["w_glu"] = np.ascontiguousarray(p["ssm_w_glu"], dtype=f32)
    m["ssm_d"] = np.ascontiguousarray(p["ssm_d"], dtype=f32)
    sm3, rep3, bl, cl = [], [], [], []
    for j in range(2):
        a, b, c, d = _s5_layout(p["ssm_a_re"][j], p["ssm_a_im"][j], p["ssm_log_step"][j], p["ssm_b_re"][j],
                                p["ssm_b_im"][j], p["ssm_c_re"][j], p["ssm_c_im"][j])
        sm3.append(a); rep3.append(b); bl.append(c); cl.append(d)
    m["s5_sm3"] = np.stack(sm3).astype(f32)
    m["s5_rep3"] = np.stack(rep3).astype(f32)
    m["s5_bl"] = np.stack(bl).astype(f32)
    m["s5_cl"] = np.stack(cl).astype(f32)
    m["w_in_c"] = np.ascontiguousarray(p["w_in_c"], dtype=f32)
    m["w_out_c"] = np.ascontiguousarray(p["w_out_c"], dtype=f32)
    zs = []
    for j in range(2):
        z, mask = _na_layout(np.asarray(p["na_rel_bias"][j], dtype=f32))
        zs.append(z)
    m["zg"] = np.stack(zs).astype(f32)
    m["zmask"] = mask
    m["ident"] = np.eye(128, dtype=f32)
    m["jmat"] = np.eye(128, dtype=f32)[::-1].copy()
    m["rot"] = _rot_tables(L)
    dt, gam, cd = _ret_consts()
    m["dtab"] = dt
    m["gam"] = np.ascontiguousarray(gam.transpose(0, 1, 2))
    m["cdec"] = cd
    m["jt"] = np.broadcast_to(np.arange(128, dtype=f32)[None, :], (128, 128)).copy()
    return m


_PROG_CACHE = {}


def run_cores(xs_per_core, cs_per_core, params, layers):
    nseq, L, _ = xs_per_core[0].shape
    key = (nseq, L, tuple(layers))
    if key not in _PROG_CACHE:
        _PROG_CACHE[key] = build(nseq, L, list(layers))
    P = _PROG_CACHE[key]
    com = _common_inputs(params, L)
    in_maps = []
    for x, c in zip(xs_per_core, cs_per_core):
        m = dict(com)
        m["x_in"] = np.ascontiguousarray(x, dtype=np.float32)
        m["cT"] = np.ascontiguousarray(c.reshape(nseq, 8, 128).transpose(2, 1, 0), dtype=np.float32)
        in_maps.append(m)
    res = run_bass_kernel_spmd(P.nc, in_maps, core_ids=list(range(len(in_maps))))
    return [np.asarray(r["y_out"]) for r in res.results]


def kernel(**inputs):
    p = {k: np.asarray(v) for k, v in inputs.items()}
    xp, xsamp = p["x_prompt"], p["x_sample"]
    cp, cs = p["c_prompt"], p["c_sample"]
    seqs = [xp[i] for i in range(4)] + [xsamp[i] for i in range(8)]
    cvs = [cp[i] for i in range(4)] + [cs[i] for i in range(8)]
    slots = [(c, 8 + c if c < 4 else c) for c in range(8)]
    xs_pc = [np.stack([seqs[a], seqs[b]]) for a, b in slots]
    cs_pc = [np.stack([cvs[a], cvs[b]]) for a, b in slots]
    outs = run_cores(xs_pc, cs_pc, p, [0, 1, 2, 3])
    res = [None] * 12
    for c, (a, b) in enumerate(slots):
        res[a] = outs[c][0]
        if c < 4:
            res[b] = outs[c][1]
    y_prompt = np.stack(res[0:4]).astype(np.float32)
    y_sample = np.stack(res[4:12]).astype(np.float32)
    return (y_prompt, y_sample)
```

```python
import contextlib
import math
import os
KSTOP = os.environ.get('KSTOP', 'all')
import numpy as np
import concourse.bass as bass
import concourse.mybir as mybir
from concourse.bass_utils import run_bass_kernel_spmd

F32 = mybir.dt.float32
BF16 = mybir.dt.bfloat16
I32 = mybir.dt.int32
ALU = mybir.AluOpType
AF = mybir.ActivationFunctionType

D = 1024
EPS = 1e-6
TWO_PI = 2.0 * math.pi


class Buf:
    __slots__ = ("name", "w", "r")

    def __init__(self, name):
        self.name = name
        self.w = []
        self.r = []


class Eng:
    def __init__(self, name):
        self.name = name
        self.ops = []
        self.known = {}
        self.sem = None
        self.cnt = 0
        self.pending = False
        self.dsems = []
        self.dvals = []
        self.dptr = 0
        self.own = set()


EPOCH = 30000
NDSEM = 10


class Tracker:
    def __init__(self, nc):
        self.nc = nc
        self.engs = {n: Eng(n) for n in ("pe", "act", "dve", "pool", "sp")}
        self.sems = []
        for e in self.engs.values():
            if e.name != "sp":
                e.sem = self._newsem(e.name)
                e.own.add(e.sem)
        self.n_ops = 0

    def _newsem(self, nm):
        h = self.nc.alloc_semaphore(name=f"{nm}_{len(self.sems)}")
        self.sems.append(h)
        return len(self.sems) - 1

    def _wait(self, e, ev):
        s, v = ev
        if e.known.get(s, 0) >= v:
            return
        if s in e.own:
            if e.name == "pe":
                return
            if s == e.sem and v > e.cnt:
                return
        e.known[s] = v
        sem = self.sems[s]
        e.ops.append(lambda q, sem=sem, v=v: q.wait_ge(sem, v))

    def _deps(self, e, reads, writes):
        for b in reads:
            for ev in b.w:
                self._wait(e, ev)
        for b in writes:
            for ev in b.w:
                self._wait(e, ev)
            for ev in b.r:
                self._wait(e, ev)

    def _commit(self, ev, reads, writes):
        for b in reads:
            for i, (s0, v0) in enumerate(b.r):
                if s0 == ev[0]:
                    b.r[i] = (s0, max(v0, ev[1]))
                    break
            else:
                b.r.append(ev)
        for b in writes:
            b.w = [ev]
            b.r = []

    def op(self, eng, fn, reads=(), writes=(), inc=True):
        e = self.engs[eng]
        self.n_ops += 1
        self._deps(e, reads, writes)
        if e.cnt >= EPOCH and inc and not e.pending:
            e.sem = self._newsem(e.name)
            e.cnt = 0
            e.own.add(e.sem)
        if inc:
            e.cnt += 1
            sem = self.sems[e.sem]
            e.ops.append(lambda q, fn=fn, sem=sem: fn(q).then_inc(sem, 1))
            ev = (e.sem, e.cnt)
            e.pending = False
        else:
            e.ops.append(lambda q, fn=fn: fn(q))
            ev = (e.sem, e.cnt + 1)
            e.pending = True
        self._commit(ev, reads, writes)

    def dma(self, eng, out, in_, reads=(), writes=()):
        e = self.engs[eng]
        self.n_ops += 1
        self._deps(e, reads, writes)
        if len(e.dsems) < NDSEM:
            e.dsems.append(self._newsem(e.name + "d"))
            e.dvals.append(0)
            k = len(e.dsems) - 1
        else:
            k = e.dptr
            e.dptr = (e.dptr + 1) % NDSEM
            self._wait(e, (e.dsems[k], e.dvals[k]))
            if e.dvals[k] >= EPOCH * 16:
                e.dsems[k] = self._newsem(e.name + "d")
                e.dvals[k] = 0
        e.dvals[k] += 16
        sem = self.sems[e.dsems[k]]
        e.ops.append(lambda q, out=out, in_=in_, sem=sem: q.dma_start(out=out, in_=in_).then_inc(sem, 16))
        ev = (e.dsems[k], e.dvals[k])
        self._commit(ev, reads, writes)
        return ev

    def finish(self):
        sp = self.engs["sp"]
        for e in self.engs.values():
            for k in range(len(e.dsems)):
                self._wait(sp, (e.dsems[k], e.dvals[k]))
            if e.sem is not None and e.cnt > 0:
                self._wait(sp, (e.sem, e.cnt))

    def replay(self):
        nc = self.nc
        E = self.engs
        with nc.Block() as block:
            @block.tensor
            def _(q):
                for f in E["pe"].ops:
                    f(q)

            @block.scalar
            def _(q):
                for f in E["act"].ops:
                    f(q)

            @block.vector
            def _(q):
                for f in E["dve"].ops:
                    f(q)

            @block.gpsimd
            def _(q):
                for f in E["pool"].ops:
                    f(q)

            @block.sync
            def _(q):
                for f in E["sp"].ops:
                    f(q)


RET_H = 4
NA_H = 16
GW = 64


def _ret_consts():
    f32 = np.float32
    h = np.arange(RET_H, dtype=f32)
    log_g = np.log1p(-np.exp2(-5.0 - h)).astype(f32)
    pos = np.arange(128, dtype=f32)
    dt = np.exp(np.abs(pos[:, None] - pos[None, :])[:, None, :] * log_g[None, :, None]).astype(f32)
    gam = np.zeros((128, 4, RET_H), f32)
    gam[:, 0, :] = np.exp(pos[:, None] * log_g[None])
    gam[:, 1, :] = np.exp((127.0 - pos)[:, None] * log_g[None])
    gam[:, 2, :] = np.exp((128.0 - pos)[:, None] * log_g[None])
    gam[:, 3, :] = np.exp((pos + 1.0)[:, None] * log_g[None])
    cdec = np.exp(128.0 * log_g).astype(f32)
    cd = np.broadcast_to(cdec[None, :], (128, RET_H)).copy()
    return dt, gam, cd


def _rot_tables(L):
    f32 = np.float32
    inv = (10000.0 ** (-np.arange(0, 128, 2, dtype=f32) / 128.0)).astype(f32)
    ang = (np.arange(L, dtype=f32)[:, None] * inv[None, :]).astype(f32)
    c = np.cos(ang).astype(f32)
    s = np.sin(ang).astype(f32)
    rq = np.zeros((L, 2, 128), f32)
    rq[:, 0, :64] = c
    rq[:, 0, 64:] = c
    rq[:, 1, :64] = -s
    rq[:, 1, 64:] = s
    rk = (rq * f32(128.0 ** -0.5)).astype(f32)
    return np.stack([rq, rk], axis=1).copy()


def _na_layout(relb):
    a = np.arange(2)[:, None, None, None]
    k = np.arange(64)[None, :, None, None]
    m = np.arange(-7, 9)[None, None, :, None]
    c = np.arange(64)[None, None, None, :]
    dr = np.clip(a - m + 7, 0, 14)
    dc = np.clip(k - c + 15, 0, 30)
    dr_b, dc_b = np.broadcast_arrays(dr, dc)
    z = relb[:, dr_b, dc_b]
    z = z.reshape(16, 128, 16 * 64).astype(np.float32)
    cs = np.clip(np.arange(64) - 8, 0, 48)
    kk = np.arange(64)[:, None]
    valid = (kk >= cs[None, :]) & (kk < cs[None, :] + 16)
    mask = np.where(valid, 0.0, -30000.0).astype(np.float32)
    mask = np.broadcast_to(mask[None, :, None, :], (2, 64, 16, 64)).reshape(128, 1024).copy()
    return z, mask


def _s5_layout(a_re, a_im, ls, b_re, b_im, c_re, c_im):
    f32 = np.float32
    def sm(a):
        return a.reshape(2, 16, 2, 64).transpose(0, 2, 3, 1).reshape(2, 128, 16).astype(f32)
    lsx = np.broadcast_to(ls[:, :, None], (2, 32, 64))
    sm3 = np.stack([sm(a_re), sm(a_im), sm(lsx)], axis=1).copy()
    rep3 = np.stack([a_re.reshape(2, 2048), a_im.reshape(2, 2048), lsx.reshape(2, 2048)], axis=1).astype(f32).copy()
    bl = np.zeros((2, 2, 128, 16, 2, 64), f32)
    cl = np.zeros((2, 2, 128, 16, 2, 16), f32)
    for s in range(16):
        for g1 in range(2):
            g = 2 * s + g1
            r0 = 32 * (s % 4) + 16 * g1
            for ri, b in enumerate((b_re, b_im)):
                bl[:, ri, r0:r0 + 16, s, g1, :] = b[:, g].transpose(0, 2, 1)
            for ri, c in enumerate((c_re, c_im)):
                cl[:, ri, 64 * g1:64 * g1 + 64, s, g1, :] = c[:, g].transpose(0, 2, 1)
    return sm3, rep3, bl.reshape(2, 2, 128, 16 * 128), cl.reshape(2, 2, 128, 16 * 32)


class Prog:
    def __init__(self, nseq, L, layers):
        self.nseq, self.L, self.layers = nseq, L, layers
        self.nch = L // 128
        nc = self.nc = bass.Bass("TRN2", target_bir_lowering=False)
        self.T = Tracker(nc)
        self.es = contextlib.ExitStack()
        self.dram = {}
        self.bufs = {}

    def sbt(self, name, shape, dt=F32):
        self._uid = getattr(self, '_uid', 0) + 1
        return self.nc.sbuf_tensor(f"{name}_u{self._uid}", list(shape), dt)

    def din(self, name, shape, dt=F32):
        t = self.nc.dram_tensor(name, list(shape), dt, kind="ExternalInput").ap()
        self.dram[name] = t
        return t

    def dout(self, name, shape, dt=F32):
        t = self.nc.dram_tensor(name, list(shape), dt, kind="ExternalOutput").ap()
        self.dram[name] = t
        return t

    def dscr(self, name, shape, dt=F32):
        t = self.nc.dram_tensor(name, list(shape), dt, kind="Internal").ap()
        self.dram[name] = t
        return t

    def sb(self, name, shape, dt=F32):
        t = self.es.enter_context(self.sbt(name, list(shape), dt))
        b = Buf(name)
        return t, b

    def ps(self, name):
        t = self.es.enter_context(self.nc.psum_tensor(name, [128, 512], F32))
        return t, Buf(name)

    def mm(self, out, lhsT, rhs, start, stop, reads, writes, inc=True):
        self.T.op("pe", lambda q: q.matmul(out, lhsT=lhsT, rhs=rhs, start=start, stop=stop),
                  reads=reads, writes=writes, inc=inc)

    def act(self, out, in_, func, reads, writes, scale=1.0, bias=None, accum=None):
        kw = {}
        if bias is not None:
            kw["bias"] = bias
        if accum is not None:
            kw["accum_out"] = accum
        self.T.op("act", lambda q: q.activation(out=out, in_=in_, func=func, scale=scale, **kw),
                  reads=reads, writes=writes)

    def tt(self, eng, out, in0, in1, op, reads, writes):
        self.T.op(eng, lambda q: q.tensor_tensor(out=out, in0=in0, in1=in1, op=op), reads=reads, writes=writes)

    def ts(self, eng, out, in0, s1, s2, op0, op1, reads, writes):
        if s2 is None:
            self.T.op(eng, lambda q: q.tensor_scalar(out=out, in0=in0, scalar1=s1, scalar2=None, op0=op0),
                      reads=reads, writes=writes)
        else:
            self.T.op(eng, lambda q: q.tensor_scalar(out=out, in0=in0, scalar1=s1, scalar2=s2, op0=op0, op1=op1),
                      reads=reads, writes=writes)

    def stt(self, eng, out, in0, scalar, in1, op0, op1, reads, writes):
        self.T.op(eng, lambda q: q.scalar_tensor_tensor(out=out, in0=in0, scalar=scalar, in1=in1, op0=op0, op1=op1),
                  reads=reads, writes=writes)

    def cp(self, eng, out, in_, reads, writes):
        self.T.op(eng, lambda q: q.tensor_copy(out=out, in_=in_), reads=reads, writes=writes)

    def memset(self, eng, ap, val, writes):
        self.T.op(eng, lambda q: q.memset(ap, val), reads=(), writes=writes)

    def dma(self, eng, out, in_, reads=(), writes=()):
        return self.T.dma(eng, out, in_, reads=reads, writes=writes)


def build(nseq, L, layers):
    P = Prog(nseq, L, layers)
    nc, T = P.nc, P.T
    nch = L // 128
    NL = 4
    x_in = P.din("x_in", [nseq, L, D])
    y_out = P.dout("y_out", [nseq, L, D])
    cT = P.din("cT", [128, 8, nseq])
    npreT = P.din("npreT", [NL, 128, 8])
    npostT = P.din("npostT", [NL, 128, 8])
    wmod = P.din("wmod", [NL, D, 3 * D])
    bmodT = P.din("bmodT", [NL, 128, 24])
    w_in_ab = P.din("w_in_ab", [2, D, 3072])
    w_out_ab = P.din("w_out_ab", [2, D, D])
    w_glu = P.din("w_glu", [2, 512, 512])
    ssm_d = P.din("ssm_d", [2, 512])
    s5_sm3 = P.din("s5_sm3", [2, 2, 3, 128, 16])
    s5_rep3 = P.din("s5_rep3", [2, 2, 3, 2048])
    s5_bl = P.din("s5_bl", [2, 2, 2, 128, 2048])
    s5_cl = P.din("s5_cl", [2, 2, 2, 128, 512])
    w_in_c = P.din("w_in_c", [2, D, 4096])
    w_out_c = P.din("w_out_c", [2, D, D])
    zg = P.din("zg", [2, 16, 128, 1024])
    zmask = P.din("zmask", [128, 1024])
    ident_d = P.din("ident", [128, 128])
    jmat_d = P.din("jmat", [128, 128])
    rot_d = P.din("rot", [L, 2, 2, 128])
    dtab_d = P.din("dtab", [128, 4, 128])
    gam_d = P.din("gam", [128, 4, 4])
    cdec_d = P.din("cdec", [128, 4])
    jt_d = P.din("jt", [128, 128])
    xs = [P.dscr("xsA", [nseq, L, D]), P.dscr("xsB", [nseq, L, D])]
    OPs_ = P.dscr("OPs", [nseq, L, 512])
    YPs_ = P.dscr("YPs", [nseq, L, 512])
    SGs_ = P.dscr("SGs", [nseq, L, 1024])
    QTs_ = P.dscr("QTs", [nseq, nch, 128, 512], BF16)
    KRs_ = P.dscr("KRs", [nseq, L, 512], BF16)
    VBs_ = P.dscr("VBs", [nseq, L, 512], BF16)
    UBs_ = P.dscr("UBs", [nseq, L, 512], BF16)
    SCRB = [[Buf(f"{n}{i}") for n in "OP YP SG QT KR VB UB".split()] for i in range(nseq)]
    b_xs = [[Buf(f"xs{i}_{s}") for s in range(nseq)] for i in range(2)]
    b_yout = [Buf(f"yout{s}") for s in range(nseq)]
    b_xin = Buf("xin")

    ident_bf, b_ident = P.sb("ident_bf", [128, 128], BF16)
    jmat_bf, b_jmat = P.sb("jmat_bf", [128, 128], BF16)
    identf, b_identf = P.sb("identf", [128, 128])
    epsT, b_eps = P.sb("epsT", [128, 1])
    ss, b_ss = P.sb("ss", [128, 4])
    sd, b_sd = P.sb("sd", [128, 4])
    rstd, b_rstd = P.sb("rstd", [128, 4])
    scT, b_scT = P.sb("scT", [128, 8, nseq])
    cTs, b_cTs = P.sb("cTs", [128, 8, nseq])
    modT, b_modT = P.sb("modT", [128, 24, nseq])
    gsT, b_gsT = P.sb("gsT", [128, 8, nseq])
    ggT, b_ggT = P.sb("ggT", [128, 8, nseq])
    ggrow = [P.sb(f"ggrow{s}", [128, 1024]) for s in range(nseq)]
    vecs, b_vecs = P.sb("vecs", [128, 40])
    psb = [P.ps(f"ps{i}") for i in range(8)]
    ps = [p[0] for p in psb]
    bps = [p[1] for p in psb]

    P.dma("sp", identf[:], ident_d, writes=[b_identf])
    P.dma("pool", ident_bf[:], ident_d, writes=[b_ident])
    P.dma("pool", jmat_bf[:], jmat_d, writes=[b_jmat])
    P.memset("pool", epsT[:], EPS, writes=[b_eps])
    P.dma("sp", cTs[:], cT, writes=[b_cTs])
    P.act(scT[:], cTs[:], AF.Sigmoid, reads=[b_cTs], writes=[b_scT])
    P.tt("dve", scT[:], scT[:], cTs[:], ALU.mult, reads=[b_scT, b_cTs], writes=[b_scT])

    def rstd_from_ss(col, n_feat):
        P.act(sd[:, col:col + 1], ss[:, col:col + 1], AF.Sqrt, reads=[b_ss, b_eps], writes=[b_sd],
              scale=1.0 / n_feat, bias=epsT[:, 0:1])
        P.T.op("dve", lambda q: q.reciprocal(out=rstd[:, col:col + 1], in_=sd[:, col:col + 1]),
               reads=[b_sd], writes=[b_rstd])

    def adaln(li, st_unused):
      with contextlib.ExitStack() as st:
        wm = [st.enter_context(P.sbt(f"wm{j}_{li}", [128, 8, 128], F32)) for j in range(2)]
        bwm = [Buf("wm0"), Buf("wm1")]
        gbl = st.enter_context(P.sbt(f"gbl_{li}", [128, 8, 128], F32))
        b_gbl = Buf("gbl")
        P.dma("sp", vecs[:, 0:8], npreT[li], writes=[b_vecs])
        P.dma("sp", vecs[:, 8:16], npostT[li], writes=[b_vecs])
        P.dma("sp", vecs[:, 16:40], bmodT[li], writes=[b_vecs])
        wsrc = wmod[li].rearrange("(k p) n -> p k n", p=128)
        for j in range(24):
            sl = j % 2
            P.dma("sp", wm[sl][:], wsrc[:, :, j * 128:(j + 1) * 128], writes=[bwm[sl]])
            for k in range(8):
                P.mm(ps[7][:, j * nseq:(j + 1) * nseq], wm[sl][:, k, :], scT[:, k, :], k == 0, k == 7,
                     reads=[bwm[sl], b_scT], writes=[bps[7]], inc=(k == 7))
        psv = ps[7][:, 0:24 * nseq].rearrange("p (j s) -> p j s", s=nseq)
        P.tt("dve", modT[:], psv, vecs[:, 16:40].unsqueeze(2).to_broadcast([128, 24, nseq]), ALU.add,
             reads=[bps[7], b_vecs], writes=[b_modT])
        P.ts("dve", gsT[:], modT[:, 8:16, :], 1.0, None, ALU.add, None, reads=[b_modT], writes=[b_gsT])
        P.tt("dve", gsT[:], gsT[:], vecs[:, 0:8].unsqueeze(2).to_broadcast([128, 8, nseq]), ALU.mult,
             reads=[b_gsT, b_vecs], writes=[b_gsT])
        P.tt("dve", ggT[:], modT[:, 16:24, :], vecs[:, 8:16].unsqueeze(2).to_broadcast([128, 8, nseq]), ALU.mult,
             reads=[b_modT, b_vecs], writes=[b_ggT])
        for s in range(nseq):
            P.cp("dve", gbl[:], ggT[:, :, s:s + 1].to_broadcast([128, 8, 128]), reads=[b_ggT], writes=[b_gbl])
            for c in range(8):
                bk = 5 + c // 4
                P.mm(ps[bk][:, (c % 4) * 128:(c % 4 + 1) * 128], gbl[:, c, :], identf[:], True, True,
                     reads=[b_gbl, b_identf], writes=[bps[bk]], inc=(c % 4 == 3))
            P.cp("dve", ggrow[s][0][:, 0:512], ps[5][:], reads=[bps[5]], writes=[ggrow[s][1]])
            P.act(ggrow[s][0][:, 512:1024], ps[6][:], AF.Copy, reads=[bps[6]], writes=[ggrow[s][1]])

    def make_common(st, tag):
        C = {}
        C["xt"] = [st.enter_context(P.sbt(f"xt{j}_{tag}", [128, 1024], F32)) for j in range(2)]
        C["bxt"] = [Buf("xt0"), Buf("xt1")]
        C["xn"] = st.enter_context(P.sbt(f"xn_{tag}", [128, 1024], BF16))
        C["bxn"] = Buf("xn")
        C["junk"] = st.enter_context(P.sbt(f"junk_{tag}", [128, 1024], BF16))
        C["bjunk"] = Buf("junk")
        C["yt"] = st.enter_context(P.sbt(f"yt_{tag}", [128, 1024], F32))
        C["byt"] = Buf("yt")
        C["cnt"] = 0
        return C

    def prenorm_parts(C, s, src, bsrc, n, hT, bhT, col0):
        sl = C["cnt"] % 2
        C["cnt"] += 1
        xt, bxt = C["xt"][sl], C["bxt"][sl]

        def p1():
            P.dma("sp", xt[:], src[s, n * 128:(n + 1) * 128, :], reads=[bsrc], writes=[bxt])
            P.act(C["yt"][:], xt[:], AF.Square, reads=[bxt], writes=[C["byt"]])
            P.T.op("dve", lambda q, yt_=C["yt"]: q.reduce_sum(out=ss[:, 0:1], in_=yt_[:], axis=mybir.AxisListType.X),
                   reads=[C["byt"]], writes=[b_ss])
            P.act(sd[:, 0:1], ss[:, 0:1], AF.Sqrt, reads=[b_ss, b_eps], writes=[b_sd], scale=1.0 / D, bias=epsT[:, 0:1])

        def p2():
            P.T.op("dve", lambda q: q.reciprocal(out=rstd[:, 0:1], in_=sd[:, 0:1]), reads=[b_sd], writes=[b_rstd])
            P.act(C["xn"][:], xt[:], AF.Copy, reads=[bxt, b_rstd], writes=[C["bxn"]], scale=rstd[:, 0:1])
            for b in range(2):
                for j in range(4):
                    k = 4 * b + j
                    P.mm(ps[b][:, j * 128:(j + 1) * 128], C["xn"][:, k * 128:(k + 1) * 128], ident_bf[:], True, True,
                         reads=[C["bxn"], b_ident], writes=[bps[b]], inc=(j == 3))

        def p3():
            for b in range(2):
                for j in range(4):
                    k = 4 * b + j
                    o = hT[:, k, col0:col0 + 128]
                    i_ = ps[b][:, j * 128:(j + 1) * 128]
                    if b == 0:
                        P.act(o, i_, AF.Identity, reads=[bps[b], b_gsT, b_modT], writes=[bhT],
                              scale=gsT[:, k, s:s + 1], bias=modT[:, k, s:s + 1])
                    else:
                        P.ts("dve", o, i_, gsT[:, k, s:s + 1], modT[:, k, s:s + 1], ALU.mult, ALU.add,
                             reads=[bps[b], b_gsT, b_modT], writes=[bhT])
        return p1, p2, p3

    def prenorm(C, s, src, bsrc, n, hT, bhT, col0):
        p1, p2, p3 = prenorm_parts(C, s, src, bsrc, n, hT, bhT, col0)
        p1()
        p2()
        p3()

    def post(C, s, ypb, xt, bxt, dst, bdst, n):
        yt, byt = C["yt"], C["byt"]
        for h in range(2):
            P.act(yt[:, h * 512:(h + 1) * 512], ps[ypb[h]][:], AF.Square, reads=[bps[ypb[h]]], writes=[byt])
        P.T.op("dve", lambda q, yt_=yt: q.reduce_sum(out=ss[:, 3:4], in_=yt_[:], axis=mybir.AxisListType.X),
               reads=[byt], writes=[b_ss])
        rstd_from_ss(3, D)
        for h in range(2):
            P.act(yt[:, h * 512:(h + 1) * 512], ps[ypb[h]][:], AF.Copy, reads=[bps[ypb[h]], b_rstd], writes=[byt],
                  scale=rstd[:, 3:4])
        P.tt("dve", yt[:], yt[:], ggrow[s][0][:], ALU.mult, reads=[byt, ggrow[s][1]], writes=[byt])
        P.tt("pool", yt[:], yt[:], xt[:], ALU.add, reads=[byt, bxt], writes=[byt])
        P.dma("pool", dst[s, n * 128:(n + 1) * 128, :], yt[:], reads=[byt], writes=[bdst])

    def sincos(st, tag, phi, bphi, shape, out_sin, out_cos, bout):
        tf = st.enter_context(P.sbt(f"sc_tf_{tag}", shape, F32))
        ti = st.enter_context(P.sbt(f"sc_ti_{tag}", shape, I32))
        btf, bti = Buf("tf"), Buf("ti")
        for shift, o in ((0.0, out_sin), (0.5 * math.pi, out_cos)):
            P.ts("dve", tf[:], phi, shift, 1.0 / TWO_PI, ALU.add, ALU.mult, reads=[bphi], writes=[btf])
            P.cp("dve", ti[:], tf[:], reads=[btf], writes=[bti])
            P.cp("dve", tf[:], ti[:], reads=[bti], writes=[btf])
            P.stt("dve", tf[:], tf[:], -TWO_PI, phi, ALU.mult, ALU.add, reads=[btf, bphi], writes=[btf])
            P.ts("dve", tf[:], tf[:], shift, 0.999999, ALU.add, ALU.mult, reads=[btf], writes=[btf])
            P.act(o, tf[:], AF.Sin, reads=[btf], writes=[bout])

    def even_layer(li, src, bsrc, dst, bdst):
        j = li // 2
        with contextlib.ExitStack() as st:
            def S(name, shape, dt=F32):
                return st.enter_context(P.sbt(f"{name}_e{li}", list(shape), dt)), Buf(name)
            adaln(li, st)
            C = make_common(st, f"e{li}")
            win, b_win = S("win", [128, 8, 3072], BF16)
            wout, b_wout = S("wout", [128, 8, 1024], BF16)
            wglu, b_wglu = S("wglu", [128, 4, 512], BF16)
            drow, b_drow = S("drow", [128, 512])
            dtab, b_dtab = S("dtab", [128, 4, 128])
            gam, b_gam = S("gam", [128, 4, 4])
            cdec, b_cdec = S("cdec", [128, 4])
            jt, b_jt = S("jt", [128, 128])
            wsrc = w_in_ab[j].rearrange("(k p) n -> p k n", p=128)
            for k in range(8):
                P.dma("pool", win[:, k, :], wsrc[:, k, :], writes=[b_win])
            P.dma("pool", wout[:], w_out_ab[j].rearrange("(k p) n -> p k n", p=128), writes=[b_wout])
            P.dma("pool", wglu[:], w_glu[j].rearrange("(k p) n -> p k n", p=128), writes=[b_wglu])
            P.dma("sp", drow[:], ssm_d[j].partition_broadcast(128), writes=[b_drow])
            P.dma("sp", dtab[:], dtab_d, writes=[b_dtab])
            P.dma("sp", gam[:], gam_d, writes=[b_gam])
            P.dma("sp", cdec[:], cdec_d, writes=[b_cdec])
            P.dma("sp", jt[:], jt_d, writes=[b_jt])
            WB = [S(f"WB{r}", [128, 2048], BF16) for r in range(2)]
            WC = [S(f"WC{r}", [128, 512], BF16) for r in range(3)]
            COS, b_COS = S("COS", [128, 16, 128])
            SIN, b_SIN = S("SIN", [128, 16, 128])
            RHO0, b_RHO0 = S("RHO0", [128, 16, 128])
            Gc, b_Gc = S("Gc", [128, 2, 16])
            cin = [S(f"cin{i}", [128, 2, 16]) for i in range(2)]
            stf, b_stf = S("stf", [128, 512])
            stbf, b_stbf = S("stbf", [128, 512], BF16)
            hn, b_hn = S("hn", [128, 16])
            lst, b_lst = S("lst", [128, 8, 16])
            lastb = [S(f"lastb{i}", [128, 2, 16]) for i in range(2)]

            def s5_setup(d):
                with contextlib.ExitStack() as s2:
                    def S2(name, shape, dt=F32):
                        return s2.enter_context(P.sbt(f"{name}_e{li}d{d}", list(shape), dt)), Buf(name)
                    sm, b_sm = S2("sm", [128, 3, 16])
                    P.dma("sp", sm[:], s5_sm3[j, d].rearrange("t p s -> p t s"), writes=[b_sm])
                    dl, b_dl = S2("dl", [128, 16])
                    zr, b_zr = S2("zr", [128, 16])
                    zi, b_zi = S2("zi", [128, 16])
                    rho, b_rho = S2("rho", [128, 16])
                    P.act(dl[:], sm[:, 2, :], AF.Exp, reads=[b_sm], writes=[b_dl])
                    P.tt("dve", zr[:], sm[:, 0, :], dl[:], ALU.mult, reads=[b_sm, b_dl], writes=[b_zr])
                    P.tt("dve", zi[:], sm[:, 1, :], dl[:], ALU.mult, reads=[b_sm, b_dl], writes=[b_zi])
                    P.act(rho[:], zr[:], AF.Exp, reads=[b_zr], writes=[b_rho])
                    for g4 in range(4):
                        with contextlib.ExitStack() as s4:
                            phi = s4.enter_context(P.sbt(f"phi_e{li}d{d}g{g4}", [128, 4, 128], F32))
                            b_phi = Buf("phi")
                            P.tt("dve", phi[:], zi[:, 4 * g4:4 * g4 + 4].unsqueeze(2).to_broadcast([128, 4, 128]),
                                 jt[:].unsqueeze(1).to_broadcast([128, 4, 128]), ALU.mult, reads=[b_zi, b_jt], writes=[b_phi])
                            sincos(s4, f"t{li}{d}{g4}", phi[:], b_phi, [128, 4, 128], SIN[:, 4 * g4:4 * g4 + 4, :],
                                   COS[:, 4 * g4:4 * g4 + 4, :], b_COS)
                            T_barrier()
                    P.cp("dve", RHO0[:], rho[:].unsqueeze(2).to_broadcast([128, 16, 128]), reads=[b_rho], writes=[b_RHO0])
                    P.memset("dve", RHO0[:, :, 0:1], 0.0, writes=[b_RHO0])
                    ph2, b_ph2 = S2("ph2", [128, 16])
                    sn2, b_sn2 = S2("sn2", [128, 2, 16])
                    P.ts("dve", ph2[:], zi[:], 128.0, None, ALU.mult, None, reads=[b_zi], writes=[b_ph2])
                    sincos(s2, f"g{li}{d}", ph2[:], b_ph2, [128, 16], sn2[:, 1, :], sn2[:, 0, :], b_sn2)
                    P.tt("dve", Gc[:], sn2[:], rho[:].unsqueeze(1).to_broadcast([128, 2, 16]), ALU.mult,
                         reads=[b_sn2, b_rho], writes=[b_Gc])
                    T_barrier()
                for cc in range(8):
                  with contextlib.ExitStack() as s3:
                    def S3(name, shape, dt=F32):
                        return s3.enter_context(P.sbt(f"{name}_e{li}d{d}c{cc}", list(shape), dt)), Buf(name)
                    rp, b_rp = S3("rp", [128, 3, 256])
                    P.dma("sp", rp[:], s5_rep3[j, d][:, cc * 256:(cc + 1) * 256].partition_broadcast(128), writes=[b_rp])
                    r_dl, b_r_dl = S3("r_dl", [128, 256])
                    r_zr, b_r_zr = S3("r_zr", [128, 256])
                    r_zi, b_r_zi = S3("r_zi", [128, 256])
                    r_rho, b_r_rho = S3("r_rho", [128, 256])
                    r_sn, b_r_sn = S3("r_sn", [128, 2, 256])
                    P.act(r_dl[:], rp[:, 2, :], AF.Exp, reads=[b_rp], writes=[b_r_dl])
                    P.tt("dve", r_zr[:], rp[:, 0, :], r_dl[:], ALU.mult, reads=[b_rp, b_r_dl], writes=[b_r_zr])
                    P.tt("dve", r_zi[:], rp[:, 1, :], r_dl[:], ALU.mult, reads=[b_rp, b_r_dl], writes=[b_r_zi])
                    P.act(r_rho[:], r_zr[:], AF.Exp, reads=[b_r_zr], writes=[b_r_rho])
                    sincos(s3, f"r{li}{d}{cc}", r_zi[:], b_r_zi, [128, 256], r_sn[:, 1, :], r_sn[:, 0, :], b_r_sn)
                    P.tt("dve", r_sn[:], r_sn[:], r_rho[:].unsqueeze(1).to_broadcast([128, 2, 256]), ALU.mult,
                         reads=[b_r_sn, b_r_rho], writes=[b_r_sn])
                    P.ts("dve", r_sn[:, 0, :], r_sn[:, 0, :], -1.0, None, ALU.add, None, reads=[b_r_sn], writes=[b_r_sn])
                    P.tt("dve", r_dl[:], rp[:, 0, :], rp[:, 0, :], ALU.mult, reads=[b_rp], writes=[b_r_dl])
                    P.tt("dve", r_zr[:], rp[:, 1, :], rp[:, 1, :], ALU.mult, reads=[b_rp], writes=[b_r_zr])
                    P.tt("dve", r_dl[:], r_dl[:], r_zr[:], ALU.add, reads=[b_r_dl, b_r_zr], writes=[b_r_dl])
                    P.T.op("dve", lambda q, r_dl=r_dl: q.reciprocal(out=r_dl[:], in_=r_dl[:]), reads=[b_r_dl], writes=[b_r_dl])
                    P.tt("dve", r_zr[:], r_sn[:, 0, :], rp[:, 0, :], ALU.mult, reads=[b_r_sn, b_rp], writes=[b_r_zr])
                    P.tt("dve", r_rho[:], r_sn[:, 1, :], rp[:, 1, :], ALU.mult, reads=[b_r_sn, b_rp], writes=[b_r_rho])
                    P.tt("dve", r_zr[:], r_zr[:], r_rho[:], ALU.add, reads=[b_r_zr, b_r_rho], writes=[b_r_zr])
                    P.tt("dve", r_zr[:], r_zr[:], r_dl[:], ALU.mult, reads=[b_r_zr, b_r_dl], writes=[b_r_zr])
                    P.tt("dve", r_zi[:], r_sn[:, 1, :], rp[:, 0, :], ALU.mult, reads=[b_r_sn, b_rp], writes=[b_r_zi])
                    P.tt("dve", r_rho[:], r_sn[:, 0, :], rp[:, 1, :], ALU.mult, reads=[b_r_sn, b_rp], writes=[b_r_rho])
                    P.tt("dve", r_zi[:], r_zi[:], r_rho[:], ALU.subtract, reads=[b_r_zi, b_r_rho], writes=[b_r_zi])
                    P.tt("dve", r_zi[:], r_zi[:], r_dl[:], ALU.mult, reads=[b_r_zi, b_r_dl], writes=[b_r_zi])
                    bl, b_bl = S3("bl", [128, 2, 256])
                    P.dma("sp", bl[:], s5_bl[j, d][:, :, cc * 256:(cc + 1) * 256].rearrange("r p n -> p r n"), writes=[b_bl])
                    P.tt("dve", r_dl[:], r_zr[:], bl[:, 0, :], ALU.mult, reads=[b_r_zr, b_bl], writes=[b_r_dl])
                    P.tt("dve", r_rho[:], r_zi[:], bl[:, 1, :], ALU.mult, reads=[b_r_zi, b_bl], writes=[b_r_rho])
                    P.tt("dve", WB[0][0][:, cc * 256:(cc + 1) * 256], r_dl[:], r_rho[:], ALU.subtract, reads=[b_r_dl, b_r_rho], writes=[WB[0][1]])
                    P.tt("dve", r_dl[:], r_zr[:], bl[:, 1, :], ALU.mult, reads=[b_r_zr, b_bl], writes=[b_r_dl])
                    P.tt("dve", r_rho[:], r_zi[:], bl[:, 0, :], ALU.mult, reads=[b_r_zi, b_bl], writes=[b_r_rho])
                    P.tt("dve", WB[1][0][:, cc * 256:(cc + 1) * 256], r_dl[:], r_rho[:], ALU.add, reads=[b_r_dl, b_r_rho], writes=[WB[1][1]])

                    T_barrier()
                with contextlib.ExitStack() as s2:
                    def S2(name, shape, dt=F32):
                        return s2.enter_context(P.sbt(f"{name}_e{li}d{d}x", list(shape), dt)), Buf(name)
                    cl, b_cl = S2("cl", [128, 2, 512])
                    P.dma("sp", cl[:], s5_cl[j, d].rearrange("r p n -> p r n"), writes=[b_cl])
                    P.cp("dve", WC[0][0][:], cl[:, 0, :], reads=[b_cl], writes=[WC[0][1]])
                    P.ts("dve", WC[1][0][:], cl[:, 0, :], -1.0, None, ALU.mult, None, reads=[b_cl], writes=[WC[1][1]])
                    P.ts("dve", WC[2][0][:], cl[:, 1, :], -1.0, None, ALU.mult, None, reads=[b_cl], writes=[WC[2][1]])
                    P.memset("dve", cin[0][0][:], 0.0, writes=[cin[0][1]])
                    T_barrier()

            import types

            def alloc_work(wk, passB):
                def W(name, shape, dt=F32):
                    return wk.enter_context(P.sbt(f"{name}_e{li}", list(shape), dt)), Buf(name)
                w = types.SimpleNamespace()
                w.g = [types.SimpleNamespace() for _ in range(2)]
                tmps = {nm: W(f"{nm}m", [128, 512]) for nm in ("tA", "tB", "tC", "tD")}
                Pk1 = [W(f"Pk_{k}", [128, 512], BF16) for k in range(4)]
                for i, g in enumerate(w.g):
                    for nm in ("wre", "wim", "sre", "sim"):
                        setattr(g, nm, W(f"{nm}{i}", [128, 512]))
                    for nm in ("tA", "tB", "tC", "tD"):
                        setattr(g, nm, tmps[nm])
                    g.Pk = Pk1
                w.uT = W("uT", [128, 4, 128], BF16)
                w.kf = W("kf", [128, 512], BF16)
                w.tC = W("tCr", [128, 512])
                if not passB:
                    w.hT = [W(f"hT{i}", [128, 8, 128], BF16) for i in range(2)]
                    w.rot = [W(f"rot{i}", [128, 2, 2, 128]) for i in range(2)]
                    w.rA = W("rA", [128, 512])
                    w.rB = W("rB", [128, 512])
                    w.qr = W("qr", [128, 512], BF16)
                    w.kr = [W("kr", [128, 512], BF16)]
                    w.vb = [W("vb", [128, 512], BF16)]
                    w.QT = [W("QT", [128, 512], BF16)]
                    w.KT = W("KT", [128, 512], BF16)
                    w.STb = W("STb", [128, 512], BF16)
                    w.sg = [W("sg", [128, 1024])]
                    w.du = W("du", [128, 512])
                    w.ub = [W("ub", [128, 512], BF16)]
                    w.opt = [W("opt", [128, 512])]
                    w.ypt = [W("ypt", [128, 512])]
                else:
                    w.kr = [W(f"kr{i}", [128, 512], BF16) for i in range(2)]
                    w.vb = [W(f"vb{i}", [128, 512], BF16) for i in range(2)]
                    w.QT = [W(f"QT{i}", [128, 512], BF16) for i in range(2)]
                    w.sga = W("sga", [128, 512])
                    w.sgb = [W(f"sgb{i}", [128, 512]) for i in range(2)]
                    w.uT2 = W("uT2", [128, 4, 128], BF16)
                    w.ub = [W(f"ub{i}", [128, 512], BF16) for i in range(2)]
                    w.opt = [W(f"opt{i}", [128, 512]) for i in range(2)]
                    w.ypt = [W(f"ypt{i}", [128, 512]) for i in range(2)]
                    w.oab = [W(f"oab{i}", [128, 1024], BF16) for i in range(2)]
                    w.oT = W("oT", [128, 8, 128], BF16)
                    w.ygb = W("ygb", [128, 512], BF16)
                    w.tA = W("tAh", [128, 512])
                    w.tB = W("tBh", [128, 512])
                    w.tD = W("tDh", [128, 512])
                return w

            def s5_chunk(w, ci, rev, hooks=()):
                cur, nxt = cin[ci % 2], cin[(ci + 1) % 2]
                lb, b_lb = lastb[ci % 2]
                uT, b_uT = w.uT
                hooks = list(hooks)

                def hook():
                    if hooks:
                        hooks.pop(0)()

                def banks(g):
                    return (3, 4) if g % 2 == 0 else (5, 6)

                def stageApe(g):
                    br, bi = banks(g)
                    for t in range(4):
                        s_ = 4 * g + t
                        P.mm(ps[br][:, t * 128:(t + 1) * 128], WB[0][0][:, s_ * 128:(s_ + 1) * 128], uT[:, g, :], True, False,
                             reads=[WB[0][1], b_uT], writes=[bps[br]], inc=False)
                        P.mm(ps[bi][:, t * 128:(t + 1) * 128], WB[1][0][:, s_ * 128:(s_ + 1) * 128], uT[:, g, :], True, False,
                             reads=[WB[1][1], b_uT], writes=[bps[bi]], inc=False)
                    o_r = ps[br][:].rearrange("p (a b) -> p a b", a=4)[:, :, 0:1]
                    o_i = ps[bi][:].rearrange("p (a b) -> p a b", a=4)[:, :, 0:1]
                    P.mm(o_r, identf[:], cur[0][:, 0, 4 * g:4 * g + 4].unsqueeze(2), False, True,
                         reads=[b_identf, cur[1]], writes=[bps[br]], inc=False)
                    P.mm(o_i, identf[:], cur[0][:, 1, 4 * g:4 * g + 4].unsqueeze(2), False, True,
                         reads=[b_identf, cur[1]], writes=[bps[bi]], inc=True)

                def stageAve(g):
                    br, bi = banks(g)
                    G = w.g[g % 2]
                    Cg = COS[:, 4 * g:4 * g + 4, :].rearrange("p a b -> p (a b)")
                    Sg = SIN[:, 4 * g:4 * g + 4, :].rearrange("p a b -> p (a b)")
                    P.tt("dve", G.tA[0][:], ps[br][:], Cg, ALU.mult, reads=[bps[br], b_COS], writes=[G.tA[1]])
                    P.tt("dve", G.tB[0][:], ps[bi][:], Sg, ALU.mult, reads=[bps[bi], b_COS], writes=[G.tB[1]])
                    P.tt("pool", G.wre[0][:], G.tA[0][:], G.tB[0][:], ALU.add, reads=[G.tA[1], G.tB[1]], writes=[G.wre[1]])
                    P.tt("dve", G.tC[0][:], ps[bi][:], Cg, ALU.mult, reads=[bps[bi], b_COS], writes=[G.tC[1]])
                    P.stt("dve", G.tD[0][:], ps[br][:], -1.0, Sg, ALU.mult, ALU.mult, reads=[bps[br], b_COS], writes=[G.tD[1]])
                    P.tt("pool", G.wim[0][:], G.tC[0][:], G.tD[0][:], ALU.add, reads=[G.tC[1], G.tD[1]], writes=[G.wim[1]])

                def stageB(g):
                    G = w.g[g % 2]
                    Rg = RHO0[:, 4 * g:4 * g + 4, :].rearrange("p a b -> p (a b)")
                    P.T.op("dve", lambda q, Rg=Rg, o=G.sre[0], i=G.wre[0]: q.tensor_tensor_scan(
                        out=o[:], data0=Rg, data1=i[:], initial=0.0, op0=ALU.mult, op1=ALU.add),
                        reads=[b_RHO0, G.wre[1]], writes=[G.sre[1]])
                    P.T.op("dve", lambda q, Rg=Rg, o=G.sim[0], i=G.wim[0]: q.tensor_tensor_scan(
                        out=o[:], data0=Rg, data1=i[:], initial=0.0, op0=ALU.mult, op1=ALU.add),
                        reads=[b_RHO0, G.wim[1]], writes=[G.sim[1]])
                    sr3 = G.sre[0][:].rearrange("p (a b) -> p a b", a=4)
                    si3 = G.sim[0][:].rearrange("p (a b) -> p a b", a=4)
                    P.act(lb[:, 0, 4 * g:4 * g + 4].unsqueeze(2), sr3[:, :, 127:128], AF.Copy, reads=[G.sre[1]], writes=[b_lb])
                    P.act(lb[:, 1, 4 * g:4 * g + 4].unsqueeze(2), si3[:, :, 127:128], AF.Copy, reads=[G.sim[1]], writes=[b_lb])

                    def pv(ap):
                        v = ap.rearrange("p (a b) -> p a b", a=4)
                        return v[:, :, ::-1] if rev else v
                    C3 = COS[:, 4 * g:4 * g + 4, :]
                    S3 = SIN[:, 4 * g:4 * g + 4, :]
                    e2 = "dve" if rev else "pool"
                    P.tt("dve", pv(G.Pk[0][0][:]), sr3, C3, ALU.mult, reads=[G.sre[1], b_COS], writes=[G.Pk[0][1]])
                    P.tt(e2, pv(G.Pk[1][0][:]), si3, S3, ALU.mult, reads=[G.sim[1], b_COS], writes=[G.Pk[1][1]])
                    P.tt("dve", pv(G.Pk[2][0][:]), sr3, S3, ALU.mult, reads=[G.sre[1], b_COS], writes=[G.Pk[2][1]])
                    P.tt(e2, pv(G.Pk[3][0][:]), si3, C3, ALU.mult, reads=[G.sim[1], b_COS], writes=[G.Pk[3][1]])
                    for t in range(4):
                        s_ = 4 * g + t
                        o = ps[7][:, 32 * s_:32 * s_ + 32]
                        wsl = slice(32 * s_, 32 * s_ + 32)
                        tsl = slice(128 * t, 128 * t + 128)
                        Pk = G.Pk
                        P.mm(o, Pk[0][0][:, tsl], WC[0][0][:, wsl], True, False, reads=[Pk[0][1], WC[0][1]], writes=[bps[7]], inc=False)
                        P.mm(o, Pk[1][0][:, tsl], WC[1][0][:, wsl], False, False, reads=[Pk[1][1], WC[1][1]], writes=[bps[7]], inc=False)
                        P.mm(o, Pk[2][0][:, tsl], WC[2][0][:, wsl], False, False, reads=[Pk[2][1], WC[2][1]], writes=[bps[7]], inc=False)
                        P.mm(o, Pk[3][0][:, tsl], WC[2][0][:, wsl], False, True, reads=[Pk[3][1], WC[2][1]], writes=[bps[7]], inc=(t == 3))

                stageApe(0)
                stageAve(0)
                stageApe(1)
                hook()
                stageAve(1)
                hook()
                stageApe(2)
                stageB(0)
                hook()
                stageAve(2)
                hook()
                stageApe(3)
                stageB(1)
                hook()
                stageAve(3)
                hook()
                stageB(2)
                hook()
                stageB(3)
                hook()
                while hooks:
                    hook()
                l4 = lst[:]
                P.tt("pool", l4[:, 0, :], lb[:, 0, :], Gc[:, 0, :], ALU.mult, reads=[b_lb, b_Gc], writes=[b_lst])
                P.tt("pool", l4[:, 1, :], lb[:, 1, :], Gc[:, 1, :], ALU.mult, reads=[b_lb, b_Gc], writes=[b_lst])
                P.tt("pool", l4[:, 2, :], lb[:, 1, :], Gc[:, 0, :], ALU.mult, reads=[b_lb, b_Gc], writes=[b_lst])
                P.tt("pool", l4[:, 3, :], lb[:, 0, :], Gc[:, 1, :], ALU.mult, reads=[b_lb, b_Gc], writes=[b_lst])
                P.tt("pool", nxt[0][:, 0, :], l4[:, 0, :], l4[:, 1, :], ALU.subtract, reads=[b_lst], writes=[nxt[1]])
                P.tt("pool", nxt[0][:, 1, :], l4[:, 2, :], l4[:, 3, :], ALU.add, reads=[b_lst], writes=[nxt[1]])

            def ret_state_update(w, kbuf, b_kbuf, vbt, b_vbt, gcol):
                kf, b_kf = w.kf
                P.tt("pool", kf[:].rearrange("p (h e) -> p h e", h=4), kbuf[:].rearrange("p (h e) -> p h e", h=4),
                     gam[:, gcol, :].unsqueeze(2).to_broadcast([128, 4, 128]), ALU.mult,
                     reads=[b_kbuf, b_gam], writes=[b_kf])
                for h in range(4):
                    hs = slice(h * 128, (h + 1) * 128)
                    P.mm(ps[5][:, hs], kf[:, hs], vbt[:, hs], True, True, reads=[b_kf, b_vbt], writes=[bps[5]], inc=(h == 3))
                P.tt("pool", stf[:].rearrange("p (h e) -> p h e", h=4), stf[:].rearrange("p (h e) -> p h e", h=4),
                     cdec[:].unsqueeze(2).to_broadcast([128, 4, 128]), ALU.mult, reads=[b_stf, b_cdec], writes=[b_stf])
                P.tt("dve", stf[:], stf[:], ps[5][:], ALU.add, reads=[b_stf, bps[5]], writes=[b_stf])
                P.act(stbf[:], stf[:], AF.Copy, reads=[b_stf], writes=[b_stbf])

            def passA(s, w):
                OPs, YPs, SGs, QTs, KRs, VBs, UBs = OPs_[s], YPs_[s], SGs_[s], QTs_[s], KRs_[s], VBs_[s], UBs_[s]
                b_OPs, b_YPs, b_SGs, b_QTs, b_KRs, b_VBs, b_UBs = SCRB[s]
                P.memset("dve", stf[:], 0.0, writes=[b_stf])
                P.memset("pool", stbf[:], 0.0, writes=[b_stbf])
                P.memset("dve", cin[0][0][:], 0.0, writes=[cin[0][1]])
                nA = nch if KSTOP not in ('setup',) else 0
                if nA:
                    prenorm(C, s, src, bsrc[s], 0, w.hT[0][0], w.hT[0][1], 0)
                for n in range(nA):
                    tsl = slice(n * 128, (n + 1) * 128)
                    hT, b_hT = w.hT[n % 2]
                    rot, b_rot = w.rot[n % 2]
                    P.dma("sp", rot[:], rot_d[tsl], writes=[b_rot])
                    if n + 1 < nA:
                        p1, p2, p3 = prenorm_parts(C, s, src, bsrc[s], n + 1, w.hT[(n + 1) % 2][0], w.hT[(n + 1) % 2][1], 0)
                    else:
                        p1 = p2 = p3 = (lambda: None)
                    qr, b_qr = w.qr
                    kr, b_kr = w.kr[0]
                    vb, b_vb = w.vb[0]
                    QT, b_QT = w.QT[0]
                    KT, b_KT = w.KT
                    STb, b_STb = w.STb
                    sg, b_sg = w.sg[0]
                    du, b_du = w.du
                    ub, b_ub = w.ub[0]
                    opt, b_opt = w.opt[0]
                    ypt, b_ypt = w.ypt[0]
                    tC, b_tC = w.tC
                    uT, b_uT = w.uT
                    def inproj(col, bank):
                        for k in range(8):
                            P.mm(ps[bank][:], hT[:, k, :], win[:, k, col * 512:(col + 1) * 512], k == 0, k == 7,
                                 reads=[b_hT, b_win], writes=[bps[bank]], inc=(k == 7))

                    def rotary(zb, tbl, outb, b_outb):
                        rA, b_rA = w.rA
                        rB, b_rB = w.rB
                        z3 = ps[zb][:].rearrange("p (h e) -> p h e", h=4)
                        a3 = rA[:].rearrange("p (h e) -> p h e", h=4)
                        b3 = rB[:].rearrange("p (h e) -> p h e", h=4)
                        P.tt("dve", a3, z3, rot[:, tbl, 0, :].unsqueeze(1).to_broadcast([128, 4, 128]), ALU.mult,
                             reads=[bps[zb], b_rot], writes=[b_rA])
                        P.tt("dve", b3[:, :, 0:64], z3[:, :, 64:128], rot[:, tbl, 1, 0:64].unsqueeze(1).to_broadcast([128, 4, 64]),
                             ALU.mult, reads=[bps[zb], b_rot], writes=[b_rB])
                        P.tt("dve", b3[:, :, 64:128], z3[:, :, 0:64], rot[:, tbl, 1, 64:128].unsqueeze(1).to_broadcast([128, 4, 64]),
                             ALU.mult, reads=[bps[zb], b_rot], writes=[b_rB])
                        P.tt("pool", outb[:], rA[:], rB[:], ALU.add, reads=[b_rA, b_rB], writes=[b_outb])

                    inproj(4, 2)
                    P.act(ub[:], ps[2][:], AF.Copy, reads=[bps[2]], writes=[b_ub])
                    P.tt("dve", du[:], ps[2][:], drow[:], ALU.mult, reads=[bps[2], b_drow, b_ub], writes=[b_du])
                    P.dma("pool", UBs[tsl], ub[:], reads=[b_ub], writes=[b_UBs])
                    for q_ in range(4):
                        qs = slice(q_ * 128, (q_ + 1) * 128)
                        P.mm(ps[0][:, qs], ub[:, qs], ident_bf[:], True, True, reads=[b_ub, b_ident], writes=[bps[0]], inc=(q_ == 3))
                    P.act(uT[:].rearrange("p a b -> p (a b)"), ps[0][:], AF.Copy, reads=[bps[0]], writes=[b_uT])

                    def H0():
                        inproj(0, 2)
                        inproj(1, 1)
                        rotary(2, 0, qr, b_qr)
                        rotary(1, 1, kr, b_kr)

                    def H1():
                        inproj(2, 2)
                        P.act(vb[:], ps[2][:], AF.Copy, reads=[bps[2]], writes=[b_vb])
                        inproj(3, 0)
                        P.act(sg[:, 0:512], ps[0][:], AF.Silu, reads=[bps[0]], writes=[b_sg])
                        inproj(5, 1)
                        P.act(sg[:, 512:1024], ps[1][:], AF.Silu, reads=[bps[1]], writes=[b_sg])
                        P.dma("pool", SGs[tsl], sg[:], reads=[b_sg], writes=[b_SGs])

                    def R1():
                        for h in range(4):
                            hs = slice(h * 128, (h + 1) * 128)
                            P.mm(ps[0][:, hs], qr[:, hs], ident_bf[:], True, True, reads=[b_qr, b_ident], writes=[bps[0]], inc=(h == 3))
                        for h in range(4):
                            hs = slice(h * 128, (h + 1) * 128)
                            P.mm(ps[1][:, hs], kr[:, hs], ident_bf[:], True, True, reads=[b_kr, b_ident], writes=[bps[1]], inc=(h == 3))
                        P.act(QT[:], ps[0][:], AF.Copy, reads=[bps[0]], writes=[b_QT])
                        P.act(KT[:], ps[1][:], AF.Copy, reads=[bps[1]], writes=[b_KT])
                        p1()
                    def R2():
                        for h in range(4):
                            hs = slice(h * 128, (h + 1) * 128)
                            P.mm(ps[2][:, hs], KT[:, hs], QT[:, hs], True, True, reads=[b_KT, b_QT], writes=[bps[2]], inc=(h == 3))
                        P.tt("dve", STb[:], ps[2][:], dtab[:].rearrange("p h e -> p (h e)"), ALU.mult,
                             reads=[bps[2], b_dtab], writes=[b_STb])
                    def R3():
                        for h in range(4):
                            hs = slice(h * 128, (h + 1) * 128)
                            P.mm(ps[2][:, hs], STb[:, hs], vb[:, hs], True, True, reads=[b_STb, b_vb], writes=[bps[2]], inc=(h == 3))
                        for h in range(4):
                            hs = slice(h * 128, (h + 1) * 128)
                            P.mm(ps[0][:, hs], QT[:, hs], stbf[:, hs], True, True, reads=[b_QT, b_stbf], writes=[bps[0]], inc=(h == 3))
                        P.tt("dve", tC[:].rearrange("p (h e) -> p h e", h=4), ps[0][:].rearrange("p (h e) -> p h e", h=4),
                             gam[:, 0, :].unsqueeze(2).to_broadcast([128, 4, 128]), ALU.mult, reads=[bps[0], b_gam], writes=[b_tC])
                        P.tt("dve", opt[:], tC[:], ps[2][:], ALU.add, reads=[b_tC, bps[2]], writes=[b_opt])
                        P.dma("pool", OPs[tsl], opt[:], reads=[b_opt], writes=[b_OPs])
                    def R4():
                        ret_state_update(w, kr, b_kr, vb, b_vb, 2)
                    def R5():
                        P.dma("pool", QTs[n], QT[:], reads=[b_QT], writes=[b_QTs])
                        P.dma("pool", KRs[tsl], kr[:], reads=[b_kr], writes=[b_KRs])
                        P.dma("pool", VBs[tsl], vb[:], reads=[b_vb], writes=[b_VBs])
                    def R45():
                        R4()
                        R5()

                    def P23():
                        p2()
                        p3()
                    s5_chunk(w, n, False, hooks=[H0, H1, R1, R2, R3, R45, P23])
                    P.tt("dve", ypt[:], ps[7][:], du[:], ALU.add, reads=[bps[7], b_du], writes=[b_ypt])
                    P.dma("pool", YPs[tsl], ypt[:], reads=[b_ypt], writes=[b_YPs])

            def passB(s, w):
                OPs, YPs, SGs, QTs, KRs, VBs, UBs = OPs_[s], YPs_[s], SGs_[s], QTs_[s], KRs_[s], VBs_[s], UBs_[s]
                b_OPs, b_YPs, b_SGs, b_QTs, b_KRs, b_VBs, b_UBs = SCRB[s]
                P.memset("dve", stf[:], 0.0, writes=[b_stf])
                P.memset("pool", stbf[:], 0.0, writes=[b_stbf])
                P.memset("dve", cin[0][0][:], 0.0, writes=[cin[0][1]])
                order = list(range(nch - 1, -1, -1)) if KSTOP == 'all' else []

                def loads(ci):
                    n = order[ci]
                    tsl = slice(n * 128, (n + 1) * 128)
                    r = ci % 2
                    P.dma("sp", w.ub[r][0][:], UBs[tsl], reads=[b_UBs], writes=[w.ub[r][1]])
                    P.dma("sp", w.QT[r][0][:], QTs[n], reads=[b_QTs], writes=[w.QT[r][1]])
                    P.dma("sp", w.opt[r][0][:], OPs[tsl], reads=[b_OPs], writes=[w.opt[r][1]])
                    P.dma("sp", w.kr[r][0][:], KRs[tsl], reads=[b_KRs], writes=[w.kr[r][1]])
                    P.dma("sp", w.vb[r][0][:], VBs[tsl], reads=[b_VBs], writes=[w.vb[r][1]])

                def make_tail(ci, n):
                    r = ci % 2
                    tsl = slice(n * 128, (n + 1) * 128)
                    ypt, b_ypt = w.ypt[r]
                    sgb, b_sgb = w.sgb[r]
                    oab, b_oab = w.oab[r]
                    tB, b_tB = w.tB
                    tD, b_tD = w.tD
                    uT2, b_uT2 = w.uT2
                    oT, b_oT = w.oT
                    ygb, b_ygb = w.ygb

                    def T1():
                        P.tt("pool", tB[:], ypt[:], ypt[:], ALU.mult, reads=[b_ypt], writes=[b_tB])
                        P.ts("dve", tB[:], tB[:], 0.044715, 1.0, ALU.mult, ALU.add, reads=[b_tB], writes=[b_tB])
                        P.tt("pool", tB[:], tB[:], ypt[:], ALU.mult, reads=[b_tB, b_ypt], writes=[b_tB])
                        P.act(tB[:], tB[:], AF.Sigmoid, reads=[b_tB], writes=[b_tB], scale=1.5957691216057308)
                        P.tt("dve", tD[:], ypt[:], tB[:], ALU.mult, reads=[b_ypt, b_tB], writes=[b_tD])
                        P.act(ygb[:], tD[:], AF.Copy, reads=[b_tD], writes=[b_ygb])

                    def T2():
                        for q_ in range(4):
                            qs = slice(q_ * 128, (q_ + 1) * 128)
                            P.mm(ps[0][:, qs], ygb[:, qs], ident_bf[:], True, True, reads=[b_ygb, b_ident], writes=[bps[0]], inc=(q_ == 3))
                        P.act(uT2[:].rearrange("p a b -> p (a b)"), ps[0][:], AF.Copy, reads=[bps[0]], writes=[b_uT2])
                        for q_ in range(4):
                            P.mm(ps[1][:], uT2[:, q_, :], wglu[:, q_, :], q_ == 0, q_ == 3, reads=[b_uT2, b_wglu], writes=[bps[1]], inc=(q_ == 3))
                        P.act(tB[:], ps[1][:], AF.Sigmoid, reads=[bps[1]], writes=[b_tB])
                        P.tt("dve", tD[:], tD[:], tB[:], ALU.mult, reads=[b_tD, b_tB], writes=[b_tD])
                        P.tt("pool", oab[:, 512:1024], tD[:], sgb[:], ALU.mult, reads=[b_tD, b_sgb], writes=[b_oab])

                    def T3():
                        for b_ in range(2):
                            for jj in range(4):
                                k = 4 * b_ + jj
                                P.mm(ps[b_][:, jj * 128:(jj + 1) * 128], oab[:, k * 128:(k + 1) * 128], ident_bf[:], True, True,
                                     reads=[b_oab, b_ident], writes=[bps[b_]], inc=(jj == 3))
                        P.act(oT[:, 0:4, :].rearrange("p a b -> p (a b)"), ps[0][:], AF.Copy, reads=[bps[0]], writes=[b_oT])
                        P.act(oT[:, 4:8, :].rearrange("p a b -> p (a b)"), ps[1][:], AF.Copy, reads=[bps[1]], writes=[b_oT])
                        for hh, bk in ((0, 2), (1, 0)):
                            for k in range(8):
                                P.mm(ps[bk][:], oT[:, k, :], wout[:, k, hh * 512:(hh + 1) * 512], k == 0, k == 7,
                                     reads=[b_oT, b_wout], writes=[bps[bk]], inc=(k == 7))

                    def T4():
                        sl = C["cnt"] % 2
                        C["cnt"] += 1
                        xt, bxt = C["xt"][sl], C["bxt"][sl]
                        P.dma("sp", xt[:], src[s, tsl, :], reads=[bsrc[s]], writes=[bxt])
                        post(C, s, (2, 0), xt, bxt, dst, bdst[s], n)
                    return [T1, T2, T3, T4]

                if order:
                    loads(0)
                tail = []
                for ci, n in enumerate(order):
                    tsl = slice(n * 128, (n + 1) * 128)
                    r = ci % 2
                    QT, b_QT = w.QT[r]
                    opt, b_opt = w.opt[r]
                    kr, b_kr = w.kr[r]
                    vb, b_vb = w.vb[r]
                    ub, b_ub = w.ub[r]
                    sga, b_sga = w.sga
                    sgb, b_sgb = w.sgb[r]
                    ypt, b_ypt = w.ypt[r]
                    tA, b_tA = w.tA
                    tC, b_tC = w.tC
                    uT, b_uT = w.uT
                    oab, b_oab = w.oab[r]
                    kf, b_kf = w.kf
                    P.dma("sp", sga[:], SGs[tsl, 0:512], reads=[b_SGs], writes=[b_sga])
                    P.dma("sp", sgb[:], SGs[tsl, 512:1024], reads=[b_SGs], writes=[b_sgb])
                    P.dma("sp", ypt[:], YPs[tsl], reads=[b_YPs], writes=[b_ypt])
                    for q_ in range(4):
                        qs = slice(q_ * 128, (q_ + 1) * 128)
                        P.mm(ps[0][:, qs], ub[:, qs], jmat_bf[:], True, True, reads=[b_ub, b_jmat], writes=[bps[0]], inc=(q_ == 3))
                    P.act(uT[:].rearrange("p a b -> p (a b)"), ps[0][:], AF.Copy, reads=[bps[0]], writes=[b_uT])
                    if ci + 1 < len(order):
                        loads(ci + 1)

                    def RB1():
                        for h in range(4):
                            hs = slice(h * 128, (h + 1) * 128)
                            P.mm(ps[2][:, hs], QT[:, hs], stbf[:, hs], True, True, reads=[b_QT, b_stbf], writes=[bps[2]], inc=(h == 3))
                        P.tt("dve", tC[:].rearrange("p (h e) -> p h e", h=4), ps[2][:].rearrange("p (h e) -> p h e", h=4),
                             gam[:, 1, :].unsqueeze(2).to_broadcast([128, 4, 128]), ALU.mult, reads=[bps[2], b_gam], writes=[b_tC])
                        P.tt("pool", opt[:], opt[:], tC[:], ALU.add, reads=[b_opt, b_tC], writes=[b_opt])
                        P.tt("pool", kf[:].rearrange("p (h e) -> p h e", h=4), kr[:].rearrange("p (h e) -> p h e", h=4),
                             gam[:, 3, :].unsqueeze(2).to_broadcast([128, 4, 128]), ALU.mult,
                             reads=[b_kr, b_gam], writes=[b_kf])
                        for h in range(4):
                            hs = slice(h * 128, (h + 1) * 128)
                            P.mm(ps[1][:, hs], kf[:, hs], vb[:, hs], True, True, reads=[b_kf, b_vb], writes=[bps[1]], inc=(h == 3))
                        P.tt("pool", stf[:].rearrange("p (h e) -> p h e", h=4), stf[:].rearrange("p (h e) -> p h e", h=4),
                             cdec[:].unsqueeze(2).to_broadcast([128, 4, 128]), ALU.mult, reads=[b_stf, b_cdec], writes=[b_stf])

                    def RB2():
                        P.tt("dve", stf[:], stf[:], ps[1][:], ALU.add, reads=[b_stf, bps[1]], writes=[b_stf])
                        P.act(stbf[:], stf[:], AF.Copy, reads=[b_stf], writes=[b_stbf])
                        o3 = opt[:].rearrange("p (h e) -> p h e", h=4)
                        P.T.op("dve", lambda q, o3=o3: q.reduce_sum(out=hn[:, 0:4], in_=o3, axis=mybir.AxisListType.X),
                               reads=[b_opt], writes=[b_hn])
                        P.tt("pool", tA[:], opt[:], opt[:], ALU.mult, reads=[b_opt], writes=[b_tA])
                        P.T.op("dve", lambda q, tA=tA: q.reduce_sum(out=hn[:, 4:8], in_=tA[:].rearrange("p (h e) -> p h e", h=4),
                                                                  axis=mybir.AxisListType.X), reads=[b_tA], writes=[b_hn])
                        P.ts("dve", hn[:, 0:8], hn[:, 0:8], 1.0 / 128.0, None, ALU.mult, None, reads=[b_hn], writes=[b_hn])
                        P.tt("dve", hn[:, 8:12], hn[:, 0:4], hn[:, 0:4], ALU.mult, reads=[b_hn], writes=[b_hn])
                        P.tt("dve", hn[:, 4:8], hn[:, 4:8], hn[:, 8:12], ALU.subtract, reads=[b_hn], writes=[b_hn])
                        P.act(hn[:, 8:12], hn[:, 4:8], AF.Sqrt, reads=[b_hn, b_eps], writes=[b_hn], bias=epsT[:, 0:1])

                    def RB3():
                        o3 = opt[:].rearrange("p (h e) -> p h e", h=4)
                        P.T.op("dve", lambda q: q.reciprocal(out=hn[:, 12:16], in_=hn[:, 8:12]), reads=[b_hn], writes=[b_hn])
                        a3 = tA[:].rearrange("p (h e) -> p h e", h=4)
                        P.tt("pool", a3, o3, hn[:, 0:4].unsqueeze(2).to_broadcast([128, 4, 128]), ALU.subtract,
                             reads=[b_opt, b_hn], writes=[b_tA])
                        P.tt("pool", a3, a3, hn[:, 12:16].unsqueeze(2).to_broadcast([128, 4, 128]), ALU.mult,
                             reads=[b_tA, b_hn], writes=[b_tA])
                        P.tt("pool", oab[:, 0:512], tA[:], sga[:], ALU.mult, reads=[b_tA, b_sga], writes=[b_oab])

                    hooks = [RB1, RB2, RB3] + tail
                    s5_chunk(w, ci, True, hooks=hooks)
                    P.tt("dve", ypt[:], ypt[:], ps[7][:], ALU.add, reads=[b_ypt, bps[7]], writes=[b_ypt])
                    tail = make_tail(ci, n)
                for t_ in tail:
                    t_()

            s5_setup(0)
            with contextlib.ExitStack() as wk:
                w = alloc_work(wk, False)
                for s in range(nseq):
                    passA(s, w)
                T_barrier()
            s5_setup(1)
            with contextlib.ExitStack() as wk:
                w = alloc_work(wk, True)
                for s in range(nseq):
                    passB(s, w)
                T_barrier()

    def T_barrier():
        evs = []
        for e in T.engs.values():
            for k in range(len(e.dsems)):
                evs.append((e.dsems[k], e.dvals[k]))
            if e.sem is not None and e.cnt > 0 and not e.pending:
                evs.append((e.sem, e.cnt))
        for e in T.engs.values():
            for ev in evs:
                T._wait(e, ev)

    def odd_layer(li, src, bsrc, dst, bdst):
        raise NotImplementedError

    P.odd_layer_hook = None
    cur, bcur = x_in, [b_xin] * nseq
    for idx, li in enumerate(layers):
        last = idx == len(layers) - 1
        dstt, bd = (y_out, b_yout) if last else (xs[idx % 2], b_xs[idx % 2])
        if li % 2 == 0:
            even_layer(li, cur, bcur, dstt, bd)
        else:
            ODD_IMPL(P, locals(), li, cur, bcur, dstt, bd)
        cur, bcur = dstt, bd
    T.finish()
    T.replay()
    P.es.close()
    return P


def ODD_IMPL(P, env, li, src, bsrc, dst, bdst):
    nc, T = P.nc, P.T
    E = env
    ps, bps = E["ps"], E["bps"]
    nseq, L = P.nseq, P.L
    ident_bf, b_ident = E["ident_bf"], E["b_ident"]
    j = li // 2
    rows = L // 64
    nblk = L // 256

    def rs(r):
        return min(max(r - 4, 0), rows - 8)

    with contextlib.ExitStack() as st:
        def S(name, shape, dt=F32):
            return st.enter_context(P.sbt(f"{name}_o{li}", list(shape), dt)), Buf(name)
        E["adaln"](li, None)
        C = E["make_common"](st, f"o{li}")
        winc, b_winc = S("winc", [128, 8, 4096], BF16)
        woutc, b_woutc = S("woutc", [128, 8, 1024], BF16)
        Z, b_Z = S("Z", [128, 16, 1024], BF16)
        hT, b_hT = S("hT", [128, 8, 256], BF16)
        KT = [S(f"KT{i}", [128, 8, 256], BF16) for i in range(3)]
        V = [S(f"V{i}", [128, 2, 8, 3, 64], BF16) for i in range(3)]
        QT = [S(f"QT{i}", [128, 8, 256], BF16) for i in range(2)]
        GT = [S(f"GT{i}", [128, 8, 256], BF16) for i in range(2)]
        pT = [S(f"pT{i}", [128, 256], BF16) for i in range(3)]
        og, b_og = S("og", [128, 8, 256], BF16)
        rd2 = [S(f"rd{i}", [128, 256]) for i in range(2)]
        rb2 = [S(f"rb{i}", [128, 256]) for i in range(2)]
        t22 = [S(f"t2{i}", [128, 256]) for i in range(2)]
        onesf, b_onesf = S("onesf", [128, 64])
        zer, b_zer = S("zer", [128, 256], BF16)
        wsrc = E["w_in_c"][j].rearrange("(k p) n -> p k n", p=128)
        for k in range(8):
            P.dma("pool", winc[:, k, :], wsrc[:, k, :], writes=[b_winc])
        P.dma("pool", woutc[:], E["w_out_c"][j].rearrange("(k p) n -> p k n", p=128), writes=[b_woutc])
        P.memset("dve", onesf[:], 1.0, writes=[b_onesf])
        P.memset("dve", zer[:], 0.0, writes=[b_zer])
        for i in range(3):
            P.memset("pool", V[i][0][:], 1.0, writes=[V[i][1]])
        with contextlib.ExitStack() as s2:
            zm = s2.enter_context(P.sbt(f"zm_o{li}", [128, 1024], F32)); b_zm = Buf("zm")
            zt0_ = s2.enter_context(P.sbt(f"zt0_o{li}", [128, 512], F32))
            zt = [zt0_, zt0_]
            b_zt0_ = Buf("zt0")
            b_zt = [b_zt0_, b_zt0_]
            P.dma("sp", zm[:], E["zmask"], writes=[b_zm])
            for h in range(16):
                for hf in range(2):
                    P.dma("sp", zt[hf][:], E["zg"][j, h][:, hf * 512:(hf + 1) * 512], writes=[b_zt[hf]])
                    P.tt("dve", Z[:, h, hf * 512:(hf + 1) * 512], zt[hf][:], zm[:, hf * 512:(hf + 1) * 512], ALU.add,
                         reads=[b_zt[hf], b_zm], writes=[b_Z])
            E["T_barrier"]()

        def proj(s, b):
            ring = b % 3
            sl = b % 2
            for t in range(2):
                E["prenorm"](C, s, src, bsrc[s], 2 * b + t, hT, b_hT, t * 128)
            cnt = 0
            for (col0, kind) in ((0, "q"), (1024, "k"), (3072, "g")):
                for hp2 in range(4):
                    bank = 2 + cnt % 2
                    cnt += 1
                    for hh in range(2):
                        hp = 2 * hp2 + hh
                        for k in range(8):
                            P.mm(ps[bank][:, hh * 256:(hh + 1) * 256], winc[:, k, col0 + hp * 128:col0 + (hp + 1) * 128], hT[:, k, :],
                                 k == 0, k == 7, reads=[b_winc, b_hT], writes=[bps[bank]], inc=(k == 7 and hh == 1))
                    if kind == "q":
                        P.act(QT[sl][0][:, 2 * hp2:2 * hp2 + 2, :].rearrange("p a b -> p (a b)"), ps[bank][:], AF.Copy,
                              reads=[bps[bank]], writes=[QT[sl][1]], scale=0.125)
                    elif kind == "k":
                        P.cp("dve", KT[ring][0][:, 2 * hp2:2 * hp2 + 2, :].rearrange("p a b -> p (a b)"), ps[bank][:],
                             reads=[bps[bank]], writes=[KT[ring][1]])
                    else:
                        P.act(GT[sl][0][:, 2 * hp2:2 * hp2 + 2, :].rearrange("p a b -> p (a b)"), ps[bank][:], AF.Silu,
                              reads=[bps[bank]], writes=[GT[sl][1]])
            for t in range(2):
                for half in range(2):
                    bank = 2 + cnt % 2
                    cnt += 1
                    for k in range(8):
                        P.mm(ps[bank][:], hT[:, k, t * 128:(t + 1) * 128], winc[:, k, 2048 + half * 512:2048 + (half + 1) * 512],
                             k == 0, k == 7, reads=[b_hT, b_winc], writes=[bps[bank]], inc=(k == 7))
                    src4 = ps[bank][:].rearrange("p (a c d) -> p a c d", a=4, c=2)
                    P.cp("dve", V[ring][0][:, t, 4 * half:4 * half + 4, 0:3:2, :], src4, reads=[bps[bank]], writes=[V[ring][1]])

        def attn(s, b):
            R = 4 * b
            sl = b % 2
            lo = rs(R) & ~1
            hi = (rs(R + 3) + 7) & ~1
            tiles = []
            for r0 in range(lo, hi + 1, 2):
                qs = [r for r in range(R, R + 4) if (rs(r) <= r0 + 1 and r0 <= rs(r) + 7)]
                if not qs:
                    continue
                qa, qb = qs[0], qs[-1]
                partial = []
                for r in qs:
                    for a in range(2):
                        if not (rs(r) <= r0 + a <= rs(r) + 7):
                            partial.append((r, a))
                tiles.append((r0, qa, qb, partial))
            pcount = [0]
            W = [(h, ti) for h in range(16) for ti in range(len(tiles))]
            info = {}
            deferred = []

            def emit_scores(h, ti):
                hp, base = h // 2, 64 * (h % 2)
                r0, qa, qb, partial = tiles[ti]
                kb = r0 // 4
                tt_ = (r0 % 4) // 2
                kring = kb % 3
                c0, c1 = (qa - R) * 64, (qb - R + 1) * 64
                z0, z1 = (qa - r0 + 7) * 64, (qb - r0 + 8) * 64
                bank = 4 + pcount[0] % 2
                pt, b_pt = pT[pcount[0] % 3]
                pcount[0] += 1
                P.mm(ps[bank][:, c0:c1], KT[kring][0][base:base + 64, hp, tt_ * 128:(tt_ + 1) * 128],
                     QT[sl][0][base:base + 64, hp, c0:c1], True, False,
                     reads=[KT[kring][1], QT[sl][1]], writes=[bps[bank]], inc=False)
                P.mm(ps[bank][:, c0:c1], ident_bf[:], Z[:, h, z0:z1], False, True,
                     reads=[b_ident, b_Z], writes=[bps[bank]], inc=True)
                P.act(pt[:, c0:c1], ps[bank][:, c0:c1], AF.Exp, reads=[bps[bank]], writes=[b_pt])
                for (r, a_) in partial:
                    cc = (r - R) * 64
                    P.memset("pool", pt[64 * a_:64 * a_ + 64, cc:cc + 64], 0.0, writes=[b_pt])
                info[(h, ti)] = (pt, b_pt, c0, c1, kring, tt_)

            def emit_pv(h, ti, idx):
                hp = h // 2
                pt, b_pt, c0, c1, kring, tt_ = info.pop((h, ti))
                ob = 6 + h % 2
                if ti == 0:
                    P.mm(ps[ob][:, 0:256], zer[:, 0:128], zer[:], True, False, reads=[b_zer], writes=[bps[ob]], inc=False)
                va = V[kring][0][:, tt_, hp, 0:2, :] if h % 2 == 0 else V[kring][0][:, tt_, hp, 1:3, :]
                last = ti == len(tiles) - 1
                P.mm(ps[ob][:, c0:c1], va.rearrange("p a b -> p (a b)"), pt[:, c0:c1], False, last,
                     reads=[V[kring][1], b_pt], writes=[bps[ob]], inc=last)
                if last:
                    par = h % 2
                    dr, orow = (64, 0) if par == 0 else (0, 64)
                    rdp, b_rdp = rd2[par]
                    rbp, b_rbp = rb2[par]
                    t2p, b_t2p = t22[par]
                    P.T.op("dve", lambda q, dr=dr, ob=ob, rdp=rdp: q.reciprocal(out=rdp[dr:dr + 1, :], in_=ps[ob][dr:dr + 1, 0:256]),
                           reads=[bps[ob]], writes=[b_rdp])

                    def part2(h=h, hp=hp, par=par, dr=dr, orow=orow, ob=ob, rdp=rdp, b_rdp=b_rdp, rbp=rbp, b_rbp=b_rbp, t2p=t2p, b_t2p=b_t2p):
                        P.mm(ps[par][orow:orow + 64, 0:256], onesf[dr:dr + 1, 0:64], rdp[dr:dr + 1, :], True, True,
                             reads=[b_onesf, b_rdp], writes=[bps[par]])
                        P.act(rbp[orow:orow + 64, :], ps[par][orow:orow + 64, 0:256], AF.Copy, reads=[bps[par]], writes=[b_rbp])
                        P.tt("dve", t2p[orow:orow + 64, :], ps[ob][orow:orow + 64, 0:256], rbp[orow:orow + 64, :], ALU.mult,
                             reads=[bps[ob], b_rbp], writes=[b_t2p])
                        P.tt("pool", og[orow:orow + 64, hp, :], t2p[orow:orow + 64, :], GT[sl][0][orow:orow + 64, hp, :], ALU.mult,
                             reads=[b_t2p, GT[sl][1]], writes=[b_og])
                    deferred.append((idx + 2, part2))

            LA = 1
            for idx in range(len(W) + LA):
                if idx < len(W):
                    emit_scores(*W[idx])
                if idx >= LA:
                    emit_pv(W[idx - LA][0], W[idx - LA][1], idx)
                while deferred and deferred[0][0] <= idx:
                    deferred.pop(0)[1]()
            while deferred:
                deferred.pop(0)[1]()
            for t in range(2):
                n = 2 * b + t
                sx = C["cnt"] % 2
                C["cnt"] += 1
                xt, bxt = C["xt"][sx], C["bxt"][sx]
                P.dma("sp", xt[:], src[s, n * 128:(n + 1) * 128, :], reads=[bsrc[s]], writes=[bxt])
                for hh in range(2):
                    for k in range(8):
                        P.mm(ps[2 + hh][:], og[:, k, t * 128:(t + 1) * 128], woutc[:, k, hh * 512:(hh + 1) * 512], k == 0, k == 7,
                             reads=[b_og, b_woutc], writes=[bps[2 + hh]], inc=(k == 7))
                E["post"](C, s, (2, 3), xt, bxt, dst, bdst[s], n)

        for s in range(nseq):
            proj(s, 0)
            for b in range(nblk):
                if b + 1 < nblk:
                    proj(s, b + 1)
                attn(s, b)
        E["T_barrier"]()


def _common_inputs(p, L):
    f32 = np.float32
    m = {}
    def T8(a):
        return np.ascontiguousarray(a.reshape(a.shape[0], -1, 128).transpose(0, 2, 1)).astype(f32)
    m["npreT"] = T8(p["norm_pre"])
    m["npostT"] = T8(p["norm_post"])
    m["wmod"] = np.ascontiguousarray(p["w_mod"], dtype=f32)
    m["bmodT"] = T8(p["b_mod"])
    m["w_in_ab"] = np.ascontiguousarray(p["w_in_ab"], dtype=f32)
    m["w_out_ab"] = np.ascontiguousarray(p["w_out_ab"], dtype=f32)
    m["w_glu"] = np.ascontiguousarray(p["ssm_w_glu"], dtype=f32)
    m["ssm_d"] = np.ascontiguousarray(p["ssm_d"], dtype=f32)
    sm3, rep3, bl, cl = [], [], [], []
    for j in range(2):
        a, b, c, d = _s5_layout(p["ssm_a_re"][j], p["ssm_a_im"][j], p["ssm_log_step"][j], p["ssm_b_re"][j],
                                p["ssm_b_im"][j], p["ssm_c_re"][j], p["ssm_c_im"][j])
        sm3.append(a); rep3.append(b); bl.append(c); cl.append(d)
    m["s5_sm3"] = np.stack(sm3).astype(f32)
    m["s5_rep3"] = np.stack(rep3).astype(f32)
    m["s5_bl"] = np.stack(bl).astype(f32)
    m["s5_cl"] = np.stack(cl).astype(f32)
    m["w_in_c"] = np.ascontiguousarray(p["w_in_c"], dtype=f32)
    m["w_out_c"] = np.ascontiguousarray(p["w_out_c"], dtype=f32)
    zs = []
    for j in range(2):
        z, mask = _na_layout(np.asarray(p["na_rel_bias"][j], dtype=f32))
        zs.append(z)
    m["zg"] = np.stack(zs).astype(f32)
    m["zmask"] = mask
    m["ident"] = np.eye(128, dtype=f32)
    m["jmat"] = np.eye(128, dtype=f32)[::-1].copy()
    m["rot"] = _rot_tables(L)
    dt, gam, cd = _ret_consts()
    m["dtab"] = dt
    m["gam"] = np.ascontiguousarray(gam.transpose(0, 1, 2))
    m["cdec"] = cd
    m["jt"] = np.broadcast_to(np.arange(128, dtype=f32)[None, :], (128, 128)).copy()
    return m


_PROG_CACHE = {}


def run_cores(xs_per_core, cs_per_core, params, layers):
    nseq, L, _ = xs_per_core[0].shape
    key = (nseq, L, tuple(layers))
    if key not in _PROG_CACHE:
        _PROG_CACHE[key] = build(nseq, L, list(layers))
    P = _PROG_CACHE[key]
    com = _common_inputs(params, L)
    in_maps = []
    for x, c in zip(xs_per_core, cs_per_core):
        m = dict(com)
        m["x_in"] = np.ascontiguousarray(x, dtype=np.float32)
        m["cT"] = np.ascontiguousarray(c.reshape(nseq, 8, 128).transpose(2, 1, 0), dtype=np.float32)
        in_maps.append(m)
    res = run_bass_kernel_spmd(P.nc, in_maps, core_ids=list(range(len(in_maps))))
    return [np.asarray(r["y_out"]) for r in res.results]


def kernel(**inputs):
    p = {k: np.asarray(v) for k, v in inputs.items()}
    xp, xsamp = p["x_prompt"], p["x_sample"]
    cp, cs = p["c_prompt"], p["c_sample"]
    seqs = [xp[i] for i in range(4)] + [xsamp[i] for i in range(8)]
    cvs = [cp[i] for i in range(4)] + [cs[i] for i in range(8)]
    slots = [(c, 8 + c if c < 4 else c) for c in range(8)]
    xs_pc = [np.stack([seqs[a], seqs[b]]) for a, b in slots]
    cs_pc = [np.stack([cvs[a], cvs[b]]) for a, b in slots]
    outs = run_cores(xs_pc, cs_pc, p, [0, 1, 2, 3])
    res = [None] * 12
    for c, (a, b) in enumerate(slots):
        res[a] = outs[c][0]
        if c < 4:
            res[b] = outs[c][1]
    y_prompt = np.stack(res[0:4]).astype(np.float32)
    y_sample = np.stack(res[4:12]).astype(np.float32)
    return (y_prompt, y_sample)
```

```python
import contextlib
import math
import os
KSTOP = os.environ.get('KSTOP', 'all')
import numpy as np
import concourse.bass as bass
import concourse.mybir as mybir
from concourse.bass_utils import run_bass_kernel_spmd

F32 = mybir.dt.float32
BF16 = mybir.dt.bfloat16
I32 = mybir.dt.int32
ALU = mybir.AluOpType
AF = mybir.ActivationFunctionType

D = 1024
EPS = 1e-6
TWO_PI = 2.0 * math.pi


class Buf:
    __slots__ = ("name", "w", "r")

    def __init__(self, name):
        self.name = name
        self.w = []
        self.r = []


class Eng:
    def __init__(self, name):
        self.name = name
        self.ops = []
        self.known = {}
        self.sem = None
        self.cnt = 0
        self.pending = False
        self.dsems = []
        self.dvals = []
        self.dptr = 0
        self.own = set()


EPOCH = 30000
NDSEM = 10


class Tracker:
    def __init__(self, nc):
        self.nc = nc
        self.engs = {n: Eng(n) for n in ("pe", "act", "dve", "pool", "sp")}
        self.sems = []
        for e in self.engs.values():
            if e.name != "sp":
                e.sem = self._newsem(e.name)
                e.own.add(e.sem)
        self.n_ops = 0

    def _newsem(self, nm):
        h = self.nc.alloc_semaphore(name=f"{nm}_{len(self.sems)}")
        self.sems.append(h)
        return len(self.sems) - 1

    def _wait(self, e, ev):
        s, v = ev
        if e.known.get(s, 0) >= v:
            return
        if s in e.own:
            if e.name == "pe":
                return
            if s == e.sem and v > e.cnt:
                return
        e.known[s] = v
        sem = self.sems[s]
        e.ops.append(lambda q, sem=sem, v=v: q.wait_ge(sem, v))

    def _deps(self, e, reads, writes):
        for b in reads:
            for ev in b.w:
                self._wait(e, ev)
        for b in writes:
            for ev in b.w:
                self._wait(e, ev)
            for ev in b.r:
                self._wait(e, ev)

    def _commit(self, ev, reads, writes):
        for b in reads:
            for i, (s0, v0) in enumerate(b.r):
                if s0 == ev[0]:
                    b.r[i] = (s0, max(v0, ev[1]))
                    break
            else:
                b.r.append(ev)
        for b in writes:
            b.w = [ev]
            b.r = []

    def op(self, eng, fn, reads=(), writes=(), inc=True):
        e = self.engs[eng]
        self.n_ops += 1
        self._deps(e, reads, writes)
        if e.cnt >= EPOCH and inc and not e.pending:
            e.sem = self._newsem(e.name)
            e.cnt = 0
            e.own.add(e.sem)
        if inc:
            e.cnt += 1
            sem = self.sems[e.sem]
            e.ops.append(lambda q, fn=fn, sem=sem: fn(q).then_inc(sem, 1))
            ev = (e.sem, e.cnt)
            e.pending = False
        else:
            e.ops.append(lambda q, fn=fn: fn(q))
            ev = (e.sem, e.cnt + 1)
            e.pending = True
        self._commit(ev, reads, writes)

    def dma(self, eng, out, in_, reads=(), writes=()):
        e = self.engs[eng]
        self.n_ops += 1
        self._deps(e, reads, writes)
        if len(e.dsems) < NDSEM:
            e.dsems.append(self._newsem(e.name + "d"))
            e.dvals.append(0)
            k = len(e.dsems) - 1
        else:
            k = e.dptr
            e.dptr = (e.dptr + 1) % NDSEM
            self._wait(e, (e.dsems[k], e.dvals[k]))
            if e.dvals[k] >= EPOCH * 16:
                e.dsems[k] = self._newsem(e.name + "d")
                e.dvals[k] = 0
        e.dvals[k] += 16
        sem = self.sems[e.dsems[k]]
        e.ops.append(lambda q, out=out, in_=in_, sem=sem: q.dma_start(out=out, in_=in_).then_inc(sem, 16))
        ev = (e.dsems[k], e.dvals[k])
        self._commit(ev, reads, writes)
        return ev

    def finish(self):
        sp = self.engs["sp"]
        for e in self.engs.values():
            for k in range(len(e.dsems)):
                self._wait(sp, (e.dsems[k], e.dvals[k]))
            if e.sem is not None and e.cnt > 0:
                self._wait(sp, (e.sem, e.cnt))

    def replay(self):
        nc = self.nc
        E = self.engs
        with nc.Block() as block:
            @block.tensor
            def _(q):
                for f in E["pe"].ops:
                    f(q)

            @block.scalar
            def _(q):
                for f in E["act"].ops:
                    f(q)

            @block.vector
            def _(q):
                for f in E["dve"].ops:
                    f(q)

            @block.gpsimd
            def _(q):
                for f in E["pool"].ops:
                    f(q)

            @block.sync
            def _(q):
                for f in E["sp"].ops:
                    f(q)


RET_H = 4
NA_H = 16
GW = 64


def _ret_consts():
    f32 = np.float32
    h = np.arange(RET_H, dtype=f32)
    log_g = np.log1p(-np.exp2(-5.0 - h)).astype(f32)
    pos = np.arange(128, dtype=f32)
    dt = np.exp(np.abs(pos[:, None] - pos[None, :])[:, None, :] * log_g[None, :, None]).astype(f32)
    gam = np.zeros((128, 4, RET_H), f32)
    gam[:, 0, :] = np.exp(pos[:, None] * log_g[None])
    gam[:, 1, :] = np.exp((127.0 - pos)[:, None] * log_g[None])
    gam[:, 2, :] = np.exp((128.0 - pos)[:, None] * log_g[None])
    gam[:, 3, :] = np.exp((pos + 1.0)[:, None] * log_g[None])
    cdec = np.exp(128.0 * log_g).astype(f32)
    cd = np.broadcast_to(cdec[None, :], (128, RET_H)).copy()
    return dt, gam, cd


def _rot_tables(L):
    f32 = np.float32
    inv = (10000.0 ** (-np.arange(0, 128, 2, dtype=f32) / 128.0)).astype(f32)
    ang = (np.arange(L, dtype=f32)[:, None] * inv[None, :]).astype(f32)
    c = np.cos(ang).astype(f32)
    s = np.sin(ang).astype(f32)
    rq = np.zeros((L, 2, 128), f32)
    rq[:, 0, :64] = c
    rq[:, 0, 64:] = c
    rq[:, 1, :64] = -s
    rq[:, 1, 64:] = s
    rk = (rq * f32(128.0 ** -0.5)).astype(f32)
    return np.stack([rq, rk], axis=1).copy()


def _na_layout(relb):
    a = np.arange(2)[:, None, None, None]
    k = np.arange(64)[None, :, None, None]
    m = np.arange(-7, 9)[None, None, :, None]
    c = np.arange(64)[None, None, None, :]
    dr = np.clip(a - m + 7, 0, 14)
    dc = np.clip(k - c + 15, 0, 30)
    dr_b, dc_b = np.broadcast_arrays(dr, dc)
    z = relb[:, dr_b, dc_b]
    z = z.reshape(16, 128, 16 * 64).astype(np.float32)
    cs = np.clip(np.arange(64) - 8, 0, 48)
    kk = np.arange(64)[:, None]
    valid = (kk >= cs[None, :]) & (kk < cs[None, :] + 16)
    mask = np.where(valid, 0.0, -30000.0).astype(np.float32)
    mask = np.broadcast_to(mask[None, :, None, :], (2, 64, 16, 64)).reshape(128, 1024).copy()
    return z, mask


def _s5_layout(a_re, a_im, ls, b_re, b_im, c_re, c_im):
    f32 = np.float32
    def sm(a):
        return a.reshape(2, 16, 2, 64).transpose(0, 2, 3, 1).reshape(2, 128, 16).astype(f32)
    lsx = np.broadcast_to(ls[:, :, None], (2, 32, 64))
    sm3 = np.stack([sm(a_re), sm(a_im), sm(lsx)], axis=1).copy()
    rep3 = np.stack([a_re.reshape(2, 2048), a_im.reshape(2, 2048), lsx.reshape(2, 2048)], axis=1).astype(f32).copy()
    bl = np.zeros((2, 2, 128, 16, 2, 64), f32)
    cl = np.zeros((2, 2, 128, 16, 2, 16), f32)
    for s in range(16):
        for g1 in range(2):
            g = 2 * s + g1
            r0 = 32 * (s % 4) + 16 * g1
            for ri, b in enumerate((b_re, b_im)):
                bl[:, ri, r0:r0 + 16, s, g1, :] = b[:, g].transpose(0, 2, 1)
            for ri, c in enumerate((c_re, c_im)):
                cl[:, ri, 64 * g1:64 * g1 + 64, s, g1, :] = c[:, g].transpose(0, 2, 1)
    return sm3, rep3, bl.reshape(2, 2, 128, 16 * 128), cl.reshape(2, 2, 128, 16 * 32)


class Prog:
    def __init__(self, nseq, L, layers):
        self.nseq, self.L, self.layers = nseq, L, layers
        self.nch = L // 128
        nc = self.nc = bass.Bass("TRN2", target_bir_lowering=False)
        self.T = Tracker(nc)
        self.es = contextlib.ExitStack()
        self.dram = {}
        self.bufs = {}

    def sbt(self, name, shape, dt=F32):
        self._uid = getattr(self, '_uid', 0) + 1
        return self.nc.sbuf_tensor(f"{name}_u{self._uid}", list(shape), dt)

    def din(self, name, shape, dt=F32):
        t = self.nc.dram_tensor(name, list(shape), dt, kind="ExternalInput").ap()
        self.dram[name] = t
        return t

    def dout(self, name, shape, dt=F32):
        t = self.nc.dram_tensor(name, list(shape), dt, kind="ExternalOutput").ap()
        self.dram[name] = t
        return t

    def dscr(self, name, shape, dt=F32):
        t = self.nc.dram_tensor(name, list(shape), dt, kind="Internal").ap()
        self.dram[name] = t
        return t

    def sb(self, name, shape, dt=F32):
        t = self.es.enter_context(self.sbt(name, list(shape), dt))
        b = Buf(name)
        return t, b

    def ps(self, name):
        t = self.es.enter_context(self.nc.psum_tensor(name, [128, 512], F32))
        return t, Buf(name)

    def mm(self, out, lhsT, rhs, start, stop, reads, writes, inc=True):
        self.T.op("pe", lambda q: q.matmul(out, lhsT=lhsT, rhs=rhs, start=start, stop=stop),
                  reads=reads, writes=writes, inc=inc)

    def act(self, out, in_, func, reads, writes, scale=1.0, bias=None, accum=None):
        kw = {}
        if bias is not None:
            kw["bias"] = bias
        if accum is not None:
            kw["accum_out"] = accum
        self.T.op("act", lambda q: q.activation(out=out, in_=in_, func=func, scale=scale, **kw),
                  reads=reads, writes=writes)

    def tt(self, eng, out, in0, in1, op, reads, writes):
        self.T.op(eng, lambda q: q.tensor_tensor(out=out, in0=in0, in1=in1, op=op), reads=reads, writes=writes)

    def ts(self, eng, out, in0, s1, s2, op0, op1, reads, writes):
        if s2 is None:
            self.T.op(eng, lambda q: q.tensor_scalar(out=out, in0=in0, scalar1=s1, scalar2=None, op0=op0),
                      reads=reads, writes=writes)
        else:
            self.T.op(eng, lambda q: q.tensor_scalar(out=out, in0=in0, scalar1=s1, scalar2=s2, op0=op0, op1=op1),
                      reads=reads, writes=writes)

    def stt(self, eng, out, in0, scalar, in1, op0, op1, reads, writes):
        self.T.op(eng, lambda q: q.scalar_tensor_tensor(out=out, in0=in0, scalar=scalar, in1=in1, op0=op0, op1=op1),
                  reads=reads, writes=writes)

    def cp(self, eng, out, in_, reads, writes):
        self.T.op(eng, lambda q: q.tensor_copy(out=out, in_=in_), reads=reads, writes=writes)

    def memset(self, eng, ap, val, writes):
        self.T.op(eng, lambda q: q.memset(ap, val), reads=(), writes=writes)

    def dma(self, eng, out, in_, reads=(), writes=()):
        return self.T.dma(eng, out, in_, reads=reads, writes=writes)


def build(nseq, L, layers):
    P = Prog(nseq, L, layers)
    nc, T = P.nc, P.T
    nch = L // 128
    NL = 4
    x_in = P.din("x_in", [nseq, L, D])
    y_out = P.dout("y_out", [nseq, L, D])
    cT = P.din("cT", [128, 8, nseq])
    npreT = P.din("npreT", [NL, 128, 8])
    npostT = P.din("npostT", [NL, 128, 8])
    wmod = P.din("wmod", [NL, D, 3 * D])
    bmodT = P.din("bmodT", [NL, 128, 24])
    w_in_ab = P.din("w_in_ab", [2, D, 3072])
    w_out_ab = P.din("w_out_ab", [2, D, D])
    w_glu = P.din("w_glu", [2, 512, 512])
    ssm_d = P.din("ssm_d", [2, 512])
    s5_sm3 = P.din("s5_sm3", [2, 2, 3, 128, 16])
    s5_rep3 = P.din("s5_rep3", [2, 2, 3, 2048])
    s5_bl = P.din("s5_bl", [2, 2, 2, 128, 2048])
    s5_cl = P.din("s5_cl", [2, 2, 2, 128, 512])
    w_in_c = P.din("w_in_c", [2, D, 4096])
    w_out_c = P.din("w_out_c", [2, D, D])
    zg = P.din("zg", [2, 16, 128, 1024])
    zmask = P.din("zmask", [128, 1024])
    ident_d = P.din("ident", [128, 128])
    jmat_d = P.din("jmat", [128, 128])
    rot_d = P.din("rot", [L, 2, 2, 128])
    dtab_d = P.din("dtab", [128, 4, 128])
    gam_d = P.din("gam", [128, 4, 4])
    cdec_d = P.din("cdec", [128, 4])
    jt_d = P.din("jt", [128, 128])
    sel_d = P.din("sel", [128, 64])
    xs = [P.dscr("xsA", [nseq, L, D]), P.dscr("xsB", [nseq, L, D])]
    OPs_ = P.dscr("OPs", [nseq, L, 512])
    YPs_ = P.dscr("YPs", [nseq, L, 512])
    SGs_ = P.dscr("SGs", [nseq, L, 1024])
    QTs_ = P.dscr("QTs", [nseq, nch, 128, 512], BF16)
    KRs_ = P.dscr("KRs", [nseq, L, 512], BF16)
    VBs_ = P.dscr("VBs", [nseq, L, 512], BF16)
    UBs_ = P.dscr("UBs", [nseq, L, 512], BF16)
    SCRB = [[Buf(f"{n}{i}") for n in "OP YP SG QT KR VB UB".split()] for i in range(nseq)]
    b_xs = [[Buf(f"xs{i}_{s}") for s in range(nseq)] for i in range(2)]
    b_yout = [Buf(f"yout{s}") for s in range(nseq)]
    b_xin = Buf("xin")

    ident_bf, b_ident = P.sb("ident_bf", [128, 128], BF16)
    jmat_bf, b_jmat = P.sb("jmat_bf", [128, 128], BF16)
    identf, b_identf = P.sb("identf", [128, 128])
    epsT, b_eps = P.sb("epsT", [128, 1])
    ss, b_ss = P.sb("ss", [128, 4])
    sd, b_sd = P.sb("sd", [128, 4])
    rstd, b_rstd = P.sb("rstd", [128, 4])
    scT, b_scT = P.sb("scT", [128, 8, nseq])
    cTs, b_cTs = P.sb("cTs", [128, 8, nseq])
    modT, b_modT = P.sb("modT", [128, 24, nseq])
    gsT, b_gsT = P.sb("gsT", [128, 8, nseq])
    ggT, b_ggT = P.sb("ggT", [128, 8, nseq])
    ggrow = [P.sb(f"ggrow{s}", [128, 1024]) for s in range(nseq)]
    vecs, b_vecs = P.sb("vecs", [128, 40])
    psb = [P.ps(f"ps{i}") for i in range(8)]
    ps = [p[0] for p in psb]
    bps = [p[1] for p in psb]

    P.dma("sp", identf[:], ident_d, writes=[b_identf])
    P.dma("pool", ident_bf[:], ident_d, writes=[b_ident])
    P.dma("pool", jmat_bf[:], jmat_d, writes=[b_jmat])
    P.memset("pool", epsT[:], EPS, writes=[b_eps])
    P.dma("sp", cTs[:], cT, writes=[b_cTs])
    P.act(scT[:], cTs[:], AF.Sigmoid, reads=[b_cTs], writes=[b_scT])
    P.tt("dve", scT[:], scT[:], cTs[:], ALU.mult, reads=[b_scT, b_cTs], writes=[b_scT])

    def rstd_from_ss(col, n_feat):
        P.act(sd[:, col:col + 1], ss[:, col:col + 1], AF.Sqrt, reads=[b_ss, b_eps], writes=[b_sd],
              scale=1.0 / n_feat, bias=epsT[:, 0:1])
        P.T.op("dve", lambda q: q.reciprocal(out=rstd[:, col:col + 1], in_=sd[:, col:col + 1]),
               reads=[b_sd], writes=[b_rstd])

    def adaln(li, st_unused):
      with contextlib.ExitStack() as st:
        wm = [st.enter_context(P.sbt(f"wm{j}_{li}", [128, 8, 128], F32)) for j in range(2)]
        bwm = [Buf("wm0"), Buf("wm1")]
        gbl = st.enter_context(P.sbt(f"gbl_{li}", [128, 8, 128], F32))
        b_gbl = Buf("gbl")
        P.dma("sp", vecs[:, 0:8], npreT[li], writes=[b_vecs])
        P.dma("sp", vecs[:, 8:16], npostT[li], writes=[b_vecs])
        P.dma("sp", vecs[:, 16:40], bmodT[li], writes=[b_vecs])
        wsrc = wmod[li].rearrange("(k p) n -> p k n", p=128)
        for j in range(24):
            sl = j % 2
            P.dma("sp", wm[sl][:], wsrc[:, :, j * 128:(j + 1) * 128], writes=[bwm[sl]])
            for k in range(8):
                P.mm(ps[7][:, j * nseq:(j + 1) * nseq], wm[sl][:, k, :], scT[:, k, :], k == 0, k == 7,
                     reads=[bwm[sl], b_scT], writes=[bps[7]], inc=(k == 7))
        psv = ps[7][:, 0:24 * nseq].rearrange("p (j s) -> p j s", s=nseq)
        P.tt("dve", modT[:], psv, vecs[:, 16:40].unsqueeze(2).to_broadcast([128, 24, nseq]), ALU.add,
             reads=[bps[7], b_vecs], writes=[b_modT])
        P.ts("dve", gsT[:], modT[:, 8:16, :], 1.0, None, ALU.add, None, reads=[b_modT], writes=[b_gsT])
        P.tt("dve", gsT[:], gsT[:], vecs[:, 0:8].unsqueeze(2).to_broadcast([128, 8, nseq]), ALU.mult,
             reads=[b_gsT, b_vecs], writes=[b_gsT])
        P.tt("dve", ggT[:], modT[:, 16:24, :], vecs[:, 8:16].unsqueeze(2).to_broadcast([128, 8, nseq]), ALU.mult,
             reads=[b_modT, b_vecs], writes=[b_ggT])
        for s in range(nseq):
            P.cp("dve", gbl[:], ggT[:, :, s:s + 1].to_broadcast([128, 8, 128]), reads=[b_ggT], writes=[b_gbl])
            for c in range(8):
                bk = 5 + c // 4
                P.mm(ps[bk][:, (c % 4) * 128:(c % 4 + 1) * 128], gbl[:, c, :], identf[:], True, True,
                     reads=[b_gbl, b_identf], writes=[bps[bk]], inc=(c % 4 == 3))
            P.cp("dve", ggrow[s][0][:, 0:512], ps[5][:], reads=[bps[5]], writes=[ggrow[s][1]])
            P.act(ggrow[s][0][:, 512:1024], ps[6][:], AF.Copy, reads=[bps[6]], writes=[ggrow[s][1]])

    def make_common(st, tag):
        C = {}
        C["xt"] = [st.enter_context(P.sbt(f"xt{j}_{tag}", [128, 1024], F32)) for j in range(2)]
        C["bxt"] = [Buf("xt0"), Buf("xt1")]
        C["xn"] = st.enter_context(P.sbt(f"xn_{tag}", [128, 1024], BF16))
        C["bxn"] = Buf("xn")
        C["junk"] = st.enter_context(P.sbt(f"junk_{tag}", [128, 1024], BF16))
        C["bjunk"] = Buf("junk")
        C["yt"] = st.enter_context(P.sbt(f"yt_{tag}", [128, 1024], F32))
        C["byt"] = Buf("yt")
        C["cnt"] = 0
        return C

    def prenorm_parts(C, s, src, bsrc, n, hT, bhT, col0):
        sl = C["cnt"] % 2
        C["cnt"] += 1
        xt, bxt = C["xt"][sl], C["bxt"][sl]

        def p1():
            P.dma("sp", xt[:], src[s, n * 128:(n + 1) * 128, :], reads=[bsrc], writes=[bxt])
            P.act(C["yt"][:], xt[:], AF.Square, reads=[bxt], writes=[C["byt"]])
            P.T.op("dve", lambda q, yt_=C["yt"]: q.reduce_sum(out=ss[:, 0:1], in_=yt_[:], axis=mybir.AxisListType.X),
                   reads=[C["byt"]], writes=[b_ss])
            P.act(sd[:, 0:1], ss[:, 0:1], AF.Sqrt, reads=[b_ss, b_eps], writes=[b_sd], scale=1.0 / D, bias=epsT[:, 0:1])

        def p2():
            P.T.op("dve", lambda q: q.reciprocal(out=rstd[:, 0:1], in_=sd[:, 0:1]), reads=[b_sd], writes=[b_rstd])
            P.act(C["xn"][:], xt[:], AF.Copy, reads=[bxt, b_rstd], writes=[C["bxn"]], scale=rstd[:, 0:1])
            for b in range(2):
                for j in range(4):
                    k = 4 * b + j
                    P.mm(ps[b][:, j * 128:(j + 1) * 128], C["xn"][:, k * 128:(k + 1) * 128], ident_bf[:], True, True,
                         reads=[C["bxn"], b_ident], writes=[bps[b]], inc=(j == 3))

        def p3():
            for b in range(2):
                for j in range(4):
                    k = 4 * b + j
                    o = hT[:, k, col0:col0 + 128]
                    i_ = ps[b][:, j * 128:(j + 1) * 128]
                    if b == 0:
                        P.act(o, i_, AF.Identity, reads=[bps[b], b_gsT, b_modT], writes=[bhT],
                              scale=gsT[:, k, s:s + 1], bias=modT[:, k, s:s + 1])
                    else:
                        P.ts("dve", o, i_, gsT[:, k, s:s + 1], modT[:, k, s:s + 1], ALU.mult, ALU.add,
                             reads=[bps[b], b_gsT, b_modT], writes=[bhT])
        return p1, p2, p3

    def prenorm(C, s, src, bsrc, n, hT, bhT, col0):
        p1, p2, p3 = prenorm_parts(C, s, src, bsrc, n, hT, bhT, col0)
        p1()
        p2()
        p3()

    def post(C, s, ypb, xt, bxt, dst, bdst, n):
        yt, byt = C["yt"], C["byt"]
        for h in range(2):
            P.act(yt[:, h * 512:(h + 1) * 512], ps[ypb[h]][:], AF.Square, reads=[bps[ypb[h]]], writes=[byt])
        P.T.op("dve", lambda q, yt_=yt: q.reduce_sum(out=ss[:, 3:4], in_=yt_[:], axis=mybir.AxisListType.X),
               reads=[byt], writes=[b_ss])
        rstd_from_ss(3, D)
        for h in range(2):
            P.act(yt[:, h * 512:(h + 1) * 512], ps[ypb[h]][:], AF.Copy, reads=[bps[ypb[h]], b_rstd], writes=[byt],
                  scale=rstd[:, 3:4])
        P.tt("dve", yt[:], yt[:], ggrow[s][0][:], ALU.mult, reads=[byt, ggrow[s][1]], writes=[byt])
        P.tt("pool", yt[:], yt[:], xt[:], ALU.add, reads=[byt, bxt], writes=[byt])
        P.dma("pool", dst[s, n * 128:(n + 1) * 128, :], yt[:], reads=[byt], writes=[bdst])

    def sincos(st, tag, phi, bphi, shape, out_sin, out_cos, bout):
        tf = st.enter_context(P.sbt(f"sc_tf_{tag}", shape, F32))
        ti = st.enter_context(P.sbt(f"sc_ti_{tag}", shape, I32))
        btf, bti = Buf("tf"), Buf("ti")
        for shift, o in ((0.0, out_sin), (0.5 * math.pi, out_cos)):
            P.ts("dve", tf[:], phi, shift, 1.0 / TWO_PI, ALU.add, ALU.mult, reads=[bphi], writes=[btf])
            P.cp("dve", ti[:], tf[:], reads=[btf], writes=[bti])
            P.cp("dve", tf[:], ti[:], reads=[bti], writes=[btf])
            P.stt("dve", tf[:], tf[:], -TWO_PI, phi, ALU.mult, ALU.add, reads=[btf, bphi], writes=[btf])
            P.ts("dve", tf[:], tf[:], shift, 0.999999, ALU.add, ALU.mult, reads=[btf], writes=[btf])
            P.act(o, tf[:], AF.Sin, reads=[btf], writes=[bout])

    def even_layer(li, src, bsrc, dst, bdst):
        j = li // 2
        with contextlib.ExitStack() as st:
            def S(name, shape, dt=F32):
                return st.enter_context(P.sbt(f"{name}_e{li}", list(shape), dt)), Buf(name)
            adaln(li, st)
            C = make_common(st, f"e{li}")
            win, b_win = S("win", [128, 8, 3072], BF16)
            wout, b_wout = S("wout", [128, 8, 1024], BF16)
            wglu, b_wglu = S("wglu", [128, 4, 512], BF16)
            drow, b_drow = S("drow", [128, 512])
            dtab, b_dtab = S("dtab", [128, 4, 128])
            gam, b_gam = S("gam", [128, 4, 4])
            cdec, b_cdec = S("cdec", [128, 4])
            jt, b_jt = S("jt", [128, 128])
            wsrc = w_in_ab[j].rearrange("(k p) n -> p k n", p=128)
            for k in range(8):
                P.dma("pool", win[:, k, :], wsrc[:, k, :], writes=[b_win])
            P.dma("pool", wout[:], w_out_ab[j].rearrange("(k p) n -> p k n", p=128), writes=[b_wout])
            P.dma("pool", wglu[:], w_glu[j].rearrange("(k p) n -> p k n", p=128), writes=[b_wglu])
            P.dma("sp", drow[:], ssm_d[j].partition_broadcast(128), writes=[b_drow])
            P.dma("sp", dtab[:], dtab_d, writes=[b_dtab])
            P.dma("sp", gam[:], gam_d, writes=[b_gam])
            P.dma("sp", cdec[:], cdec_d, writes=[b_cdec])
            P.dma("sp", jt[:], jt_d, writes=[b_jt])
            WB = [S(f"WB{r}", [128, 2048], BF16) for r in range(2)]
            WC = [S(f"WC{r}", [128, 512], BF16) for r in range(3)]
            COS, b_COS = S("COS", [128, 16, 128])
            SIN, b_SIN = S("SIN", [128, 16, 128])
            RHO0, b_RHO0 = S("RHO0", [128, 16, 128])
            Gc, b_Gc = S("Gc", [128, 2, 16])
            cin = [S(f"cin{i}", [128, 2, 16]) for i in range(2)]
            stf, b_stf = S("stf", [128, 512])
            stbf, b_stbf = S("stbf", [128, 512], BF16)
            hn, b_hn = S("hn", [128, 16])
            lst, b_lst = S("lst", [128, 8, 16])
            lastb = [S(f"lastb{i}", [128, 2, 16]) for i in range(2)]

            def s5_setup(d):
                with contextlib.ExitStack() as s2:
                    def S2(name, shape, dt=F32):
                        return s2.enter_context(P.sbt(f"{name}_e{li}d{d}", list(shape), dt)), Buf(name)
                    sm, b_sm = S2("sm", [128, 3, 16])
                    P.dma("sp", sm[:], s5_sm3[j, d].rearrange("t p s -> p t s"), writes=[b_sm])
                    dl, b_dl = S2("dl", [128, 16])
                    zr, b_zr = S2("zr", [128, 16])
                    zi, b_zi = S2("zi", [128, 16])
                    rho, b_rho = S2("rho", [128, 16])
                    P.act(dl[:], sm[:, 2, :], AF.Exp, reads=[b_sm], writes=[b_dl])
                    P.tt("dve", zr[:], sm[:, 0, :], dl[:], ALU.mult, reads=[b_sm, b_dl], writes=[b_zr])
                    P.tt("dve", zi[:], sm[:, 1, :], dl[:], ALU.mult, reads=[b_sm, b_dl], writes=[b_zi])
                    P.act(rho[:], zr[:], AF.Exp, reads=[b_zr], writes=[b_rho])
                    for g4 in range(4):
                        with contextlib.ExitStack() as s4:
                            phi = s4.enter_context(P.sbt(f"phi_e{li}d{d}g{g4}", [128, 4, 128], F32))
                            b_phi = Buf("phi")
                            P.tt("dve", phi[:], zi[:, 4 * g4:4 * g4 + 4].unsqueeze(2).to_broadcast([128, 4, 128]),
                                 jt[:].unsqueeze(1).to_broadcast([128, 4, 128]), ALU.mult, reads=[b_zi, b_jt], writes=[b_phi])
                            sincos(s4, f"t{li}{d}{g4}", phi[:], b_phi, [128, 4, 128], SIN[:, 4 * g4:4 * g4 + 4, :],
                                   COS[:, 4 * g4:4 * g4 + 4, :], b_COS)
                            T_barrier()
                    P.cp("dve", RHO0[:], rho[:].unsqueeze(2).to_broadcast([128, 16, 128]), reads=[b_rho], writes=[b_RHO0])
                    P.memset("dve", RHO0[:, :, 0:1], 0.0, writes=[b_RHO0])
                    ph2, b_ph2 = S2("ph2", [128, 16])
                    sn2, b_sn2 = S2("sn2", [128, 2, 16])
                    P.ts("dve", ph2[:], zi[:], 128.0, None, ALU.mult, None, reads=[b_zi], writes=[b_ph2])
                    sincos(s2, f"g{li}{d}", ph2[:], b_ph2, [128, 16], sn2[:, 1, :], sn2[:, 0, :], b_sn2)
                    P.tt("dve", Gc[:], sn2[:], rho[:].unsqueeze(1).to_broadcast([128, 2, 16]), ALU.mult,
                         reads=[b_sn2, b_rho], writes=[b_Gc])
                    T_barrier()
                for cc in range(8):
                  with contextlib.ExitStack() as s3:
                    def S3(name, shape, dt=F32):
                        return s3.enter_context(P.sbt(f"{name}_e{li}d{d}c{cc}", list(shape), dt)), Buf(name)
                    rp, b_rp = S3("rp", [128, 3, 256])
                    P.dma("sp", rp[:], s5_rep3[j, d][:, cc * 256:(cc + 1) * 256].partition_broadcast(128), writes=[b_rp])
                    r_dl, b_r_dl = S3("r_dl", [128, 256])
                    r_zr, b_r_zr = S3("r_zr", [128, 256])
                    r_zi, b_r_zi = S3("r_zi", [128, 256])
                    r_rho, b_r_rho = S3("r_rho", [128, 256])
                    r_sn, b_r_sn = S3("r_sn", [128, 2, 256])
                    P.act(r_dl[:], rp[:, 2, :], AF.Exp, reads=[b_rp], writes=[b_r_dl])
                    P.tt("dve", r_zr[:], rp[:, 0, :], r_dl[:], ALU.mult, reads=[b_rp, b_r_dl], writes=[b_r_zr])
                    P.tt("dve", r_zi[:], rp[:, 1, :], r_dl[:], ALU.mult, reads=[b_rp, b_r_dl], writes=[b_r_zi])
                    P.act(r_rho[:], r_zr[:], AF.Exp, reads=[b_r_zr], writes=[b_r_rho])
                    sincos(s3, f"r{li}{d}{cc}", r_zi[:], b_r_zi, [128, 256], r_sn[:, 1, :], r_sn[:, 0, :], b_r_sn)
                    P.tt("dve", r_sn[:], r_sn[:], r_rho[:].unsqueeze(1).to_broadcast([128, 2, 256]), ALU.mult,
                         reads=[b_r_sn, b_r_rho], writes=[b_r_sn])
                    P.ts("dve", r_sn[:, 0, :], r_sn[:, 0, :], -1.0, None, ALU.add, None, reads=[b_r_sn], writes=[b_r_sn])
                    P.tt("dve", r_dl[:], rp[:, 0, :], rp[:, 0, :], ALU.mult, reads=[b_rp], writes=[b_r_dl])
                    P.tt("dve", r_zr[:], rp[:, 1, :], rp[:, 1, :], ALU.mult, reads=[b_rp], writes=[b_r_zr])
                    P.tt("dve", r_dl[:], r_dl[:], r_zr[:], ALU.add, reads=[b_r_dl, b_r_zr], writes=[b_r_dl])
                    P.T.op("dve", lambda q, r_dl=r_dl: q.reciprocal(out=r_dl[:], in_=r_dl[:]), reads=[b_r_dl], writes=[b_r_dl])
                    P.tt("dve", r_zr[:], r_sn[:, 0, :], rp[:, 0, :], ALU.mult, reads=[b_r_sn, b_rp], writes=[b_r_zr])
                    P.tt("dve", r_rho[:], r_sn[:, 1, :], rp[:, 1, :], ALU.mult, reads=[b_r_sn, b_rp], writes=[b_r_rho])
                    P.tt("dve", r_zr[:], r_zr[:], r_rho[:], ALU.add, reads=[b_r_zr, b_r_rho], writes=[b_r_zr])
                    P.tt("dve", r_zr[:], r_zr[:], r_dl[:], ALU.mult, reads=[b_r_zr, b_r_dl], writes=[b_r_zr])
                    P.tt("dve", r_zi[:], r_sn[:, 1, :], rp[:, 0, :], ALU.mult, reads=[b_r_sn, b_rp], writes=[b_r_zi])
                    P.tt("dve", r_rho[:], r_sn[:, 0, :], rp[:, 1, :], ALU.mult, reads=[b_r_sn, b_rp], writes=[b_r_rho])
                    P.tt("dve", r_zi[:], r_zi[:], r_rho[:], ALU.subtract, reads=[b_r_zi, b_r_rho], writes=[b_r_zi])
                    P.tt("dve", r_zi[:], r_zi[:], r_dl[:], ALU.mult, reads=[b_r_zi, b_r_dl], writes=[b_r_zi])
                    bl, b_bl = S3("bl", [128, 2, 256])
                    P.dma("sp", bl[:], s5_bl[j, d][:, :, cc * 256:(cc + 1) * 256].rearrange("r p n -> p r n"), writes=[b_bl])
                    P.tt("dve", r_dl[:], r_zr[:], bl[:, 0, :], ALU.mult, reads=[b_r_zr, b_bl], writes=[b_r_dl])
                    P.tt("dve", r_rho[:], r_zi[:], bl[:, 1, :], ALU.mult, reads=[b_r_zi, b_bl], writes=[b_r_rho])
                    P.tt("dve", WB[0][0][:, cc * 256:(cc + 1) * 256], r_dl[:], r_rho[:], ALU.subtract, reads=[b_r_dl, b_r_rho], writes=[WB[0][1]])
                    P.tt("dve", r_dl[:], r_zr[:], bl[:, 1, :], ALU.mult, reads=[b_r_zr, b_bl], writes=[b_r_dl])
                    P.tt("dve", r_rho[:], r_zi[:], bl[:, 0, :], ALU.mult, reads=[b_r_zi, b_bl], writes=[b_r_rho])
                    P.tt("dve", WB[1][0][:, cc * 256:(cc + 1) * 256], r_dl[:], r_rho[:], ALU.add, reads=[b_r_dl, b_r_rho], writes=[WB[1][1]])

                    T_barrier()
                with contextlib.ExitStack() as s2:
                    def S2(name, shape, dt=F32):
                        return s2.enter_context(P.sbt(f"{name}_e{li}d{d}x", list(shape), dt)), Buf(name)
                    cl, b_cl = S2("cl", [128, 2, 512])
                    P.dma("sp", cl[:], s5_cl[j, d].rearrange("r p n -> p r n"), writes=[b_cl])
                    P.cp("dve", WC[0][0][:], cl[:, 0, :], reads=[b_cl], writes=[WC[0][1]])
                    P.ts("dve", WC[1][0][:], cl[:, 0, :], -1.0, None, ALU.mult, None, reads=[b_cl], writes=[WC[1][1]])
                    P.ts("dve", WC[2][0][:], cl[:, 1, :], -1.0, None, ALU.mult, None, reads=[b_cl], writes=[WC[2][1]])
                    P.memset("dve", cin[0][0][:], 0.0, writes=[cin[0][1]])
                    T_barrier()

            import types

            def alloc_work(wk, passB):
                def W(name, shape, dt=F32):
                    return wk.enter_context(P.sbt(f"{name}_e{li}", list(shape), dt)), Buf(name)
                w = types.SimpleNamespace()
                w.g = [types.SimpleNamespace() for _ in range(2)]
                tmps = {nm: W(f"{nm}m", [128, 512]) for nm in ("tA", "tB", "tC", "tD")}
                Pk1 = [W(f"Pk_{k}", [128, 512], BF16) for k in range(4)]
                for i, g in enumerate(w.g):
                    for nm in ("wre", "wim", "sre", "sim"):
                        setattr(g, nm, W(f"{nm}{i}", [128, 512]))
                    for nm in ("tA", "tB", "tC", "tD"):
                        setattr(g, nm, tmps[nm])
                    g.Pk = Pk1
                w.uT = W("uT", [128, 4, 128], BF16)
                w.kf = W("kf", [128, 512], BF16)
                w.tC = W("tCr", [128, 512])
                if not passB:
                    w.hT = [W(f"hT{i}", [128, 8, 128], BF16) for i in range(2)]
                    w.rot = [W(f"rot{i}", [128, 2, 2, 128]) for i in range(2)]
                    w.rA = W("rA", [128, 512])
                    w.rB = W("rB", [128, 512])
                    w.qr = W("qr", [128, 512], BF16)
                    w.kr = [W("kr", [128, 512], BF16)]
                    w.vb = [W("vb", [128, 512], BF16)]
                    w.QT = [W("QT", [128, 512], BF16)]
                    w.KT = W("KT", [128, 512], BF16)
                    w.STb = W("STb", [128, 512], BF16)
                    w.sg = [W("sg", [128, 1024])]
                    w.du = W("du", [128, 512])
                    w.ub = [W("ub", [128, 512], BF16)]
                    w.opt = [W("opt", [128, 512])]
                    w.ypt = [W("ypt", [128, 512])]
                else:
                    w.kr = [W(f"kr{i}", [128, 512], BF16) for i in range(2)]
                    w.vb = [W(f"vb{i}", [128, 512], BF16) for i in range(2)]
                    w.QT = [W(f"QT{i}", [128, 512], BF16) for i in range(2)]
                    w.sga = W("sga", [128, 512])
                    w.sgb = [W(f"sgb{i}", [128, 512]) for i in range(2)]
                    w.uT2 = W("uT2", [128, 4, 128], BF16)
                    w.ub = [W(f"ub{i}", [128, 512], BF16) for i in range(2)]
                    w.opt = [W(f"opt{i}", [128, 512]) for i in range(2)]
                    w.ypt = [W(f"ypt{i}", [128, 512]) for i in range(2)]
                    w.oab = [W(f"oab{i}", [128, 1024], BF16) for i in range(2)]
                    w.oT = W("oT", [128, 8, 128], BF16)
                    w.ygb = W("ygb", [128, 512], BF16)
                    w.tA = W("tAh", [128, 512])
                    w.tB = W("tBh", [128, 512])
                    w.tD = W("tDh", [128, 512])
                return w

            def s5_chunk(w, ci, rev, hooks=()):
                cur, nxt = cin[ci % 2], cin[(ci + 1) % 2]
                lb, b_lb = lastb[ci % 2]
                uT, b_uT = w.uT
                hooks = list(hooks)

                def hook():
                    if hooks:
                        hooks.pop(0)()

                def banks(g):
                    return (3, 4) if g % 2 == 0 else (5, 6)

                def stageApe(g):
                    br, bi = banks(g)
                    for t in range(4):
                        s_ = 4 * g + t
                        P.mm(ps[br][:, t * 128:(t + 1) * 128], WB[0][0][:, s_ * 128:(s_ + 1) * 128], uT[:, g, :], True, False,
                             reads=[WB[0][1], b_uT], writes=[bps[br]], inc=False)
                        P.mm(ps[bi][:, t * 128:(t + 1) * 128], WB[1][0][:, s_ * 128:(s_ + 1) * 128], uT[:, g, :], True, False,
                             reads=[WB[1][1], b_uT], writes=[bps[bi]], inc=False)
                    o_r = ps[br][:].rearrange("p (a b) -> p a b", a=4)[:, :, 0:1]
                    o_i = ps[bi][:].rearrange("p (a b) -> p a b", a=4)[:, :, 0:1]
                    P.mm(o_r, identf[:], cur[0][:, 0, 4 * g:4 * g + 4].unsqueeze(2), False, True,
                         reads=[b_identf, cur[1]], writes=[bps[br]], inc=False)
                    P.mm(o_i, identf[:], cur[0][:, 1, 4 * g:4 * g + 4].unsqueeze(2), False, True,
                         reads=[b_identf, cur[1]], writes=[bps[bi]], inc=True)

                def stageAve(g):
                    br, bi = banks(g)
                    G = w.g[g % 2]
                    Cg = COS[:, 4 * g:4 * g + 4, :].rearrange("p a b -> p (a b)")
                    Sg = SIN[:, 4 * g:4 * g + 4, :].rearrange("p a b -> p (a b)")
                    P.tt("dve", G.tA[0][:], ps[br][:], Cg, ALU.mult, reads=[bps[br], b_COS], writes=[G.tA[1]])
                    P.tt("dve", G.tB[0][:], ps[bi][:], Sg, ALU.mult, reads=[bps[bi], b_COS], writes=[G.tB[1]])
                    P.tt("pool", G.wre[0][:], G.tA[0][:], G.tB[0][:], ALU.add, reads=[G.tA[1], G.tB[1]], writes=[G.wre[1]])
                    P.tt("dve", G.tC[0][:], ps[bi][:], Cg, ALU.mult, reads=[bps[bi], b_COS], writes=[G.tC[1]])
                    P.stt("dve", G.tD[0][:], ps[br][:], -1.0, Sg, ALU.mult, ALU.mult, reads=[bps[br], b_COS], writes=[G.tD[1]])
                    P.tt("pool", G.wim[0][:], G.tC[0][:], G.tD[0][:], ALU.add, reads=[G.tC[1], G.tD[1]], writes=[G.wim[1]])

                def stageB(g):
                    G = w.g[g % 2]
                    Rg = RHO0[:, 4 * g:4 * g + 4, :].rearrange("p a b -> p (a b)")
                    P.T.op("dve", lambda q, Rg=Rg, o=G.sre[0], i=G.wre[0]: q.tensor_tensor_scan(
                        out=o[:], data0=Rg, data1=i[:], initial=0.0, op0=ALU.mult, op1=ALU.add),
                        reads=[b_RHO0, G.wre[1]], writes=[G.sre[1]])
                    P.T.op("dve", lambda q, Rg=Rg, o=G.sim[0], i=G.wim[0]: q.tensor_tensor_scan(
                        out=o[:], data0=Rg, data1=i[:], initial=0.0, op0=ALU.mult, op1=ALU.add),
                        reads=[b_RHO0, G.wim[1]], writes=[G.sim[1]])
                    sr3 = G.sre[0][:].rearrange("p (a b) -> p a b", a=4)
                    si3 = G.sim[0][:].rearrange("p (a b) -> p a b", a=4)
                    P.act(lb[:, 0, 4 * g:4 * g + 4].unsqueeze(2), sr3[:, :, 127:128], AF.Copy, reads=[G.sre[1]], writes=[b_lb])
                    P.act(lb[:, 1, 4 * g:4 * g + 4].unsqueeze(2), si3[:, :, 127:128], AF.Copy, reads=[G.sim[1]], writes=[b_lb])

                    def pv(ap):
                        v = ap.rearrange("p (a b) -> p a b", a=4)
                        return v[:, :, ::-1] if rev else v
                    C3 = COS[:, 4 * g:4 * g + 4, :]
                    S3 = SIN[:, 4 * g:4 * g + 4, :]
                    e2 = "dve" if rev else "pool"
                    P.tt("dve", pv(G.Pk[0][0][:]), sr3, C3, ALU.mult, reads=[G.sre[1], b_COS], writes=[G.Pk[0][1]])
                    P.tt(e2, pv(G.Pk[1][0][:]), si3, S3, ALU.mult, reads=[G.sim[1], b_COS], writes=[G.Pk[1][1]])
                    P.tt("dve", pv(G.Pk[2][0][:]), sr3, S3, ALU.mult, reads=[G.sre[1], b_COS], writes=[G.Pk[2][1]])
                    P.tt(e2, pv(G.Pk[3][0][:]), si3, C3, ALU.mult, reads=[G.sim[1], b_COS], writes=[G.Pk[3][1]])
                    for t in range(4):
                        s_ = 4 * g + t
                        o = ps[7][:, 32 * s_:32 * s_ + 32]
                        wsl = slice(32 * s_, 32 * s_ + 32)
                        tsl = slice(128 * t, 128 * t + 128)
                        Pk = G.Pk
                        P.mm(o, Pk[0][0][:, tsl], WC[0][0][:, wsl], True, False, reads=[Pk[0][1], WC[0][1]], writes=[bps[7]], inc=False)
                        P.mm(o, Pk[1][0][:, tsl], WC[1][0][:, wsl], False, False, reads=[Pk[1][1], WC[1][1]], writes=[bps[7]], inc=False)
                        P.mm(o, Pk[2][0][:, tsl], WC[2][0][:, wsl], False, False, reads=[Pk[2][1], WC[2][1]], writes=[bps[7]], inc=False)
                        P.mm(o, Pk[3][0][:, tsl], WC[2][0][:, wsl], False, True, reads=[Pk[3][1], WC[2][1]], writes=[bps[7]], inc=(t == 3))

                stageApe(0)
                stageAve(0)
                stageApe(1)
                hook()
                stageAve(1)
                hook()
                stageApe(2)
                stageB(0)
                hook()
                stageAve(2)
                hook()
                stageApe(3)
                stageB(1)
                hook()
                stageAve(3)
                hook()
                stageB(2)
                hook()
                stageB(3)
                hook()
                while hooks:
                    hook()
                l4 = lst[:]
                P.tt("pool", l4[:, 0, :], lb[:, 0, :], Gc[:, 0, :], ALU.mult, reads=[b_lb, b_Gc], writes=[b_lst])
                P.tt("pool", l4[:, 1, :], lb[:, 1, :], Gc[:, 1, :], ALU.mult, reads=[b_lb, b_Gc], writes=[b_lst])
                P.tt("pool", l4[:, 2, :], lb[:, 1, :], Gc[:, 0, :], ALU.mult, reads=[b_lb, b_Gc], writes=[b_lst])
                P.tt("pool", l4[:, 3, :], lb[:, 0, :], Gc[:, 1, :], ALU.mult, reads=[b_lb, b_Gc], writes=[b_lst])
                P.tt("pool", nxt[0][:, 0, :], l4[:, 0, :], l4[:, 1, :], ALU.subtract, reads=[b_lst], writes=[nxt[1]])
                P.tt("pool", nxt[0][:, 1, :], l4[:, 2, :], l4[:, 3, :], ALU.add, reads=[b_lst], writes=[nxt[1]])

            def ret_state_update(w, kbuf, b_kbuf, vbt, b_vbt, gcol):
                kf, b_kf = w.kf
                P.tt("pool", kf[:].rearrange("p (h e) -> p h e", h=4), kbuf[:].rearrange("p (h e) -> p h e", h=4),
                     gam[:, gcol, :].unsqueeze(2).to_broadcast([128, 4, 128]), ALU.mult,
                     reads=[b_kbuf, b_gam], writes=[b_kf])
                for h in range(4):
                    hs = slice(h * 128, (h + 1) * 128)
                    P.mm(ps[5][:, hs], kf[:, hs], vbt[:, hs], True, True, reads=[b_kf, b_vbt], writes=[bps[5]], inc=(h == 3))
                P.tt("pool", stf[:].rearrange("p (h e) -> p h e", h=4), stf[:].rearrange("p (h e) -> p h e", h=4),
                     cdec[:].unsqueeze(2).to_broadcast([128, 4, 128]), ALU.mult, reads=[b_stf, b_cdec], writes=[b_stf])
                P.tt("dve", stf[:], stf[:], ps[5][:], ALU.add, reads=[b_stf, bps[5]], writes=[b_stf])
                P.act(stbf[:], stf[:], AF.Copy, reads=[b_stf], writes=[b_stbf])

            def passA(s, w):
                OPs, YPs, SGs, QTs, KRs, VBs, UBs = OPs_[s], YPs_[s], SGs_[s], QTs_[s], KRs_[s], VBs_[s], UBs_[s]
                b_OPs, b_YPs, b_SGs, b_QTs, b_KRs, b_VBs, b_UBs = SCRB[s]
                P.memset("dve", stf[:], 0.0, writes=[b_stf])
                P.memset("pool", stbf[:], 0.0, writes=[b_stbf])
                P.memset("dve", cin[0][0][:], 0.0, writes=[cin[0][1]])
                nA = nch if KSTOP not in ('setup',) else 0
                if nA:
                    prenorm(C, s, src, bsrc[s], 0, w.hT[0][0], w.hT[0][1], 0)
                for n in range(nA):
                    tsl = slice(n * 128, (n + 1) * 128)
                    hT, b_hT = w.hT[n % 2]
                    rot, b_rot = w.rot[n % 2]
                    P.dma("sp", rot[:], rot_d[tsl], writes=[b_rot])
                    if n + 1 < nA:
                        p1, p2, p3 = prenorm_parts(C, s, src, bsrc[s], n + 1, w.hT[(n + 1) % 2][0], w.hT[(n + 1) % 2][1], 0)
                    else:
                        p1 = p2 = p3 = (lambda: None)
                    qr, b_qr = w.qr
                    kr, b_kr = w.kr[0]
                    vb, b_vb = w.vb[0]
                    QT, b_QT = w.QT[0]
                    KT, b_KT = w.KT
                    STb, b_STb = w.STb
                    sg, b_sg = w.sg[0]
                    du, b_du = w.du
                    ub, b_ub = w.ub[0]
                    opt, b_opt = w.opt[0]
                    ypt, b_ypt = w.ypt[0]
                    tC, b_tC = w.tC
                    uT, b_uT = w.uT
                    def inproj(col, bank):
                        for k in range(8):
                            P.mm(ps[bank][:], hT[:, k, :], win[:, k, col * 512:(col + 1) * 512], k == 0, k == 7,
                                 reads=[b_hT, b_win], writes=[bps[bank]], inc=(k == 7))

                    def rotary(zb, tbl, outb, b_outb):
                        rA, b_rA = w.rA
                        rB, b_rB = w.rB
                        z3 = ps[zb][:].rearrange("p (h e) -> p h e", h=4)
                        a3 = rA[:].rearrange("p (h e) -> p h e", h=4)
                        b3 = rB[:].rearrange("p (h e) -> p h e", h=4)
                        P.tt("dve", a3, z3, rot[:, tbl, 0, :].unsqueeze(1).to_broadcast([128, 4, 128]), ALU.mult,
                             reads=[bps[zb], b_rot], writes=[b_rA])
                        P.tt("dve", b3[:, :, 0:64], z3[:, :, 64:128], rot[:, tbl, 1, 0:64].unsqueeze(1).to_broadcast([128, 4, 64]),
                             ALU.mult, reads=[bps[zb], b_rot], writes=[b_rB])
                        P.tt("dve", b3[:, :, 64:128], z3[:, :, 0:64], rot[:, tbl, 1, 64:128].unsqueeze(1).to_broadcast([128, 4, 64]),
                             ALU.mult, reads=[bps[zb], b_rot], writes=[b_rB])
                        P.tt("pool", outb[:], rA[:], rB[:], ALU.add, reads=[b_rA, b_rB], writes=[b_outb])

                    inproj(4, 2)
                    P.act(ub[:], ps[2][:], AF.Copy, reads=[bps[2]], writes=[b_ub])
                    P.tt("dve", du[:], ps[2][:], drow[:], ALU.mult, reads=[bps[2], b_drow, b_ub], writes=[b_du])
                    P.dma("pool", UBs[tsl], ub[:], reads=[b_ub], writes=[b_UBs])
                    for q_ in range(4):
                        qs = slice(q_ * 128, (q_ + 1) * 128)
                        P.mm(ps[0][:, qs], ub[:, qs], ident_bf[:], True, True, reads=[b_ub, b_ident], writes=[bps[0]], inc=(q_ == 3))
                    P.act(uT[:].rearrange("p a b -> p (a b)"), ps[0][:], AF.Copy, reads=[bps[0]], writes=[b_uT])

                    def H0():
                        inproj(0, 2)
                        inproj(1, 1)
                        rotary(2, 0, qr, b_qr)
                        rotary(1, 1, kr, b_kr)

                    def H1():
                        inproj(2, 2)
                        P.act(vb[:], ps[2][:], AF.Copy, reads=[bps[2]], writes=[b_vb])
                        inproj(3, 0)
                        P.act(sg[:, 0:512], ps[0][:], AF.Silu, reads=[bps[0]], writes=[b_sg])
                        inproj(5, 1)
                        P.act(sg[:, 512:1024], ps[1][:], AF.Silu, reads=[bps[1]], writes=[b_sg])
                        P.dma("pool", SGs[tsl], sg[:], reads=[b_sg], writes=[b_SGs])

                    def R1():
                        for h in range(4):
                            hs = slice(h * 128, (h + 1) * 128)
                            P.mm(ps[0][:, hs], qr[:, hs], ident_bf[:], True, True, reads=[b_qr, b_ident], writes=[bps[0]], inc=(h == 3))
                        for h in range(4):
                            hs = slice(h * 128, (h + 1) * 128)
                            P.mm(ps[1][:, hs], kr[:, hs], ident_bf[:], True, True, reads=[b_kr, b_ident], writes=[bps[1]], inc=(h == 3))
                        P.act(QT[:], ps[0][:], AF.Copy, reads=[bps[0]], writes=[b_QT])
                        P.act(KT[:], ps[1][:], AF.Copy, reads=[bps[1]], writes=[b_KT])
                        p1()
                    def R2():
                        for h in range(4):
                            hs = slice(h * 128, (h + 1) * 128)
                            P.mm(ps[2][:, hs], KT[:, hs], QT[:, hs], True, True, reads=[b_KT, b_QT], writes=[bps[2]], inc=(h == 3))
                        P.tt("dve", STb[:], ps[2][:], dtab[:].rearrange("p h e -> p (h e)"), ALU.mult,
                             reads=[bps[2], b_dtab], writes=[b_STb])
                    def R3():
                        for h in range(4):
                            hs = slice(h * 128, (h + 1) * 128)
                            P.mm(ps[2][:, hs], STb[:, hs], vb[:, hs], True, True, reads=[b_STb, b_vb], writes=[bps[2]], inc=(h == 3))
                        for h in range(4):
                            hs = slice(h * 128, (h + 1) * 128)
                            P.mm(ps[0][:, hs], QT[:, hs], stbf[:, hs], True, True, reads=[b_QT, b_stbf], writes=[bps[0]], inc=(h == 3))
                        P.tt("dve", tC[:].rearrange("p (h e) -> p h e", h=4), ps[0][:].rearrange("p (h e) -> p h e", h=4),
                             gam[:, 0, :].unsqueeze(2).to_broadcast([128, 4, 128]), ALU.mult, reads=[bps[0], b_gam], writes=[b_tC])
                        P.tt("dve", opt[:], tC[:], ps[2][:], ALU.add, reads=[b_tC, bps[2]], writes=[b_opt])
                        P.dma("pool", OPs[tsl], opt[:], reads=[b_opt], writes=[b_OPs])
                    def R4():
                        ret_state_update(w, kr, b_kr, vb, b_vb, 2)
                    def R5():
                        P.dma("pool", QTs[n], QT[:], reads=[b_QT], writes=[b_QTs])
                        P.dma("pool", KRs[tsl], kr[:], reads=[b_kr], writes=[b_KRs])
                        P.dma("pool", VBs[tsl], vb[:], reads=[b_vb], writes=[b_VBs])
                    def R45():
                        R4()
                        R5()

                    def P23():
                        p2()
                        p3()
                    s5_chunk(w, n, False, hooks=[H0, H1, R1, R2, R3, R45, P23])
                    P.tt("dve", ypt[:], ps[7][:], du[:], ALU.add, reads=[bps[7], b_du], writes=[b_ypt])
                    P.dma("pool", YPs[tsl], ypt[:], reads=[b_ypt], writes=[b_YPs])

            def passB(s, w):
                OPs, YPs, SGs, QTs, KRs, VBs, UBs = OPs_[s], YPs_[s], SGs_[s], QTs_[s], KRs_[s], VBs_[s], UBs_[s]
                b_OPs, b_YPs, b_SGs, b_QTs, b_KRs, b_VBs, b_UBs = SCRB[s]
                P.memset("dve", stf[:], 0.0, writes=[b_stf])
                P.memset("pool", stbf[:], 0.0, writes=[b_stbf])
                P.memset("dve", cin[0][0][:], 0.0, writes=[cin[0][1]])
                order = list(range(nch - 1, -1, -1)) if KSTOP == 'all' else []

                def loads(ci):
                    n = order[ci]
                    tsl = slice(n * 128, (n + 1) * 128)
                    r = ci % 2
                    P.dma("sp", w.ub[r][0][:], UBs[tsl], reads=[b_UBs], writes=[w.ub[r][1]])
                    P.dma("sp", w.QT[r][0][:], QTs[n], reads=[b_QTs], writes=[w.QT[r][1]])
                    P.dma("sp", w.opt[r][0][:], OPs[tsl], reads=[b_OPs], writes=[w.opt[r][1]])
                    P.dma("sp", w.kr[r][0][:], KRs[tsl], reads=[b_KRs], writes=[w.kr[r][1]])
                    P.dma("sp", w.vb[r][0][:], VBs[tsl], reads=[b_VBs], writes=[w.vb[r][1]])

                def make_tail(ci, n):
                    r = ci % 2
                    tsl = slice(n * 128, (n + 1) * 128)
                    ypt, b_ypt = w.ypt[r]
                    sgb, b_sgb = w.sgb[r]
                    oab, b_oab = w.oab[r]
                    tB, b_tB = w.tB
                    tD, b_tD = w.tD
                    uT2, b_uT2 = w.uT2
                    oT, b_oT = w.oT
                    ygb, b_ygb = w.ygb

                    def T1():
                        P.tt("pool", tB[:], ypt[:], ypt[:], ALU.mult, reads=[b_ypt], writes=[b_tB])
                        P.ts("dve", tB[:], tB[:], 0.044715, 1.0, ALU.mult, ALU.add, reads=[b_tB], writes=[b_tB])
                        P.tt("pool", tB[:], tB[:], ypt[:], ALU.mult, reads=[b_tB, b_ypt], writes=[b_tB])
                        P.act(tB[:], tB[:], AF.Sigmoid, reads=[b_tB], writes=[b_tB], scale=1.5957691216057308)
                        P.tt("dve", tD[:], ypt[:], tB[:], ALU.mult, reads=[b_ypt, b_tB], writes=[b_tD])
                        P.act(ygb[:], tD[:], AF.Copy, reads=[b_tD], writes=[b_ygb])

                    def T2():
                        for q_ in range(4):
                            qs = slice(q_ * 128, (q_ + 1) * 128)
                            P.mm(ps[0][:, qs], ygb[:, qs], ident_bf[:], True, True, reads=[b_ygb, b_ident], writes=[bps[0]], inc=(q_ == 3))
                        P.act(uT2[:].rearrange("p a b -> p (a b)"), ps[0][:], AF.Copy, reads=[bps[0]], writes=[b_uT2])
                        for q_ in range(4):
                            P.mm(ps[1][:], uT2[:, q_, :], wglu[:, q_, :], q_ == 0, q_ == 3, reads=[b_uT2, b_wglu], writes=[bps[1]], inc=(q_ == 3))
                        P.act(tB[:], ps[1][:], AF.Sigmoid, reads=[bps[1]], writes=[b_tB])
                        P.tt("dve", tD[:], tD[:], tB[:], ALU.mult, reads=[b_tD, b_tB], writes=[b_tD])
                        P.tt("pool", oab[:, 512:1024], tD[:], sgb[:], ALU.mult, reads=[b_tD, b_sgb], writes=[b_oab])

                    def T3():
                        for b_ in range(2):
                            for jj in range(4):
                                k = 4 * b_ + jj
                                P.mm(ps[b_][:, jj * 128:(jj + 1) * 128], oab[:, k * 128:(k + 1) * 128], ident_bf[:], True, True,
                                     reads=[b_oab, b_ident], writes=[bps[b_]], inc=(jj == 3))
                        P.act(oT[:, 0:4, :].rearrange("p a b -> p (a b)"), ps[0][:], AF.Copy, reads=[bps[0]], writes=[b_oT])
                        P.act(oT[:, 4:8, :].rearrange("p a b -> p (a b)"), ps[1][:], AF.Copy, reads=[bps[1]], writes=[b_oT])
                        for hh, bk in ((0, 2), (1, 0)):
                            for k in range(8):
                                P.mm(ps[bk][:], oT[:, k, :], wout[:, k, hh * 512:(hh + 1) * 512], k == 0, k == 7,
                                     reads=[b_oT, b_wout], writes=[bps[bk]], inc=(k == 7))

                    def T4():
                        sl = C["cnt"] % 2
                        C["cnt"] += 1
                        xt, bxt = C["xt"][sl], C["bxt"][sl]
                        P.dma("sp", xt[:], src[s, tsl, :], reads=[bsrc[s]], writes=[bxt])
                        post(C, s, (2, 0), xt, bxt, dst, bdst[s], n)
                    return [T1, T2, T3, T4]

                if order:
                    loads(0)
                tail = []
                for ci, n in enumerate(order):
                    tsl = slice(n * 128, (n + 1) * 128)
                    r = ci % 2
                    QT, b_QT = w.QT[r]
                    opt, b_opt = w.opt[r]
                    kr, b_kr = w.kr[r]
                    vb, b_vb = w.vb[r]
                    ub, b_ub = w.ub[r]
                    sga, b_sga = w.sga
                    sgb, b_sgb = w.sgb[r]
                    ypt, b_ypt = w.ypt[r]
                    tA, b_tA = w.tA
                    tC, b_tC = w.tC
                    uT, b_uT = w.uT
                    oab, b_oab = w.oab[r]
                    kf, b_kf = w.kf
                    P.dma("sp", sga[:], SGs[tsl, 0:512], reads=[b_SGs], writes=[b_sga])
                    P.dma("sp", sgb[:], SGs[tsl, 512:1024], reads=[b_SGs], writes=[b_sgb])
                    P.dma("sp", ypt[:], YPs[tsl], reads=[b_YPs], writes=[b_ypt])
                    for q_ in range(4):
                        qs = slice(q_ * 128, (q_ + 1) * 128)
                        P.mm(ps[0][:, qs], ub[:, qs], jmat_bf[:], True, True, reads=[b_ub, b_jmat], writes=[bps[0]], inc=(q_ == 3))
                    P.act(uT[:].rearrange("p a b -> p (a b)"), ps[0][:], AF.Copy, reads=[bps[0]], writes=[b_uT])
                    if ci + 1 < len(order):
                        loads(ci + 1)

                    def RB1():
                        for h in range(4):
                            hs = slice(h * 128, (h + 1) * 128)
                            P.mm(ps[2][:, hs], QT[:, hs], stbf[:, hs], True, True, reads=[b_QT, b_stbf], writes=[bps[2]], inc=(h == 3))
                        P.tt("dve", tC[:].rearrange("p (h e) -> p h e", h=4), ps[2][:].rearrange("p (h e) -> p h e", h=4),
                             gam[:, 1, :].unsqueeze(2).to_broadcast([128, 4, 128]), ALU.mult, reads=[bps[2], b_gam], writes=[b_tC])
                        P.tt("pool", opt[:], opt[:], tC[:], ALU.add, reads=[b_opt, b_tC], writes=[b_opt])
                        P.tt("pool", kf[:].rearrange("p (h e) -> p h e", h=4), kr[:].rearrange("p (h e) -> p h e", h=4),
                             gam[:, 3, :].unsqueeze(2).to_broadcast([128, 4, 128]), ALU.mult,
                             reads=[b_kr, b_gam], writes=[b_kf])
                        for h in range(4):
                            hs = slice(h * 128, (h + 1) * 128)
                            P.mm(ps[1][:, hs], kf[:, hs], vb[:, hs], True, True, reads=[b_kf, b_vb], writes=[bps[1]], inc=(h == 3))
                        P.tt("pool", stf[:].rearrange("p (h e) -> p h e", h=4), stf[:].rearrange("p (h e) -> p h e", h=4),
                             cdec[:].unsqueeze(2).to_broadcast([128, 4, 128]), ALU.mult, reads=[b_stf, b_cdec], writes=[b_stf])

                    def RB2():
                        P.tt("dve", stf[:], stf[:], ps[1][:], ALU.add, reads=[b_stf, bps[1]], writes=[b_stf])
                        P.act(stbf[:], stf[:], AF.Copy, reads=[b_stf], writes=[b_stbf])
                        o3 = opt[:].rearrange("p (h e) -> p h e", h=4)
                        P.T.op("dve", lambda q, o3=o3: q.reduce_sum(out=hn[:, 0:4], in_=o3, axis=mybir.AxisListType.X),
                               reads=[b_opt], writes=[b_hn])
                        P.tt("pool", tA[:], opt[:], opt[:], ALU.mult, reads=[b_opt], writes=[b_tA])
                        P.T.op("dve", lambda q, tA=tA: q.reduce_sum(out=hn[:, 4:8], in_=tA[:].rearrange("p (h e) -> p h e", h=4),
                                                                  axis=mybir.AxisListType.X), reads=[b_tA], writes=[b_hn])
                        P.ts("dve", hn[:, 0:8], hn[:, 0:8], 1.0 / 128.0, None, ALU.mult, None, reads=[b_hn], writes=[b_hn])
                        P.tt("dve", hn[:, 8:12], hn[:, 0:4], hn[:, 0:4], ALU.mult, reads=[b_hn], writes=[b_hn])
                        P.tt("dve", hn[:, 4:8], hn[:, 4:8], hn[:, 8:12], ALU.subtract, reads=[b_hn], writes=[b_hn])
                        P.act(hn[:, 8:12], hn[:, 4:8], AF.Sqrt, reads=[b_hn, b_eps], writes=[b_hn], bias=epsT[:, 0:1])

                    def RB3():
                        o3 = opt[:].rearrange("p (h e) -> p h e", h=4)
                        P.T.op("dve", lambda q: q.reciprocal(out=hn[:, 12:16], in_=hn[:, 8:12]), reads=[b_hn], writes=[b_hn])
                        a3 = tA[:].rearrange("p (h e) -> p h e", h=4)
                        P.tt("pool", a3, o3, hn[:, 0:4].unsqueeze(2).to_broadcast([128, 4, 128]), ALU.subtract,
                             reads=[b_opt, b_hn], writes=[b_tA])
                        P.tt("pool", a3, a3, hn[:, 12:16].unsqueeze(2).to_broadcast([128, 4, 128]), ALU.mult,
                             reads=[b_tA, b_hn], writes=[b_tA])
                        P.tt("pool", oab[:, 0:512], tA[:], sga[:], ALU.mult, reads=[b_tA, b_sga], writes=[b_oab])

                    hooks = [RB1, RB2, RB3] + tail
                    s5_chunk(w, ci, True, hooks=hooks)
                    P.tt("dve", ypt[:], ypt[:], ps[7][:], ALU.add, reads=[b_ypt, bps[7]], writes=[b_ypt])
                    tail = make_tail(ci, n)
                for t_ in tail:
                    t_()

            s5_setup(0)
            with contextlib.ExitStack() as wk:
                w = alloc_work(wk, False)
                for s in range(nseq):
                    passA(s, w)
                T_barrier()
            s5_setup(1)
            with contextlib.ExitStack() as wk:
                w = alloc_work(wk, True)
                for s in range(nseq):
                    passB(s, w)
                T_barrier()

    def T_barrier():
        evs = []
        for e in T.engs.values():
            for k in range(len(e.dsems)):
                evs.append((e.dsems[k], e.dvals[k]))
            if e.sem is not None and e.cnt > 0 and not e.pending:
                evs.append((e.sem, e.cnt))
        for e in T.engs.values():
            for ev in evs:
                T._wait(e, ev)

    def odd_layer(li, src, bsrc, dst, bdst):
        raise NotImplementedError

    P.odd_layer_hook = None
    cur, bcur = x_in, [b_xin] * nseq
    for idx, li in enumerate(layers):
        last = idx == len(layers) - 1
        dstt, bd = (y_out, b_yout) if last else (xs[idx % 2], b_xs[idx % 2])
        if li % 2 == 0:
            even_layer(li, cur, bcur, dstt, bd)
        else:
            ODD_IMPL(P, locals(), li, cur, bcur, dstt, bd)
        cur, bcur = dstt, bd
    T.finish()
    T.replay()
    P.es.close()
    return P


def ODD_IMPL(P, env, li, src, bsrc, dst, bdst):
    nc, T = P.nc, P.T
    E = env
    ps, bps = E["ps"], E["bps"]
    nseq, L = P.nseq, P.L
    ident_bf, b_ident = E["ident_bf"], E["b_ident"]
    j = li // 2
    rows = L // 64
    nblk = L // 256

    def rs(r):
        return min(max(r - 4, 0), rows - 8)

    with contextlib.ExitStack() as st:
        def S(name, shape, dt=F32):
            return st.enter_context(P.sbt(f"{name}_o{li}", list(shape), dt)), Buf(name)
        E["adaln"](li, None)
        C = E["make_common"](st, f"o{li}")
        winc, b_winc = S("winc", [128, 8, 4096], BF16)
        woutc, b_woutc = S("woutc", [128, 8, 1024], BF16)
        Z, b_Z = S("Z", [128, 16, 1024], BF16)
        hT, b_hT = S("hT", [128, 8, 256], BF16)
        KT = [S(f"KT{i}", [128, 8, 256], BF16) for i in range(3)]
        V = [S(f"V{i}", [128, 2, 8, 3, 64], BF16) for i in range(3)]
        QT = [S(f"QT{i}", [128, 8, 256], BF16) for i in range(2)]
        GT = [S(f"GT{i}", [128, 8, 256], BF16) for i in range(2)]
        pT = [S(f"pT{i}", [128, 256], BF16) for i in range(3)]
        og, b_og = S("og", [128, 8, 256], BF16)
        rd2 = [S(f"rd{i}", [128, 256]) for i in range(2)]
        rb2 = [S(f"rb{i}", [128, 256]) for i in range(2)]
        t22 = [S(f"t2{i}", [128, 256]) for i in range(2)]
        selb, b_selb = S("selb", [128, 64], BF16)
        RBh = [S(f"RBh{i}", [128, 256], BF16) for i in range(2)]
        zer, b_zer = S("zer", [128, 256], BF16)
        wsrc = E["w_in_c"][j].rearrange("(k p) n -> p k n", p=128)
        for k in range(8):
            P.dma("pool", winc[:, k, :], wsrc[:, k, :], writes=[b_winc])
        P.dma("pool", woutc[:], E["w_out_c"][j].rearrange("(k p) n -> p k n", p=128), writes=[b_woutc])
        P.dma("pool", selb[:], E["sel_d"], writes=[b_selb])
        for i in range(2):
            P.memset("dve", RBh[i][0][:], 0.0, writes=[RBh[i][1]])
        P.memset("dve", zer[:], 0.0, writes=[b_zer])
        for i in range(3):
            P.memset("pool", V[i][0][:], 1.0, writes=[V[i][1]])
        with contextlib.ExitStack() as s2:
            zm = s2.enter_context(P.sbt(f"zm_o{li}", [128, 1024], F32)); b_zm = Buf("zm")
            zt0_ = s2.enter_context(P.sbt(f"zt0_o{li}", [128, 512], F32))
            zt = [zt0_, zt0_]
            b_zt0_ = Buf("zt0")
            b_zt = [b_zt0_, b_zt0_]
            P.dma("sp", zm[:], E["zmask"], writes=[b_zm])
            for h in range(16):
                for hf in range(2):
                    P.dma("sp", zt[hf][:], E["zg"][j, h][:, hf * 512:(hf + 1) * 512], writes=[b_zt[hf]])
                    P.tt("dve", Z[:, h, hf * 512:(hf + 1) * 512], zt[hf][:], zm[:, hf * 512:(hf + 1) * 512], ALU.add,
                         reads=[b_zt[hf], b_zm], writes=[b_Z])
            E["T_barrier"]()

        def proj(s, b, do_prenorm=True):
            ring = b % 3
            sl = b % 2
            if do_prenorm:
                for t in range(2):
                    E["prenorm"](C, s, src, bsrc[s], 2 * b + t, hT, b_hT, t * 128)
            cnt = 0
            for (col0, kind) in ((0, "q"), (1024, "k"), (3072, "g")):
                for hp2 in range(4):
                    bank = 2 + cnt % 2
                    cnt += 1
                    for hh in range(2):
                        hp = 2 * hp2 + hh
                        for k in range(8):
                            P.mm(ps[bank][:, hh * 256:(hh + 1) * 256], winc[:, k, col0 + hp * 128:col0 + (hp + 1) * 128], hT[:, k, :],
                                 k == 0, k == 7, reads=[b_winc, b_hT], writes=[bps[bank]], inc=(k == 7 and hh == 1))
                    if kind == "q":
                        P.act(QT[sl][0][:, 2 * hp2:2 * hp2 + 2, :].rearrange("p a b -> p (a b)"), ps[bank][:], AF.Copy,
                              reads=[bps[bank]], writes=[QT[sl][1]], scale=0.125)
                    elif kind == "k":
                        P.cp("dve", KT[ring][0][:, 2 * hp2:2 * hp2 + 2, :].rearrange("p a b -> p (a b)"), ps[bank][:],
                             reads=[bps[bank]], writes=[KT[ring][1]])
                    else:
                        P.act(GT[sl][0][:, 2 * hp2:2 * hp2 + 2, :].rearrange("p a b -> p (a b)"), ps[bank][:], AF.Silu,
                              reads=[bps[bank]], writes=[GT[sl][1]])
            for t in range(2):
                for half in range(2):
                    bank = 2 + cnt % 2
                    cnt += 1
                    for k in range(8):
                        P.mm(ps[bank][:], hT[:, k, t * 128:(t + 1) * 128], winc[:, k, 2048 + half * 512:2048 + (half + 1) * 512],
                             k == 0, k == 7, reads=[b_hT, b_winc], writes=[bps[bank]], inc=(k == 7))
                    src4 = ps[bank][:].rearrange("p (a c d) -> p a c d", a=4, c=2)
                    P.cp("dve", V[ring][0][:, t, 4 * half:4 * half + 4, 0:3:2, :], src4, reads=[bps[bank]], writes=[V[ring][1]])

        def attn(s, b, hooks=()):
            hooks = list(hooks)
            R = 4 * b
            sl = b % 2
            lo = rs(R) & ~1
            hi = (rs(R + 3) + 7) & ~1
            tiles = []
            for r0 in range(lo, hi + 1, 2):
                qs = [r for r in range(R, R + 4) if (rs(r) <= r0 + 1 and r0 <= rs(r) + 7)]
                if not qs:
                    continue
                qa, qb = qs[0], qs[-1]
                partial = []
                for r in qs:
                    for a in range(2):
                        if not (rs(r) <= r0 + a <= rs(r) + 7):
                            partial.append((r, a))
                tiles.append((r0, qa, qb, partial))
            tiles.sort(key=lambda t: (0 if (t[1] == R and t[2] == R + 3 and not t[3]) else 1))
            assert tiles[0][1] == R and tiles[0][2] == R + 3 and not tiles[0][3]
            pcount = [0]
            W = [(h, ti) for h in range(16) for ti in range(len(tiles))]
            info = {}
            deferred = []

            def emit_scores(h, ti):
                hp, base = h // 2, 64 * (h % 2)
                r0, qa, qb, partial = tiles[ti]
                kb = r0 // 4
                tt_ = (r0 % 4) // 2
                kring = kb % 3
                c0, c1 = (qa - R) * 64, (qb - R + 1) * 64
                z0, z1 = (qa - r0 + 7) * 64, (qb - r0 + 8) * 64
                bank = 4 + pcount[0] % 2
                pt, b_pt = pT[pcount[0] % 3]
                pcount[0] += 1
                P.mm(ps[bank][:, c0:c1], KT[kring][0][base:base + 64, hp, tt_ * 128:(tt_ + 1) * 128],
                     QT[sl][0][base:base + 64, hp, c0:c1], True, False,
                     reads=[KT[kring][1], QT[sl][1]], writes=[bps[bank]], inc=False)
                P.mm(ps[bank][:, c0:c1], ident_bf[:], Z[:, h, z0:z1], False, True,
                     reads=[b_ident, b_Z], writes=[bps[bank]], inc=True)
                P.act(pt[:, c0:c1], ps[bank][:, c0:c1], AF.Exp, reads=[bps[bank]], writes=[b_pt])
                for (r, a_) in partial:
                    cc = (r - R) * 64
                    P.memset("pool", pt[64 * a_:64 * a_ + 64, cc:cc + 64], 0.0, writes=[b_pt])
                info[(h, ti)] = (pt, b_pt, c0, c1, kring, tt_)

            def emit_pv(h, ti, idx):
                hp = h // 2
                pt, b_pt, c0, c1, kring, tt_ = info.pop((h, ti))
                ob = 6 + h % 2
                va = V[kring][0][:, tt_, hp, 0:2, :] if h % 2 == 0 else V[kring][0][:, tt_, hp, 1:3, :]
                last = ti == len(tiles) - 1
                P.mm(ps[ob][:, c0:c1], va.rearrange("p a b -> p (a b)"), pt[:, c0:c1], ti == 0, last,
                     reads=[V[kring][1], b_pt], writes=[bps[ob]], inc=last)
                if last:
                    par = h % 2
                    dr, orow = (64, 0) if par == 0 else (0, 64)
                    rdp, b_rdp = rd2[par]
                    rbp, b_rbp = rb2[par]
                    t2p, b_t2p = t22[par]
                    rbh, b_rbh = RBh[par]
                    P.T.op("dve", lambda q, dr=dr, ob=ob, rdp=rdp: q.reciprocal(out=rdp[dr:dr + 33, :], in_=ps[ob][dr:dr + 33, 0:256]),
                           reads=[bps[ob]], writes=[b_rdp])
                    P.cp("dve", rbh[dr:dr + 33, :], rdp[dr:dr + 33, :], reads=[b_rdp], writes=[b_rbh])
                    P.tt("dve", rbh[dr:dr + 1, :], rdp[dr:dr + 1, :], rbh[dr:dr + 1, :], ALU.subtract, reads=[b_rdp, b_rbh], writes=[b_rbh])

                    def part2(h=h, hp=hp, par=par, dr=dr, orow=orow, ob=ob, rdp=rdp, b_rdp=b_rdp, rbp=rbp, b_rbp=b_rbp, t2p=t2p, b_t2p=b_t2p,
                              rbh=rbh, b_rbh=b_rbh):
                        P.mm(ps[2 + par][orow:orow + 64, 0:256], selb[dr:dr + 33, 0:64], rbh[dr:dr + 33, :], True, True,
                             reads=[b_selb, b_rbh], writes=[bps[2 + par]])
                        P.act(rbp[orow:orow + 64, :], ps[2 + par][orow:orow + 64, 0:256], AF.Copy, reads=[bps[2 + par]], writes=[b_rbp])
                        P.tt("dve", t2p[orow:orow + 64, :], ps[ob][orow:orow + 64, 0:256], rbp[orow:orow + 64, :], ALU.mult,
                             reads=[bps[ob], b_rbp], writes=[b_t2p])
                        P.tt("pool", og[orow:orow + 64, hp, :], t2p[orow:orow + 64, :], GT[sl][0][orow:orow + 64, hp, :], ALU.mult,
                             reads=[b_t2p, GT[sl][1]], writes=[b_og])
                    deferred.append((idx + min(4, len(tiles) - 1), part2))
                    if h % 2 == 1 and hooks:
                        deferred.append((idx + 1, hooks.pop(0)))
                        deferred.sort(key=lambda d: d[0])

            LA = 1
            for idx in range(len(W) + LA):
                if idx < len(W):
                    emit_scores(*W[idx])
                if idx >= LA:
                    emit_pv(W[idx - LA][0], W[idx - LA][1], idx)
                while deferred and deferred[0][0] <= idx:
                    deferred.pop(0)[1]()
            while deferred:
                deferred.pop(0)[1]()
            for t in range(2):
                n = 2 * b + t
                sx = C["cnt"] % 2
                C["cnt"] += 1
                xt, bxt = C["xt"][sx], C["bxt"][sx]
                P.dma("sp", xt[:], src[s, n * 128:(n + 1) * 128, :], reads=[bsrc[s]], writes=[bxt])
                for hh in range(2):
                    for k in range(8):
                        P.mm(ps[2 + hh][:], og[:, k, t * 128:(t + 1) * 128], woutc[:, k, hh * 512:(hh + 1) * 512], k == 0, k == 7,
                             reads=[b_og, b_woutc], writes=[bps[2 + hh]], inc=(k == 7))
                E["post"](C, s, (2, 3), xt, bxt, dst, bdst[s], n)

        for s in range(nseq):
            proj(s, 0)
            if nblk > 1:
                proj(s, 1)
            for b in range(nblk):
                if b >= 1 and b + 1 < nblk:
                    proj(s, b + 1, do_prenorm=False)
                hk = []
                if b + 2 < nblk:
                    for t in range(2):
                        hk += list(E["prenorm_parts"](C, s, src, bsrc[s], 2 * (b + 2) + t, hT, b_hT, t * 128))
                attn(s, b, hooks=hk)
        E["T_barrier"]()


def _common_inputs(p, L):
    f32 = np.float32
    m = {}
    def T8(a):
        return np.ascontiguousarray(a.reshape(a.shape[0], -1, 128).transpose(0, 2, 1)).astype(f32)
    m["npreT"] = T8(p["norm_pre"])
    m["npostT"] = T8(p["norm_post"])
    m["wmod"] = np.ascontiguousarray(p["w_mod"], dtype=f32)
    m["bmodT"] = T8(p["b_mod"])
    m["w_in_ab"] = np.ascontiguousarray(p["w_in_ab"], dtype=f32)
    m["w_out_ab"] = np.ascontiguousarray(p["w_out_ab"], dtype=f32)
    m["w_glu"] = np.ascontiguousarray(p["ssm_w_glu"], dtype=f32)
    m["ssm_d"] = np.ascontiguousarray(p["ssm_d"], dtype=f32)
    sm3, rep3, bl, cl = [], [], [], []
    for j in range(2):
        a, b, c, d = _s5_layout(p["ssm_a_re"][j], p["ssm_a_im"][j], p["ssm_log_step"][j], p["ssm_b_re"][j],
                                p["ssm_b_im"][j], p["ssm_c_re"][j], p["ssm_c_im"][j])
        sm3.append(a); rep3.append(b); bl.append(c); cl.append(d)
    m["s5_sm3"] = np.stack(sm3).astype(f32)
    m["s5_rep3"] = np.stack(rep3).astype(f32)
    m["s5_bl"] = np.stack(bl).astype(f32)
    m["s5_cl"] = np.stack(cl).astype(f32)
    m["w_in_c"] = np.ascontiguousarray(p["w_in_c"], dtype=f32)
    m["w_out_c"] = np.ascontiguousarray(p["w_out_c"], dtype=f32)
    zs = []
    for j in range(2):
        z, mask = _na_layout(np.asarray(p["na_rel_bias"][j], dtype=f32))
        zs.append(z)
    m["zg"] = np.stack(zs).astype(f32)
    m["zmask"] = mask
    m["ident"] = np.eye(128, dtype=f32)
    m["jmat"] = np.eye(128, dtype=f32)[::-1].copy()
    m["rot"] = _rot_tables(L)
    dt, gam, cd = _ret_consts()
    m["dtab"] = dt
    m["gam"] = np.ascontiguousarray(gam.transpose(0, 1, 2))
    m["cdec"] = cd
    m["jt"] = np.broadcast_to(np.arange(128, dtype=f32)[None, :], (128, 128)).copy()
    sel = np.zeros((128, 64), f32)
    sel[[0, 32, 64, 96], :] = 1.0
    m["sel"] = sel
    return m


_PROG_CACHE = {}


def run_cores(xs_per_core, cs_per_core, params, layers):
    nseq, L, _ = xs_per_core[0].shape
    key = (nseq, L, tuple(layers))
    if key not in _PROG_CACHE:
        _PROG_CACHE[key] = build(nseq, L, list(layers))
    P = _PROG_CACHE[key]
    com = _common_inputs(params, L)
    in_maps = []
    for x, c in zip(xs_per_core, cs_per_core):
        m = dict(com)
        m["x_in"] = np.ascontiguousarray(x, dtype=np.float32)
        m["cT"] = np.ascontiguousarray(c.reshape(nseq, 8, 128).transpose(2, 1, 0), dtype=np.float32)
        in_maps.append(m)
    res = run_bass_kernel_spmd(P.nc, in_maps, core_ids=list(range(len(in_maps))))
    return [np.asarray(r["y_out"]) for r in res.results]


def kernel(**inputs):
    p = {k: np.asarray(v) for k, v in inputs.items()}
    xp, xsamp = p["x_prompt"], p["x_sample"]
    cp, cs = p["c_prompt"], p["c_sample"]
    seqs = [xp[i] for i in range(4)] + [xsamp[i] for i in range(8)]
    cvs = [cp[i] for i in range(4)] + [cs[i] for i in range(8)]
    slots = [(c, 8 + c if c < 4 else c) for c in range(8)]
    xs_pc = [np.stack([seqs[a], seqs[b]]) for a, b in slots]
    cs_pc = [np.stack([cvs[a], cvs[b]]) for a, b in slots]
    outs = run_cores(xs_pc, cs_pc, p, [0, 1, 2, 3])
    res = [None] * 12
    for c, (a, b) in enumerate(slots):
        res[a] = outs[c][0]
        if c < 4:
            res[b] = outs[c][1]
    y_prompt = np.stack(res[0:4]).astype(np.float32)
    y_sample = np.stack(res[4:12]).astype(np.float32)
    return (y_prompt, y_sample)
```

```python
import contextlib
import math
import os
KSTOP = os.environ.get('KSTOP', 'all')
S5E = os.environ.get('S5E', 'dve')
PNE = os.environ.get('PNE', 'act')
import numpy as np
import concourse.bass as bass
import concourse.mybir as mybir
from concourse.bass_utils import run_bass_kernel_spmd

F32 = mybir.dt.float32
BF16 = mybir.dt.bfloat16
I32 = mybir.dt.int32
ALU = mybir.AluOpType
AF = mybir.ActivationFunctionType

D = 1024
EPS = 1e-6
TWO_PI = 2.0 * math.pi


class Buf:
    __slots__ = ("name", "w", "r")

    def __init__(self, name):
        self.name = name
        self.w = []
        self.r = []


class Eng:
    def __init__(self, name):
        self.name = name
        self.ops = []
        self.known = {}
        self.sem = None
        self.cnt = 0
        self.pending = False
        self.dsems = []
        self.dvals = []
        self.dptr = 0
        self.own = set()


EPOCH = 30000
NDSEM = 10


class Tracker:
    def __init__(self, nc):
        self.nc = nc
        self.engs = {n: Eng(n) for n in ("pe", "act", "dve", "pool", "sp")}
        self.sems = []
        for e in self.engs.values():
            if e.name != "sp":
                e.sem = self._newsem(e.name)
                e.own.add(e.sem)
        self.n_ops = 0

    def _newsem(self, nm):
        h = self.nc.alloc_semaphore(name=f"{nm}_{len(self.sems)}")
        self.sems.append(h)
        return len(self.sems) - 1

    def _wait(self, e, ev):
        s, v = ev
        if e.known.get(s, 0) >= v:
            return
        if s in e.own:
            if e.name == "pe":
                return
            if s == e.sem and v > e.cnt:
                return
        e.known[s] = v
        sem = self.sems[s]
        e.ops.append(lambda q, sem=sem, v=v: q.wait_ge(sem, v))

    def _deps(self, e, reads, writes):
        for b in reads:
            for ev in b.w:
                self._wait(e, ev)
        for b in writes:
            for ev in b.w:
                self._wait(e, ev)
            for ev in b.r:
                self._wait(e, ev)

    def _commit(self, ev, reads, writes):
        for b in reads:
            for i, (s0, v0) in enumerate(b.r):
                if s0 == ev[0]:
                    b.r[i] = (s0, max(v0, ev[1]))
                    break
            else:
                b.r.append(ev)
        for b in writes:
            b.w = [ev]
            b.r = []

    def op(self, eng, fn, reads=(), writes=(), inc=True):
        e = self.engs[eng]
        self.n_ops += 1
        self._deps(e, reads, writes)
        if e.cnt >= EPOCH and inc and not e.pending:
            e.sem = self._newsem(e.name)
            e.cnt = 0
            e.own.add(e.sem)
        if inc:
            e.cnt += 1
            sem = self.sems[e.sem]
            e.ops.append(lambda q, fn=fn, sem=sem: fn(q).then_inc(sem, 1))
            ev = (e.sem, e.cnt)
            e.pending = False
        else:
            e.ops.append(lambda q, fn=fn: fn(q))
            ev = (e.sem, e.cnt + 1)
            e.pending = True
        self._commit(ev, reads, writes)

    def dma(self, eng, out, in_, reads=(), writes=()):
        e = self.engs[eng]
        self.n_ops += 1
        self._deps(e, reads, writes)
        if len(e.dsems) < NDSEM:
            e.dsems.append(self._newsem(e.name + "d"))
            e.dvals.append(0)
            k = len(e.dsems) - 1
        else:
            k = e.dptr
            e.dptr = (e.dptr + 1) % NDSEM
            self._wait(e, (e.dsems[k], e.dvals[k]))
            if e.dvals[k] >= EPOCH * 16:
                e.dsems[k] = self._newsem(e.name + "d")
                e.dvals[k] = 0
        e.dvals[k] += 16
        sem = self.sems[e.dsems[k]]
        e.ops.append(lambda q, out=out, in_=in_, sem=sem: q.dma_start(out=out, in_=in_).then_inc(sem, 16))
        ev = (e.dsems[k], e.dvals[k])
        self._commit(ev, reads, writes)
        return ev

    def finish(self):
        sp = self.engs["sp"]
        for e in self.engs.values():
            for k in range(len(e.dsems)):
                self._wait(sp, (e.dsems[k], e.dvals[k]))
            if e.sem is not None and e.cnt > 0:
                self._wait(sp, (e.sem, e.cnt))

    def replay(self):
        nc = self.nc
        E = self.engs
        with nc.Block() as block:
            @block.tensor
            def _(q):
                for f in E["pe"].ops:
                    f(q)

            @block.scalar
            def _(q):
                for f in E["act"].ops:
                    f(q)

            @block.vector
            def _(q):
                for f in E["dve"].ops:
                    f(q)

            @block.gpsimd
            def _(q):
                for f in E["pool"].ops:
                    f(q)

            @block.sync
            def _(q):
                for f in E["sp"].ops:
                    f(q)


RET_H = 4
NA_H = 16
GW = 64


def _ret_consts():
    f32 = np.float32
    h = np.arange(RET_H, dtype=f32)
    log_g = np.log1p(-np.exp2(-5.0 - h)).astype(f32)
    pos = np.arange(128, dtype=f32)
    dt = np.exp(np.abs(pos[:, None] - pos[None, :])[:, None, :] * log_g[None, :, None]).astype(f32)
    gam = np.zeros((128, 4, RET_H), f32)
    gam[:, 0, :] = np.exp(pos[:, None] * log_g[None])
    gam[:, 1, :] = np.exp((127.0 - pos)[:, None] * log_g[None])
    gam[:, 2, :] = np.exp((128.0 - pos)[:, None] * log_g[None])
    gam[:, 3, :] = np.exp((pos + 1.0)[:, None] * log_g[None])
    cdec = np.exp(128.0 * log_g).astype(f32)
    cd = np.broadcast_to(cdec[None, :], (128, RET_H)).copy()
    return dt, gam, cd


def _rot_tables(L):
    f32 = np.float32
    inv = (10000.0 ** (-np.arange(0, 128, 2, dtype=f32) / 128.0)).astype(f32)
    ang = (np.arange(L, dtype=f32)[:, None] * inv[None, :]).astype(f32)
    c = np.cos(ang).astype(f32)
    s = np.sin(ang).astype(f32)
    rq = np.zeros((L, 2, 128), f32)
    rq[:, 0, :64] = c
    rq[:, 0, 64:] = c
    rq[:, 1, :64] = -s
    rq[:, 1, 64:] = s
    rk = (rq * f32(128.0 ** -0.5)).astype(f32)
    return np.stack([rq, rk], axis=1).copy()


def _na_layout(relb):
    a = np.arange(2)[:, None, None, None]
    k = np.arange(64)[None, :, None, None]
    m = np.arange(-7, 9)[None, None, :, None]
    c = np.arange(64)[None, None, None, :]
    dr = np.clip(a - m + 7, 0, 14)
    dc = np.clip(k - c + 15, 0, 30)
    dr_b, dc_b = np.broadcast_arrays(dr, dc)
    z = relb[:, dr_b, dc_b]
    z = z.reshape(16, 128, 16 * 64).astype(np.float32)
    cs = np.clip(np.arange(64) - 8, 0, 48)
    kk = np.arange(64)[:, None]
    valid = (kk >= cs[None, :]) & (kk < cs[None, :] + 16)
    mask = np.where(valid, 0.0, -30000.0).astype(np.float32)
    mask = np.broadcast_to(mask[None, :, None, :], (2, 64, 16, 64)).reshape(128, 1024).copy()
    return z, mask


def _s5_layout(a_re, a_im, ls, b_re, b_im, c_re, c_im):
    f32 = np.float32
    def sm(a):
        return a.reshape(2, 16, 2, 64).transpose(0, 2, 3, 1).reshape(2, 128, 16).astype(f32)
    lsx = np.broadcast_to(ls[:, :, None], (2, 32, 64))
    sm3 = np.stack([sm(a_re), sm(a_im), sm(lsx)], axis=1).copy()
    rep3 = np.stack([a_re.reshape(2, 2048), a_im.reshape(2, 2048), lsx.reshape(2, 2048)], axis=1).astype(f32).copy()
    bl = np.zeros((2, 2, 128, 16, 2, 64), f32)
    cl = np.zeros((2, 2, 128, 16, 2, 16), f32)
    for s in range(16):
        for g1 in range(2):
            g = 2 * s + g1
            r0 = 32 * (s % 4) + 16 * g1
            for ri, b in enumerate((b_re, b_im)):
                bl[:, ri, r0:r0 + 16, s, g1, :] = b[:, g].transpose(0, 2, 1)
            for ri, c in enumerate((c_re, c_im)):
                cl[:, ri, 64 * g1:64 * g1 + 64, s, g1, :] = c[:, g].transpose(0, 2, 1)
    return sm3, rep3, bl.reshape(2, 2, 128, 16 * 128), cl.reshape(2, 2, 128, 16 * 32)


class Prog:
    def __init__(self, nseq, L, layers):
        self.nseq, self.L, self.layers = nseq, L, layers
        self.nch = L // 128
        nc = self.nc = bass.Bass("TRN2", target_bir_lowering=False)
        self.T = Tracker(nc)
        self.es = contextlib.ExitStack()
        self.dram = {}
        self.bufs = {}

    def sbt(self, name, shape, dt=F32):
        self._uid = getattr(self, '_uid', 0) + 1
        return self.nc.sbuf_tensor(f"{name}_u{self._uid}", list(shape), dt)

    def din(self, name, shape, dt=F32):
        t = self.nc.dram_tensor(name, list(shape), dt, kind="ExternalInput").ap()
        self.dram[name] = t
        return t

    def dout(self, name, shape, dt=F32):
        t = self.nc.dram_tensor(name, list(shape), dt, kind="ExternalOutput").ap()
        self.dram[name] = t
        return t

    def dscr(self, name, shape, dt=F32):
        t = self.nc.dram_tensor(name, list(shape), dt, kind="Internal").ap()
        self.dram[name] = t
        return t

    def sb(self, name, shape, dt=F32):
        t = self.es.enter_context(self.sbt(name, list(shape), dt))
        b = Buf(name)
        return t, b

    def ps(self, name):
        t = self.es.enter_context(self.nc.psum_tensor(name, [128, 512], F32))
        return t, Buf(name)

    def mm(self, out, lhsT, rhs, start, stop, reads, writes, inc=True):
        self.T.op("pe", lambda q: q.matmul(out, lhsT=lhsT, rhs=rhs, start=start, stop=stop),
                  reads=reads, writes=writes, inc=inc)

    def act(self, out, in_, func, reads, writes, scale=1.0, bias=None, accum=None):
        kw = {}
        if bias is not None:
            kw["bias"] = bias
        if accum is not None:
            kw["accum_out"] = accum
        self.T.op("act", lambda q: q.activation(out=out, in_=in_, func=func, scale=scale, **kw),
                  reads=reads, writes=writes)

    def tt(self, eng, out, in0, in1, op, reads, writes):
        if eng == "pool" and getattr(self, "pool_to_dve", False):
            eng = "dve"
        self.T.op(eng, lambda q: q.tensor_tensor(out=out, in0=in0, in1=in1, op=op), reads=reads, writes=writes)

    def ts(self, eng, out, in0, s1, s2, op0, op1, reads, writes):
        if s2 is None:
            self.T.op(eng, lambda q: q.tensor_scalar(out=out, in0=in0, scalar1=s1, scalar2=None, op0=op0),
                      reads=reads, writes=writes)
        else:
            self.T.op(eng, lambda q: q.tensor_scalar(out=out, in0=in0, scalar1=s1, scalar2=s2, op0=op0, op1=op1),
                      reads=reads, writes=writes)

    def stt(self, eng, out, in0, scalar, in1, op0, op1, reads, writes):
        self.T.op(eng, lambda q: q.scalar_tensor_tensor(out=out, in0=in0, scalar=scalar, in1=in1, op0=op0, op1=op1),
                  reads=reads, writes=writes)

    def cp(self, eng, out, in_, reads, writes):
        self.T.op(eng, lambda q: q.tensor_copy(out=out, in_=in_), reads=reads, writes=writes)

    def memset(self, eng, ap, val, writes):
        self.T.op(eng, lambda q: q.memset(ap, val), reads=(), writes=writes)

    def dma(self, eng, out, in_, reads=(), writes=()):
        return self.T.dma(eng, out, in_, reads=reads, writes=writes)


def build(nseq, L, layers):
    P = Prog(nseq, L, layers)
    nc, T = P.nc, P.T
    nch = L // 128
    NL = 4
    x_in = P.din("x_in", [nseq, L, D])
    y_out = P.dout("y_out", [nseq, L, D])
    cT = P.din("cT", [128, 8, nseq])
    npreT = P.din("npreT", [NL, 128, 8])
    npostT = P.din("npostT", [NL, 128, 8])
    wmod = P.din("wmod", [NL, D, 3 * D])
    bmodT = P.din("bmodT", [NL, 128, 24])
    w_in_ab = P.din("w_in_ab", [2, D, 3072])
    w_out_ab = P.din("w_out_ab", [2, D, D])
    w_glu = P.din("w_glu", [2, 512, 512])
    ssm_d = P.din("ssm_d", [2, 512])
    s5_sm3 = P.din("s5_sm3", [2, 2, 3, 128, 16])
    s5_rep3 = P.din("s5_rep3", [2, 2, 3, 2048])
    s5_bl = P.din("s5_bl", [2, 2, 2, 128, 2048])
    s5_cl = P.din("s5_cl", [2, 2, 2, 128, 512])
    w_in_c = P.din("w_in_c", [2, D, 4096])
    w_out_c = P.din("w_out_c", [2, D, D])
    zg = P.din("zg", [2, 16, 128, 1024])
    zmask = P.din("zmask", [128, 1024])
    ident_d = P.din("ident", [128, 128])
    jmat_d = P.din("jmat", [128, 128])
    rot_d = P.din("rot", [L, 2, 2, 128])
    dtab_d = P.din("dtab", [128, 4, 128])
    gam_d = P.din("gam", [128, 4, 4])
    cdec_d = P.din("cdec", [128, 4])
    jt_d = P.din("jt", [128, 128])
    sel_d = P.din("sel", [128, 64])
    xs = [P.dscr("xsA", [nseq, L, D]), P.dscr("xsB", [nseq, L, D])]
    OPs_ = P.dscr("OPs", [nseq, L, 512])
    YPs_ = P.dscr("YPs", [nseq, L, 512])
    SGs_ = P.dscr("SGs", [nseq, L, 1024])
    QTs_ = P.dscr("QTs", [nseq, nch, 128, 512], BF16)
    KRs_ = P.dscr("KRs", [nseq, L, 512], BF16)
    VBs_ = P.dscr("VBs", [nseq, L, 512], BF16)
    UBs_ = P.dscr("UBs", [nseq, L, 512], BF16)
    SCRB = [[Buf(f"{n}{i}") for n in "OP YP SG QT KR VB UB".split()] for i in range(nseq)]
    b_xs = [[Buf(f"xs{i}_{s}") for s in range(nseq)] for i in range(2)]
    b_yout = [Buf(f"yout{s}") for s in range(nseq)]
    b_xin = Buf("xin")

    ident_bf, b_ident = P.sb("ident_bf", [128, 128], BF16)
    jmat_bf, b_jmat = P.sb("jmat_bf", [128, 128], BF16)
    identf, b_identf = P.sb("identf", [128, 128])
    epsT, b_eps = P.sb("epsT", [128, 1])
    ss, b_ss = P.sb("ss", [128, 4])
    sd, b_sd = P.sb("sd", [128, 4])
    rstd, b_rstd = P.sb("rstd", [128, 4])
    scT, b_scT = P.sb("scT", [128, 8, nseq])
    cTs, b_cTs = P.sb("cTs", [128, 8, nseq])
    modT, b_modT = P.sb("modT", [128, 24, nseq])
    gsT, b_gsT = P.sb("gsT", [128, 8, nseq])
    ggT, b_ggT = P.sb("ggT", [128, 8, nseq])
    ggrow = [P.sb(f"ggrow{s}", [128, 1024]) for s in range(nseq)]
    vecs, b_vecs = P.sb("vecs", [128, 40])
    psb = [P.ps(f"ps{i}") for i in range(8)]
    ps = [p[0] for p in psb]
    bps = [p[1] for p in psb]

    P.dma("sp", identf[:], ident_d, writes=[b_identf])
    P.dma("pool", ident_bf[:], ident_d, writes=[b_ident])
    P.dma("pool", jmat_bf[:], jmat_d, writes=[b_jmat])
    P.memset("pool", epsT[:], EPS, writes=[b_eps])
    P.dma("sp", cTs[:], cT, writes=[b_cTs])
    P.act(scT[:], cTs[:], AF.Sigmoid, reads=[b_cTs], writes=[b_scT])
    P.tt("dve", scT[:], scT[:], cTs[:], ALU.mult, reads=[b_scT, b_cTs], writes=[b_scT])

    def rstd_from_ss(col, n_feat):
        P.act(sd[:, col:col + 1], ss[:, col:col + 1], AF.Sqrt, reads=[b_ss, b_eps], writes=[b_sd],
              scale=1.0 / n_feat, bias=epsT[:, 0:1])
        P.T.op("dve", lambda q: q.reciprocal(out=rstd[:, col:col + 1], in_=sd[:, col:col + 1]),
               reads=[b_sd], writes=[b_rstd])

    def adaln(li, st_unused):
      with contextlib.ExitStack() as st:
        wm = [st.enter_context(P.sbt(f"wm{j}_{li}", [128, 8, 128], F32)) for j in range(2)]
        bwm = [Buf("wm0"), Buf("wm1")]
        gbl = st.enter_context(P.sbt(f"gbl_{li}", [128, 8, 128], F32))
        b_gbl = Buf("gbl")
        P.dma("sp", vecs[:, 0:8], npreT[li], writes=[b_vecs])
        P.dma("sp", vecs[:, 8:16], npostT[li], writes=[b_vecs])
        P.dma("sp", vecs[:, 16:40], bmodT[li], writes=[b_vecs])
        wsrc = wmod[li].rearrange("(k p) n -> p k n", p=128)
        for j in range(24):
            sl = j % 2
            P.dma("sp", wm[sl][:], wsrc[:, :, j * 128:(j + 1) * 128], writes=[bwm[sl]])
            for k in range(8):
                P.mm(ps[7][:, j * nseq:(j + 1) * nseq], wm[sl][:, k, :], scT[:, k, :], k == 0, k == 7,
                     reads=[bwm[sl], b_scT], writes=[bps[7]], inc=(k == 7))
        psv = ps[7][:, 0:24 * nseq].rearrange("p (j s) -> p j s", s=nseq)
        P.tt("dve", modT[:], psv, vecs[:, 16:40].unsqueeze(2).to_broadcast([128, 24, nseq]), ALU.add,
             reads=[bps[7], b_vecs], writes=[b_modT])
        P.ts("dve", gsT[:], modT[:, 8:16, :], 1.0, None, ALU.add, None, reads=[b_modT], writes=[b_gsT])
        P.tt("dve", gsT[:], gsT[:], vecs[:, 0:8].unsqueeze(2).to_broadcast([128, 8, nseq]), ALU.mult,
             reads=[b_gsT, b_vecs], writes=[b_gsT])
        P.tt("dve", ggT[:], modT[:, 16:24, :], vecs[:, 8:16].unsqueeze(2).to_broadcast([128, 8, nseq]), ALU.mult,
             reads=[b_modT, b_vecs], writes=[b_ggT])
        for s in range(nseq):
            P.cp("dve", gbl[:], ggT[:, :, s:s + 1].to_broadcast([128, 8, 128]), reads=[b_ggT], writes=[b_gbl])
            for c in range(8):
                bk = 5 + c // 4
                P.mm(ps[bk][:, (c % 4) * 128:(c % 4 + 1) * 128], gbl[:, c, :], identf[:], True, True,
                     reads=[b_gbl, b_identf], writes=[bps[bk]], inc=(c % 4 == 3))
            P.cp("dve", ggrow[s][0][:, 0:512], ps[5][:], reads=[bps[5]], writes=[ggrow[s][1]])
            P.act(ggrow[s][0][:, 512:1024], ps[6][:], AF.Copy, reads=[bps[6]], writes=[ggrow[s][1]])

    def make_common(st, tag):
        C = {}
        C["xt"] = [st.enter_context(P.sbt(f"xt{j}_{tag}", [128, 1024], F32)) for j in range(2)]
        C["bxt"] = [Buf("xt0"), Buf("xt1")]
        C["xn"] = st.enter_context(P.sbt(f"xn_{tag}", [128, 1024], BF16))
        C["bxn"] = Buf("xn")
        C["junk"] = st.enter_context(P.sbt(f"junk_{tag}", [128, 1024], BF16))
        C["bjunk"] = Buf("junk")
        C["yt"] = st.enter_context(P.sbt(f"yt_{tag}", [128, 1024], F32))
        C["byt"] = Buf("yt")
        C["cnt"] = 0
        return C

    def prenorm_parts(C, s, src, bsrc, n, hT, bhT, col0):
        sl = C["cnt"] % 2
        C["cnt"] += 1
        xt, bxt = C["xt"][sl], C["bxt"][sl]

        def p1():
            P.dma("sp", xt[:], src[s, n * 128:(n + 1) * 128, :], reads=[bsrc], writes=[bxt])
            P.act(C["yt"][:], xt[:], AF.Square, reads=[bxt], writes=[C["byt"]])
            P.T.op("dve", lambda q, yt_=C["yt"]: q.reduce_sum(out=ss[:, 0:1], in_=yt_[:], axis=mybir.AxisListType.X),
                   reads=[C["byt"]], writes=[b_ss])
            P.act(sd[:, 0:1], ss[:, 0:1], AF.Sqrt, reads=[b_ss, b_eps], writes=[b_sd], scale=1.0 / D, bias=epsT[:, 0:1])

        def p2():
            P.T.op("dve", lambda q: q.reciprocal(out=rstd[:, 0:1], in_=sd[:, 0:1]), reads=[b_sd], writes=[b_rstd])
            P.act(C["xn"][:], xt[:], AF.Copy, reads=[bxt, b_rstd], writes=[C["bxn"]], scale=rstd[:, 0:1])
            for b in range(2):
                for j in range(4):
                    k = 4 * b + j
                    P.mm(ps[b][:, j * 128:(j + 1) * 128], C["xn"][:, k * 128:(k + 1) * 128], ident_bf[:], True, True,
                         reads=[C["bxn"], b_ident], writes=[bps[b]], inc=(j == 3))

        def p3():
            for b in range(2):
                for j in range(4):
                    k = 4 * b + j
                    o = hT[:, k, col0:col0 + 128]
                    i_ = ps[b][:, j * 128:(j + 1) * 128]
                    if b == 0 or P.pne == 'act':
                        P.act(o, i_, AF.Identity, reads=[bps[b], b_gsT, b_modT], writes=[bhT],
                              scale=gsT[:, k, s:s + 1], bias=modT[:, k, s:s + 1])
                    else:
                        P.ts("dve", o, i_, gsT[:, k, s:s + 1], modT[:, k, s:s + 1], ALU.mult, ALU.add,
                             reads=[bps[b], b_gsT, b_modT], writes=[bhT])
        return p1, p2, p3

    def prenorm(C, s, src, bsrc, n, hT, bhT, col0):
        p1, p2, p3 = prenorm_parts(C, s, src, bsrc, n, hT, bhT, col0)
        p1()
        p2()
        p3()

    def post(C, s, ypb, xt, bxt, dst, bdst, n):
        yt, byt = C["yt"], C["byt"]
        for h in range(2):
            P.act(yt[:, h * 512:(h + 1) * 512], ps[ypb[h]][:], AF.Square, reads=[bps[ypb[h]]], writes=[byt])
        P.T.op("dve", lambda q, yt_=yt: q.reduce_sum(out=ss[:, 3:4], in_=yt_[:], axis=mybir.AxisListType.X),
               reads=[byt], writes=[b_ss])
        rstd_from_ss(3, D)
        for h in range(2):
            P.act(yt[:, h * 512:(h + 1) * 512], ps[ypb[h]][:], AF.Copy, reads=[bps[ypb[h]], b_rstd], writes=[byt],
                  scale=rstd[:, 3:4])
        P.tt("dve", yt[:], yt[:], ggrow[s][0][:], ALU.mult, reads=[byt, ggrow[s][1]], writes=[byt])
        P.tt("pool", yt[:], yt[:], xt[:], ALU.add, reads=[byt, bxt], writes=[byt])
        P.dma("pool", dst[s, n * 128:(n + 1) * 128, :], yt[:], reads=[byt], writes=[bdst])

    def sincos(st, tag, phi, bphi, shape, out_sin, out_cos, bout):
        tf = st.enter_context(P.sbt(f"sc_tf_{tag}", shape, F32))
        ti = st.enter_context(P.sbt(f"sc_ti_{tag}", shape, I32))
        btf, bti = Buf("tf"), Buf("ti")
        for shift, o in ((0.0, out_sin), (0.5 * math.pi, out_cos)):
            P.ts("dve", tf[:], phi, shift, 1.0 / TWO_PI, ALU.add, ALU.mult, reads=[bphi], writes=[btf])
            P.cp("dve", ti[:], tf[:], reads=[btf], writes=[bti])
            P.cp("dve", tf[:], ti[:], reads=[bti], writes=[btf])
            P.stt("dve", tf[:], tf[:], -TWO_PI, phi, ALU.mult, ALU.add, reads=[btf, bphi], writes=[btf])
            P.ts("dve", tf[:], tf[:], shift, 0.999999, ALU.add, ALU.mult, reads=[btf], writes=[btf])
            P.act(o, tf[:], AF.Sin, reads=[btf], writes=[bout])

    def even_layer(li, src, bsrc, dst, bdst):
        j = li // 2
        P.pool_to_dve = (os.environ.get('P2D', '0') == '1')
        P.pne = 'act'
        with contextlib.ExitStack() as st:
            def S(name, shape, dt=F32):
                return st.enter_context(P.sbt(f"{name}_e{li}", list(shape), dt)), Buf(name)
            adaln(li, st)
            C = make_common(st, f"e{li}")
            win, b_win = S("win", [128, 8, 3072], BF16)
            wout, b_wout = S("wout", [128, 8, 1024], BF16)
            wglu, b_wglu = S("wglu", [128, 4, 512], BF16)
            drow, b_drow = S("drow", [128, 512])
            dtab, b_dtab = S("dtab", [128, 4, 128])
            gam, b_gam = S("gam", [128, 4, 4])
            cdec, b_cdec = S("cdec", [128, 4])
            jt, b_jt = S("jt", [128, 128])
            wsrc = w_in_ab[j].rearrange("(k p) n -> p k n", p=128)
            for k in range(8):
                P.dma("pool", win[:, k, :], wsrc[:, k, :], writes=[b_win])
            P.dma("pool", wout[:], w_out_ab[j].rearrange("(k p) n -> p k n", p=128), writes=[b_wout])
            P.dma("pool", wglu[:], w_glu[j].rearrange("(k p) n -> p k n", p=128), writes=[b_wglu])
            P.dma("sp", drow[:], ssm_d[j].partition_broadcast(128), writes=[b_drow])
            P.dma("sp", dtab[:], dtab_d, writes=[b_dtab])
            P.dma("sp", gam[:], gam_d, writes=[b_gam])
            P.dma("sp", cdec[:], cdec_d, writes=[b_cdec])
            P.dma("sp", jt[:], jt_d, writes=[b_jt])
            WB = [S(f"WB{r}", [128, 2048], BF16) for r in range(2)]
            WC = [S(f"WC{r}", [128, 512], BF16) for r in range(3)]
            COS, b_COS = S("COS", [128, 16, 128])
            SIN, b_SIN = S("SIN", [128, 16, 128])
            RHO0, b_RHO0 = S("RHO0", [128, 16, 128])
            Gc, b_Gc = S("Gc", [128, 2, 16])
            cin = [S(f"cin{i}", [128, 2, 16]) for i in range(2)]
            stf, b_stf = S("stf", [128, 512])
            stbf, b_stbf = S("stbf", [128, 512], BF16)
            hn, b_hn = S("hn", [128, 16])
            lst, b_lst = S("lst", [128, 8, 16])
            lastb = [S(f"lastb{i}", [128, 2, 16]) for i in range(2)]

            def s5_setup(d):
                with contextlib.ExitStack() as s2:
                    def S2(name, shape, dt=F32):
                        return s2.enter_context(P.sbt(f"{name}_e{li}d{d}", list(shape), dt)), Buf(name)
                    sm, b_sm = S2("sm", [128, 3, 16])
                    P.dma("sp", sm[:], s5_sm3[j, d].rearrange("t p s -> p t s"), writes=[b_sm])
                    dl, b_dl = S2("dl", [128, 16])
                    zr, b_zr = S2("zr", [128, 16])
                    zi, b_zi = S2("zi", [128, 16])
                    rho, b_rho = S2("rho", [128, 16])
                    P.act(dl[:], sm[:, 2, :], AF.Exp, reads=[b_sm], writes=[b_dl])
                    P.tt("dve", zr[:], sm[:, 0, :], dl[:], ALU.mult, reads=[b_sm, b_dl], writes=[b_zr])
                    P.tt("dve", zi[:], sm[:, 1, :], dl[:], ALU.mult, reads=[b_sm, b_dl], writes=[b_zi])
                    P.act(rho[:], zr[:], AF.Exp, reads=[b_zr], writes=[b_rho])
                    for g4 in range(4):
                        with contextlib.ExitStack() as s4:
                            phi = s4.enter_context(P.sbt(f"phi_e{li}d{d}g{g4}", [128, 4, 128], F32))
                            b_phi = Buf("phi")
                            P.tt("dve", phi[:], zi[:, 4 * g4:4 * g4 + 4].unsqueeze(2).to_broadcast([128, 4, 128]),
                                 jt[:].unsqueeze(1).to_broadcast([128, 4, 128]), ALU.mult, reads=[b_zi, b_jt], writes=[b_phi])
                            sincos(s4, f"t{li}{d}{g4}", phi[:], b_phi, [128, 4, 128], SIN[:, 4 * g4:4 * g4 + 4, :],
                                   COS[:, 4 * g4:4 * g4 + 4, :], b_COS)
                            T_barrier()
                    P.cp("dve", RHO0[:], rho[:].unsqueeze(2).to_broadcast([128, 16, 128]), reads=[b_rho], writes=[b_RHO0])
                    P.memset("dve", RHO0[:, :, 0:1], 0.0, writes=[b_RHO0])
                    ph2, b_ph2 = S2("ph2", [128, 16])
                    sn2, b_sn2 = S2("sn2", [128, 2, 16])
                    P.ts("dve", ph2[:], zi[:], 128.0, None, ALU.mult, None, reads=[b_zi], writes=[b_ph2])
                    sincos(s2, f"g{li}{d}", ph2[:], b_ph2, [128, 16], sn2[:, 1, :], sn2[:, 0, :], b_sn2)
                    P.tt("dve", Gc[:], sn2[:], rho[:].unsqueeze(1).to_broadcast([128, 2, 16]), ALU.mult,
                         reads=[b_sn2, b_rho], writes=[b_Gc])
                    T_barrier()
                for cc in range(8):
                  with contextlib.ExitStack() as s3:
                    def S3(name, shape, dt=F32):
                        return s3.enter_context(P.sbt(f"{name}_e{li}d{d}c{cc}", list(shape), dt)), Buf(name)
                    rp, b_rp = S3("rp", [128, 3, 256])
                    P.dma("sp", rp[:], s5_rep3[j, d][:, cc * 256:(cc + 1) * 256].partition_broadcast(128), writes=[b_rp])
                    r_dl, b_r_dl = S3("r_dl", [128, 256])
                    r_zr, b_r_zr = S3("r_zr", [128, 256])
                    r_zi, b_r_zi = S3("r_zi", [128, 256])
                    r_rho, b_r_rho = S3("r_rho", [128, 256])
                    r_sn, b_r_sn = S3("r_sn", [128, 2, 256])
                    P.act(r_dl[:], rp[:, 2, :], AF.Exp, reads=[b_rp], writes=[b_r_dl])
                    P.tt("dve", r_zr[:], rp[:, 0, :], r_dl[:], ALU.mult, reads=[b_rp, b_r_dl], writes=[b_r_zr])
                    P.tt("dve", r_zi[:], rp[:, 1, :], r_dl[:], ALU.mult, reads=[b_rp, b_r_dl], writes=[b_r_zi])
                    P.act(r_rho[:], r_zr[:], AF.Exp, reads=[b_r_zr], writes=[b_r_rho])
                    sincos(s3, f"r{li}{d}{cc}", r_zi[:], b_r_zi, [128, 256], r_sn[:, 1, :], r_sn[:, 0, :], b_r_sn)
                    P.tt("dve", r_sn[:], r_sn[:], r_rho[:].unsqueeze(1).to_broadcast([128, 2, 256]), ALU.mult,
                         reads=[b_r_sn, b_r_rho], writes=[b_r_sn])
                    P.ts("dve", r_sn[:, 0, :], r_sn[:, 0, :], -1.0, None, ALU.add, None, reads=[b_r_sn], writes=[b_r_sn])
                    P.tt("dve", r_dl[:], rp[:, 0, :], rp[:, 0, :], ALU.mult, reads=[b_rp], writes=[b_r_dl])
                    P.tt("dve", r_zr[:], rp[:, 1, :], rp[:, 1, :], ALU.mult, reads=[b_rp], writes=[b_r_zr])
                    P.tt("dve", r_dl[:], r_dl[:], r_zr[:], ALU.add, reads=[b_r_dl, b_r_zr], writes=[b_r_dl])
                    P.T.op("dve", lambda q, r_dl=r_dl: q.reciprocal(out=r_dl[:], in_=r_dl[:]), reads=[b_r_dl], writes=[b_r_dl])
                    P.tt("dve", r_zr[:], r_sn[:, 0, :], rp[:, 0, :], ALU.mult, reads=[b_r_sn, b_rp], writes=[b_r_zr])
                    P.tt("dve", r_rho[:], r_sn[:, 1, :], rp[:, 1, :], ALU.mult, reads=[b_r_sn, b_rp], writes=[b_r_rho])
                    P.tt("dve", r_zr[:], r_zr[:], r_rho[:], ALU.add, reads=[b_r_zr, b_r_rho], writes=[b_r_zr])
                    P.tt("dve", r_zr[:], r_zr[:], r_dl[:], ALU.mult, reads=[b_r_zr, b_r_dl], writes=[b_r_zr])
                    P.tt("dve", r_zi[:], r_sn[:, 1, :], rp[:, 0, :], ALU.mult, reads=[b_r_sn, b_rp], writes=[b_r_zi])
                    P.tt("dve", r_rho[:], r_sn[:, 0, :], rp[:, 1, :], ALU.mult, reads=[b_r_sn, b_rp], writes=[b_r_rho])
                    P.tt("dve", r_zi[:], r_zi[:], r_rho[:], ALU.subtract, reads=[b_r_zi, b_r_rho], writes=[b_r_zi])
                    P.tt("dve", r_zi[:], r_zi[:], r_dl[:], ALU.mult, reads=[b_r_zi, b_r_dl], writes=[b_r_zi])
                    bl, b_bl = S3("bl", [128, 2, 256])
                    P.dma("sp", bl[:], s5_bl[j, d][:, :, cc * 256:(cc + 1) * 256].rearrange("r p n -> p r n"), writes=[b_bl])
                    P.tt("dve", r_dl[:], r_zr[:], bl[:, 0, :], ALU.mult, reads=[b_r_zr, b_bl], writes=[b_r_dl])
                    P.tt("dve", r_rho[:], r_zi[:], bl[:, 1, :], ALU.mult, reads=[b_r_zi, b_bl], writes=[b_r_rho])
                    P.tt("dve", WB[0][0][:, cc * 256:(cc + 1) * 256], r_dl[:], r_rho[:], ALU.subtract, reads=[b_r_dl, b_r_rho], writes=[WB[0][1]])
                    P.tt("dve", r_dl[:], r_zr[:], bl[:, 1, :], ALU.mult, reads=[b_r_zr, b_bl], writes=[b_r_dl])
                    P.tt("dve", r_rho[:], r_zi[:], bl[:, 0, :], ALU.mult, reads=[b_r_zi, b_bl], writes=[b_r_rho])
                    P.tt("dve", WB[1][0][:, cc * 256:(cc + 1) * 256], r_dl[:], r_rho[:], ALU.add, reads=[b_r_dl, b_r_rho], writes=[WB[1][1]])

                    T_barrier()
                with contextlib.ExitStack() as s2:
                    def S2(name, shape, dt=F32):
                        return s2.enter_context(P.sbt(f"{name}_e{li}d{d}x", list(shape), dt)), Buf(name)
                    cl, b_cl = S2("cl", [128, 2, 512])
                    P.dma("sp", cl[:], s5_cl[j, d].rearrange("r p n -> p r n"), writes=[b_cl])
                    P.cp("dve", WC[0][0][:], cl[:, 0, :], reads=[b_cl], writes=[WC[0][1]])
                    P.ts("dve", WC[1][0][:], cl[:, 0, :], -1.0, None, ALU.mult, None, reads=[b_cl], writes=[WC[1][1]])
                    P.ts("dve", WC[2][0][:], cl[:, 1, :], -1.0, None, ALU.mult, None, reads=[b_cl], writes=[WC[2][1]])
                    P.memset("dve", cin[0][0][:], 0.0, writes=[cin[0][1]])
                    T_barrier()

            import types

            def alloc_work(wk, passB):
                def W(name, shape, dt=F32):
                    return wk.enter_context(P.sbt(f"{name}_e{li}", list(shape), dt)), Buf(name)
                w = types.SimpleNamespace()
                w.g = [types.SimpleNamespace() for _ in range(2)]
                tmps = {nm: W(f"{nm}m", [128, 512]) for nm in ("tA", "tB", "tC", "tD")}
                Pk1 = [W(f"Pk_{k}", [128, 512], BF16) for k in range(4)]
                for i, g in enumerate(w.g):
                    for nm in ("wre", "wim", "sre", "sim"):
                        setattr(g, nm, W(f"{nm}{i}", [128, 512]))
                    for nm in ("tA", "tB", "tC", "tD"):
                        setattr(g, nm, tmps[nm])
                    g.Pk = Pk1
                w.uT = W("uT", [128, 4, 128], BF16)
                w.kf = W("kf", [128, 512], BF16)
                w.tC = W("tCr", [128, 512])
                if not passB:
                    w.hT = [W(f"hT{i}", [128, 8, 128], BF16) for i in range(2)]
                    w.rot = [W(f"rot{i}", [128, 2, 2, 128]) for i in range(2)]
                    w.rA = W("rA", [128, 512])
                    w.rB = W("rB", [128, 512])
                    w.qr = W("qr", [128, 512], BF16)
                    w.kr = [W("kr", [128, 512], BF16)]
                    w.vb = [W("vb", [128, 512], BF16)]
                    w.QT = [W("QT", [128, 512], BF16)]
                    w.KT = W("KT", [128, 512], BF16)
                    w.STb = W("STb", [128, 512], BF16)
                    w.sg = [W("sg", [128, 1024])]
                    w.du = W("du", [128, 512])
                    w.ub = [W("ub", [128, 512], BF16)]
                    w.opt = [W("opt", [128, 512])]
                    w.ypt = [W("ypt", [128, 512])]
                else:
                    w.kr = [W(f"kr{i}", [128, 512], BF16) for i in range(2)]
                    w.vb = [W(f"vb{i}", [128, 512], BF16) for i in range(2)]
                    w.QT = [W(f"QT{i}", [128, 512], BF16) for i in range(2)]
                    w.sga = W("sga", [128, 512])
                    w.sgb = [W(f"sgb{i}", [128, 512]) for i in range(2)]
                    w.uT2 = W("uT2", [128, 4, 128], BF16)
                    w.ub = [W(f"ub{i}", [128, 512], BF16) for i in range(2)]
                    w.opt = [W(f"opt{i}", [128, 512]) for i in range(2)]
                    w.ypt = [W(f"ypt{i}", [128, 512]) for i in range(2)]
                    w.oab = [W(f"oab{i}", [128, 1024], BF16) for i in range(2)]
                    w.oT = W("oT", [128, 8, 128], BF16)
                    w.ygb = W("ygb", [128, 512], BF16)
                    w.tA = W("tAh", [128, 512])
                    w.tB = W("tBh", [128, 512])
                    w.tD = W("tDh", [128, 512])
                return w

            def s5_chunk(w, ci, rev, hooks=()):
                cur, nxt = cin[ci % 2], cin[(ci + 1) % 2]
                lb, b_lb = lastb[ci % 2]
                uT, b_uT = w.uT
                hooks = list(hooks)

                def hook():
                    if hooks:
                        hooks.pop(0)()

                def banks(g):
                    return (3, 4) if g % 2 == 0 else (5, 6)

                def stageApe(g):
                    br, bi = banks(g)
                    for t in range(4):
                        s_ = 4 * g + t
                        P.mm(ps[br][:, t * 128:(t + 1) * 128], WB[0][0][:, s_ * 128:(s_ + 1) * 128], uT[:, g, :], True, False,
                             reads=[WB[0][1], b_uT], writes=[bps[br]], inc=False)
                        P.mm(ps[bi][:, t * 128:(t + 1) * 128], WB[1][0][:, s_ * 128:(s_ + 1) * 128], uT[:, g, :], True, False,
                             reads=[WB[1][1], b_uT], writes=[bps[bi]], inc=False)
                    o_r = ps[br][:].rearrange("p (a b) -> p a b", a=4)[:, :, 0:1]
                    o_i = ps[bi][:].rearrange("p (a b) -> p a b", a=4)[:, :, 0:1]
                    P.mm(o_r, identf[:], cur[0][:, 0, 4 * g:4 * g + 4].unsqueeze(2), False, True,
                         reads=[b_identf, cur[1]], writes=[bps[br]], inc=False)
                    P.mm(o_i, identf[:], cur[0][:, 1, 4 * g:4 * g + 4].unsqueeze(2), False, True,
                         reads=[b_identf, cur[1]], writes=[bps[bi]], inc=True)

                def stageAve(g):
                    br, bi = banks(g)
                    G = w.g[g % 2]
                    Cg = COS[:, 4 * g:4 * g + 4, :].rearrange("p a b -> p (a b)")
                    Sg = SIN[:, 4 * g:4 * g + 4, :].rearrange("p a b -> p (a b)")
                    P.tt("dve", G.tA[0][:], ps[br][:], Cg, ALU.mult, reads=[bps[br], b_COS], writes=[G.tA[1]])
                    P.tt("dve", G.tB[0][:], ps[bi][:], Sg, ALU.mult, reads=[bps[bi], b_COS], writes=[G.tB[1]])
                    P.tt(S5E, G.wre[0][:], G.tA[0][:], G.tB[0][:], ALU.add, reads=[G.tA[1], G.tB[1]], writes=[G.wre[1]])
                    P.tt("dve", G.tC[0][:], ps[bi][:], Cg, ALU.mult, reads=[bps[bi], b_COS], writes=[G.tC[1]])
                    P.stt("dve", G.tD[0][:], ps[br][:], -1.0, Sg, ALU.mult, ALU.mult, reads=[bps[br], b_COS], writes=[G.tD[1]])
                    P.tt(S5E, G.wim[0][:], G.tC[0][:], G.tD[0][:], ALU.add, reads=[G.tC[1], G.tD[1]], writes=[G.wim[1]])

                def stageB(g):
                    G = w.g[g % 2]
                    Rg = RHO0[:, 4 * g:4 * g + 4, :].rearrange("p a b -> p (a b)")
                    P.T.op("dve", lambda q, Rg=Rg, o=G.sre[0], i=G.wre[0]: q.tensor_tensor_scan(
                        out=o[:], data0=Rg, data1=i[:], initial=0.0, op0=ALU.mult, op1=ALU.add),
                        reads=[b_RHO0, G.wre[1]], writes=[G.sre[1]])
                    P.T.op("dve", lambda q, Rg=Rg, o=G.sim[0], i=G.wim[0]: q.tensor_tensor_scan(
                        out=o[:], data0=Rg, data1=i[:], initial=0.0, op0=ALU.mult, op1=ALU.add),
                        reads=[b_RHO0, G.wim[1]], writes=[G.sim[1]])
                    sr3 = G.sre[0][:].rearrange("p (a b) -> p a b", a=4)
                    si3 = G.sim[0][:].rearrange("p (a b) -> p a b", a=4)
                    P.act(lb[:, 0, 4 * g:4 * g + 4].unsqueeze(2), sr3[:, :, 127:128], AF.Copy, reads=[G.sre[1]], writes=[b_lb])
                    P.act(lb[:, 1, 4 * g:4 * g + 4].unsqueeze(2), si3[:, :, 127:128], AF.Copy, reads=[G.sim[1]], writes=[b_lb])

                    def pv(ap):
                        v = ap.rearrange("p (a b) -> p a b", a=4)
                        return v[:, :, ::-1] if rev else v
                    C3 = COS[:, 4 * g:4 * g + 4, :]
                    S3 = SIN[:, 4 * g:4 * g + 4, :]
                    e2 = "dve" if rev else S5E
                    P.tt("dve", pv(G.Pk[0][0][:]), sr3, C3, ALU.mult, reads=[G.sre[1], b_COS], writes=[G.Pk[0][1]])
                    P.tt(e2, pv(G.Pk[1][0][:]), si3, S3, ALU.mult, reads=[G.sim[1], b_COS], writes=[G.Pk[1][1]])
                    P.tt("dve", pv(G.Pk[2][0][:]), sr3, S3, ALU.mult, reads=[G.sre[1], b_COS], writes=[G.Pk[2][1]])
                    P.tt(e2, pv(G.Pk[3][0][:]), si3, C3, ALU.mult, reads=[G.sim[1], b_COS], writes=[G.Pk[3][1]])
                    for t in range(4):
                        s_ = 4 * g + t
                        o = ps[7][:, 32 * s_:32 * s_ + 32]
                        wsl = slice(32 * s_, 32 * s_ + 32)
                        tsl = slice(128 * t, 128 * t + 128)
                        Pk = G.Pk
                        P.mm(o, Pk[0][0][:, tsl], WC[0][0][:, wsl], True, False, reads=[Pk[0][1], WC[0][1]], writes=[bps[7]], inc=False)
                        P.mm(o, Pk[1][0][:, tsl], WC[1][0][:, wsl], False, False, reads=[Pk[1][1], WC[1][1]], writes=[bps[7]], inc=False)
                        P.mm(o, Pk[2][0][:, tsl], WC[2][0][:, wsl], False, False, reads=[Pk[2][1], WC[2][1]], writes=[bps[7]], inc=False)
                        P.mm(o, Pk[3][0][:, tsl], WC[2][0][:, wsl], False, True, reads=[Pk[3][1], WC[2][1]], writes=[bps[7]], inc=(t == 3))

                stageApe(0)
                stageAve(0)
                stageApe(1)
                hook()
                stageAve(1)
                hook()
                stageApe(2)
                stageB(0)
                hook()
                stageAve(2)
                hook()
                stageApe(3)
                stageB(1)
                hook()
                stageAve(3)
                hook()
                stageB(2)
                hook()
                stageB(3)
                hook()
                while hooks:
                    hook()
                l4 = lst[:]
                P.tt("pool", l4[:, 0, :], lb[:, 0, :], Gc[:, 0, :], ALU.mult, reads=[b_lb, b_Gc], writes=[b_lst])
                P.tt("pool", l4[:, 1, :], lb[:, 1, :], Gc[:, 1, :], ALU.mult, reads=[b_lb, b_Gc], writes=[b_lst])
                P.tt("pool", l4[:, 2, :], lb[:, 1, :], Gc[:, 0, :], ALU.mult, reads=[b_lb, b_Gc], writes=[b_lst])
                P.tt("pool", l4[:, 3, :], lb[:, 0, :], Gc[:, 1, :], ALU.mult, reads=[b_lb, b_Gc], writes=[b_lst])
                P.tt("pool", nxt[0][:, 0, :], l4[:, 0, :], l4[:, 1, :], ALU.subtract, reads=[b_lst], writes=[nxt[1]])
                P.tt("pool", nxt[0][:, 1, :], l4[:, 2, :], l4[:, 3, :], ALU.add, reads=[b_lst], writes=[nxt[1]])

            def ret_state_update(w, kbuf, b_kbuf, vbt, b_vbt, gcol):
                kf, b_kf = w.kf
                P.tt("pool", kf[:].rearrange("p (h e) -> p h e", h=4), kbuf[:].rearrange("p (h e) -> p h e", h=4),
                     gam[:, gcol, :].unsqueeze(2).to_broadcast([128, 4, 128]), ALU.mult,
                     reads=[b_kbuf, b_gam], writes=[b_kf])
                for h in range(4):
                    hs = slice(h * 128, (h + 1) * 128)
                    P.mm(ps[5][:, hs], kf[:, hs], vbt[:, hs], True, True, reads=[b_kf, b_vbt], writes=[bps[5]], inc=(h == 3))
                P.tt("pool", stf[:].rearrange("p (h e) -> p h e", h=4), stf[:].rearrange("p (h e) -> p h e", h=4),
                     cdec[:].unsqueeze(2).to_broadcast([128, 4, 128]), ALU.mult, reads=[b_stf, b_cdec], writes=[b_stf])
                P.tt("dve", stf[:], stf[:], ps[5][:], ALU.add, reads=[b_stf, bps[5]], writes=[b_stf])
                P.act(stbf[:], stf[:], AF.Copy, reads=[b_stf], writes=[b_stbf])

            def passA(s, w):
                OPs, YPs, SGs, QTs, KRs, VBs, UBs = OPs_[s], YPs_[s], SGs_[s], QTs_[s], KRs_[s], VBs_[s], UBs_[s]
                b_OPs, b_YPs, b_SGs, b_QTs, b_KRs, b_VBs, b_UBs = SCRB[s]
                P.memset("dve", stf[:], 0.0, writes=[b_stf])
                P.memset("pool", stbf[:], 0.0, writes=[b_stbf])
                P.memset("dve", cin[0][0][:], 0.0, writes=[cin[0][1]])
                nA = nch if KSTOP not in ('setup',) else 0
                if nA:
                    prenorm(C, s, src, bsrc[s], 0, w.hT[0][0], w.hT[0][1], 0)
                for n in range(nA):
                    tsl = slice(n * 128, (n + 1) * 128)
                    hT, b_hT = w.hT[n % 2]
                    rot, b_rot = w.rot[n % 2]
                    P.dma("sp", rot[:], rot_d[tsl], writes=[b_rot])
                    if n + 1 < nA:
                        p1, p2, p3 = prenorm_parts(C, s, src, bsrc[s], n + 1, w.hT[(n + 1) % 2][0], w.hT[(n + 1) % 2][1], 0)
                    else:
                        p1 = p2 = p3 = (lambda: None)
                    qr, b_qr = w.qr
                    kr, b_kr = w.kr[0]
                    vb, b_vb = w.vb[0]
                    QT, b_QT = w.QT[0]
                    KT, b_KT = w.KT
                    STb, b_STb = w.STb
                    sg, b_sg = w.sg[0]
                    du, b_du = w.du
                    ub, b_ub = w.ub[0]
                    opt, b_opt = w.opt[0]
                    ypt, b_ypt = w.ypt[0]
                    tC, b_tC = w.tC
                    uT, b_uT = w.uT
                    def inproj(col, bank):
                        for k in range(8):
                            P.mm(ps[bank][:], hT[:, k, :], win[:, k, col * 512:(col + 1) * 512], k == 0, k == 7,
                                 reads=[b_hT, b_win], writes=[bps[bank]], inc=(k == 7))

                    def rotary(zb, tbl, outb, b_outb):
                        rA, b_rA = w.rA
                        rB, b_rB = w.rB
                        z3 = ps[zb][:].rearrange("p (h e) -> p h e", h=4)
                        a3 = rA[:].rearrange("p (h e) -> p h e", h=4)
                        b3 = rB[:].rearrange("p (h e) -> p h e", h=4)
                        P.tt("dve", a3, z3, rot[:, tbl, 0, :].unsqueeze(1).to_broadcast([128, 4, 128]), ALU.mult,
                             reads=[bps[zb], b_rot], writes=[b_rA])
                        P.tt("dve", b3[:, :, 0:64], z3[:, :, 64:128], rot[:, tbl, 1, 0:64].unsqueeze(1).to_broadcast([128, 4, 64]),
                             ALU.mult, reads=[bps[zb], b_rot], writes=[b_rB])
                        P.tt("dve", b3[:, :, 64:128], z3[:, :, 0:64], rot[:, tbl, 1, 64:128].unsqueeze(1).to_broadcast([128, 4, 64]),
                             ALU.mult, reads=[bps[zb], b_rot], writes=[b_rB])
                        P.tt("pool", outb[:], rA[:], rB[:], ALU.add, reads=[b_rA, b_rB], writes=[b_outb])

                    inproj(4, 2)
                    P.act(ub[:], ps[2][:], AF.Copy, reads=[bps[2]], writes=[b_ub])
                    P.tt("dve", du[:], ps[2][:], drow[:], ALU.mult, reads=[bps[2], b_drow, b_ub], writes=[b_du])
                    P.dma("pool", UBs[tsl], ub[:], reads=[b_ub], writes=[b_UBs])
                    for q_ in range(4):
                        qs = slice(q_ * 128, (q_ + 1) * 128)
                        P.mm(ps[0][:, qs], ub[:, qs], ident_bf[:], True, True, reads=[b_ub, b_ident], writes=[bps[0]], inc=(q_ == 3))
                    P.act(uT[:].rearrange("p a b -> p (a b)"), ps[0][:], AF.Copy, reads=[bps[0]], writes=[b_uT])

                    def H0():
                        inproj(0, 2)
                        inproj(1, 1)
                        rotary(2, 0, qr, b_qr)
                        rotary(1, 1, kr, b_kr)

                    def H1():
                        inproj(2, 2)
                        P.act(vb[:], ps[2][:], AF.Copy, reads=[bps[2]], writes=[b_vb])
                        inproj(3, 0)
                        P.act(sg[:, 0:512], ps[0][:], AF.Silu, reads=[bps[0]], writes=[b_sg])
                        inproj(5, 1)
                        P.act(sg[:, 512:1024], ps[1][:], AF.Silu, reads=[bps[1]], writes=[b_sg])
                        P.dma("pool", SGs[tsl], sg[:], reads=[b_sg], writes=[b_SGs])

                    def R1():
                        for h in range(4):
                            hs = slice(h * 128, (h + 1) * 128)
                            P.mm(ps[0][:, hs], qr[:, hs], ident_bf[:], True, True, reads=[b_qr, b_ident], writes=[bps[0]], inc=(h == 3))
                        for h in range(4):
                            hs = slice(h * 128, (h + 1) * 128)
                            P.mm(ps[1][:, hs], kr[:, hs], ident_bf[:], True, True, reads=[b_kr, b_ident], writes=[bps[1]], inc=(h == 3))
                        P.act(QT[:], ps[0][:], AF.Copy, reads=[bps[0]], writes=[b_QT])
                        P.act(KT[:], ps[1][:], AF.Copy, reads=[bps[1]], writes=[b_KT])
                        p1()
                    def R2():
                        for h in range(4):
                            hs = slice(h * 128, (h + 1) * 128)
                            P.mm(ps[2][:, hs], KT[:, hs], QT[:, hs], True, True, reads=[b_KT, b_QT], writes=[bps[2]], inc=(h == 3))
                        P.tt("dve", STb[:], ps[2][:], dtab[:].rearrange("p h e -> p (h e)"), ALU.mult,
                             reads=[bps[2], b_dtab], writes=[b_STb])
                    def R3():
                        for h in range(4):
                            hs = slice(h * 128, (h + 1) * 128)
                            P.mm(ps[2][:, hs], STb[:, hs], vb[:, hs], True, True, reads=[b_STb, b_vb], writes=[bps[2]], inc=(h == 3))
                        for h in range(4):
                            hs = slice(h * 128, (h + 1) * 128)
                            P.mm(ps[0][:, hs], QT[:, hs], stbf[:, hs], True, True, reads=[b_QT, b_stbf], writes=[bps[0]], inc=(h == 3))
                        P.tt("dve", tC[:].rearrange("p (h e) -> p h e", h=4), ps[0][:].rearrange("p (h e) -> p h e", h=4),
                             gam[:, 0, :].unsqueeze(2).to_broadcast([128, 4, 128]), ALU.mult, reads=[bps[0], b_gam], writes=[b_tC])
                        P.tt("dve", opt[:], tC[:], ps[2][:], ALU.add, reads=[b_tC, bps[2]], writes=[b_opt])
                        P.dma("pool", OPs[tsl], opt[:], reads=[b_opt], writes=[b_OPs])
                    def R4():
                        ret_state_update(w, kr, b_kr, vb, b_vb, 2)
                    def R5():
                        P.dma("pool", QTs[n], QT[:], reads=[b_QT], writes=[b_QTs])
                        P.dma("pool", KRs[tsl], kr[:], reads=[b_kr], writes=[b_KRs])
                        P.dma("pool", VBs[tsl], vb[:], reads=[b_vb], writes=[b_VBs])
                    def R45():
                        R4()
                        R5()

                    def P23():
                        p2()
                        p3()
                    s5_chunk(w, n, False, hooks=[H0, H1, R1, R2, R3, R45, P23])
                    P.tt("dve", ypt[:], ps[7][:], du[:], ALU.add, reads=[bps[7], b_du], writes=[b_ypt])
                    P.dma("pool", YPs[tsl], ypt[:], reads=[b_ypt], writes=[b_YPs])

            def passB(s, w):
                OPs, YPs, SGs, QTs, KRs, VBs, UBs = OPs_[s], YPs_[s], SGs_[s], QTs_[s], KRs_[s], VBs_[s], UBs_[s]
                b_OPs, b_YPs, b_SGs, b_QTs, b_KRs, b_VBs, b_UBs = SCRB[s]
                P.memset("dve", stf[:], 0.0, writes=[b_stf])
                P.memset("pool", stbf[:], 0.0, writes=[b_stbf])
                P.memset("dve", cin[0][0][:], 0.0, writes=[cin[0][1]])
                order = list(range(nch - 1, -1, -1)) if KSTOP == 'all' else []

                def loads(ci):
                    n = order[ci]
                    tsl = slice(n * 128, (n + 1) * 128)
                    r = ci % 2
                    P.dma("sp", w.ub[r][0][:], UBs[tsl], reads=[b_UBs], writes=[w.ub[r][1]])
                    P.dma("sp", w.QT[r][0][:], QTs[n], reads=[b_QTs], writes=[w.QT[r][1]])
                    P.dma("sp", w.opt[r][0][:], OPs[tsl], reads=[b_OPs], writes=[w.opt[r][1]])
                    P.dma("sp", w.kr[r][0][:], KRs[tsl], reads=[b_KRs], writes=[w.kr[r][1]])
                    P.dma("sp", w.vb[r][0][:], VBs[tsl], reads=[b_VBs], writes=[w.vb[r][1]])

                def make_tail(ci, n):
                    r = ci % 2
                    tsl = slice(n * 128, (n + 1) * 128)
                    ypt, b_ypt = w.ypt[r]
                    sgb, b_sgb = w.sgb[r]
                    oab, b_oab = w.oab[r]
                    tB, b_tB = w.tB
                    tD, b_tD = w.tD
                    uT2, b_uT2 = w.uT2
                    oT, b_oT = w.oT
                    ygb, b_ygb = w.ygb

                    def T1():
                        P.tt("pool", tB[:], ypt[:], ypt[:], ALU.mult, reads=[b_ypt], writes=[b_tB])
                        P.ts("dve", tB[:], tB[:], 0.044715, 1.0, ALU.mult, ALU.add, reads=[b_tB], writes=[b_tB])
                        P.tt("pool", tB[:], tB[:], ypt[:], ALU.mult, reads=[b_tB, b_ypt], writes=[b_tB])
                        P.act(tB[:], tB[:], AF.Sigmoid, reads=[b_tB], writes=[b_tB], scale=1.5957691216057308)
                        P.tt("dve", tD[:], ypt[:], tB[:], ALU.mult, reads=[b_ypt, b_tB], writes=[b_tD])
                        P.act(ygb[:], tD[:], AF.Copy, reads=[b_tD], writes=[b_ygb])

                    def T2():
                        for q_ in range(4):
                            qs = slice(q_ * 128, (q_ + 1) * 128)
                            P.mm(ps[0][:, qs], ygb[:, qs], ident_bf[:], True, True, reads=[b_ygb, b_ident], writes=[bps[0]], inc=(q_ == 3))
                        P.act(uT2[:].rearrange("p a b -> p (a b)"), ps[0][:], AF.Copy, reads=[bps[0]], writes=[b_uT2])
                        for q_ in range(4):
                            P.mm(ps[1][:], uT2[:, q_, :], wglu[:, q_, :], q_ == 0, q_ == 3, reads=[b_uT2, b_wglu], writes=[bps[1]], inc=(q_ == 3))
                        P.act(tB[:], ps[1][:], AF.Sigmoid, reads=[bps[1]], writes=[b_tB])
                        P.tt("dve", tD[:], tD[:], tB[:], ALU.mult, reads=[b_tD, b_tB], writes=[b_tD])
                        P.tt("pool", oab[:, 512:1024], tD[:], sgb[:], ALU.mult, reads=[b_tD, b_sgb], writes=[b_oab])

                    def T3():
                        for b_ in range(2):
                            for jj in range(4):
                                k = 4 * b_ + jj
                                P.mm(ps[b_][:, jj * 128:(jj + 1) * 128], oab[:, k * 128:(k + 1) * 128], ident_bf[:], True, True,
                                     reads=[b_oab, b_ident], writes=[bps[b_]], inc=(jj == 3))
                        P.act(oT[:, 0:4, :].rearrange("p a b -> p (a b)"), ps[0][:], AF.Copy, reads=[bps[0]], writes=[b_oT])
                        P.act(oT[:, 4:8, :].rearrange("p a b -> p (a b)"), ps[1][:], AF.Copy, reads=[bps[1]], writes=[b_oT])
                        for hh, bk in ((0, 2), (1, 0)):
                            for k in range(8):
                                P.mm(ps[bk][:], oT[:, k, :], wout[:, k, hh * 512:(hh + 1) * 512], k == 0, k == 7,
                                     reads=[b_oT, b_wout], writes=[bps[bk]], inc=(k == 7))

                    def T4():
                        sl = C["cnt"] % 2
                        C["cnt"] += 1
                        xt, bxt = C["xt"][sl], C["bxt"][sl]
                        P.dma("sp", xt[:], src[s, tsl, :], reads=[bsrc[s]], writes=[bxt])
                        post(C, s, (2, 0), xt, bxt, dst, bdst[s], n)
                    return [T1, T2, T3, T4]

                if order:
                    loads(0)
                tail = []
                for ci, n in enumerate(order):
                    tsl = slice(n * 128, (n + 1) * 128)
                    r = ci % 2
                    QT, b_QT = w.QT[r]
                    opt, b_opt = w.opt[r]
                    kr, b_kr = w.kr[r]
                    vb, b_vb = w.vb[r]
                    ub, b_ub = w.ub[r]
                    sga, b_sga = w.sga
                    sgb, b_sgb = w.sgb[r]
                    ypt, b_ypt = w.ypt[r]
                    tA, b_tA = w.tA
                    tC, b_tC = w.tC
                    uT, b_uT = w.uT
                    oab, b_oab = w.oab[r]
                    kf, b_kf = w.kf
                    P.dma("sp", sga[:], SGs[tsl, 0:512], reads=[b_SGs], writes=[b_sga])
                    P.dma("sp", sgb[:], SGs[tsl, 512:1024], reads=[b_SGs], writes=[b_sgb])
                    P.dma("sp", ypt[:], YPs[tsl], reads=[b_YPs], writes=[b_ypt])
                    for q_ in range(4):
                        qs = slice(q_ * 128, (q_ + 1) * 128)
                        P.mm(ps[0][:, qs], ub[:, qs], jmat_bf[:], True, True, reads=[b_ub, b_jmat], writes=[bps[0]], inc=(q_ == 3))
                    P.act(uT[:].rearrange("p a b -> p (a b)"), ps[0][:], AF.Copy, reads=[bps[0]], writes=[b_uT])
                    if ci + 1 < len(order):
                        loads(ci + 1)

                    def RB1():
                        for h in range(4):
                            hs = slice(h * 128, (h + 1) * 128)
                            P.mm(ps[2][:, hs], QT[:, hs], stbf[:, hs], True, True, reads=[b_QT, b_stbf], writes=[bps[2]], inc=(h == 3))
                        P.tt("dve", tC[:].rearrange("p (h e) -> p h e", h=4), ps[2][:].rearrange("p (h e) -> p h e", h=4),
                             gam[:, 1, :].unsqueeze(2).to_broadcast([128, 4, 128]), ALU.mult, reads=[bps[2], b_gam], writes=[b_tC])
                        P.tt("pool", opt[:], opt[:], tC[:], ALU.add, reads=[b_opt, b_tC], writes=[b_opt])
                        P.tt("pool", kf[:].rearrange("p (h e) -> p h e", h=4), kr[:].rearrange("p (h e) -> p h e", h=4),
                             gam[:, 3, :].unsqueeze(2).to_broadcast([128, 4, 128]), ALU.mult,
                             reads=[b_kr, b_gam], writes=[b_kf])
                        for h in range(4):
                            hs = slice(h * 128, (h + 1) * 128)
                            P.mm(ps[1][:, hs], kf[:, hs], vb[:, hs], True, True, reads=[b_kf, b_vb], writes=[bps[1]], inc=(h == 3))
                        P.tt("pool", stf[:].rearrange("p (h e) -> p h e", h=4), stf[:].rearrange("p (h e) -> p h e", h=4),
                             cdec[:].unsqueeze(2).to_broadcast([128, 4, 128]), ALU.mult, reads=[b_stf, b_cdec], writes=[b_stf])

                    def RB2():
                        P.tt("dve", stf[:], stf[:], ps[1][:], ALU.add, reads=[b_stf, bps[1]], writes=[b_stf])
                        P.act(stbf[:], stf[:], AF.Copy, reads=[b_stf], writes=[b_stbf])
                        o3 = opt[:].rearrange("p (h e) -> p h e", h=4)
                        P.T.op("dve", lambda q, o3=o3: q.reduce_sum(out=hn[:, 0:4], in_=o3, axis=mybir.AxisListType.X),
                               reads=[b_opt], writes=[b_hn])
                        P.tt("pool", tA[:], opt[:], opt[:], ALU.mult, reads=[b_opt], writes=[b_tA])
                        P.T.op("dve", lambda q, tA=tA: q.reduce_sum(out=hn[:, 4:8], in_=tA[:].rearrange("p (h e) -> p h e", h=4),
                                                                  axis=mybir.AxisListType.X), reads=[b_tA], writes=[b_hn])
                        P.ts("dve", hn[:, 0:8], hn[:, 0:8], 1.0 / 128.0, None, ALU.mult, None, reads=[b_hn], writes=[b_hn])
                        P.tt("dve", hn[:, 8:12], hn[:, 0:4], hn[:, 0:4], ALU.mult, reads=[b_hn], writes=[b_hn])
                        P.tt("dve", hn[:, 4:8], hn[:, 4:8], hn[:, 8:12], ALU.subtract, reads=[b_hn], writes=[b_hn])
                        P.act(hn[:, 8:12], hn[:, 4:8], AF.Sqrt, reads=[b_hn, b_eps], writes=[b_hn], bias=epsT[:, 0:1])

                    def RB3():
                        o3 = opt[:].rearrange("p (h e) -> p h e", h=4)
                        P.T.op("dve", lambda q: q.reciprocal(out=hn[:, 12:16], in_=hn[:, 8:12]), reads=[b_hn], writes=[b_hn])
                        a3 = tA[:].rearrange("p (h e) -> p h e", h=4)
                        P.tt("pool", a3, o3, hn[:, 0:4].unsqueeze(2).to_broadcast([128, 4, 128]), ALU.subtract,
                             reads=[b_opt, b_hn], writes=[b_tA])
                        P.tt("pool", a3, a3, hn[:, 12:16].unsqueeze(2).to_broadcast([128, 4, 128]), ALU.mult,
                             reads=[b_tA, b_hn], writes=[b_tA])
                        P.tt("pool", oab[:, 0:512], tA[:], sga[:], ALU.mult, reads=[b_tA, b_sga], writes=[b_oab])

                    hooks = [RB1, RB2, RB3] + tail
                    s5_chunk(w, ci, True, hooks=hooks)
                    P.tt("dve", ypt[:], ypt[:], ps[7][:], ALU.add, reads=[b_ypt, bps[7]], writes=[b_ypt])
                    tail = make_tail(ci, n)
                for t_ in tail:
                    t_()

            s5_setup(0)
            with contextlib.ExitStack() as wk:
                w = alloc_work(wk, False)
                for s in range(nseq):
                    passA(s, w)
                T_barrier()
            s5_setup(1)
            with contextlib.ExitStack() as wk:
                w = alloc_work(wk, True)
                for s in range(nseq):
                    passB(s, w)
                T_barrier()

    def T_barrier():
        evs = []
        for e in T.engs.values():
            for k in range(len(e.dsems)):
                evs.append((e.dsems[k], e.dvals[k]))
            if e.sem is not None and e.cnt > 0 and not e.pending:
                evs.append((e.sem, e.cnt))
        for e in T.engs.values():
            for ev in evs:
                T._wait(e, ev)

    def odd_layer(li, src, bsrc, dst, bdst):
        raise NotImplementedError

    P.odd_layer_hook = None
    cur, bcur = x_in, [b_xin] * nseq
    for idx, li in enumerate(layers):
        last = idx == len(layers) - 1
        dstt, bd = (y_out, b_yout) if last else (xs[idx % 2], b_xs[idx % 2])
        if li % 2 == 0:
            even_layer(li, cur, bcur, dstt, bd)
        else:
            ODD_IMPL(P, locals(), li, cur, bcur, dstt, bd)
        cur, bcur = dstt, bd
    T.finish()
    T.replay()
    P.es.close()
    return P


def ODD_IMPL(P, env, li, src, bsrc, dst, bdst):
    nc, T = P.nc, P.T
    P.pool_to_dve = False
    P.pne = 'mix'
    E = env
    ps, bps = E["ps"], E["bps"]
    nseq, L = P.nseq, P.L
    ident_bf, b_ident = E["ident_bf"], E["b_ident"]
    j = li // 2
    rows = L // 64
    nblk = L // 256

    def rs(r):
        return min(max(r - 4, 0), rows - 8)

    with contextlib.ExitStack() as st:
        def S(name, shape, dt=F32):
            return st.enter_context(P.sbt(f"{name}_o{li}", list(shape), dt)), Buf(name)
        E["adaln"](li, None)
        C = E["make_common"](st, f"o{li}")
        winc, b_winc = S("winc", [128, 8, 4096], BF16)
        woutc, b_woutc = S("woutc", [128, 8, 1024], BF16)
        Z, b_Z = S("Z", [128, 16, 1024], BF16)
        hT, b_hT = S("hT", [128, 8, 256], BF16)
        KT = [S(f"KT{i}", [128, 8, 256], BF16) for i in range(3)]
        V = [S(f"V{i}", [128, 2, 8, 3, 64], BF16) for i in range(3)]
        QT = [S(f"QT{i}", [128, 8, 256], BF16) for i in range(2)]
        GT = [S(f"GT{i}", [128, 8, 256], BF16) for i in range(2)]
        pT = [S(f"pT{i}", [128, 256], BF16) for i in range(3)]
        og, b_og = S("og", [128, 8, 256], BF16)
        rd2 = [S(f"rd{i}", [128, 256]) for i in range(2)]
        rb2 = [S(f"rb{i}", [128, 256]) for i in range(2)]
        t22 = [S(f"t2{i}", [128, 256]) for i in range(2)]
        selb, b_selb = S("selb", [128, 64], BF16)
        RBh = [S(f"RBh{i}", [128, 256], BF16) for i in range(2)]
        zer, b_zer = S("zer", [128, 256], BF16)
        wsrc = E["w_in_c"][j].rearrange("(k p) n -> p k n", p=128)
        for k in range(8):
            P.dma("pool", winc[:, k, :], wsrc[:, k, :], writes=[b_winc])
        P.dma("pool", woutc[:], E["w_out_c"][j].rearrange("(k p) n -> p k n", p=128), writes=[b_woutc])
        P.dma("pool", selb[:], E["sel_d"], writes=[b_selb])
        for i in range(2):
            P.memset("dve", RBh[i][0][:], 0.0, writes=[RBh[i][1]])
        P.memset("dve", zer[:], 0.0, writes=[b_zer])
        for i in range(3):
            P.memset("pool", V[i][0][:], 1.0, writes=[V[i][1]])
        with contextlib.ExitStack() as s2:
            zm = s2.enter_context(P.sbt(f"zm_o{li}", [128, 1024], F32)); b_zm = Buf("zm")
            zt0_ = s2.enter_context(P.sbt(f"zt0_o{li}", [128, 512], F32))
            zt = [zt0_, zt0_]
            b_zt0_ = Buf("zt0")
            b_zt = [b_zt0_, b_zt0_]
            P.dma("sp", zm[:], E["zmask"], writes=[b_zm])
            for h in range(16):
                for hf in range(2):
                    P.dma("sp", zt[hf][:], E["zg"][j, h][:, hf * 512:(hf + 1) * 512], writes=[b_zt[hf]])
                    P.tt("dve", Z[:, h, hf * 512:(hf + 1) * 512], zt[hf][:], zm[:, hf * 512:(hf + 1) * 512], ALU.add,
                         reads=[b_zt[hf], b_zm], writes=[b_Z])
            E["T_barrier"]()

        def proj(s, b, do_prenorm=True):
            ring = b % 3
            sl = b % 2
            if do_prenorm:
                for t in range(2):
                    E["prenorm"](C, s, src, bsrc[s], 2 * b + t, hT, b_hT, t * 128)
            cnt = 0
            for (col0, kind) in ((0, "q"), (1024, "k"), (3072, "g")):
                for hp2 in range(4):
                    bank = 2 + cnt % 2
                    cnt += 1
                    for hh in range(2):
                        hp = 2 * hp2 + hh
                        for k in range(8):
                            P.mm(ps[bank][:, hh * 256:(hh + 1) * 256], winc[:, k, col0 + hp * 128:col0 + (hp + 1) * 128], hT[:, k, :],
                                 k == 0, k == 7, reads=[b_winc, b_hT], writes=[bps[bank]], inc=(k == 7 and hh == 1))
                    if kind == "q":
                        P.act(QT[sl][0][:, 2 * hp2:2 * hp2 + 2, :].rearrange("p a b -> p (a b)"), ps[bank][:], AF.Copy,
                              reads=[bps[bank]], writes=[QT[sl][1]], scale=0.125)
                    elif kind == "k":
                        P.cp("dve", KT[ring][0][:, 2 * hp2:2 * hp2 + 2, :].rearrange("p a b -> p (a b)"), ps[bank][:],
                             reads=[bps[bank]], writes=[KT[ring][1]])
                    else:
                        P.act(GT[sl][0][:, 2 * hp2:2 * hp2 + 2, :].rearrange("p a b -> p (a b)"), ps[bank][:], AF.Silu,
                              reads=[bps[bank]], writes=[GT[sl][1]])
            for t in range(2):
                for half in range(2):
                    bank = 2 + cnt % 2
                    cnt += 1
                    for k in range(8):
                        P.mm(ps[bank][:], hT[:, k, t * 128:(t + 1) * 128], winc[:, k, 2048 + half * 512:2048 + (half + 1) * 512],
                             k == 0, k == 7, reads=[b_hT, b_winc], writes=[bps[bank]], inc=(k == 7))
                    src4 = ps[bank][:].rearrange("p (a c d) -> p a c d", a=4, c=2)
                    P.cp("dve", V[ring][0][:, t, 4 * half:4 * half + 4, 0:3:2, :], src4, reads=[bps[bank]], writes=[V[ring][1]])

        def attn(s, b, hooks=()):
            hooks = list(hooks)
            R = 4 * b
            sl = b % 2
            lo = rs(R) & ~1
            hi = (rs(R + 3) + 7) & ~1
            tiles = []
            for r0 in range(lo, hi + 1, 2):
                qs = [r for r in range(R, R + 4) if (rs(r) <= r0 + 1 and r0 <= rs(r) + 7)]
                if not qs:
                    continue
                qa, qb = qs[0], qs[-1]
                partial = []
                for r in qs:
                    for a in range(2):
                        if not (rs(r) <= r0 + a <= rs(r) + 7):
                            partial.append((r, a))
                tiles.append((r0, qa, qb, partial))
            tiles.sort(key=lambda t: (0 if (t[1] == R and t[2] == R + 3 and not t[3]) else 1))
            assert tiles[0][1] == R and tiles[0][2] == R + 3 and not tiles[0][3]
            pcount = [0]
            W = [(h, ti) for h in range(16) for ti in range(len(tiles))]
            info = {}
            deferred = []

            def emit_scores(h, ti):
                hp, base = h // 2, 64 * (h % 2)
                r0, qa, qb, partial = tiles[ti]
                kb = r0 // 4
                tt_ = (r0 % 4) // 2
                kring = kb % 3
                c0, c1 = (qa - R) * 64, (qb - R + 1) * 64
                z0, z1 = (qa - r0 + 7) * 64, (qb - r0 + 8) * 64
                bank = 4 + pcount[0] % 2
                pt, b_pt = pT[pcount[0] % 3]
                pcount[0] += 1
                P.mm(ps[bank][:, c0:c1], KT[kring][0][base:base + 64, hp, tt_ * 128:(tt_ + 1) * 128],
                     QT[sl][0][base:base + 64, hp, c0:c1], True, False,
                     reads=[KT[kring][1], QT[sl][1]], writes=[bps[bank]], inc=False)
                P.mm(ps[bank][:, c0:c1], ident_bf[:], Z[:, h, z0:z1], False, True,
                     reads=[b_ident, b_Z], writes=[bps[bank]], inc=True)
                P.act(pt[:, c0:c1], ps[bank][:, c0:c1], AF.Exp, reads=[bps[bank]], writes=[b_pt])
                for (r, a_) in partial:
                    cc = (r - R) * 64
                    P.memset("pool", pt[64 * a_:64 * a_ + 64, cc:cc + 64], 0.0, writes=[b_pt])
                info[(h, ti)] = (pt, b_pt, c0, c1, kring, tt_)

            def emit_pv(h, ti, idx):
                hp = h // 2
                pt, b_pt, c0, c1, kring, tt_ = info.pop((h, ti))
                ob = 6 + h % 2
                va = V[kring][0][:, tt_, hp, 0:2, :] if h % 2 == 0 else V[kring][0][:, tt_, hp, 1:3, :]
                last = ti == len(tiles) - 1
                P.mm(ps[ob][:, c0:c1], va.rearrange("p a b -> p (a b)"), pt[:, c0:c1], ti == 0, last,
                     reads=[V[kring][1], b_pt], writes=[bps[ob]], inc=last)
                if last:
                    par = h % 2
                    dr, orow = (64, 0) if par == 0 else (0, 64)
                    rdp, b_rdp = rd2[par]
                    rbp, b_rbp = rb2[par]
                    t2p, b_t2p = t22[par]
                    rbh, b_rbh = RBh[par]
                    P.T.op("dve", lambda q, dr=dr, ob=ob, rdp=rdp: q.reciprocal(out=rdp[dr:dr + 33, :], in_=ps[ob][dr:dr + 33, 0:256]),
                           reads=[bps[ob]], writes=[b_rdp])
                    P.cp("dve", rbh[dr:dr + 33, :], rdp[dr:dr + 33, :], reads=[b_rdp], writes=[b_rbh])
                    P.tt("dve", rbh[dr:dr + 1, :], rdp[dr:dr + 1, :], rbh[dr:dr + 1, :], ALU.subtract, reads=[b_rdp, b_rbh], writes=[b_rbh])

                    def part2(h=h, hp=hp, par=par, dr=dr, orow=orow, ob=ob, rdp=rdp, b_rdp=b_rdp, rbp=rbp, b_rbp=b_rbp, t2p=t2p, b_t2p=b_t2p,
                              rbh=rbh, b_rbh=b_rbh):
                        P.mm(ps[2 + par][orow:orow + 64, 0:256], selb[dr:dr + 33, 0:64], rbh[dr:dr + 33, :], True, True,
                             reads=[b_selb, b_rbh], writes=[bps[2 + par]])
                        P.act(rbp[orow:orow + 64, :], ps[2 + par][orow:orow + 64, 0:256], AF.Copy, reads=[bps[2 + par]], writes=[b_rbp])
                        P.tt("dve", t2p[orow:orow + 64, :], ps[ob][orow:orow + 64, 0:256], rbp[orow:orow + 64, :], ALU.mult,
                             reads=[bps[ob], b_rbp], writes=[b_t2p])
                        P.tt("pool", og[orow:orow + 64, hp, :], t2p[orow:orow + 64, :], GT[sl][0][orow:orow + 64, hp, :], ALU.mult,
                             reads=[b_t2p, GT[sl][1]], writes=[b_og])
                    deferred.append((idx + min(4, len(tiles) - 1), part2))
                    if h % 2 == 1 and hooks:
                        deferred.append((idx + 1, hooks.pop(0)))
                        deferred.sort(key=lambda d: d[0])

            LA = 1
            for idx in range(len(W) + LA):
                if idx < len(W):
                    emit_scores(*W[idx])
                if idx >= LA:
                    emit_pv(W[idx - LA][0], W[idx - LA][1], idx)
                while deferred and deferred[0][0] <= idx:
                    deferred.pop(0)[1]()
            while deferred:
                deferred.pop(0)[1]()
            for t in range(2):
                n = 2 * b + t
                sx = C["cnt"] % 2
                C["cnt"] += 1
                xt, bxt = C["xt"][sx], C["bxt"][sx]
                P.dma("sp", xt[:], src[s, n * 128:(n + 1) * 128, :], reads=[bsrc[s]], writes=[bxt])
                for hh in range(2):
                    for k in range(8):
                        P.mm(ps[2 + hh][:], og[:, k, t * 128:(t + 1) * 128], woutc[:, k, hh * 512:(hh + 1) * 512], k == 0, k == 7,
                             reads=[b_og, b_woutc], writes=[bps[2 + hh]], inc=(k == 7))
                E["post"](C, s, (2, 3), xt, bxt, dst, bdst[s], n)

        for s in range(nseq):
            proj(s, 0)
            if nblk > 1:
                proj(s, 1)
            for b in range(nblk):
                if b >= 1 and b + 1 < nblk:
                    proj(s, b + 1, do_prenorm=False)
                hk = []
                if b + 2 < nblk:
                    for t in range(2):
                        hk += list(E["prenorm_parts"](C, s, src, bsrc[s], 2 * (b + 2) + t, hT, b_hT, t * 128))
                attn(s, b, hooks=hk)
        E["T_barrier"]()


def _common_inputs(p, L):
    f32 = np.float32
    m = {}
    def T8(a):
        return np.ascontiguousarray(a.reshape(a.shape[0], -1, 128).transpose(0, 2, 1)).astype(f32)
    m["npreT"] = T8(p["norm_pre"])
    m["npostT"] = T8(p["norm_post"])
    m["wmod"] = np.ascontiguousarray(p["w_mod"], dtype=f32)
    m["bmodT"] = T8(p["b_mod"])
    m["w_in_ab"] = np.ascontiguousarray(p["w_in_ab"], dtype=f32)
    m["w_out_ab"] = np.ascontiguousarray(p["w_out_ab"], dtype=f32)
    m["w_glu"] = np.ascontiguousarray(p["ssm_w_glu"], dtype=f32)
    m["ssm_d"] = np.ascontiguousarray(p["ssm_d"], dtype=f32)
    sm3, rep3, bl, cl = [], [], [], []
    for j in range(2):
        a, b, c, d = _s5_layout(p["ssm_a_re"][j], p["ssm_a_im"][j], p["ssm_log_step"][j], p["ssm_b_re"][j],
                                p["ssm_b_im"][j], p["ssm_c_re"][j], p["ssm_c_im"][j])
        sm3.append(a); rep3.append(b); bl.append(c); cl.append(d)
    m["s5_sm3"] = np.stack(sm3).astype(f32)
    m["s5_rep3"] = np.stack(rep3).astype(f32)
    m["s5_bl"] = np.stack(bl).astype(f32)
    m["s5_cl"] = np.stack(cl).astype(f32)
    m["w_in_c"] = np.ascontiguousarray(p["w_in_c"], dtype=f32)
    m["w_out_c"] = np.ascontiguousarray(p["w_out_c"], dtype=f32)
    zs = []
    for j in range(2):
        z, mask = _na_layout(np.asarray(p["na_rel_bias"][j], dtype=f32))
        zs.append(z)
    m["zg"] = np.stack(zs).astype(f32)
    m["zmask"] = mask
    m["ident"] = np.eye(128, dtype=f32)
    m["jmat"] = np.eye(128, dtype=f32)[::-1].copy()
    m["rot"] = _rot_tables(L)
    dt, gam, cd = _ret_consts()
    m["dtab"] = dt
    m["gam"] = np.ascontiguousarray(gam.transpose(0, 1, 2))
    m["cdec"] = cd
    m["jt"] = np.broadcast_to(np.arange(128, dtype=f32)[None, :], (128, 128)).copy()
    sel = np.zeros((128, 64), f32)
    sel[[0, 32, 64, 96], :] = 1.0
    m["sel"] = sel
    return m


_PROG_CACHE = {}


def run_cores(xs_per_core, cs_per_core, params, layers):
    nseq, L, _ = xs_per_core[0].shape
    key = (nseq, L, tuple(layers))
    if key not in _PROG_CACHE:
        _PROG_CACHE[key] = build(nseq, L, list(layers))
    P = _PROG_CACHE[key]
    com = _common_inputs(params, L)
    in_maps = []
    for x, c in zip(xs_per_core, cs_per_core):
        m = dict(com)
        m["x_in"] = np.ascontiguousarray(x, dtype=np.float32)
        m["cT"] = np.ascontiguousarray(c.reshape(nseq, 8, 128).transpose(2, 1, 0), dtype=np.float32)
        in_maps.append(m)
    res = run_bass_kernel_spmd(P.nc, in_maps, core_ids=list(range(len(in_maps))))
    return [np.asarray(r["y_out"]) for r in res.results]


def kernel(**inputs):
    p = {k: np.asarray(v) for k, v in inputs.items()}
    xp, xsamp = p["x_prompt"], p["x_sample"]
    cp, cs = p["c_prompt"], p["c_sample"]
    seqs = [xp[i] for i in range(4)] + [xsamp[i] for i in range(8)]
    cvs = [cp[i] for i in range(4)] + [cs[i] for i in range(8)]
    slots = [(c, 8 + c if c < 4 else c) for c in range(8)]
    xs_pc = [np.stack([seqs[a], seqs[b]]) for a, b in slots]
    cs_pc = [np.stack([cvs[a], cvs[b]]) for a, b in slots]
    outs = run_cores(xs_pc, cs_pc, p, [0, 1, 2, 3])
    res = [None] * 12
    for c, (a, b) in enumerate(slots):
        res[a] = outs[c][0]
        if c < 4:
            res[b] = outs[c][1]
    y_prompt = np.stack(res[0:4]).astype(np.float32)
    y_sample = np.stack(res[4:12]).astype(np.float32)
    return (y_prompt, y_sample)
```

```python
import contextlib
import math
import os
KSTOP = os.environ.get('KSTOP', 'all')
S5E = os.environ.get('S5E', 'dve')
STQ = os.environ.get('STQ', 'sp')
PNE = os.environ.get('PNE', 'act')
import numpy as np
import concourse.bass as bass
import concourse.mybir as mybir
from concourse.bass_utils import run_bass_kernel_spmd

F32 = mybir.dt.float32
BF16 = mybir.dt.bfloat16
I32 = mybir.dt.int32
ALU = mybir.AluOpType
AF = mybir.ActivationFunctionType

D = 1024
EPS = 1e-6
TWO_PI = 2.0 * math.pi


class Buf:
    __slots__ = ("name", "w", "r")

    def __init__(self, name):
        self.name = name
        self.w = []
        self.r = []


class Eng:
    def __init__(self, name):
        self.name = name
        self.ops = []
        self.known = {}
        self.sem = None
        self.cnt = 0
        self.pending = False
        self.dsems = []
        self.dvals = []
        self.dptr = 0
        self.own = set()


EPOCH = 30000
NDSEM = 10


class Tracker:
    def __init__(self, nc):
        self.nc = nc
        self.engs = {n: Eng(n) for n in ("pe", "act", "dve", "pool", "sp")}
        self.sems = []
        for e in self.engs.values():
            if e.name != "sp":
                e.sem = self._newsem(e.name)
                e.own.add(e.sem)
        self.n_ops = 0

    def _newsem(self, nm):
        h = self.nc.alloc_semaphore(name=f"{nm}_{len(self.sems)}")
        self.sems.append(h)
        return len(self.sems) - 1

    def _wait(self, e, ev):
        s, v = ev
        if e.known.get(s, 0) >= v:
            return
        if s in e.own:
            if e.name == "pe":
                return
            if s == e.sem and v > e.cnt:
                return
        e.known[s] = v
        sem = self.sems[s]
        e.ops.append(lambda q, sem=sem, v=v: q.wait_ge(sem, v))

    def _deps(self, e, reads, writes):
        for b in reads:
            for ev in b.w:
                self._wait(e, ev)
        for b in writes:
            for ev in b.w:
                self._wait(e, ev)
            for ev in b.r:
                self._wait(e, ev)

    def _commit(self, ev, reads, writes):
        for b in reads:
            for i, (s0, v0) in enumerate(b.r):
                if s0 == ev[0]:
                    b.r[i] = (s0, max(v0, ev[1]))
                    break
            else:
                b.r.append(ev)
        for b in writes:
            b.w = [ev]
            b.r = []

    def op(self, eng, fn, reads=(), writes=(), inc=True):
        e = self.engs[eng]
        self.n_ops += 1
        self._deps(e, reads, writes)
        if e.cnt >= EPOCH and inc and not e.pending:
            e.sem = self._newsem(e.name)
            e.cnt = 0
            e.own.add(e.sem)
        if inc:
            e.cnt += 1
            sem = self.sems[e.sem]
            e.ops.append(lambda q, fn=fn, sem=sem: fn(q).then_inc(sem, 1))
            ev = (e.sem, e.cnt)
            e.pending = False
        else:
            e.ops.append(lambda q, fn=fn: fn(q))
            ev = (e.sem, e.cnt + 1)
            e.pending = True
        self._commit(ev, reads, writes)

    def dma(self, eng, out, in_, reads=(), writes=()):
        e = self.engs[eng]
        self.n_ops += 1
        self._deps(e, reads, writes)
        if len(e.dsems) < NDSEM:
            e.dsems.append(self._newsem(e.name + "d"))
            e.dvals.append(0)
            k = len(e.dsems) - 1
        else:
            k = e.dptr
            e.dptr = (e.dptr + 1) % NDSEM
            self._wait(e, (e.dsems[k], e.dvals[k]))
            if e.dvals[k] >= EPOCH * 16:
                e.dsems[k] = self._newsem(e.name + "d")
                e.dvals[k] = 0
        e.dvals[k] += 16
        sem = self.sems[e.dsems[k]]
        e.ops.append(lambda q, out=out, in_=in_, sem=sem: q.dma_start(out=out, in_=in_).then_inc(sem, 16))
        ev = (e.dsems[k], e.dvals[k])
        self._commit(ev, reads, writes)
        return ev

    def finish(self):
        sp = self.engs["sp"]
        for e in self.engs.values():
            for k in range(len(e.dsems)):
                self._wait(sp, (e.dsems[k], e.dvals[k]))
            if e.sem is not None and e.cnt > 0:
                self._wait(sp, (e.sem, e.cnt))

    def replay(self):
        nc = self.nc
        E = self.engs
        with nc.Block() as block:
            @block.tensor
            def _(q):
                for f in E["pe"].ops:
                    f(q)

            @block.scalar
            def _(q):
                for f in E["act"].ops:
                    f(q)

            @block.vector
            def _(q):
                for f in E["dve"].ops:
                    f(q)

            @block.gpsimd
            def _(q):
                for f in E["pool"].ops:
                    f(q)

            @block.sync
            def _(q):
                for f in E["sp"].ops:
                    f(q)


RET_H = 4
NA_H = 16
GW = 64


def _ret_consts():
    f32 = np.float32
    h = np.arange(RET_H, dtype=f32)
    log_g = np.log1p(-np.exp2(-5.0 - h)).astype(f32)
    pos = np.arange(128, dtype=f32)
    dt = np.exp(np.abs(pos[:, None] - pos[None, :])[:, None, :] * log_g[None, :, None]).astype(f32)
    gam = np.zeros((128, 4, RET_H), f32)
    gam[:, 0, :] = np.exp(pos[:, None] * log_g[None])
    gam[:, 1, :] = np.exp((127.0 - pos)[:, None] * log_g[None])
    gam[:, 2, :] = np.exp((128.0 - pos)[:, None] * log_g[None])
    gam[:, 3, :] = np.exp((pos + 1.0)[:, None] * log_g[None])
    cdec = np.exp(128.0 * log_g).astype(f32)
    cd = np.broadcast_to(cdec[None, :], (128, RET_H)).copy()
    return dt, gam, cd


def _rot_tables(L):
    f32 = np.float32
    inv = (10000.0 ** (-np.arange(0, 128, 2, dtype=f32) / 128.0)).astype(f32)
    ang = (np.arange(L, dtype=f32)[:, None] * inv[None, :]).astype(f32)
    c = np.cos(ang).astype(f32)
    s = np.sin(ang).astype(f32)
    rq = np.zeros((L, 2, 128), f32)
    rq[:, 0, :64] = c
    rq[:, 0, 64:] = c
    rq[:, 1, :64] = -s
    rq[:, 1, 64:] = s
    rk = (rq * f32(128.0 ** -0.5)).astype(f32)
    return np.stack([rq, rk], axis=1).copy()


def _na_layout(relb):
    a = np.arange(2)[:, None, None, None]
    k = np.arange(64)[None, :, None, None]
    m = np.arange(-7, 9)[None, None, :, None]
    c = np.arange(64)[None, None, None, :]
    dr = np.clip(a - m + 7, 0, 14)
    dc = np.clip(k - c + 15, 0, 30)
    dr_b, dc_b = np.broadcast_arrays(dr, dc)
    z = relb[:, dr_b, dc_b]
    z = z.reshape(16, 128, 16 * 64).astype(np.float32)
    cs = np.clip(np.arange(64) - 8, 0, 48)
    kk = np.arange(64)[:, None]
    valid = (kk >= cs[None, :]) & (kk < cs[None, :] + 16)
    mask = np.where(valid, 0.0, -30000.0).astype(np.float32)
    mask = np.broadcast_to(mask[None, :, None, :], (2, 64, 16, 64)).reshape(128, 1024).copy()
    return z, mask


def _s5_layout(a_re, a_im, ls, b_re, b_im, c_re, c_im):
    f32 = np.float32
    def sm(a):
        return a.reshape(2, 16, 2, 64).transpose(0, 2, 3, 1).reshape(2, 128, 16).astype(f32)
    lsx = np.broadcast_to(ls[:, :, None], (2, 32, 64))
    sm3 = np.stack([sm(a_re), sm(a_im), sm(lsx)], axis=1).copy()
    rep3 = np.stack([a_re.reshape(2, 2048), a_im.reshape(2, 2048), lsx.reshape(2, 2048)], axis=1).astype(f32).copy()
    bl = np.zeros((2, 2, 128, 16, 2, 64), f32)
    cl = np.zeros((2, 2, 128, 16, 2, 16), f32)
    for s in range(16):
        for g1 in range(2):
            g = 2 * s + g1
            r0 = 32 * (s % 4) + 16 * g1
            for ri, b in enumerate((b_re, b_im)):
                bl[:, ri, r0:r0 + 16, s, g1, :] = b[:, g].transpose(0, 2, 1)
            for ri, c in enumerate((c_re, c_im)):
                cl[:, ri, 64 * g1:64 * g1 + 64, s, g1, :] = c[:, g].transpose(0, 2, 1)
    return sm3, rep3, bl.reshape(2, 2, 128, 16 * 128), cl.reshape(2, 2, 128, 16 * 32)


class Prog:
    def __init__(self, nseq, L, layers):
        self.nseq, self.L, self.layers = nseq, L, layers
        self.nch = L // 128
        nc = self.nc = bass.Bass("TRN2", target_bir_lowering=False)
        self.T = Tracker(nc)
        self.es = contextlib.ExitStack()
        self.dram = {}
        self.bufs = {}

    def sbt(self, name, shape, dt=F32):
        self._uid = getattr(self, '_uid', 0) + 1
        return self.nc.sbuf_tensor(f"{name}_u{self._uid}", list(shape), dt)

    def din(self, name, shape, dt=F32):
        t = self.nc.dram_tensor(name, list(shape), dt, kind="ExternalInput").ap()
        self.dram[name] = t
        return t

    def dout(self, name, shape, dt=F32):
        t = self.nc.dram_tensor(name, list(shape), dt, kind="ExternalOutput").ap()
        self.dram[name] = t
        return t

    def dscr(self, name, shape, dt=F32):
        t = self.nc.dram_tensor(name, list(shape), dt, kind="Internal").ap()
        self.dram[name] = t
        return t

    def sb(self, name, shape, dt=F32):
        t = self.es.enter_context(self.sbt(name, list(shape), dt))
        b = Buf(name)
        return t, b

    def ps(self, name):
        t = self.es.enter_context(self.nc.psum_tensor(name, [128, 512], F32))
        return t, Buf(name)

    def mm(self, out, lhsT, rhs, start, stop, reads, writes, inc=True):
        self.T.op("pe", lambda q: q.matmul(out, lhsT=lhsT, rhs=rhs, start=start, stop=stop),
                  reads=reads, writes=writes, inc=inc)

    def act(self, out, in_, func, reads, writes, scale=1.0, bias=None, accum=None):
        kw = {}
        if bias is not None:
            kw["bias"] = bias
        if accum is not None:
            kw["accum_out"] = accum
        self.T.op("act", lambda q: q.activation(out=out, in_=in_, func=func, scale=scale, **kw),
                  reads=reads, writes=writes)

    def tt(self, eng, out, in0, in1, op, reads, writes):
        if eng == "pool" and getattr(self, "pool_to_dve", False):
            eng = "dve"
        self.T.op(eng, lambda q: q.tensor_tensor(out=out, in0=in0, in1=in1, op=op), reads=reads, writes=writes)

    def ts(self, eng, out, in0, s1, s2, op0, op1, reads, writes):
        if s2 is None:
            self.T.op(eng, lambda q: q.tensor_scalar(out=out, in0=in0, scalar1=s1, scalar2=None, op0=op0),
                      reads=reads, writes=writes)
        else:
            self.T.op(eng, lambda q: q.tensor_scalar(out=out, in0=in0, scalar1=s1, scalar2=s2, op0=op0, op1=op1),
                      reads=reads, writes=writes)

    def stt(self, eng, out, in0, scalar, in1, op0, op1, reads, writes):
        self.T.op(eng, lambda q: q.scalar_tensor_tensor(out=out, in0=in0, scalar=scalar, in1=in1, op0=op0, op1=op1),
                  reads=reads, writes=writes)

    def cp(self, eng, out, in_, reads, writes):
        self.T.op(eng, lambda q: q.tensor_copy(out=out, in_=in_), reads=reads, writes=writes)

    def memset(self, eng, ap, val, writes):
        self.T.op(eng, lambda q: q.memset(ap, val), reads=(), writes=writes)

    def dma(self, eng, out, in_, reads=(), writes=()):
        return self.T.dma(eng, out, in_, reads=reads, writes=writes)


def build(nseq, L, layers):
    P = Prog(nseq, L, layers)
    nc, T = P.nc, P.T
    nch = L // 128
    NL = 4
    x_in = P.din("x_in", [nseq, L, D])
    y_out = P.dout("y_out", [nseq, L, D])
    cT = P.din("cT", [128, 8, nseq])
    npreT = P.din("npreT", [NL, 128, 8])
    npostT = P.din("npostT", [NL, 128, 8])
    wmod = P.din("wmod", [NL, D, 3 * D])
    bmodT = P.din("bmodT", [NL, 128, 24])
    w_in_ab = P.din("w_in_ab", [2, D, 3072])
    w_out_ab = P.din("w_out_ab", [2, D, D])
    w_glu = P.din("w_glu", [2, 512, 512])
    ssm_d = P.din("ssm_d", [2, 512])
    s5_sm3 = P.din("s5_sm3", [2, 2, 3, 128, 16])
    s5_rep3 = P.din("s5_rep3", [2, 2, 3, 2048])
    s5_bl = P.din("s5_bl", [2, 2, 2, 128, 2048])
    s5_cl = P.din("s5_cl", [2, 2, 2, 128, 512])
    w_in_c = P.din("w_in_c", [2, D, 4096])
    w_out_c = P.din("w_out_c", [2, D, D])
    zg = P.din("zg", [2, 16, 128, 1024])
    zmask = P.din("zmask", [128, 1024])
    ident_d = P.din("ident", [128, 128])
    jmat_d = P.din("jmat", [128, 128])
    rot_d = P.din("rot", [L, 2, 2, 128])
    dtab_d = P.din("dtab", [128, 4, 128])
    gam_d = P.din("gam", [128, 4, 4])
    cdec_d = P.din("cdec", [128, 4])
    jt_d = P.din("jt", [128, 128])
    sel_d = P.din("sel", [128, 64])
    xs = [P.dscr("xsA", [nseq, L, D]), P.dscr("xsB", [nseq, L, D])]
    OPs_ = P.dscr("OPs", [nseq, L, 512])
    YPs_ = P.dscr("YPs", [nseq, L, 512])
    SGs_ = P.dscr("SGs", [nseq, L, 1024])
    QTs_ = P.dscr("QTs", [nseq, nch, 128, 512], BF16)
    KRs_ = P.dscr("KRs", [nseq, L, 512], BF16)
    VBs_ = P.dscr("VBs", [nseq, L, 512], BF16)
    UBs_ = P.dscr("UBs", [nseq, L, 512], BF16)
    SCRB = [[Buf(f"{n}{i}") for n in "OP YP SG QT KR VB UB".split()] for i in range(nseq)]
    b_xs = [[Buf(f"xs{i}_{s}") for s in range(nseq)] for i in range(2)]
    b_yout = [Buf(f"yout{s}") for s in range(nseq)]
    b_xin = Buf("xin")

    ident_bf, b_ident = P.sb("ident_bf", [128, 128], BF16)
    jmat_bf, b_jmat = P.sb("jmat_bf", [128, 128], BF16)
    identf, b_identf = P.sb("identf", [128, 128])
    epsT, b_eps = P.sb("epsT", [128, 1])
    ss, b_ss = P.sb("ss", [128, 4])
    sd, b_sd = P.sb("sd", [128, 4])
    rstd, b_rstd = P.sb("rstd", [128, 4])
    scT, b_scT = P.sb("scT", [128, 8, nseq])
    cTs, b_cTs = P.sb("cTs", [128, 8, nseq])
    modT, b_modT = P.sb("modT", [128, 24, nseq])
    gsT, b_gsT = P.sb("gsT", [128, 8, nseq])
    ggT, b_ggT = P.sb("ggT", [128, 8, nseq])
    ggrow = [P.sb(f"ggrow{s}", [128, 1024]) for s in range(nseq)]
    vecs, b_vecs = P.sb("vecs", [128, 40])
    psb = [P.ps(f"ps{i}") for i in range(8)]
    ps = [p[0] for p in psb]
    bps = [p[1] for p in psb]

    P.dma("sp", identf[:], ident_d, writes=[b_identf])
    P.dma("pool", ident_bf[:], ident_d, writes=[b_ident])
    P.dma("pool", jmat_bf[:], jmat_d, writes=[b_jmat])
    P.memset("pool", epsT[:], EPS, writes=[b_eps])
    P.dma("sp", cTs[:], cT, writes=[b_cTs])
    P.act(scT[:], cTs[:], AF.Sigmoid, reads=[b_cTs], writes=[b_scT])
    P.tt("dve", scT[:], scT[:], cTs[:], ALU.mult, reads=[b_scT, b_cTs], writes=[b_scT])

    def rstd_from_ss(col, n_feat):
        P.act(sd[:, col:col + 1], ss[:, col:col + 1], AF.Sqrt, reads=[b_ss, b_eps], writes=[b_sd],
              scale=1.0 / n_feat, bias=epsT[:, 0:1])
        P.T.op("dve", lambda q: q.reciprocal(out=rstd[:, col:col + 1], in_=sd[:, col:col + 1]),
               reads=[b_sd], writes=[b_rstd])

    def adaln(li, st_unused):
      with contextlib.ExitStack() as st:
        wm = [st.enter_context(P.sbt(f"wm{j}_{li}", [128, 8, 128], F32)) for j in range(2)]
        bwm = [Buf("wm0"), Buf("wm1")]
        gbl = st.enter_context(P.sbt(f"gbl_{li}", [128, 8, 128], F32))
        b_gbl = Buf("gbl")
        P.dma("sp", vecs[:, 0:8], npreT[li], writes=[b_vecs])
        P.dma("sp", vecs[:, 8:16], npostT[li], writes=[b_vecs])
        P.dma("sp", vecs[:, 16:40], bmodT[li], writes=[b_vecs])
        wsrc = wmod[li].rearrange("(k p) n -> p k n", p=128)
        for j in range(24):
            sl = j % 2
            P.dma("sp", wm[sl][:], wsrc[:, :, j * 128:(j + 1) * 128], writes=[bwm[sl]])
            for k in range(8):
                P.mm(ps[7][:, j * nseq:(j + 1) * nseq], wm[sl][:, k, :], scT[:, k, :], k == 0, k == 7,
                     reads=[bwm[sl], b_scT], writes=[bps[7]], inc=(k == 7))
        psv = ps[7][:, 0:24 * nseq].rearrange("p (j s) -> p j s", s=nseq)
        P.tt("dve", modT[:], psv, vecs[:, 16:40].unsqueeze(2).to_broadcast([128, 24, nseq]), ALU.add,
             reads=[bps[7], b_vecs], writes=[b_modT])
        P.ts("dve", gsT[:], modT[:, 8:16, :], 1.0, None, ALU.add, None, reads=[b_modT], writes=[b_gsT])
        P.tt("dve", gsT[:], gsT[:], vecs[:, 0:8].unsqueeze(2).to_broadcast([128, 8, nseq]), ALU.mult,
             reads=[b_gsT, b_vecs], writes=[b_gsT])
        P.tt("dve", ggT[:], modT[:, 16:24, :], vecs[:, 8:16].unsqueeze(2).to_broadcast([128, 8, nseq]), ALU.mult,
             reads=[b_modT, b_vecs], writes=[b_ggT])
        for s in range(nseq):
            P.cp("dve", gbl[:], ggT[:, :, s:s + 1].to_broadcast([128, 8, 128]), reads=[b_ggT], writes=[b_gbl])
            for c in range(8):
                bk = 5 + c // 4
                P.mm(ps[bk][:, (c % 4) * 128:(c % 4 + 1) * 128], gbl[:, c, :], identf[:], True, True,
                     reads=[b_gbl, b_identf], writes=[bps[bk]], inc=(c % 4 == 3))
            P.cp("dve", ggrow[s][0][:, 0:512], ps[5][:], reads=[bps[5]], writes=[ggrow[s][1]])
            P.act(ggrow[s][0][:, 512:1024], ps[6][:], AF.Copy, reads=[bps[6]], writes=[ggrow[s][1]])

    def make_common(st, tag):
        C = {}
        C["xt"] = [st.enter_context(P.sbt(f"xt{j}_{tag}", [128, 1024], F32)) for j in range(2)]
        C["bxt"] = [Buf("xt0"), Buf("xt1")]
        C["xn"] = st.enter_context(P.sbt(f"xn_{tag}", [128, 1024], BF16))
        C["bxn"] = Buf("xn")
        C["junk"] = st.enter_context(P.sbt(f"junk_{tag}", [128, 1024], BF16))
        C["bjunk"] = Buf("junk")
        C["yt"] = st.enter_context(P.sbt(f"yt_{tag}", [128, 1024], F32))
        C["byt"] = Buf("yt")
        C["cnt"] = 0
        return C

    def prenorm_parts(C, s, src, bsrc, n, hT, bhT, col0):
        sl = C["cnt"] % 2
        C["cnt"] += 1
        xt, bxt = C["xt"][sl], C["bxt"][sl]

        def p1():
            P.dma("sp", xt[:], src[s, n * 128:(n + 1) * 128, :], reads=[bsrc], writes=[bxt])
            P.act(C["yt"][:], xt[:], AF.Square, reads=[bxt], writes=[C["byt"]])
            P.T.op("dve", lambda q, yt_=C["yt"]: q.reduce_sum(out=ss[:, 0:1], in_=yt_[:], axis=mybir.AxisListType.X),
                   reads=[C["byt"]], writes=[b_ss])
            P.act(sd[:, 0:1], ss[:, 0:1], AF.Sqrt, reads=[b_ss, b_eps], writes=[b_sd], scale=1.0 / D, bias=epsT[:, 0:1])

        def p2():
            P.T.op("dve", lambda q: q.reciprocal(out=rstd[:, 0:1], in_=sd[:, 0:1]), reads=[b_sd], writes=[b_rstd])
            P.act(C["xn"][:], xt[:], AF.Copy, reads=[bxt, b_rstd], writes=[C["bxn"]], scale=rstd[:, 0:1])
            for b in range(2):
                for j in range(4):
                    k = 4 * b + j
                    P.mm(ps[b][:, j * 128:(j + 1) * 128], C["xn"][:, k * 128:(k + 1) * 128], ident_bf[:], True, True,
                         reads=[C["bxn"], b_ident], writes=[bps[b]], inc=(j == 3))

        def p3():
            for b in range(2):
                for j in range(4):
                    k = 4 * b + j
                    o = hT[:, k, col0:col0 + 128]
                    i_ = ps[b][:, j * 128:(j + 1) * 128]
                    if b == 0 or P.pne == 'act':
                        P.act(o, i_, AF.Identity, reads=[bps[b], b_gsT, b_modT], writes=[bhT],
                              scale=gsT[:, k, s:s + 1], bias=modT[:, k, s:s + 1])
                    else:
                        P.ts("dve", o, i_, gsT[:, k, s:s + 1], modT[:, k, s:s + 1], ALU.mult, ALU.add,
                             reads=[bps[b], b_gsT, b_modT], writes=[bhT])
        return p1, p2, p3

    def prenorm(C, s, src, bsrc, n, hT, bhT, col0):
        p1, p2, p3 = prenorm_parts(C, s, src, bsrc, n, hT, bhT, col0)
        p1()
        p2()
        p3()

    def post(C, s, ypb, xt, bxt, dst, bdst, n):
        yt, byt = C["yt"], C["byt"]
        for h in range(2):
            P.act(yt[:, h * 512:(h + 1) * 512], ps[ypb[h]][:], AF.Square, reads=[bps[ypb[h]]], writes=[byt])
        P.T.op("dve", lambda q, yt_=yt: q.reduce_sum(out=ss[:, 3:4], in_=yt_[:], axis=mybir.AxisListType.X),
               reads=[byt], writes=[b_ss])
        rstd_from_ss(3, D)
        for h in range(2):
            P.act(yt[:, h * 512:(h + 1) * 512], ps[ypb[h]][:], AF.Copy, reads=[bps[ypb[h]], b_rstd], writes=[byt],
                  scale=rstd[:, 3:4])
        P.tt("dve", yt[:], yt[:], ggrow[s][0][:], ALU.mult, reads=[byt, ggrow[s][1]], writes=[byt])
        P.tt("pool", yt[:], yt[:], xt[:], ALU.add, reads=[byt, bxt], writes=[byt])
        P.dma(getattr(P, "stq", "pool"), dst[s, n * 128:(n + 1) * 128, :], yt[:], reads=[byt], writes=[bdst])

    def sincos(st, tag, phi, bphi, shape, out_sin, out_cos, bout):
        tf = st.enter_context(P.sbt(f"sc_tf_{tag}", shape, F32))
        ti = st.enter_context(P.sbt(f"sc_ti_{tag}", shape, I32))
        btf, bti = Buf("tf"), Buf("ti")
        for shift, o in ((0.0, out_sin), (0.5 * math.pi, out_cos)):
            P.ts("dve", tf[:], phi, shift, 1.0 / TWO_PI, ALU.add, ALU.mult, reads=[bphi], writes=[btf])
            P.cp("dve", ti[:], tf[:], reads=[btf], writes=[bti])
            P.cp("dve", tf[:], ti[:], reads=[bti], writes=[btf])
            P.stt("dve", tf[:], tf[:], -TWO_PI, phi, ALU.mult, ALU.add, reads=[btf, bphi], writes=[btf])
            P.ts("dve", tf[:], tf[:], shift, 0.999999, ALU.add, ALU.mult, reads=[btf], writes=[btf])
            P.act(o, tf[:], AF.Sin, reads=[btf], writes=[bout])

    def even_layer(li, src, bsrc, dst, bdst):
        j = li // 2
        P.pool_to_dve = (os.environ.get('P2D', '0') == '1')
        P.pne = 'act'
        P.stq = STQ
        with contextlib.ExitStack() as st:
            def S(name, shape, dt=F32):
                return st.enter_context(P.sbt(f"{name}_e{li}", list(shape), dt)), Buf(name)
            adaln(li, st)
            C = make_common(st, f"e{li}")
            win, b_win = S("win", [128, 8, 3072], BF16)
            wout, b_wout = S("wout", [128, 8, 1024], BF16)
            wglu, b_wglu = S("wglu", [128, 4, 512], BF16)
            drow, b_drow = S("drow", [128, 512])
            dtab, b_dtab = S("dtab", [128, 4, 128])
            gam, b_gam = S("gam", [128, 4, 4])
            cdec, b_cdec = S("cdec", [128, 4])
            jt, b_jt = S("jt", [128, 128])
            wsrc = w_in_ab[j].rearrange("(k p) n -> p k n", p=128)
            for k in range(8):
                P.dma("pool", win[:, k, :], wsrc[:, k, :], writes=[b_win])
            P.dma("pool", wout[:], w_out_ab[j].rearrange("(k p) n -> p k n", p=128), writes=[b_wout])
            P.dma("pool", wglu[:], w_glu[j].rearrange("(k p) n -> p k n", p=128), writes=[b_wglu])
            P.dma("sp", drow[:], ssm_d[j].partition_broadcast(128), writes=[b_drow])
            P.dma("sp", dtab[:], dtab_d, writes=[b_dtab])
            P.dma("sp", gam[:], gam_d, writes=[b_gam])
            P.dma("sp", cdec[:], cdec_d, writes=[b_cdec])
            P.dma("sp", jt[:], jt_d, writes=[b_jt])
            WB = [S(f"WB{r}", [128, 2048], BF16) for r in range(2)]
            WC = [S(f"WC{r}", [128, 512], BF16) for r in range(3)]
            COS, b_COS = S("COS", [128, 16, 128])
            SIN, b_SIN = S("SIN", [128, 16, 128])
            RHO0, b_RHO0 = S("RHO0", [128, 16, 128])
            Gc, b_Gc = S("Gc", [128, 2, 16])
            cin = [S(f"cin{i}", [128, 2, 16]) for i in range(2)]
            stf, b_stf = S("stf", [128, 512])
            stbf, b_stbf = S("stbf", [128, 512], BF16)
            hn, b_hn = S("hn", [128, 16])
            lst, b_lst = S("lst", [128, 8, 16])
            lastb = [S(f"lastb{i}", [128, 2, 16]) for i in range(2)]

            def s5_setup(d):
                with contextlib.ExitStack() as s2:
                    def S2(name, shape, dt=F32):
                        return s2.enter_context(P.sbt(f"{name}_e{li}d{d}", list(shape), dt)), Buf(name)
                    sm, b_sm = S2("sm", [128, 3, 16])
                    P.dma("sp", sm[:], s5_sm3[j, d].rearrange("t p s -> p t s"), writes=[b_sm])
                    dl, b_dl = S2("dl", [128, 16])
                    zr, b_zr = S2("zr", [128, 16])
                    zi, b_zi = S2("zi", [128, 16])
                    rho, b_rho = S2("rho", [128, 16])
                    P.act(dl[:], sm[:, 2, :], AF.Exp, reads=[b_sm], writes=[b_dl])
                    P.tt("dve", zr[:], sm[:, 0, :], dl[:], ALU.mult, reads=[b_sm, b_dl], writes=[b_zr])
                    P.tt("dve", zi[:], sm[:, 1, :], dl[:], ALU.mult, reads=[b_sm, b_dl], writes=[b_zi])
                    P.act(rho[:], zr[:], AF.Exp, reads=[b_zr], writes=[b_rho])
                    for g4 in range(4):
                        with contextlib.ExitStack() as s4:
                            phi = s4.enter_context(P.sbt(f"phi_e{li}d{d}g{g4}", [128, 4, 128], F32))
                            b_phi = Buf("phi")
                            P.tt("dve", phi[:], zi[:, 4 * g4:4 * g4 + 4].unsqueeze(2).to_broadcast([128, 4, 128]),
                                 jt[:].unsqueeze(1).to_broadcast([128, 4, 128]), ALU.mult, reads=[b_zi, b_jt], writes=[b_phi])
                            sincos(s4, f"t{li}{d}{g4}", phi[:], b_phi, [128, 4, 128], SIN[:, 4 * g4:4 * g4 + 4, :],
                                   COS[:, 4 * g4:4 * g4 + 4, :], b_COS)
                            T_barrier()
                    P.cp("dve", RHO0[:], rho[:].unsqueeze(2).to_broadcast([128, 16, 128]), reads=[b_rho], writes=[b_RHO0])
                    P.memset("dve", RHO0[:, :, 0:1], 0.0, writes=[b_RHO0])
                    ph2, b_ph2 = S2("ph2", [128, 16])
                    sn2, b_sn2 = S2("sn2", [128, 2, 16])
                    P.ts("dve", ph2[:], zi[:], 128.0, None, ALU.mult, None, reads=[b_zi], writes=[b_ph2])
                    sincos(s2, f"g{li}{d}", ph2[:], b_ph2, [128, 16], sn2[:, 1, :], sn2[:, 0, :], b_sn2)
                    P.tt("dve", Gc[:], sn2[:], rho[:].unsqueeze(1).to_broadcast([128, 2, 16]), ALU.mult,
                         reads=[b_sn2, b_rho], writes=[b_Gc])
                    T_barrier()
                for cc in range(8):
                  with contextlib.ExitStack() as s3:
                    def S3(name, shape, dt=F32):
                        return s3.enter_context(P.sbt(f"{name}_e{li}d{d}c{cc}", list(shape), dt)), Buf(name)
                    rp, b_rp = S3("rp", [128, 3, 256])
                    P.dma("sp", rp[:], s5_rep3[j, d][:, cc * 256:(cc + 1) * 256].partition_broadcast(128), writes=[b_rp])
                    r_dl, b_r_dl = S3("r_dl", [128, 256])
                    r_zr, b_r_zr = S3("r_zr", [128, 256])
                    r_zi, b_r_zi = S3("r_zi", [128, 256])
                    r_rho, b_r_rho = S3("r_rho", [128, 256])
                    r_sn, b_r_sn = S3("r_sn", [128, 2, 256])
                    P.act(r_dl[:], rp[:, 2, :], AF.Exp, reads=[b_rp], writes=[b_r_dl])
                    P.tt("dve", r_zr[:], rp[:, 0, :], r_dl[:], ALU.mult, reads=[b_rp, b_r_dl], writes=[b_r_zr])
                    P.tt("dve", r_zi[:], rp[:, 1, :], r_dl[:], ALU.mult, reads=[b_rp, b_r_dl], writes=[b_r_zi])
                    P.act(r_rho[:], r_zr[:], AF.Exp, reads=[b_r_zr], writes=[b_r_rho])
                    sincos(s3, f"r{li}{d}{cc}", r_zi[:], b_r_zi, [128, 256], r_sn[:, 1, :], r_sn[:, 0, :], b_r_sn)
                    P.tt("dve", r_sn[:], r_sn[:], r_rho[:].unsqueeze(1).to_broadcast([128, 2, 256]), ALU.mult,
                         reads=[b_r_sn, b_r_rho], writes=[b_r_sn])
                    P.ts("dve", r_sn[:, 0, :], r_sn[:, 0, :], -1.0, None, ALU.add, None, reads=[b_r_sn], writes=[b_r_sn])
                    P.tt("dve", r_dl[:], rp[:, 0, :], rp[:, 0, :], ALU.mult, reads=[b_rp], writes=[b_r_dl])
                    P.tt("dve", r_zr[:], rp[:, 1, :], rp[:, 1, :], ALU.mult, reads=[b_rp], writes=[b_r_zr])
                    P.tt("dve", r_dl[:], r_dl[:], r_zr[:], ALU.add, reads=[b_r_dl, b_r_zr], writes=[b_r_dl])
                    P.T.op("dve", lambda q, r_dl=r_dl: q.reciprocal(out=r_dl[:], in_=r_dl[:]), reads=[b_r_dl], writes=[b_r_dl])
                    P.tt("dve", r_zr[:], r_sn[:, 0, :], rp[:, 0, :], ALU.mult, reads=[b_r_sn, b_rp], writes=[b_r_zr])
                    P.tt("dve", r_rho[:], r_sn[:, 1, :], rp[:, 1, :], ALU.mult, reads=[b_r_sn, b_rp], writes=[b_r_rho])
                    P.tt("dve", r_zr[:], r_zr[:], r_rho[:], ALU.add, reads=[b_r_zr, b_r_rho], writes=[b_r_zr])
                    P.tt("dve", r_zr[:], r_zr[:], r_dl[:], ALU.mult, reads=[b_r_zr, b_r_dl], writes=[b_r_zr])
                    P.tt("dve", r_zi[:], r_sn[:, 1, :], rp[:, 0, :], ALU.mult, reads=[b_r_sn, b_rp], writes=[b_r_zi])
                    P.tt("dve", r_rho[:], r_sn[:, 0, :], rp[:, 1, :], ALU.mult, reads=[b_r_sn, b_rp], writes=[b_r_rho])
                    P.tt("dve", r_zi[:], r_zi[:], r_rho[:], ALU.subtract, reads=[b_r_zi, b_r_rho], writes=[b_r_zi])
                    P.tt("dve", r_zi[:], r_zi[:], r_dl[:], ALU.mult, reads=[b_r_zi, b_r_dl], writes=[b_r_zi])
                    bl, b_bl = S3("bl", [128, 2, 256])
                    P.dma("sp", bl[:], s5_bl[j, d][:, :, cc * 256:(cc + 1) * 256].rearrange("r p n -> p r n"), writes=[b_bl])
                    P.tt("dve", r_dl[:], r_zr[:], bl[:, 0, :], ALU.mult, reads=[b_r_zr, b_bl], writes=[b_r_dl])
                    P.tt("dve", r_rho[:], r_zi[:], bl[:, 1, :], ALU.mult, reads=[b_r_zi, b_bl], writes=[b_r_rho])
                    P.tt("dve", WB[0][0][:, cc * 256:(cc + 1) * 256], r_dl[:], r_rho[:], ALU.subtract, reads=[b_r_dl, b_r_rho], writes=[WB[0][1]])
                    P.tt("dve", r_dl[:], r_zr[:], bl[:, 1, :], ALU.mult, reads=[b_r_zr, b_bl], writes=[b_r_dl])
                    P.tt("dve", r_rho[:], r_zi[:], bl[:, 0, :], ALU.mult, reads=[b_r_zi, b_bl], writes=[b_r_rho])
                    P.tt("dve", WB[1][0][:, cc * 256:(cc + 1) * 256], r_dl[:], r_rho[:], ALU.add, reads=[b_r_dl, b_r_rho], writes=[WB[1][1]])

                    T_barrier()
                with contextlib.ExitStack() as s2:
                    def S2(name, shape, dt=F32):
                        return s2.enter_context(P.sbt(f"{name}_e{li}d{d}x", list(shape), dt)), Buf(name)
                    cl, b_cl = S2("cl", [128, 2, 512])
                    P.dma("sp", cl[:], s5_cl[j, d].rearrange("r p n -> p r n"), writes=[b_cl])
                    P.cp("dve", WC[0][0][:], cl[:, 0, :], reads=[b_cl], writes=[WC[0][1]])
                    P.ts("dve", WC[1][0][:], cl[:, 0, :], -1.0, None, ALU.mult, None, reads=[b_cl], writes=[WC[1][1]])
                    P.ts("dve", WC[2][0][:], cl[:, 1, :], -1.0, None, ALU.mult, None, reads=[b_cl], writes=[WC[2][1]])
                    P.memset("dve", cin[0][0][:], 0.0, writes=[cin[0][1]])
                    T_barrier()

            import types

            def alloc_work(wk, passB):
                def W(name, shape, dt=F32):
                    return wk.enter_context(P.sbt(f"{name}_e{li}", list(shape), dt)), Buf(name)
                w = types.SimpleNamespace()
                w.g = [types.SimpleNamespace() for _ in range(2)]
                tmps = {nm: W(f"{nm}m", [128, 512]) for nm in ("tA", "tB", "tC", "tD")}
                Pk1 = [W(f"Pk_{k}", [128, 512], BF16) for k in range(4)]
                for i, g in enumerate(w.g):
                    for nm in ("wre", "wim", "sre", "sim"):
                        setattr(g, nm, W(f"{nm}{i}", [128, 512]))
                    for nm in ("tA", "tB", "tC", "tD"):
                        setattr(g, nm, tmps[nm])
                    g.Pk = Pk1
                w.uT = W("uT", [128, 4, 128], BF16)
                w.kf = W("kf", [128, 512], BF16)
                w.tC = W("tCr", [128, 512])
                if not passB:
                    w.hT = [W(f"hT{i}", [128, 8, 128], BF16) for i in range(2)]
                    w.rot = [W(f"rot{i}", [128, 2, 2, 128]) for i in range(2)]
                    w.rA = W("rA", [128, 512])
                    w.rB = W("rB", [128, 512])
                    w.qr = W("qr", [128, 512], BF16)
                    w.kr = [W("kr", [128, 512], BF16)]
                    w.vb = [W("vb", [128, 512], BF16)]
                    w.QT = [W("QT", [128, 512], BF16)]
                    w.KT = W("KT", [128, 512], BF16)
                    w.STb = W("STb", [128, 512], BF16)
                    w.sg = [W("sg", [128, 1024])]
                    w.du = W("du", [128, 512])
                    w.ub = [W("ub", [128, 512], BF16)]
                    w.opt = [W("opt", [128, 512])]
                    w.ypt = [W("ypt", [128, 512])]
                else:
                    w.kr = [W(f"kr{i}", [128, 512], BF16) for i in range(2)]
                    w.vb = [W(f"vb{i}", [128, 512], BF16) for i in range(2)]
                    w.QT = [W(f"QT{i}", [128, 512], BF16) for i in range(2)]
                    w.sga = W("sga", [128, 512])
                    w.sgb = [W(f"sgb{i}", [128, 512]) for i in range(2)]
                    w.uT2 = W("uT2", [128, 4, 128], BF16)
                    w.ub = [W(f"ub{i}", [128, 512], BF16) for i in range(2)]
                    w.opt = [W(f"opt{i}", [128, 512]) for i in range(2)]
                    w.ypt = [W(f"ypt{i}", [128, 512]) for i in range(2)]
                    w.oab = [W(f"oab{i}", [128, 1024], BF16) for i in range(2)]
                    w.oT = W("oT", [128, 8, 128], BF16)
                    w.ygb = W("ygb", [128, 512], BF16)
                    w.tA = W("tAh", [128, 512])
                    w.tB = W("tBh", [128, 512])
                    w.tD = W("tDh", [128, 512])
                return w

            def s5_chunk(w, ci, rev, hooks=()):
                cur, nxt = cin[ci % 2], cin[(ci + 1) % 2]
                lb, b_lb = lastb[ci % 2]
                uT, b_uT = w.uT
                hooks = list(hooks)

                def hook():
                    if hooks:
                        hooks.pop(0)()

                def banks(g):
                    return (3, 4) if g % 2 == 0 else (5, 6)

                def stageApe(g):
                    br, bi = banks(g)
                    for t in range(4):
                        s_ = 4 * g + t
                        P.mm(ps[br][:, t * 128:(t + 1) * 128], WB[0][0][:, s_ * 128:(s_ + 1) * 128], uT[:, g, :], True, False,
                             reads=[WB[0][1], b_uT], writes=[bps[br]], inc=False)
                        P.mm(ps[bi][:, t * 128:(t + 1) * 128], WB[1][0][:, s_ * 128:(s_ + 1) * 128], uT[:, g, :], True, False,
                             reads=[WB[1][1], b_uT], writes=[bps[bi]], inc=False)
                    o_r = ps[br][:].rearrange("p (a b) -> p a b", a=4)[:, :, 0:1]
                    o_i = ps[bi][:].rearrange("p (a b) -> p a b", a=4)[:, :, 0:1]
                    P.mm(o_r, identf[:], cur[0][:, 0, 4 * g:4 * g + 4].unsqueeze(2), False, True,
                         reads=[b_identf, cur[1]], writes=[bps[br]], inc=False)
                    P.mm(o_i, identf[:], cur[0][:, 1, 4 * g:4 * g + 4].unsqueeze(2), False, True,
                         reads=[b_identf, cur[1]], writes=[bps[bi]], inc=True)

                def stageAve(g):
                    br, bi = banks(g)
                    G = w.g[g % 2]
                    Cg = COS[:, 4 * g:4 * g + 4, :].rearrange("p a b -> p (a b)")
                    Sg = SIN[:, 4 * g:4 * g + 4, :].rearrange("p a b -> p (a b)")
                    P.tt("dve", G.tA[0][:], ps[br][:], Cg, ALU.mult, reads=[bps[br], b_COS], writes=[G.tA[1]])
                    P.tt("dve", G.tB[0][:], ps[bi][:], Sg, ALU.mult, reads=[bps[bi], b_COS], writes=[G.tB[1]])
                    P.tt(S5E, G.wre[0][:], G.tA[0][:], G.tB[0][:], ALU.add, reads=[G.tA[1], G.tB[1]], writes=[G.wre[1]])
                    P.tt("dve", G.tC[0][:], ps[bi][:], Cg, ALU.mult, reads=[bps[bi], b_COS], writes=[G.tC[1]])
                    P.stt("dve", G.tD[0][:], ps[br][:], -1.0, Sg, ALU.mult, ALU.mult, reads=[bps[br], b_COS], writes=[G.tD[1]])
                    P.tt(S5E, G.wim[0][:], G.tC[0][:], G.tD[0][:], ALU.add, reads=[G.tC[1], G.tD[1]], writes=[G.wim[1]])

                def stageB(g):
                    G = w.g[g % 2]
                    Rg = RHO0[:, 4 * g:4 * g + 4, :].rearrange("p a b -> p (a b)")
                    P.T.op("dve", lambda q, Rg=Rg, o=G.sre[0], i=G.wre[0]: q.tensor_tensor_scan(
                        out=o[:], data0=Rg, data1=i[:], initial=0.0, op0=ALU.mult, op1=ALU.add),
                        reads=[b_RHO0, G.wre[1]], writes=[G.sre[1]])
                    P.T.op("dve", lambda q, Rg=Rg, o=G.sim[0], i=G.wim[0]: q.tensor_tensor_scan(
                        out=o[:], data0=Rg, data1=i[:], initial=0.0, op0=ALU.mult, op1=ALU.add),
                        reads=[b_RHO0, G.wim[1]], writes=[G.sim[1]])
                    sr3 = G.sre[0][:].rearrange("p (a b) -> p a b", a=4)
                    si3 = G.sim[0][:].rearrange("p (a b) -> p a b", a=4)
                    P.act(lb[:, 0, 4 * g:4 * g + 4].unsqueeze(2), sr3[:, :, 127:128], AF.Copy, reads=[G.sre[1]], writes=[b_lb])
                    P.act(lb[:, 1, 4 * g:4 * g + 4].unsqueeze(2), si3[:, :, 127:128], AF.Copy, reads=[G.sim[1]], writes=[b_lb])

                    def pv(ap):
                        v = ap.rearrange("p (a b) -> p a b", a=4)
                        return v[:, :, ::-1] if rev else v
                    C3 = COS[:, 4 * g:4 * g + 4, :]
                    S3 = SIN[:, 4 * g:4 * g + 4, :]
                    e2 = "dve" if rev else S5E
                    P.tt("dve", pv(G.Pk[0][0][:]), sr3, C3, ALU.mult, reads=[G.sre[1], b_COS], writes=[G.Pk[0][1]])
                    P.tt(e2, pv(G.Pk[1][0][:]), si3, S3, ALU.mult, reads=[G.sim[1], b_COS], writes=[G.Pk[1][1]])
                    P.tt("dve", pv(G.Pk[2][0][:]), sr3, S3, ALU.mult, reads=[G.sre[1], b_COS], writes=[G.Pk[2][1]])
                    P.tt(e2, pv(G.Pk[3][0][:]), si3, C3, ALU.mult, reads=[G.sim[1], b_COS], writes=[G.Pk[3][1]])
                    for t in range(4):
                        s_ = 4 * g + t
                        o = ps[7][:, 32 * s_:32 * s_ + 32]
                        wsl = slice(32 * s_, 32 * s_ + 32)
                        tsl = slice(128 * t, 128 * t + 128)
                        Pk = G.Pk
                        P.mm(o, Pk[0][0][:, tsl], WC[0][0][:, wsl], True, False, reads=[Pk[0][1], WC[0][1]], writes=[bps[7]], inc=False)
                        P.mm(o, Pk[1][0][:, tsl], WC[1][0][:, wsl], False, False, reads=[Pk[1][1], WC[1][1]], writes=[bps[7]], inc=False)
                        P.mm(o, Pk[2][0][:, tsl], WC[2][0][:, wsl], False, False, reads=[Pk[2][1], WC[2][1]], writes=[bps[7]], inc=False)
                        P.mm(o, Pk[3][0][:, tsl], WC[2][0][:, wsl], False, True, reads=[Pk[3][1], WC[2][1]], writes=[bps[7]], inc=(t == 3))

                stageApe(0)
                stageAve(0)
                stageApe(1)
                hook()
                stageAve(1)
                hook()
                stageApe(2)
                stageB(0)
                hook()
                stageAve(2)
                hook()
                stageApe(3)
                stageB(1)
                hook()
                stageAve(3)
                hook()
                stageB(2)
                hook()
                stageB(3)
                hook()
                while hooks:
                    hook()
                l4 = lst[:]
                P.tt("pool", l4[:, 0, :], lb[:, 0, :], Gc[:, 0, :], ALU.mult, reads=[b_lb, b_Gc], writes=[b_lst])
                P.tt("pool", l4[:, 1, :], lb[:, 1, :], Gc[:, 1, :], ALU.mult, reads=[b_lb, b_Gc], writes=[b_lst])
                P.tt("pool", l4[:, 2, :], lb[:, 1, :], Gc[:, 0, :], ALU.mult, reads=[b_lb, b_Gc], writes=[b_lst])
                P.tt("pool", l4[:, 3, :], lb[:, 0, :], Gc[:, 1, :], ALU.mult, reads=[b_lb, b_Gc], writes=[b_lst])
                P.tt("pool", nxt[0][:, 0, :], l4[:, 0, :], l4[:, 1, :], ALU.subtract, reads=[b_lst], writes=[nxt[1]])
                P.tt("pool", nxt[0][:, 1, :], l4[:, 2, :], l4[:, 3, :], ALU.add, reads=[b_lst], writes=[nxt[1]])

            def ret_state_update(w, kbuf, b_kbuf, vbt, b_vbt, gcol):
                kf, b_kf = w.kf
                P.tt("pool", kf[:].rearrange("p (h e) -> p h e", h=4), kbuf[:].rearrange("p (h e) -> p h e", h=4),
                     gam[:, gcol, :].unsqueeze(2).to_broadcast([128, 4, 128]), ALU.mult,
                     reads=[b_kbuf, b_gam], writes=[b_kf])
                for h in range(4):
                    hs = slice(h * 128, (h + 1) * 128)
                    P.mm(ps[5][:, hs], kf[:, hs], vbt[:, hs], True, True, reads=[b_kf, b_vbt], writes=[bps[5]], inc=(h == 3))
                P.tt("pool", stf[:].rearrange("p (h e) -> p h e", h=4), stf[:].rearrange("p (h e) -> p h e", h=4),
                     cdec[:].unsqueeze(2).to_broadcast([128, 4, 128]), ALU.mult, reads=[b_stf, b_cdec], writes=[b_stf])
                P.tt("dve", stf[:], stf[:], ps[5][:], ALU.add, reads=[b_stf, bps[5]], writes=[b_stf])
                P.act(stbf[:], stf[:], AF.Copy, reads=[b_stf], writes=[b_stbf])

            def passA(s, w):
                OPs, YPs, SGs, QTs, KRs, VBs, UBs = OPs_[s], YPs_[s], SGs_[s], QTs_[s], KRs_[s], VBs_[s], UBs_[s]
                b_OPs, b_YPs, b_SGs, b_QTs, b_KRs, b_VBs, b_UBs = SCRB[s]
                P.memset("dve", stf[:], 0.0, writes=[b_stf])
                P.memset("pool", stbf[:], 0.0, writes=[b_stbf])
                P.memset("dve", cin[0][0][:], 0.0, writes=[cin[0][1]])
                nA = nch if KSTOP not in ('setup',) else 0
                if nA:
                    prenorm(C, s, src, bsrc[s], 0, w.hT[0][0], w.hT[0][1], 0)
                for n in range(nA):
                    tsl = slice(n * 128, (n + 1) * 128)
                    hT, b_hT = w.hT[n % 2]
                    rot, b_rot = w.rot[n % 2]
                    if n == 0:
                        P.dma("sp", rot[:], rot_d[tsl], writes=[b_rot])
                    if n + 1 < nA:
                        P.dma("sp", w.rot[(n + 1) % 2][0][:], rot_d[(n + 1) * 128:(n + 2) * 128], writes=[w.rot[(n + 1) % 2][1]])
                    if n + 1 < nA:
                        p1, p2, p3 = prenorm_parts(C, s, src, bsrc[s], n + 1, w.hT[(n + 1) % 2][0], w.hT[(n + 1) % 2][1], 0)
                    else:
                        p1 = p2 = p3 = (lambda: None)
                    qr, b_qr = w.qr
                    kr, b_kr = w.kr[0]
                    vb, b_vb = w.vb[0]
                    QT, b_QT = w.QT[0]
                    KT, b_KT = w.KT
                    STb, b_STb = w.STb
                    sg, b_sg = w.sg[0]
                    du, b_du = w.du
                    ub, b_ub = w.ub[0]
                    opt, b_opt = w.opt[0]
                    ypt, b_ypt = w.ypt[0]
                    tC, b_tC = w.tC
                    uT, b_uT = w.uT
                    def inproj(col, bank):
                        for k in range(8):
                            P.mm(ps[bank][:], hT[:, k, :], win[:, k, col * 512:(col + 1) * 512], k == 0, k == 7,
                                 reads=[b_hT, b_win], writes=[bps[bank]], inc=(k == 7))

                    def rotary(zb, tbl, outb, b_outb):
                        rA, b_rA = w.rA
                        rB, b_rB = w.rB
                        z3 = ps[zb][:].rearrange("p (h e) -> p h e", h=4)
                        a3 = rA[:].rearrange("p (h e) -> p h e", h=4)
                        b3 = rB[:].rearrange("p (h e) -> p h e", h=4)
                        P.tt("dve", a3, z3, rot[:, tbl, 0, :].unsqueeze(1).to_broadcast([128, 4, 128]), ALU.mult,
                             reads=[bps[zb], b_rot], writes=[b_rA])
                        P.tt("dve", b3[:, :, 0:64], z3[:, :, 64:128], rot[:, tbl, 1, 0:64].unsqueeze(1).to_broadcast([128, 4, 64]),
                             ALU.mult, reads=[bps[zb], b_rot], writes=[b_rB])
                        P.tt("dve", b3[:, :, 64:128], z3[:, :, 0:64], rot[:, tbl, 1, 64:128].unsqueeze(1).to_broadcast([128, 4, 64]),
                             ALU.mult, reads=[bps[zb], b_rot], writes=[b_rB])
                        P.tt("pool", outb[:], rA[:], rB[:], ALU.add, reads=[b_rA, b_rB], writes=[b_outb])

                    inproj(4, 2)
                    P.act(ub[:], ps[2][:], AF.Copy, reads=[bps[2]], writes=[b_ub])
                    P.tt("dve", du[:], ps[2][:], drow[:], ALU.mult, reads=[bps[2], b_drow, b_ub], writes=[b_du])
                    P.dma(STQ, UBs[tsl], ub[:], reads=[b_ub], writes=[b_UBs])
                    for q_ in range(4):
                        qs = slice(q_ * 128, (q_ + 1) * 128)
                        P.mm(ps[0][:, qs], ub[:, qs], ident_bf[:], True, True, reads=[b_ub, b_ident], writes=[bps[0]], inc=(q_ == 3))
                    P.act(uT[:].rearrange("p a b -> p (a b)"), ps[0][:], AF.Copy, reads=[bps[0]], writes=[b_uT])

                    def H0():
                        inproj(0, 2)
                        inproj(1, 1)
                        rotary(2, 0, qr, b_qr)
                        rotary(1, 1, kr, b_kr)

                    def H1():
                        inproj(2, 2)
                        P.act(vb[:], ps[2][:], AF.Copy, reads=[bps[2]], writes=[b_vb])
                        inproj(3, 0)
                        P.act(sg[:, 0:512], ps[0][:], AF.Silu, reads=[bps[0]], writes=[b_sg])
                        inproj(5, 1)
                        P.act(sg[:, 512:1024], ps[1][:], AF.Silu, reads=[bps[1]], writes=[b_sg])
                        P.dma(STQ, SGs[tsl], sg[:], reads=[b_sg], writes=[b_SGs])

                    def R1():
                        for h in range(4):
                            hs = slice(h * 128, (h + 1) * 128)
                            P.mm(ps[0][:, hs], qr[:, hs], ident_bf[:], True, True, reads=[b_qr, b_ident], writes=[bps[0]], inc=(h == 3))
                        for h in range(4):
                            hs = slice(h * 128, (h + 1) * 128)
                            P.mm(ps[1][:, hs], kr[:, hs], ident_bf[:], True, True, reads=[b_kr, b_ident], writes=[bps[1]], inc=(h == 3))
                        P.act(QT[:], ps[0][:], AF.Copy, reads=[bps[0]], writes=[b_QT])
                        P.act(KT[:], ps[1][:], AF.Copy, reads=[bps[1]], writes=[b_KT])
                        p1()
                    def R2():
                        for h in range(4):
                            hs = slice(h * 128, (h + 1) * 128)
                            P.mm(ps[2][:, hs], KT[:, hs], QT[:, hs], True, True, reads=[b_KT, b_QT], writes=[bps[2]], inc=(h == 3))
                        P.tt("dve", STb[:], ps[2][:], dtab[:].rearrange("p h e -> p (h e)"), ALU.mult,
                             reads=[bps[2], b_dtab], writes=[b_STb])
                    def R3():
                        for h in range(4):
                            hs = slice(h * 128, (h + 1) * 128)
                            P.mm(ps[2][:, hs], STb[:, hs], vb[:, hs], True, True, reads=[b_STb, b_vb], writes=[bps[2]], inc=(h == 3))
                        for h in range(4):
                            hs = slice(h * 128, (h + 1) * 128)
                            P.mm(ps[0][:, hs], QT[:, hs], stbf[:, hs], True, True, reads=[b_QT, b_stbf], writes=[bps[0]], inc=(h == 3))
                        P.tt("dve", tC[:].rearrange("p (h e) -> p h e", h=4), ps[0][:].rearrange("p (h e) -> p h e", h=4),
                             gam[:, 0, :].unsqueeze(2).to_broadcast([128, 4, 128]), ALU.mult, reads=[bps[0], b_gam], writes=[b_tC])
                        P.tt("dve", opt[:], tC[:], ps[2][:], ALU.add, reads=[b_tC, bps[2]], writes=[b_opt])
                        P.dma(STQ, OPs[tsl], opt[:], reads=[b_opt], writes=[b_OPs])
                    def R4():
                        ret_state_update(w, kr, b_kr, vb, b_vb, 2)
                    def R5():
                        P.dma(STQ, QTs[n], QT[:], reads=[b_QT], writes=[b_QTs])
                        P.dma(STQ, KRs[tsl], kr[:], reads=[b_kr], writes=[b_KRs])
                        P.dma(STQ, VBs[tsl], vb[:], reads=[b_vb], writes=[b_VBs])
                    def R45():
                        R4()
                        R5()

                    def P23():
                        p2()
                        p3()
                    s5_chunk(w, n, False, hooks=[H0, H1, R1, R2, R3, R45, P23])
                    P.tt("dve", ypt[:], ps[7][:], du[:], ALU.add, reads=[bps[7], b_du], writes=[b_ypt])
                    P.dma(STQ, YPs[tsl], ypt[:], reads=[b_ypt], writes=[b_YPs])

            def passB(s, w):
                OPs, YPs, SGs, QTs, KRs, VBs, UBs = OPs_[s], YPs_[s], SGs_[s], QTs_[s], KRs_[s], VBs_[s], UBs_[s]
                b_OPs, b_YPs, b_SGs, b_QTs, b_KRs, b_VBs, b_UBs = SCRB[s]
                P.memset("dve", stf[:], 0.0, writes=[b_stf])
                P.memset("pool", stbf[:], 0.0, writes=[b_stbf])
                P.memset("dve", cin[0][0][:], 0.0, writes=[cin[0][1]])
                order = list(range(nch - 1, -1, -1)) if KSTOP == 'all' else []

                def loads(ci):
                    n = order[ci]
                    tsl = slice(n * 128, (n + 1) * 128)
                    r = ci % 2
                    P.dma("sp", w.ub[r][0][:], UBs[tsl], reads=[b_UBs], writes=[w.ub[r][1]])
                    P.dma("sp", w.QT[r][0][:], QTs[n], reads=[b_QTs], writes=[w.QT[r][1]])
                    P.dma("sp", w.opt[r][0][:], OPs[tsl], reads=[b_OPs], writes=[w.opt[r][1]])
                    P.dma("sp", w.kr[r][0][:], KRs[tsl], reads=[b_KRs], writes=[w.kr[r][1]])
                    P.dma("sp", w.vb[r][0][:], VBs[tsl], reads=[b_VBs], writes=[w.vb[r][1]])

                def make_tail(ci, n):
                    r = ci % 2
                    tsl = slice(n * 128, (n + 1) * 128)
                    ypt, b_ypt = w.ypt[r]
                    sgb, b_sgb = w.sgb[r]
                    oab, b_oab = w.oab[r]
                    tB, b_tB = w.tB
                    tD, b_tD = w.tD
                    uT2, b_uT2 = w.uT2
                    oT, b_oT = w.oT
                    ygb, b_ygb = w.ygb

                    def T1():
                        P.tt("pool", tB[:], ypt[:], ypt[:], ALU.mult, reads=[b_ypt], writes=[b_tB])
                        P.ts("dve", tB[:], tB[:], 0.044715, 1.0, ALU.mult, ALU.add, reads=[b_tB], writes=[b_tB])
                        P.tt("pool", tB[:], tB[:], ypt[:], ALU.mult, reads=[b_tB, b_ypt], writes=[b_tB])
                        P.act(tB[:], tB[:], AF.Sigmoid, reads=[b_tB], writes=[b_tB], scale=1.5957691216057308)
                        P.tt("dve", tD[:], ypt[:], tB[:], ALU.mult, reads=[b_ypt, b_tB], writes=[b_tD])
                        P.act(ygb[:], tD[:], AF.Copy, reads=[b_tD], writes=[b_ygb])

                    def T2():
                        for q_ in range(4):
                            qs = slice(q_ * 128, (q_ + 1) * 128)
                            P.mm(ps[0][:, qs], ygb[:, qs], ident_bf[:], True, True, reads=[b_ygb, b_ident], writes=[bps[0]], inc=(q_ == 3))
                        P.act(uT2[:].rearrange("p a b -> p (a b)"), ps[0][:], AF.Copy, reads=[bps[0]], writes=[b_uT2])
                        for q_ in range(4):
                            P.mm(ps[1][:], uT2[:, q_, :], wglu[:, q_, :], q_ == 0, q_ == 3, reads=[b_uT2, b_wglu], writes=[bps[1]], inc=(q_ == 3))
                        P.act(tB[:], ps[1][:], AF.Sigmoid, reads=[bps[1]], writes=[b_tB])
                        P.tt("dve", tD[:], tD[:], tB[:], ALU.mult, reads=[b_tD, b_tB], writes=[b_tD])
                        P.tt("pool", oab[:, 512:1024], tD[:], sgb[:], ALU.mult, reads=[b_tD, b_sgb], writes=[b_oab])

                    def T3():
                        for b_ in range(2):
                            for jj in range(4):
                                k = 4 * b_ + jj
                                P.mm(ps[b_][:, jj * 128:(jj + 1) * 128], oab[:, k * 128:(k + 1) * 128], ident_bf[:], True, True,
                                     reads=[b_oab, b_ident], writes=[bps[b_]], inc=(jj == 3))
                        P.act(oT[:, 0:4, :].rearrange("p a b -> p (a b)"), ps[0][:], AF.Copy, reads=[bps[0]], writes=[b_oT])
                        P.act(oT[:, 4:8, :].rearrange("p a b -> p (a b)"), ps[1][:], AF.Copy, reads=[bps[1]], writes=[b_oT])
                        for hh, bk in ((0, 2), (1, 0)):
                            for k in range(8):
                                P.mm(ps[bk][:], oT[:, k, :], wout[:, k, hh * 512:(hh + 1) * 512], k == 0, k == 7,
                                     reads=[b_oT, b_wout], writes=[bps[bk]], inc=(k == 7))

                    def T4():
                        sl = C["cnt"] % 2
                        C["cnt"] += 1
                        xt, bxt = C["xt"][sl], C["bxt"][sl]
                        P.dma("sp", xt[:], src[s, tsl, :], reads=[bsrc[s]], writes=[bxt])
                        post(C, s, (2, 0), xt, bxt, dst, bdst[s], n)
                    return [T1, T2, T3, T4]

                if order:
                    loads(0)
                tail = []
                for ci, n in enumerate(order):
                    tsl = slice(n * 128, (n + 1) * 128)
                    r = ci % 2
                    QT, b_QT = w.QT[r]
                    opt, b_opt = w.opt[r]
                    kr, b_kr = w.kr[r]
                    vb, b_vb = w.vb[r]
                    ub, b_ub = w.ub[r]
                    sga, b_sga = w.sga
                    sgb, b_sgb = w.sgb[r]
                    ypt, b_ypt = w.ypt[r]
                    tA, b_tA = w.tA
                    tC, b_tC = w.tC
                    uT, b_uT = w.uT
                    oab, b_oab = w.oab[r]
                    kf, b_kf = w.kf
                    P.dma("sp", sga[:], SGs[tsl, 0:512], reads=[b_SGs], writes=[b_sga])
                    P.dma("sp", sgb[:], SGs[tsl, 512:1024], reads=[b_SGs], writes=[b_sgb])
                    P.dma("sp", ypt[:], YPs[tsl], reads=[b_YPs], writes=[b_ypt])
                    for q_ in range(4):
                        qs = slice(q_ * 128, (q_ + 1) * 128)
                        P.mm(ps[0][:, qs], ub[:, qs], jmat_bf[:], True, True, reads=[b_ub, b_jmat], writes=[bps[0]], inc=(q_ == 3))
                    P.act(uT[:].rearrange("p a b -> p (a b)"), ps[0][:], AF.Copy, reads=[bps[0]], writes=[b_uT])
                    if ci + 1 < len(order):
                        loads(ci + 1)

                    def RB1():
                        for h in range(4):
                            hs = slice(h * 128, (h + 1) * 128)
                            P.mm(ps[2][:, hs], QT[:, hs], stbf[:, hs], True, True, reads=[b_QT, b_stbf], writes=[bps[2]], inc=(h == 3))
                        P.tt("dve", tC[:].rearrange("p (h e) -> p h e", h=4), ps[2][:].rearrange("p (h e) -> p h e", h=4),
                             gam[:, 1, :].unsqueeze(2).to_broadcast([128, 4, 128]), ALU.mult, reads=[bps[2], b_gam], writes=[b_tC])
                        P.tt("pool", opt[:], opt[:], tC[:], ALU.add, reads=[b_opt, b_tC], writes=[b_opt])
                        P.tt("pool", kf[:].rearrange("p (h e) -> p h e", h=4), kr[:].rearrange("p (h e) -> p h e", h=4),
                             gam[:, 3, :].unsqueeze(2).to_broadcast([128, 4, 128]), ALU.mult,
                             reads=[b_kr, b_gam], writes=[b_kf])
                        for h in range(4):
                            hs = slice(h * 128, (h + 1) * 128)
                            P.mm(ps[1][:, hs], kf[:, hs], vb[:, hs], True, True, reads=[b_kf, b_vb], writes=[bps[1]], inc=(h == 3))
                        P.tt("pool", stf[:].rearrange("p (h e) -> p h e", h=4), stf[:].rearrange("p (h e) -> p h e", h=4),
                             cdec[:].unsqueeze(2).to_broadcast([128, 4, 128]), ALU.mult, reads=[b_stf, b_cdec], writes=[b_stf])

                    def RB2():
                        P.tt("dve", stf[:], stf[:], ps[1][:], ALU.add, reads=[b_stf, bps[1]], writes=[b_stf])
                        P.act(stbf[:], stf[:], AF.Copy, reads=[b_stf], writes=[b_stbf])
                        o3 = opt[:].rearrange("p (h e) -> p h e", h=4)
                        P.T.op("dve", lambda q, o3=o3: q.reduce_sum(out=hn[:, 0:4], in_=o3, axis=mybir.AxisListType.X),
                               reads=[b_opt], writes=[b_hn])
                        P.tt("pool", tA[:], opt[:], opt[:], ALU.mult, reads=[b_opt], writes=[b_tA])
                        P.T.op("dve", lambda q, tA=tA: q.reduce_sum(out=hn[:, 4:8], in_=tA[:].rearrange("p (h e) -> p h e", h=4),
                                                                  axis=mybir.AxisListType.X), reads=[b_tA], writes=[b_hn])
                        P.ts("dve", hn[:, 0:8], hn[:, 0:8], 1.0 / 128.0, None, ALU.mult, None, reads=[b_hn], writes=[b_hn])
                        P.tt("dve", hn[:, 8:12], hn[:, 0:4], hn[:, 0:4], ALU.mult, reads=[b_hn], writes=[b_hn])
                        P.tt("dve", hn[:, 4:8], hn[:, 4:8], hn[:, 8:12], ALU.subtract, reads=[b_hn], writes=[b_hn])
                        P.act(hn[:, 8:12], hn[:, 4:8], AF.Sqrt, reads=[b_hn, b_eps], writes=[b_hn], bias=epsT[:, 0:1])

                    def RB3():
                        o3 = opt[:].rearrange("p (h e) -> p h e", h=4)
                        P.T.op("dve", lambda q: q.reciprocal(out=hn[:, 12:16], in_=hn[:, 8:12]), reads=[b_hn], writes=[b_hn])
                        a3 = tA[:].rearrange("p (h e) -> p h e", h=4)
                        P.tt("pool", a3, o3, hn[:, 0:4].unsqueeze(2).to_broadcast([128, 4, 128]), ALU.subtract,
                             reads=[b_opt, b_hn], writes=[b_tA])
                        P.tt("pool", a3, a3, hn[:, 12:16].unsqueeze(2).to_broadcast([128, 4, 128]), ALU.mult,
                             reads=[b_tA, b_hn], writes=[b_tA])
                        P.tt("pool", oab[:, 0:512], tA[:], sga[:], ALU.mult, reads=[b_tA, b_sga], writes=[b_oab])

                    hooks = [RB1, RB2, RB3] + tail
                    s5_chunk(w, ci, True, hooks=hooks)
                    P.tt("dve", ypt[:], ypt[:], ps[7][:], ALU.add, reads=[b_ypt, bps[7]], writes=[b_ypt])
                    tail = make_tail(ci, n)
                for t_ in tail:
                    t_()

            s5_setup(0)
            with contextlib.ExitStack() as wk:
                w = alloc_work(wk, False)
                for s in range(nseq):
                    passA(s, w)
                T_barrier()
            s5_setup(1)
            with contextlib.ExitStack() as wk:
                w = alloc_work(wk, True)
                for s in range(nseq):
                    passB(s, w)
                T_barrier()

    def T_barrier():
        evs = []
        for e in T.engs.values():
            for k in range(len(e.dsems)):
                evs.append((e.dsems[k], e.dvals[k]))
            if e.sem is not None and e.cnt > 0 and not e.pending:
                evs.append((e.sem, e.cnt))
        for e in T.engs.values():
            for ev in evs:
                T._wait(e, ev)

    def odd_layer(li, src, bsrc, dst, bdst):
        raise NotImplementedError

    P.odd_layer_hook = None
    cur, bcur = x_in, [b_xin] * nseq
    for idx, li in enumerate(layers):
        last = idx == len(layers) - 1
        dstt, bd = (y_out, b_yout) if last else (xs[idx % 2], b_xs[idx % 2])
        if li % 2 == 0:
            even_layer(li, cur, bcur, dstt, bd)
        else:
            ODD_IMPL(P, locals(), li, cur, bcur, dstt, bd)
        cur, bcur = dstt, bd
    T.finish()
    T.replay()
    P.es.close()
    return P


def ODD_IMPL(P, env, li, src, bsrc, dst, bdst):
    nc, T = P.nc, P.T
    P.pool_to_dve = False
    P.pne = 'mix'
    P.stq = os.environ.get('STQO', 'sp')
    E = env
    ps, bps = E["ps"], E["bps"]
    nseq, L = P.nseq, P.L
    ident_bf, b_ident = E["ident_bf"], E["b_ident"]
    j = li // 2
    rows = L // 64
    nblk = L // 256

    def rs(r):
        return min(max(r - 4, 0), rows - 8)

    with contextlib.ExitStack() as st:
        def S(name, shape, dt=F32):
            return st.enter_context(P.sbt(f"{name}_o{li}", list(shape), dt)), Buf(name)
        E["adaln"](li, None)
        C = E["make_common"](st, f"o{li}")
        winc, b_winc = S("winc", [128, 8, 4096], BF16)
        woutc, b_woutc = S("woutc", [128, 8, 1024], BF16)
        Z, b_Z = S("Z", [128, 16, 1024], BF16)
        hT, b_hT = S("hT", [128, 8, 256], BF16)
        KT = [S(f"KT{i}", [128, 8, 256], BF16) for i in range(3)]
        V = [S(f"V{i}", [128, 2, 8, 3, 64], BF16) for i in range(3)]
        QT = [S(f"QT{i}", [128, 8, 256], BF16) for i in range(2)]
        GT = [S(f"GT{i}", [128, 8, 256], BF16) for i in range(2)]
        pT = [S(f"pT{i}", [128, 256], BF16) for i in range(3)]
        og, b_og = S("og", [128, 8, 256], BF16)
        rd2 = [S(f"rd{i}", [128, 256]) for i in range(2)]
        rb2 = [S(f"rb{i}", [128, 256]) for i in range(2)]
        t22 = [S(f"t2{i}", [128, 256]) for i in range(2)]
        selb, b_selb = S("selb", [128, 64], BF16)
        RBh = [S(f"RBh{i}", [128, 256], BF16) for i in range(2)]
        zer, b_zer = S("zer", [128, 256], BF16)
        wsrc = E["w_in_c"][j].rearrange("(k p) n -> p k n", p=128)
        for k in range(8):
            P.dma("pool", winc[:, k, :], wsrc[:, k, :], writes=[b_winc])
        P.dma("pool", woutc[:], E["w_out_c"][j].rearrange("(k p) n -> p k n", p=128), writes=[b_woutc])
        P.dma("pool", selb[:], E["sel_d"], writes=[b_selb])
        for i in range(2):
            P.memset("dve", RBh[i][0][:], 0.0, writes=[RBh[i][1]])
        P.memset("dve", zer[:], 0.0, writes=[b_zer])
        for i in range(3):
            P.memset("pool", V[i][0][:], 1.0, writes=[V[i][1]])
        with contextlib.ExitStack() as s2:
            zm = s2.enter_context(P.sbt(f"zm_o{li}", [128, 1024], F32)); b_zm = Buf("zm")
            zt0_ = s2.enter_context(P.sbt(f"zt0_o{li}", [128, 512], F32))
            zt = [zt0_, zt0_]
            b_zt0_ = Buf("zt0")
            b_zt = [b_zt0_, b_zt0_]
            P.dma("sp", zm[:], E["zmask"], writes=[b_zm])
            for h in range(16):
                for hf in range(2):
                    P.dma("sp", zt[hf][:], E["zg"][j, h][:, hf * 512:(hf + 1) * 512], writes=[b_zt[hf]])
                    P.tt("dve", Z[:, h, hf * 512:(hf + 1) * 512], zt[hf][:], zm[:, hf * 512:(hf + 1) * 512], ALU.add,
                         reads=[b_zt[hf], b_zm], writes=[b_Z])
            E["T_barrier"]()

        def proj(s, b, do_prenorm=True):
            ring = b % 3
            sl = b % 2
            if do_prenorm:
                for t in range(2):
                    E["prenorm"](C, s, src, bsrc[s], 2 * b + t, hT, b_hT, t * 128)
            cnt = 0
            for (col0, kind) in ((0, "q"), (1024, "k"), (3072, "g")):
                for hp2 in range(4):
                    bank = 2 + cnt % 2
                    cnt += 1
                    for hh in range(2):
                        hp = 2 * hp2 + hh
                        for k in range(8):
                            P.mm(ps[bank][:, hh * 256:(hh + 1) * 256], winc[:, k, col0 + hp * 128:col0 + (hp + 1) * 128], hT[:, k, :],
                                 k == 0, k == 7, reads=[b_winc, b_hT], writes=[bps[bank]], inc=(k == 7 and hh == 1))
                    if kind == "q":
                        P.act(QT[sl][0][:, 2 * hp2:2 * hp2 + 2, :].rearrange("p a b -> p (a b)"), ps[bank][:], AF.Copy,
                              reads=[bps[bank]], writes=[QT[sl][1]], scale=0.125)
                    elif kind == "k":
                        P.cp("dve", KT[ring][0][:, 2 * hp2:2 * hp2 + 2, :].rearrange("p a b -> p (a b)"), ps[bank][:],
                             reads=[bps[bank]], writes=[KT[ring][1]])
                    else:
                        P.act(GT[sl][0][:, 2 * hp2:2 * hp2 + 2, :].rearrange("p a b -> p (a b)"), ps[bank][:], AF.Silu,
                              reads=[bps[bank]], writes=[GT[sl][1]])
            for t in range(2):
                for half in range(2):
                    bank = 2 + cnt % 2
                    cnt += 1
                    for k in range(8):
                        P.mm(ps[bank][:], hT[:, k, t * 128:(t + 1) * 128], winc[:, k, 2048 + half * 512:2048 + (half + 1) * 512],
                             k == 0, k == 7, reads=[b_hT, b_winc], writes=[bps[bank]], inc=(k == 7))
                    src4 = ps[bank][:].rearrange("p (a c d) -> p a c d", a=4, c=2)
                    P.cp("dve", V[ring][0][:, t, 4 * half:4 * half + 4, 0:3:2, :], src4, reads=[bps[bank]], writes=[V[ring][1]])

        def attn(s, b, hooks=()):
            hooks = list(hooks)
            R = 4 * b
            sl = b % 2
            lo = rs(R) & ~1
            hi = (rs(R + 3) + 7) & ~1
            tiles = []
            for r0 in range(lo, hi + 1, 2):
                qs = [r for r in range(R, R + 4) if (rs(r) <= r0 + 1 and r0 <= rs(r) + 7)]
                if not qs:
                    continue
                qa, qb = qs[0], qs[-1]
                partial = []
                for r in qs:
                    for a in range(2):
                        if not (rs(r) <= r0 + a <= rs(r) + 7):
                            partial.append((r, a))
                tiles.append((r0, qa, qb, partial))
            tiles.sort(key=lambda t: (0 if (t[1] == R and t[2] == R + 3 and not t[3]) else 1))
            assert tiles[0][1] == R and tiles[0][2] == R + 3 and not tiles[0][3]
            pcount = [0]
            W = [(h, ti) for h in range(16) for ti in range(len(tiles))]
            info = {}
            deferred = []

            def emit_scores(h, ti):
                hp, base = h // 2, 64 * (h % 2)
                r0, qa, qb, partial = tiles[ti]
                kb = r0 // 4
                tt_ = (r0 % 4) // 2
                kring = kb % 3
                c0, c1 = (qa - R) * 64, (qb - R + 1) * 64
                z0, z1 = (qa - r0 + 7) * 64, (qb - r0 + 8) * 64
                bank = 4 + pcount[0] % 2
                pt, b_pt = pT[pcount[0] % 3]
                pcount[0] += 1
                P.mm(ps[bank][:, c0:c1], KT[kring][0][base:base + 64, hp, tt_ * 128:(tt_ + 1) * 128],
                     QT[sl][0][base:base + 64, hp, c0:c1], True, False,
                     reads=[KT[kring][1], QT[sl][1]], writes=[bps[bank]], inc=False)
                P.mm(ps[bank][:, c0:c1], ident_bf[:], Z[:, h, z0:z1], False, True,
                     reads=[b_ident, b_Z], writes=[bps[bank]], inc=True)
                P.act(pt[:, c0:c1], ps[bank][:, c0:c1], AF.Exp, reads=[bps[bank]], writes=[b_pt])
                for (r, a_) in partial:
                    cc = (r - R) * 64
                    P.memset("pool", pt[64 * a_:64 * a_ + 64, cc:cc + 64], 0.0, writes=[b_pt])
                info[(h, ti)] = (pt, b_pt, c0, c1, kring, tt_)

            def emit_pv(h, ti, idx):
                hp = h // 2
                pt, b_pt, c0, c1, kring, tt_ = info.pop((h, ti))
                ob = 6 + h % 2
                va = V[kring][0][:, tt_, hp, 0:2, :] if h % 2 == 0 else V[kring][0][:, tt_, hp, 1:3, :]
                last = ti == len(tiles) - 1
                P.mm(ps[ob][:, c0:c1], va.rearrange("p a b -> p (a b)"), pt[:, c0:c1], ti == 0, last,
                     reads=[V[kring][1], b_pt], writes=[bps[ob]], inc=last)
                if last:
                    par = h % 2
                    dr, orow = (64, 0) if par == 0 else (0, 64)
                    rdp, b_rdp = rd2[par]
                    rbp, b_rbp = rb2[par]
                    t2p, b_t2p = t22[par]
                    rbh, b_rbh = RBh[par]
                    P.T.op("dve", lambda q, dr=dr, ob=ob, rdp=rdp: q.reciprocal(out=rdp[dr:dr + 33, :], in_=ps[ob][dr:dr + 33, 0:256]),
                           reads=[bps[ob]], writes=[b_rdp])
                    P.cp("dve", rbh[dr:dr + 33, :], rdp[dr:dr + 33, :], reads=[b_rdp], writes=[b_rbh])
                    P.tt("dve", rbh[dr:dr + 1, :], rdp[dr:dr + 1, :], rbh[dr:dr + 1, :], ALU.subtract, reads=[b_rdp, b_rbh], writes=[b_rbh])

                    def part2(h=h, hp=hp, par=par, dr=dr, orow=orow, ob=ob, rdp=rdp, b_rdp=b_rdp, rbp=rbp, b_rbp=b_rbp, t2p=t2p, b_t2p=b_t2p,
                              rbh=rbh, b_rbh=b_rbh):
                        P.mm(ps[2 + par][orow:orow + 64, 0:256], selb[dr:dr + 33, 0:64], rbh[dr:dr + 33, :], True, True,
                             reads=[b_selb, b_rbh], writes=[bps[2 + par]])
                        P.act(rbp[orow:orow + 64, :], ps[2 + par][orow:orow + 64, 0:256], AF.Copy, reads=[bps[2 + par]], writes=[b_rbp])
                        P.tt("dve", t2p[orow:orow + 64, :], ps[ob][orow:orow + 64, 0:256], rbp[orow:orow + 64, :], ALU.mult,
                             reads=[bps[ob], b_rbp], writes=[b_t2p])
                        P.tt("pool", og[orow:orow + 64, hp, :], t2p[orow:orow + 64, :], GT[sl][0][orow:orow + 64, hp, :], ALU.mult,
                             reads=[b_t2p, GT[sl][1]], writes=[b_og])
                    deferred.append((idx + min(4, len(tiles) - 1), part2))
                    if h % 2 == 1 and hooks:
                        deferred.append((idx + 1, hooks.pop(0)))
                        deferred.sort(key=lambda d: d[0])

            LA = 1
            for idx in range(len(W) + LA):
                if idx < len(W):
                    emit_scores(*W[idx])
                if idx >= LA:
                    emit_pv(W[idx - LA][0], W[idx - LA][1], idx)
                while deferred and deferred[0][0] <= idx:
                    deferred.pop(0)[1]()
            while deferred:
                deferred.pop(0)[1]()
            for t in range(2):
                n = 2 * b + t
                sx = C["cnt"] % 2
                C["cnt"] += 1
                xt, bxt = C["xt"][sx], C["bxt"][sx]
                P.dma("sp", xt[:], src[s, n * 128:(n + 1) * 128, :], reads=[bsrc[s]], writes=[bxt])
                for hh in range(2):
                    for k in range(8):
                        P.mm(ps[2 + hh][:], og[:, k, t * 128:(t + 1) * 128], woutc[:, k, hh * 512:(hh + 1) * 512], k == 0, k == 7,
                             reads=[b_og, b_woutc], writes=[bps[2 + hh]], inc=(k == 7))
                E["post"](C, s, (2, 3), xt, bxt, dst, bdst[s], n)

        for s in range(nseq):
            proj(s, 0)
            if nblk > 1:
                proj(s, 1)
            for b in range(nblk):
                if b >= 1 and b + 1 < nblk:
                    proj(s, b + 1, do_prenorm=False)
                hk = []
                if b + 2 < nblk:
                    for t in range(2):
                        hk += list(E["prenorm_parts"](C, s, src, bsrc[s], 2 * (b + 2) + t, hT, b_hT, t * 128))
                attn(s, b, hooks=hk)
        E["T_barrier"]()


def _common_inputs(p, L):
    f32 = np.float32
    m = {}
    def T8(a):
        return np.ascontiguousarray(a.reshape(a.shape[0], -1, 128).transpose(0, 2, 1)).astype(f32)
    m["npreT"] = T8(p["norm_pre"])
    m["npostT"] = T8(p["norm_post"])
    m["wmod"] = np.ascontiguousarray(p["w_mod"], dtype=f32)
    m["bmodT"] = T8(p["b_mod"])
    m["w_in_ab"] = np.ascontiguousarray(p["w_in_ab"], dtype=f32)
    m["w_out_ab"] = np.ascontiguousarray(p["w_out_ab"], dtype=f32)
    m["w_glu"] = np.ascontiguousarray(p["ssm_w_glu"], dtype=f32)
    m["ssm_d"] = np.ascontiguousarray(p["ssm_d"], dtype=f32)
    sm3, rep3, bl, cl = [], [], [], []
    for j in range(2):
        a, b, c, d = _s5_layout(p["ssm_a_re"][j], p["ssm_a_im"][j], p["ssm_log_step"][j], p["ssm_b_re"][j],
                                p["ssm_b_im"][j], p["ssm_c_re"][j], p["ssm_c_im"][j])
        sm3.append(a); rep3.append(b); bl.append(c); cl.append(d)
    m["s5_sm3"] = np.stack(sm3).astype(f32)
    m["s5_rep3"] = np.stack(rep3).astype(f32)
    m["s5_bl"] = np.stack(bl).astype(f32)
    m["s5_cl"] = np.stack(cl).astype(f32)
    m["w_in_c"] = np.ascontiguousarray(p["w_in_c"], dtype=f32)
    m["w_out_c"] = np.ascontiguousarray(p["w_out_c"], dtype=f32)
    zs = []
    for j in range(2):
        z, mask = _na_layout(np.asarray(p["na_rel_bias"][j], dtype=f32))
        zs.append(z)
    m["zg"] = np.stack(zs).astype(f32)
    m["zmask"] = mask
    m["ident"] = np.eye(128, dtype=f32)
    m["jmat"] = np.eye(128, dtype=f32)[::-1].copy()
    m["rot"] = _rot_tables(L)
    dt, gam, cd = _ret_consts()
    m["dtab"] = dt
    m["gam"] = np.ascontiguousarray(gam.transpose(0, 1, 2))
    m["cdec"] = cd
    m["jt"] = np.broadcast_to(np.arange(128, dtype=f32)[None, :], (128, 128)).copy()
    sel = np.zeros((128, 64), f32)
    sel[[0, 32, 64, 96], :] = 1.0
    m["sel"] = sel
    return m


_PROG_CACHE = {}


def run_cores(xs_per_core, cs_per_core, params, layers):
    nseq, L, _ = xs_per_core[0].shape
    key = (nseq, L, tuple(layers))
    if key not in _PROG_CACHE:
        _PROG_CACHE[key] = build(nseq, L, list(layers))
    P = _PROG_CACHE[key]
    com = _common_inputs(params, L)
    in_maps = []
    for x, c in zip(xs_per_core, cs_per_core):
        m = dict(com)
        m["x_in"] = np.ascontiguousarray(x, dtype=np.float32)
        m["cT"] = np.ascontiguousarray(c.reshape(nseq, 8, 128).transpose(2, 1, 0), dtype=np.float32)
        in_maps.append(m)
    res = run_bass_kernel_spmd(P.nc, in_maps, core_ids=list(range(len(in_maps))))
    return [np.asarray(r["y_out"]) for r in res.results]


def kernel(**inputs):
    p = {k: np.asarray(v) for k, v in inputs.items()}
    xp, xsamp = p["x_prompt"], p["x_sample"]
    cp, cs = p["c_prompt"], p["c_sample"]
    seqs = [xp[i] for i in range(4)] + [xsamp[i] for i in range(8)]
    cvs = [cp[i] for i in range(4)] + [cs[i] for i in range(8)]
    slots = [(c, 8 + c if c < 4 else c) for c in range(8)]
    xs_pc = [np.stack([seqs[a], seqs[b]]) for a, b in slots]
    cs_pc = [np.stack([cvs[a], cvs[b]]) for a, b in slots]
    outs = run_cores(xs_pc, cs_pc, p, [0, 1, 2, 3])
    res = [None] * 12
    for c, (a, b) in enumerate(slots):
        res[a] = outs[c][0]
        if c < 4:
            res[b] = outs[c][1]
    y_prompt = np.stack(res[0:4]).astype(np.float32)
    y_sample = np.stack(res[4:12]).astype(np.float32)
    return (y_prompt, y_sample)
```

```python
import contextlib
import math
import os
KSTOP = os.environ.get('KSTOP', 'all')
S5E = os.environ.get('S5E', 'dve')
STQ = os.environ.get('STQ', 'sp')
PNE = os.environ.get('PNE', 'act')
import numpy as np
import concourse.bass as bass
import concourse.mybir as mybir
from concourse.bass_utils import run_bass_kernel_spmd

F32 = mybir.dt.float32
BF16 = mybir.dt.bfloat16
I32 = mybir.dt.int32
ALU = mybir.AluOpType
AF = mybir.ActivationFunctionType

D = 1024
EPS = 1e-6
TWO_PI = 2.0 * math.pi


class Buf:
    __slots__ = ("name", "w", "r")

    def __init__(self, name):
        self.name = name
        self.w = []
        self.r = []


class Eng:
    def __init__(self, name):
        self.name = name
        self.ops = []
        self.known = {}
        self.sem = None
        self.cnt = 0
        self.pending = False
        self.dsems = []
        self.dvals = []
        self.dptr = 0
        self.own = set()


EPOCH = 30000
NDSEM = 10


class Tracker:
    def __init__(self, nc):
        self.nc = nc
        self.engs = {n: Eng(n) for n in ("pe", "act", "dve", "pool", "sp")}
        self.sems = []
        for e in self.engs.values():
            if e.name != "sp":
                e.sem = self._newsem(e.name)
                e.own.add(e.sem)
        self.n_ops = 0

    def _newsem(self, nm):
        h = self.nc.alloc_semaphore(name=f"{nm}_{len(self.sems)}")
        self.sems.append(h)
        return len(self.sems) - 1

    def _wait(self, e, ev):
        s, v = ev
        if e.known.get(s, 0) >= v:
            return
        if s in e.own:
            if e.name == "pe":
                return
            if s == e.sem and v > e.cnt:
                return
        e.known[s] = v
        sem = self.sems[s]
        e.ops.append(lambda q, sem=sem, v=v: q.wait_ge(sem, v))

    def _deps(self, e, reads, writes):
        for b in reads:
            for ev in b.w:
                self._wait(e, ev)
        for b in writes:
            for ev in b.w:
                self._wait(e, ev)
            for ev in b.r:
                self._wait(e, ev)

    def _commit(self, ev, reads, writes):
        for b in reads:
            for i, (s0, v0) in enumerate(b.r):
                if s0 == ev[0]:
                    b.r[i] = (s0, max(v0, ev[1]))
                    break
            else:
                b.r.append(ev)
        for b in writes:
            b.w = [ev]
            b.r = []

    def op(self, eng, fn, reads=(), writes=(), inc=True):
        e = self.engs[eng]
        self.n_ops += 1
        self._deps(e, reads, writes)
        if e.cnt >= EPOCH and inc and not e.pending:
            e.sem = self._newsem(e.name)
            e.cnt = 0
            e.own.add(e.sem)
        if inc:
            e.cnt += 1
            sem = self.sems[e.sem]
            e.ops.append(lambda q, fn=fn, sem=sem: fn(q).then_inc(sem, 1))
            ev = (e.sem, e.cnt)
            e.pending = False
        else:
            e.ops.append(lambda q, fn=fn: fn(q))
            ev = (e.sem, e.cnt + 1)
            e.pending = True
        self._commit(ev, reads, writes)

    def dma(self, eng, out, in_, reads=(), writes=()):
        e = self.engs[eng]
        self.n_ops += 1
        self._deps(e, reads, writes)
        if len(e.dsems) < NDSEM:
            e.dsems.append(self._newsem(e.name + "d"))
            e.dvals.append(0)
            k = len(e.dsems) - 1
        else:
            k = e.dptr
            e.dptr = (e.dptr + 1) % NDSEM
            self._wait(e, (e.dsems[k], e.dvals[k]))
            if e.dvals[k] >= EPOCH * 16:
                e.dsems[k] = self._newsem(e.name + "d")
                e.dvals[k] = 0
        e.dvals[k] += 16
        sem = self.sems[e.dsems[k]]
        e.ops.append(lambda q, out=out, in_=in_, sem=sem: q.dma_start(out=out, in_=in_).then_inc(sem, 16))
        ev = (e.dsems[k], e.dvals[k])
        self._commit(ev, reads, writes)
        return ev

    def finish(self):
        sp = self.engs["sp"]
        for e in self.engs.values():
            for k in range(len(e.dsems)):
                self._wait(sp, (e.dsems[k], e.dvals[k]))
            if e.sem is not None and e.cnt > 0:
                self._wait(sp, (e.sem, e.cnt))

    def replay(self):
        nc = self.nc
        E = self.engs
        with nc.Block() as block:
            @block.tensor
            def _(q):
                for f in E["pe"].ops:
                    f(q)

            @block.scalar
            def _(q):
                for f in E["act"].ops:
                    f(q)

            @block.vector
            def _(q):
                for f in E["dve"].ops:
                    f(q)

            @block.gpsimd
            def _(q):
                for f in E["pool"].ops:
                    f(q)

            @block.sync
            def _(q):
                for f in E["sp"].ops:
                    f(q)


RET_H = 4
NA_H = 16
GW = 64


def _ret_consts():
    f32 = np.float32
    h = np.arange(RET_H, dtype=f32)
    log_g = np.log1p(-np.exp2(-5.0 - h)).astype(f32)
    pos = np.arange(128, dtype=f32)
    dt = np.exp(np.abs(pos[:, None] - pos[None, :])[:, None, :] * log_g[None, :, None]).astype(f32)
    gam = np.zeros((128, 4, RET_H), f32)
    gam[:, 0, :] = np.exp(pos[:, None] * log_g[None])
    gam[:, 1, :] = np.exp((127.0 - pos)[:, None] * log_g[None])
    gam[:, 2, :] = np.exp((128.0 - pos)[:, None] * log_g[None])
    gam[:, 3, :] = np.exp((pos + 1.0)[:, None] * log_g[None])
    cdec = np.exp(128.0 * log_g).astype(f32)
    cd = np.broadcast_to(cdec[None, :], (128, RET_H)).copy()
    return dt, gam, cd


def _rot_tables(L):
    f32 = np.float32
    inv = (10000.0 ** (-np.arange(0, 128, 2, dtype=f32) / 128.0)).astype(f32)
    ang = (np.arange(L, dtype=f32)[:, None] * inv[None, :]).astype(f32)
    c = np.cos(ang).astype(f32)
    s = np.sin(ang).astype(f32)
    rq = np.zeros((L, 2, 128), f32)
    rq[:, 0, :64] = c
    rq[:, 0, 64:] = c
    rq[:, 1, :64] = -s
    rq[:, 1, 64:] = s
    rk = (rq * f32(128.0 ** -0.5)).astype(f32)
    return np.stack([rq, rk], axis=1).copy()


def _na_layout(relb):
    a = np.arange(2)[:, None, None, None]
    k = np.arange(64)[None, :, None, None]
    m = np.arange(-7, 9)[None, None, :, None]
    c = np.arange(64)[None, None, None, :]
    dr = np.clip(a - m + 7, 0, 14)
    dc = np.clip(k - c + 15, 0, 30)
    dr_b, dc_b = np.broadcast_arrays(dr, dc)
    z = relb[:, dr_b, dc_b]
    z = z.reshape(16, 128, 16 * 64).astype(np.float32)
    cs = np.clip(np.arange(64) - 8, 0, 48)
    kk = np.arange(64)[:, None]
    valid = (kk >= cs[None, :]) & (kk < cs[None, :] + 16)
    mask = np.where(valid, 0.0, -30000.0).astype(np.float32)
    mask = np.broadcast_to(mask[None, :, None, :], (2, 64, 16, 64)).reshape(128, 1024).copy()
    return z, mask


def _s5_layout(a_re, a_im, ls, b_re, b_im, c_re, c_im):
    f32 = np.float32
    def sm(a):
        return a.reshape(2, 16, 2, 64).transpose(0, 2, 3, 1).reshape(2, 128, 16).astype(f32)
    lsx = np.broadcast_to(ls[:, :, None], (2, 32, 64))
    sm3 = np.stack([sm(a_re), sm(a_im), sm(lsx)], axis=1).copy()
    rep3 = np.stack([a_re.reshape(2, 2048), a_im.reshape(2, 2048), lsx.reshape(2, 2048)], axis=1).astype(f32).copy()
    bl = np.zeros((2, 2, 128, 16, 2, 64), f32)
    cl = np.zeros((2, 2, 128, 16, 2, 16), f32)
    for s in range(16):
        for g1 in range(2):
            g = 2 * s + g1
            r0 = 32 * (s % 4) + 16 * g1
            for ri, b in enumerate((b_re, b_im)):
                bl[:, ri, r0:r0 + 16, s, g1, :] = b[:, g].transpose(0, 2, 1)
            for ri, c in enumerate((c_re, c_im)):
                cl[:, ri, 64 * g1:64 * g1 + 64, s, g1, :] = c[:, g].transpose(0, 2, 1)
    return sm3, rep3, bl.reshape(2, 2, 128, 16 * 128), cl.reshape(2, 2, 128, 16 * 32)


class Prog:
    def __init__(self, nseq, L, layers):
        self.nseq, self.L, self.layers = nseq, L, layers
        self.nch = L // 128
        nc = self.nc = bass.Bass("TRN2", target_bir_lowering=False)
        self.T = Tracker(nc)
        self.es = contextlib.ExitStack()
        self.dram = {}
        self.bufs = {}

    def sbt(self, name, shape, dt=F32):
        self._uid = getattr(self, '_uid', 0) + 1
        return self.nc.sbuf_tensor(f"{name}_u{self._uid}", list(shape), dt)

    def din(self, name, shape, dt=F32):
        t = self.nc.dram_tensor(name, list(shape), dt, kind="ExternalInput").ap()
        self.dram[name] = t
        return t

    def dout(self, name, shape, dt=F32):
        t = self.nc.dram_tensor(name, list(shape), dt, kind="ExternalOutput").ap()
        self.dram[name] = t
        return t

    def dscr(self, name, shape, dt=F32):
        t = self.nc.dram_tensor(name, list(shape), dt, kind="Internal").ap()
        self.dram[name] = t
        return t

    def sb(self, name, shape, dt=F32):
        t = self.es.enter_context(self.sbt(name, list(shape), dt))
        b = Buf(name)
        return t, b

    def ps(self, name):
        t = self.es.enter_context(self.nc.psum_tensor(name, [128, 512], F32))
        return t, Buf(name)

    def mm(self, out, lhsT, rhs, start, stop, reads, writes, inc=True):
        self.T.op("pe", lambda q: q.matmul(out, lhsT=lhsT, rhs=rhs, start=start, stop=stop),
                  reads=reads, writes=writes, inc=inc)

    def act(self, out, in_, func, reads, writes, scale=1.0, bias=None, accum=None):
        kw = {}
        if bias is not None:
            kw["bias"] = bias
        if accum is not None:
            kw["accum_out"] = accum
        self.T.op("act", lambda q: q.activation(out=out, in_=in_, func=func, scale=scale, **kw),
                  reads=reads, writes=writes)

    def tt(self, eng, out, in0, in1, op, reads, writes):
        if eng == "pool" and getattr(self, "pool_to_dve", False):
            eng = "dve"
        self.T.op(eng, lambda q: q.tensor_tensor(out=out, in0=in0, in1=in1, op=op), reads=reads, writes=writes)

    def ts(self, eng, out, in0, s1, s2, op0, op1, reads, writes):
        if s2 is None:
            self.T.op(eng, lambda q: q.tensor_scalar(out=out, in0=in0, scalar1=s1, scalar2=None, op0=op0),
                      reads=reads, writes=writes)
        else:
            self.T.op(eng, lambda q: q.tensor_scalar(out=out, in0=in0, scalar1=s1, scalar2=s2, op0=op0, op1=op1),
                      reads=reads, writes=writes)

    def stt(self, eng, out, in0, scalar, in1, op0, op1, reads, writes):
        self.T.op(eng, lambda q: q.scalar_tensor_tensor(out=out, in0=in0, scalar=scalar, in1=in1, op0=op0, op1=op1),
                  reads=reads, writes=writes)

    def cp(self, eng, out, in_, reads, writes):
        self.T.op(eng, lambda q: q.tensor_copy(out=out, in_=in_), reads=reads, writes=writes)

    def memset(self, eng, ap, val, writes):
        self.T.op(eng, lambda q: q.memset(ap, val), reads=(), writes=writes)

    def dma(self, eng, out, in_, reads=(), writes=()):
        return self.T.dma(eng, out, in_, reads=reads, writes=writes)


def build(nseq, L, layers):
    P = Prog(nseq, L, layers)
    nc, T = P.nc, P.T
    nch = L // 128
    NL = 4
    x_in = P.din("x_in", [nseq, L, D])
    y_out = P.dout("y_out", [nseq, L, D])
    cT = P.din("cT", [128, 8, nseq])
    npreT = P.din("npreT", [NL, 128, 8])
    npostT = P.din("npostT", [NL, 128, 8])
    wmod = P.din("wmod", [NL, D, 3 * D])
    bmodT = P.din("bmodT", [NL, 128, 24])
    w_in_ab = P.din("w_in_ab", [2, D, 3072])
    w_out_ab = P.din("w_out_ab", [2, D, D])
    w_glu = P.din("w_glu", [2, 512, 512])
    ssm_d = P.din("ssm_d", [2, 512])
    s5_sm3 = P.din("s5_sm3", [2, 2, 3, 128, 16])
    s5_rep3 = P.din("s5_rep3", [2, 2, 3, 2048])
    s5_bl = P.din("s5_bl", [2, 2, 2, 128, 2048])
    s5_cl = P.din("s5_cl", [2, 2, 2, 128, 512])
    w_in_c = P.din("w_in_c", [2, D, 4096])
    w_out_c = P.din("w_out_c", [2, D, D])
    zg = P.din("zg", [2, 16, 128, 1024])
    zmask = P.din("zmask", [128, 1024])
    ident_d = P.din("ident", [128, 128])
    jmat_d = P.din("jmat", [128, 128])
    rot_d = P.din("rot", [L, 2, 2, 128])
    dtab_d = P.din("dtab", [128, 4, 128])
    gam_d = P.din("gam", [128, 4, 4])
    cdec_d = P.din("cdec", [128, 4])
    jt_d = P.din("jt", [128, 128])
    sel_d = P.din("sel", [128, 64])
    xs = [P.dscr("xsA", [nseq, L, D]), P.dscr("xsB", [nseq, L, D])]
    OPs_ = P.dscr("OPs", [nseq, L, 512])
    YPs_ = P.dscr("YPs", [nseq, L, 512])
    SGs_ = P.dscr("SGs", [nseq, L, 1024])
    QTs_ = P.dscr("QTs", [nseq, nch, 128, 512], BF16)
    KRs_ = P.dscr("KRs", [nseq, L, 512], BF16)
    VBs_ = P.dscr("VBs", [nseq, L, 512], BF16)
    UBs_ = P.dscr("UBs", [nseq, L, 512], BF16)
    SCRB = [[Buf(f"{n}{i}") for n in "OP YP SG QT KR VB UB".split()] for i in range(nseq)]
    b_xs = [[Buf(f"xs{i}_{s}") for s in range(nseq)] for i in range(2)]
    b_yout = [Buf(f"yout{s}") for s in range(nseq)]
    b_xin = Buf("xin")

    ident_bf, b_ident = P.sb("ident_bf", [128, 128], BF16)
    jmat_bf, b_jmat = P.sb("jmat_bf", [128, 128], BF16)
    identf, b_identf = P.sb("identf", [128, 128])
    epsT, b_eps = P.sb("epsT", [128, 1])
    ss, b_ss = P.sb("ss", [128, 4])
    sd, b_sd = P.sb("sd", [128, 4])
    rstd, b_rstd = P.sb("rstd", [128, 4])
    scT, b_scT = P.sb("scT", [128, 8, nseq])
    cTs, b_cTs = P.sb("cTs", [128, 8, nseq])
    modT, b_modT = P.sb("modT", [128, 24, nseq])
    gsT, b_gsT = P.sb("gsT", [128, 8, nseq])
    ggT, b_ggT = P.sb("ggT", [128, 8, nseq])
    ggrow = [P.sb(f"ggrow{s}", [128, 1024]) for s in range(nseq)]
    vecs, b_vecs = P.sb("vecs", [128, 40])
    psb = [P.ps(f"ps{i}") for i in range(8)]
    ps = [p[0] for p in psb]
    bps = [p[1] for p in psb]

    P.dma("sp", identf[:], ident_d, writes=[b_identf])
    P.dma("pool", ident_bf[:], ident_d, writes=[b_ident])
    P.dma("pool", jmat_bf[:], jmat_d, writes=[b_jmat])
    P.memset("pool", epsT[:], EPS, writes=[b_eps])
    P.dma("sp", cTs[:], cT, writes=[b_cTs])
    P.act(scT[:], cTs[:], AF.Sigmoid, reads=[b_cTs], writes=[b_scT])
    P.tt("dve", scT[:], scT[:], cTs[:], ALU.mult, reads=[b_scT, b_cTs], writes=[b_scT])

    def rstd_from_ss(col, n_feat):
        P.act(sd[:, col:col + 1], ss[:, col:col + 1], AF.Sqrt, reads=[b_ss, b_eps], writes=[b_sd],
              scale=1.0 / n_feat, bias=epsT[:, 0:1])
        P.T.op("dve", lambda q: q.reciprocal(out=rstd[:, col:col + 1], in_=sd[:, col:col + 1]),
               reads=[b_sd], writes=[b_rstd])

    def adaln(li, st_unused):
      with contextlib.ExitStack() as st:
        wm = [st.enter_context(P.sbt(f"wm{j}_{li}", [128, 8, 128], F32)) for j in range(2)]
        bwm = [Buf("wm0"), Buf("wm1")]
        gbl = st.enter_context(P.sbt(f"gbl_{li}", [128, 8, 128], F32))
        b_gbl = Buf("gbl")
        P.dma("sp", vecs[:, 0:8], npreT[li], writes=[b_vecs])
        P.dma("sp", vecs[:, 8:16], npostT[li], writes=[b_vecs])
        P.dma("sp", vecs[:, 16:40], bmodT[li], writes=[b_vecs])
        wsrc = wmod[li].rearrange("(k p) n -> p k n", p=128)
        for j in range(24):
            sl = j % 2
            P.dma("sp", wm[sl][:], wsrc[:, :, j * 128:(j + 1) * 128], writes=[bwm[sl]])
            for k in range(8):
                P.mm(ps[7][:, j * nseq:(j + 1) * nseq], wm[sl][:, k, :], scT[:, k, :], k == 0, k == 7,
                     reads=[bwm[sl], b_scT], writes=[bps[7]], inc=(k == 7))
        psv = ps[7][:, 0:24 * nseq].rearrange("p (j s) -> p j s", s=nseq)
        P.tt("dve", modT[:], psv, vecs[:, 16:40].unsqueeze(2).to_broadcast([128, 24, nseq]), ALU.add,
             reads=[bps[7], b_vecs], writes=[b_modT])
        P.ts("dve", gsT[:], modT[:, 8:16, :], 1.0, None, ALU.add, None, reads=[b_modT], writes=[b_gsT])
        P.tt("dve", gsT[:], gsT[:], vecs[:, 0:8].unsqueeze(2).to_broadcast([128, 8, nseq]), ALU.mult,
             reads=[b_gsT, b_vecs], writes=[b_gsT])
        P.tt("dve", ggT[:], modT[:, 16:24, :], vecs[:, 8:16].unsqueeze(2).to_broadcast([128, 8, nseq]), ALU.mult,
             reads=[b_modT, b_vecs], writes=[b_ggT])
        for s in range(nseq):
            P.cp("dve", gbl[:], ggT[:, :, s:s + 1].to_broadcast([128, 8, 128]), reads=[b_ggT], writes=[b_gbl])
            for c in range(8):
                bk = 5 + c // 4
                P.mm(ps[bk][:, (c % 4) * 128:(c % 4 + 1) * 128], gbl[:, c, :], identf[:], True, True,
                     reads=[b_gbl, b_identf], writes=[bps[bk]], inc=(c % 4 == 3))
            P.cp("dve", ggrow[s][0][:, 0:512], ps[5][:], reads=[bps[5]], writes=[ggrow[s][1]])
            P.act(ggrow[s][0][:, 512:1024], ps[6][:], AF.Copy, reads=[bps[6]], writes=[ggrow[s][1]])

    def make_common(st, tag):
        C = {}
        C["xt"] = [st.enter_context(P.sbt(f"xt{j}_{tag}", [128, 1024], F32)) for j in range(2)]
        C["bxt"] = [Buf("xt0"), Buf("xt1")]
        C["xn"] = st.enter_context(P.sbt(f"xn_{tag}", [128, 1024], BF16))
        C["bxn"] = Buf("xn")
        C["junk"] = st.enter_context(P.sbt(f"junk_{tag}", [128, 1024], BF16))
        C["bjunk"] = Buf("junk")
        C["yt"] = st.enter_context(P.sbt(f"yt_{tag}", [128, 1024], F32))
        C["byt"] = Buf("yt")
        C["cnt"] = 0
        return C

    def prenorm_parts(C, s, src, bsrc, n, hT, bhT, col0):
        sl = C["cnt"] % 2
        C["cnt"] += 1
        xt, bxt = C["xt"][sl], C["bxt"][sl]

        def p1():
            P.dma("sp", xt[:], src[s, n * 128:(n + 1) * 128, :], reads=[bsrc], writes=[bxt])
            P.act(C["yt"][:], xt[:], AF.Square, reads=[bxt], writes=[C["byt"]])
            P.T.op("dve", lambda q, yt_=C["yt"]: q.reduce_sum(out=ss[:, 0:1], in_=yt_[:], axis=mybir.AxisListType.X),
                   reads=[C["byt"]], writes=[b_ss])
            P.act(sd[:, 0:1], ss[:, 0:1], AF.Sqrt, reads=[b_ss, b_eps], writes=[b_sd], scale=1.0 / D, bias=epsT[:, 0:1])

        def p2():
            P.T.op("dve", lambda q: q.reciprocal(out=rstd[:, 0:1], in_=sd[:, 0:1]), reads=[b_sd], writes=[b_rstd])
            P.act(C["xn"][:], xt[:], AF.Copy, reads=[bxt, b_rstd], writes=[C["bxn"]], scale=rstd[:, 0:1])
            for b in range(2):
                for j in range(4):
                    k = 4 * b + j
                    P.mm(ps[b][:, j * 128:(j + 1) * 128], C["xn"][:, k * 128:(k + 1) * 128], ident_bf[:], True, True,
                         reads=[C["bxn"], b_ident], writes=[bps[b]], inc=(j == 3))

        def p3():
            for b in range(2):
                for j in range(4):
                    k = 4 * b + j
                    o = hT[:, k, col0:col0 + 128]
                    i_ = ps[b][:, j * 128:(j + 1) * 128]
                    if b == 0 or P.pne == 'act':
                        P.act(o, i_, AF.Identity, reads=[bps[b], b_gsT, b_modT], writes=[bhT],
                              scale=gsT[:, k, s:s + 1], bias=modT[:, k, s:s + 1])
                    else:
                        P.ts("dve", o, i_, gsT[:, k, s:s + 1], modT[:, k, s:s + 1], ALU.mult, ALU.add,
                             reads=[bps[b], b_gsT, b_modT], writes=[bhT])
        return p1, p2, p3

    def prenorm(C, s, src, bsrc, n, hT, bhT, col0):
        p1, p2, p3 = prenorm_parts(C, s, src, bsrc, n, hT, bhT, col0)
        p1()
        p2()
        p3()

    def post(C, s, ypb, xt, bxt, dst, bdst, n):
        yt, byt = C["yt"], C["byt"]
        for h in range(2):
            P.act(yt[:, h * 512:(h + 1) * 512], ps[ypb[h]][:], AF.Square, reads=[bps[ypb[h]]], writes=[byt])
        P.T.op("dve", lambda q, yt_=yt: q.reduce_sum(out=ss[:, 3:4], in_=yt_[:], axis=mybir.AxisListType.X),
               reads=[byt], writes=[b_ss])
        rstd_from_ss(3, D)
        for h in range(2):
            P.act(yt[:, h * 512:(h + 1) * 512], ps[ypb[h]][:], AF.Copy, reads=[bps[ypb[h]], b_rstd], writes=[byt],
                  scale=rstd[:, 3:4])
        P.tt("dve", yt[:], yt[:], ggrow[s][0][:], ALU.mult, reads=[byt, ggrow[s][1]], writes=[byt])
        P.tt("pool", yt[:], yt[:], xt[:], ALU.add, reads=[byt, bxt], writes=[byt])
        P.dma(getattr(P, "stq", "pool"), dst[s, n * 128:(n + 1) * 128, :], yt[:], reads=[byt], writes=[bdst])

    def sincos(st, tag, phi, bphi, shape, out_sin, out_cos, bout):
        tf = st.enter_context(P.sbt(f"sc_tf_{tag}", shape, F32))
        ti = st.enter_context(P.sbt(f"sc_ti_{tag}", shape, I32))
        btf, bti = Buf("tf"), Buf("ti")
        for shift, o in ((0.0, out_sin), (0.5 * math.pi, out_cos)):
            P.ts("dve", tf[:], phi, shift, 1.0 / TWO_PI, ALU.add, ALU.mult, reads=[bphi], writes=[btf])
            P.cp("dve", ti[:], tf[:], reads=[btf], writes=[bti])
            P.cp("dve", tf[:], ti[:], reads=[bti], writes=[btf])
            P.stt("dve", tf[:], tf[:], -TWO_PI, phi, ALU.mult, ALU.add, reads=[btf, bphi], writes=[btf])
            P.ts("dve", tf[:], tf[:], shift, 0.999999, ALU.add, ALU.mult, reads=[btf], writes=[btf])
            P.act(o, tf[:], AF.Sin, reads=[btf], writes=[bout])

    def even_layer(li, src, bsrc, dst, bdst):
        j = li // 2
        P.pool_to_dve = (os.environ.get('P2D', '0') == '1')
        P.pne = 'act'
        P.stq = STQ
        with contextlib.ExitStack() as st:
            def S(name, shape, dt=F32):
                return st.enter_context(P.sbt(f"{name}_e{li}", list(shape), dt)), Buf(name)
            adaln(li, st)
            C = make_common(st, f"e{li}")
            win, b_win = S("win", [128, 8, 3072], BF16)
            wout, b_wout = S("wout", [128, 8, 1024], BF16)
            wglu, b_wglu = S("wglu", [128, 4, 512], BF16)
            drow, b_drow = S("drow", [128, 512])
            dtab, b_dtab = S("dtab", [128, 4, 128])
            gam, b_gam = S("gam", [128, 4, 4])
            cdec, b_cdec = S("cdec", [128, 4])
            jt, b_jt = S("jt", [128, 128])
            wsrc = w_in_ab[j].rearrange("(k p) n -> p k n", p=128)
            for k in range(8):
                P.dma("pool", win[:, k, :], wsrc[:, k, :], writes=[b_win])
            P.dma("pool", wout[:], w_out_ab[j].rearrange("(k p) n -> p k n", p=128), writes=[b_wout])
            P.dma("pool", wglu[:], w_glu[j].rearrange("(k p) n -> p k n", p=128), writes=[b_wglu])
            P.dma("sp", drow[:], ssm_d[j].partition_broadcast(128), writes=[b_drow])
            P.dma("sp", dtab[:], dtab_d, writes=[b_dtab])
            P.dma("sp", gam[:], gam_d, writes=[b_gam])
            P.dma("sp", cdec[:], cdec_d, writes=[b_cdec])
            P.dma("sp", jt[:], jt_d, writes=[b_jt])
            WB = [S(f"WB{r}", [128, 2048], BF16) for r in range(2)]
            WC = [S(f"WC{r}", [128, 512], BF16) for r in range(3)]
            COS, b_COS = S("COS", [128, 16, 128])
            SIN, b_SIN = S("SIN", [128, 16, 128])
            RHO0, b_RHO0 = S("RHO0", [128, 16, 128])
            Gc, b_Gc = S("Gc", [128, 2, 16])
            cin = [S(f"cin{i}", [128, 2, 16]) for i in range(2)]
            stf, b_stf = S("stf", [128, 512])
            stbf, b_stbf = S("stbf", [128, 512], BF16)
            hn, b_hn = S("hn", [128, 16])
            lst, b_lst = S("lst", [128, 8, 16])
            lastb = [S(f"lastb{i}", [128, 2, 16]) for i in range(2)]

            def s5_setup(d):
                with contextlib.ExitStack() as s2:
                    def S2(name, shape, dt=F32):
                        return s2.enter_context(P.sbt(f"{name}_e{li}d{d}", list(shape), dt)), Buf(name)
                    sm, b_sm = S2("sm", [128, 3, 16])
                    P.dma("sp", sm[:], s5_sm3[j, d].rearrange("t p s -> p t s"), writes=[b_sm])
                    dl, b_dl = S2("dl", [128, 16])
                    zr, b_zr = S2("zr", [128, 16])
                    zi, b_zi = S2("zi", [128, 16])
                    rho, b_rho = S2("rho", [128, 16])
                    P.act(dl[:], sm[:, 2, :], AF.Exp, reads=[b_sm], writes=[b_dl])
                    P.tt("dve", zr[:], sm[:, 0, :], dl[:], ALU.mult, reads=[b_sm, b_dl], writes=[b_zr])
                    P.tt("dve", zi[:], sm[:, 1, :], dl[:], ALU.mult, reads=[b_sm, b_dl], writes=[b_zi])
                    P.act(rho[:], zr[:], AF.Exp, reads=[b_zr], writes=[b_rho])
                    for g4 in range(4):
                        with contextlib.ExitStack() as s4:
                            phi = s4.enter_context(P.sbt(f"phi_e{li}d{d}g{g4}", [128, 4, 128], F32))
                            b_phi = Buf("phi")
                            P.tt("dve", phi[:], zi[:, 4 * g4:4 * g4 + 4].unsqueeze(2).to_broadcast([128, 4, 128]),
                                 jt[:].unsqueeze(1).to_broadcast([128, 4, 128]), ALU.mult, reads=[b_zi, b_jt], writes=[b_phi])
                            sincos(s4, f"t{li}{d}{g4}", phi[:], b_phi, [128, 4, 128], SIN[:, 4 * g4:4 * g4 + 4, :],
                                   COS[:, 4 * g4:4 * g4 + 4, :], b_COS)
                            T_barrier()
                    P.cp("dve", RHO0[:], rho[:].unsqueeze(2).to_broadcast([128, 16, 128]), reads=[b_rho], writes=[b_RHO0])
                    P.memset("dve", RHO0[:, :, 0:1], 0.0, writes=[b_RHO0])
                    ph2, b_ph2 = S2("ph2", [128, 16])
                    sn2, b_sn2 = S2("sn2", [128, 2, 16])
                    P.ts("dve", ph2[:], zi[:], 128.0, None, ALU.mult, None, reads=[b_zi], writes=[b_ph2])
                    sincos(s2, f"g{li}{d}", ph2[:], b_ph2, [128, 16], sn2[:, 1, :], sn2[:, 0, :], b_sn2)
                    P.tt("dve", Gc[:], sn2[:], rho[:].unsqueeze(1).to_broadcast([128, 2, 16]), ALU.mult,
                         reads=[b_sn2, b_rho], writes=[b_Gc])
                    T_barrier()
                for cc in range(8):
                  with contextlib.ExitStack() as s3:
                    def S3(name, shape, dt=F32):
                        return s3.enter_context(P.sbt(f"{name}_e{li}d{d}c{cc}", list(shape), dt)), Buf(name)
                    rp, b_rp = S3("rp", [128, 3, 256])
                    P.dma("sp", rp[:], s5_rep3[j, d][:, cc * 256:(cc + 1) * 256].partition_broadcast(128), writes=[b_rp])
                    r_dl, b_r_dl = S3("r_dl", [128, 256])
                    r_zr, b_r_zr = S3("r_zr", [128, 256])
                    r_zi, b_r_zi = S3("r_zi", [128, 256])
                    r_rho, b_r_rho = S3("r_rho", [128, 256])
                    r_sn, b_r_sn = S3("r_sn", [128, 2, 256])
                    P.act(r_dl[:], rp[:, 2, :], AF.Exp, reads=[b_rp], writes=[b_r_dl])
                    P.tt("dve", r_zr[:], rp[:, 0, :], r_dl[:], ALU.mult, reads=[b_rp, b_r_dl], writes=[b_r_zr])
                    P.tt("dve", r_zi[:], rp[:, 1, :], r_dl[:], ALU.mult, reads=[b_rp, b_r_dl], writes=[b_r_zi])
                    P.act(r_rho[:], r_zr[:], AF.Exp, reads=[b_r_zr], writes=[b_r_rho])
                    sincos(s3, f"r{li}{d}{cc}", r_zi[:], b_r_zi, [128, 256], r_sn[:, 1, :], r_sn[:, 0, :], b_r_sn)
                    P.tt("dve", r_sn[:], r_sn[:], r_rho[:].unsqueeze(1).to_broadcast([128, 2, 256]), ALU.mult,
                         reads=[b_r_sn, b_r_rho], writes=[b_r_sn])
                    P.ts("dve", r_sn[:, 0, :], r_sn[:, 0, :], -1.0, None, ALU.add, None, reads=[b_r_sn], writes=[b_r_sn])
                    P.tt("dve", r_dl[:], rp[:, 0, :], rp[:, 0, :], ALU.mult, reads=[b_rp], writes=[b_r_dl])
                    P.tt("dve", r_zr[:], rp[:, 1, :], rp[:, 1, :], ALU.mult, reads=[b_rp], writes=[b_r_zr])
                    P.tt("dve", r_dl[:], r_dl[:], r_zr[:], ALU.add, reads=[b_r_dl, b_r_zr], writes=[b_r_dl])
                    P.T.op("dve", lambda q, r_dl=r_dl: q.reciprocal(out=r_dl[:], in_=r_dl[:]), reads=[b_r_dl], writes=[b_r_dl])
                    P.tt("dve", r_zr[:], r_sn[:, 0, :], rp[:, 0, :], ALU.mult, reads=[b_r_sn, b_rp], writes=[b_r_zr])
                    P.tt("dve", r_rho[:], r_sn[:, 1, :], rp[:, 1, :], ALU.mult, reads=[b_r_sn, b_rp], writes=[b_r_rho])
                    P.tt("dve", r_zr[:], r_zr[:], r_rho[:], ALU.add, reads=[b_r_zr, b_r_rho], writes=[b_r_zr])
                    P.tt("dve", r_zr[:], r_zr[:], r_dl[:], ALU.mult, reads=[b_r_zr, b_r_dl], writes=[b_r_zr])
                    P.tt("dve", r_zi[:], r_sn[:, 1, :], rp[:, 0, :], ALU.mult, reads=[b_r_sn, b_rp], writes=[b_r_zi])
                    P.tt("dve", r_rho[:], r_sn[:, 0, :], rp[:, 1, :], ALU.mult, reads=[b_r_sn, b_rp], writes=[b_r_rho])
                    P.tt("dve", r_zi[:], r_zi[:], r_rho[:], ALU.subtract, reads=[b_r_zi, b_r_rho], writes=[b_r_zi])
                    P.tt("dve", r_zi[:], r_zi[:], r_dl[:], ALU.mult, reads=[b_r_zi, b_r_dl], writes=[b_r_zi])
                    bl, b_bl = S3("bl", [128, 2, 256])
                    P.dma("sp", bl[:], s5_bl[j, d][:, :, cc * 256:(cc + 1) * 256].rearrange("r p n -> p r n"), writes=[b_bl])
                    P.tt("dve", r_dl[:], r_zr[:], bl[:, 0, :], ALU.mult, reads=[b_r_zr, b_bl], writes=[b_r_dl])
                    P.tt("dve", r_rho[:], r_zi[:], bl[:, 1, :], ALU.mult, reads=[b_r_zi, b_bl], writes=[b_r_rho])
                    P.tt("dve", WB[0][0][:, cc * 256:(cc + 1) * 256], r_dl[:], r_rho[:], ALU.subtract, reads=[b_r_dl, b_r_rho], writes=[WB[0][1]])
                    P.tt("dve", r_dl[:], r_zr[:], bl[:, 1, :], ALU.mult, reads=[b_r_zr, b_bl], writes=[b_r_dl])
                    P.tt("dve", r_rho[:], r_zi[:], bl[:, 0, :], ALU.mult, reads=[b_r_zi, b_bl], writes=[b_r_rho])
                    P.tt("dve", WB[1][0][:, cc * 256:(cc + 1) * 256], r_dl[:], r_rho[:], ALU.add, reads=[b_r_dl, b_r_rho], writes=[WB[1][1]])

                    T_barrier()
                with contextlib.ExitStack() as s2:
                    def S2(name, shape, dt=F32):
                        return s2.enter_context(P.sbt(f"{name}_e{li}d{d}x", list(shape), dt)), Buf(name)
                    cl, b_cl = S2("cl", [128, 2, 512])
                    P.dma("sp", cl[:], s5_cl[j, d].rearrange("r p n -> p r n"), writes=[b_cl])
                    P.cp("dve", WC[0][0][:], cl[:, 0, :], reads=[b_cl], writes=[WC[0][1]])
                    P.ts("dve", WC[1][0][:], cl[:, 0, :], -1.0, None, ALU.mult, None, reads=[b_cl], writes=[WC[1][1]])
                    P.ts("dve", WC[2][0][:], cl[:, 1, :], -1.0, None, ALU.mult, None, reads=[b_cl], writes=[WC[2][1]])
                    P.memset("dve", cin[0][0][:], 0.0, writes=[cin[0][1]])
                    T_barrier()

            import types

            def alloc_work(wk, passB):
                def W(name, shape, dt=F32):
                    return wk.enter_context(P.sbt(f"{name}_e{li}", list(shape), dt)), Buf(name)
                w = types.SimpleNamespace()
                w.g = [types.SimpleNamespace() for _ in range(2)]
                tmps = {nm: W(f"{nm}m", [128, 512]) for nm in ("tA", "tB", "tC", "tD")}
                Pk1 = [W(f"Pk_{k}", [128, 512], BF16) for k in range(4)]
                for i, g in enumerate(w.g):
                    for nm in ("wre", "wim", "sre", "sim"):
                        setattr(g, nm, W(f"{nm}{i}", [128, 512]))
                    for nm in ("tA", "tB", "tC", "tD"):
                        setattr(g, nm, tmps[nm])
                    g.Pk = Pk1
                w.uT = W("uT", [128, 4, 128], BF16)
                w.kf = W("kf", [128, 512], BF16)
                w.tC = W("tCr", [128, 512])
                if not passB:
                    w.hT = [W(f"hT{i}", [128, 8, 128], BF16) for i in range(2)]
                    w.rot = [W(f"rot{i}", [128, 2, 2, 128]) for i in range(2)]
                    w.rA = W("rA", [128, 512])
                    w.rB = W("rB", [128, 512])
                    w.qr = W("qr", [128, 512], BF16)
                    w.kr = [W("kr", [128, 512], BF16)]
                    w.vb = [W("vb", [128, 512], BF16)]
                    w.QT = [W("QT", [128, 512], BF16)]
                    w.KT = W("KT", [128, 512], BF16)
                    w.STb = W("STb", [128, 512], BF16)
                    w.sg = [W("sg", [128, 1024])]
                    w.du = W("du", [128, 512])
                    w.ub = [W("ub", [128, 512], BF16)]
                    w.opt = [W("opt", [128, 512])]
                    w.ypt = [W("ypt", [128, 512])]
                else:
                    w.kr = [W(f"kr{i}", [128, 512], BF16) for i in range(2)]
                    w.vb = [W(f"vb{i}", [128, 512], BF16) for i in range(2)]
                    w.QT = [W(f"QT{i}", [128, 512], BF16) for i in range(2)]
                    w.sga = W("sga", [128, 512])
                    w.sgb = [W(f"sgb{i}", [128, 512]) for i in range(2)]
                    w.uT2 = W("uT2", [128, 4, 128], BF16)
                    w.ub = [W(f"ub{i}", [128, 512], BF16) for i in range(2)]
                    w.opt = [W(f"opt{i}", [128, 512]) for i in range(2)]
                    w.ypt = [W(f"ypt{i}", [128, 512]) for i in range(2)]
                    w.oab = [W(f"oab{i}", [128, 1024], BF16) for i in range(2)]
                    w.oT = W("oT", [128, 8, 128], BF16)
                    w.ygb = W("ygb", [128, 512], BF16)
                    w.tA = W("tAh", [128, 512])
                    w.tB = W("tBh", [128, 512])
                    w.tD = W("tDh", [128, 512])
                return w

            def s5_chunk(w, ci, rev, hooks=()):
                cur, nxt = cin[ci % 2], cin[(ci + 1) % 2]
                lb, b_lb = lastb[ci % 2]
                uT, b_uT = w.uT
                hooks = list(hooks)

                def hook():
                    if hooks:
                        hooks.pop(0)()

                def banks(g):
                    return (3, 4) if g % 2 == 0 else (5, 6)

                def stageApe(g):
                    br, bi = banks(g)
                    for t in range(4):
                        s_ = 4 * g + t
                        P.mm(ps[br][:, t * 128:(t + 1) * 128], WB[0][0][:, s_ * 128:(s_ + 1) * 128], uT[:, g, :], True, False,
                             reads=[WB[0][1], b_uT], writes=[bps[br]], inc=False)
                        P.mm(ps[bi][:, t * 128:(t + 1) * 128], WB[1][0][:, s_ * 128:(s_ + 1) * 128], uT[:, g, :], True, False,
                             reads=[WB[1][1], b_uT], writes=[bps[bi]], inc=False)
                    o_r = ps[br][:].rearrange("p (a b) -> p a b", a=4)[:, :, 0:1]
                    o_i = ps[bi][:].rearrange("p (a b) -> p a b", a=4)[:, :, 0:1]
                    P.mm(o_r, identf[:], cur[0][:, 0, 4 * g:4 * g + 4].unsqueeze(2), False, True,
                         reads=[b_identf, cur[1]], writes=[bps[br]], inc=False)
                    P.mm(o_i, identf[:], cur[0][:, 1, 4 * g:4 * g + 4].unsqueeze(2), False, True,
                         reads=[b_identf, cur[1]], writes=[bps[bi]], inc=True)

                def stageAve(g):
                    br, bi = banks(g)
                    G = w.g[g % 2]
                    Cg = COS[:, 4 * g:4 * g + 4, :].rearrange("p a b -> p (a b)")
                    Sg = SIN[:, 4 * g:4 * g + 4, :].rearrange("p a b -> p (a b)")
                    P.tt("dve", G.tA[0][:], ps[br][:], Cg, ALU.mult, reads=[bps[br], b_COS], writes=[G.tA[1]])
                    P.tt("dve", G.tB[0][:], ps[bi][:], Sg, ALU.mult, reads=[bps[bi], b_COS], writes=[G.tB[1]])
                    P.tt(S5E, G.wre[0][:], G.tA[0][:], G.tB[0][:], ALU.add, reads=[G.tA[1], G.tB[1]], writes=[G.wre[1]])
                    P.tt("dve", G.tC[0][:], ps[bi][:], Cg, ALU.mult, reads=[bps[bi], b_COS], writes=[G.tC[1]])
                    P.stt("dve", G.tD[0][:], ps[br][:], -1.0, Sg, ALU.mult, ALU.mult, reads=[bps[br], b_COS], writes=[G.tD[1]])
                    P.tt(S5E, G.wim[0][:], G.tC[0][:], G.tD[0][:], ALU.add, reads=[G.tC[1], G.tD[1]], writes=[G.wim[1]])

                def stageB(g):
                    G = w.g[g % 2]
                    Rg = RHO0[:, 4 * g:4 * g + 4, :].rearrange("p a b -> p (a b)")
                    P.T.op("dve", lambda q, Rg=Rg, o=G.sre[0], i=G.wre[0]: q.tensor_tensor_scan(
                        out=o[:], data0=Rg, data1=i[:], initial=0.0, op0=ALU.mult, op1=ALU.add),
                        reads=[b_RHO0, G.wre[1]], writes=[G.sre[1]])
                    P.T.op("dve", lambda q, Rg=Rg, o=G.sim[0], i=G.wim[0]: q.tensor_tensor_scan(
                        out=o[:], data0=Rg, data1=i[:], initial=0.0, op0=ALU.mult, op1=ALU.add),
                        reads=[b_RHO0, G.wim[1]], writes=[G.sim[1]])
                    sr3 = G.sre[0][:].rearrange("p (a b) -> p a b", a=4)
                    si3 = G.sim[0][:].rearrange("p (a b) -> p a b", a=4)
                    P.act(lb[:, 0, 4 * g:4 * g + 4].unsqueeze(2), sr3[:, :, 127:128], AF.Copy, reads=[G.sre[1]], writes=[b_lb])
                    P.act(lb[:, 1, 4 * g:4 * g + 4].unsqueeze(2), si3[:, :, 127:128], AF.Copy, reads=[G.sim[1]], writes=[b_lb])

                    def pv(ap):
                        v = ap.rearrange("p (a b) -> p a b", a=4)
                        return v[:, :, ::-1] if rev else v
                    C3 = COS[:, 4 * g:4 * g + 4, :]
                    S3 = SIN[:, 4 * g:4 * g + 4, :]
                    e2 = "dve" if rev else S5E
                    P.tt("dve", pv(G.Pk[0][0][:]), sr3, C3, ALU.mult, reads=[G.sre[1], b_COS], writes=[G.Pk[0][1]])
                    P.tt(e2, pv(G.Pk[1][0][:]), si3, S3, ALU.mult, reads=[G.sim[1], b_COS], writes=[G.Pk[1][1]])
                    P.tt("dve", pv(G.Pk[2][0][:]), sr3, S3, ALU.mult, reads=[G.sre[1], b_COS], writes=[G.Pk[2][1]])
                    P.tt(e2, pv(G.Pk[3][0][:]), si3, C3, ALU.mult, reads=[G.sim[1], b_COS], writes=[G.Pk[3][1]])
                    for t in range(4):
                        s_ = 4 * g + t
                        o = ps[7][:, 32 * s_:32 * s_ + 32]
                        wsl = slice(32 * s_, 32 * s_ + 32)
                        tsl = slice(128 * t, 128 * t + 128)
                        Pk = G.Pk
                        P.mm(o, Pk[0][0][:, tsl], WC[0][0][:, wsl], True, False, reads=[Pk[0][1], WC[0][1]], writes=[bps[7]], inc=False)
                        P.mm(o, Pk[1][0][:, tsl], WC[1][0][:, wsl], False, False, reads=[Pk[1][1], WC[1][1]], writes=[bps[7]], inc=False)
                        P.mm(o, Pk[2][0][:, tsl], WC[2][0][:, wsl], False, False, reads=[Pk[2][1], WC[2][1]], writes=[bps[7]], inc=False)
                        P.mm(o, Pk[3][0][:, tsl], WC[2][0][:, wsl], False, True, reads=[Pk[3][1], WC[2][1]], writes=[bps[7]], inc=(t == 3))

                stageApe(0)
                stageAve(0)
                stageApe(1)
                hook()
                stageAve(1)
                hook()
                stageApe(2)
                stageB(0)
                hook()
                stageAve(2)
                hook()
                stageApe(3)
                stageB(1)
                hook()
                stageAve(3)
                hook()
                stageB(2)
                hook()
                stageB(3)
                hook()
                while hooks:
                    hook()
                l4 = lst[:]
                P.tt("pool", l4[:, 0, :], lb[:, 0, :], Gc[:, 0, :], ALU.mult, reads=[b_lb, b_Gc], writes=[b_lst])
                P.tt("pool", l4[:, 1, :], lb[:, 1, :], Gc[:, 1, :], ALU.mult, reads=[b_lb, b_Gc], writes=[b_lst])
                P.tt("pool", l4[:, 2, :], lb[:, 1, :], Gc[:, 0, :], ALU.mult, reads=[b_lb, b_Gc], writes=[b_lst])
                P.tt("pool", l4[:, 3, :], lb[:, 0, :], Gc[:, 1, :], ALU.mult, reads=[b_lb, b_Gc], writes=[b_lst])
                P.tt("pool", nxt[0][:, 0, :], l4[:, 0, :], l4[:, 1, :], ALU.subtract, reads=[b_lst], writes=[nxt[1]])
                P.tt("pool", nxt[0][:, 1, :], l4[:, 2, :], l4[:, 3, :], ALU.add, reads=[b_lst], writes=[nxt[1]])

            def ret_state_update(w, kbuf, b_kbuf, vbt, b_vbt, gcol):
                kf, b_kf = w.kf
                P.tt("pool", kf[:].rearrange("p (h e) -> p h e", h=4), kbuf[:].rearrange("p (h e) -> p h e", h=4),
                     gam[:, gcol, :].unsqueeze(2).to_broadcast([128, 4, 128]), ALU.mult,
                     reads=[b_kbuf, b_gam], writes=[b_kf])
                for h in range(4):
                    hs = slice(h * 128, (h + 1) * 128)
                    P.mm(ps[5][:, hs], kf[:, hs], vbt[:, hs], True, True, reads=[b_kf, b_vbt], writes=[bps[5]], inc=(h == 3))
                P.tt("pool", stf[:].rearrange("p (h e) -> p h e", h=4), stf[:].rearrange("p (h e) -> p h e", h=4),
                     cdec[:].unsqueeze(2).to_broadcast([128, 4, 128]), ALU.mult, reads=[b_stf, b_cdec], writes=[b_stf])
                P.tt("dve", stf[:], stf[:], ps[5][:], ALU.add, reads=[b_stf, bps[5]], writes=[b_stf])
                P.act(stbf[:], stf[:], AF.Copy, reads=[b_stf], writes=[b_stbf])

            def passA(s, w):
                OPs, YPs, SGs, QTs, KRs, VBs, UBs = OPs_[s], YPs_[s], SGs_[s], QTs_[s], KRs_[s], VBs_[s], UBs_[s]
                b_OPs, b_YPs, b_SGs, b_QTs, b_KRs, b_VBs, b_UBs = SCRB[s]
                P.memset("dve", stf[:], 0.0, writes=[b_stf])
                P.memset("pool", stbf[:], 0.0, writes=[b_stbf])
                P.memset("dve", cin[0][0][:], 0.0, writes=[cin[0][1]])
                nA = nch if KSTOP not in ('setup',) else 0
                if nA:
                    prenorm(C, s, src, bsrc[s], 0, w.hT[0][0], w.hT[0][1], 0)
                for n in range(nA):
                    tsl = slice(n * 128, (n + 1) * 128)
                    hT, b_hT = w.hT[n % 2]
                    rot, b_rot = w.rot[n % 2]
                    if n == 0:
                        P.dma("sp", rot[:], rot_d[tsl], writes=[b_rot])
                    if n + 1 < nA:
                        P.dma("sp", w.rot[(n + 1) % 2][0][:], rot_d[(n + 1) * 128:(n + 2) * 128], writes=[w.rot[(n + 1) % 2][1]])
                    if n + 1 < nA:
                        p1, p2, p3 = prenorm_parts(C, s, src, bsrc[s], n + 1, w.hT[(n + 1) % 2][0], w.hT[(n + 1) % 2][1], 0)
                    else:
                        p1 = p2 = p3 = (lambda: None)
                    qr, b_qr = w.qr
                    kr, b_kr = w.kr[0]
                    vb, b_vb = w.vb[0]
                    QT, b_QT = w.QT[0]
                    KT, b_KT = w.KT
                    STb, b_STb = w.STb
                    sg, b_sg = w.sg[0]
                    du, b_du = w.du
                    ub, b_ub = w.ub[0]
                    opt, b_opt = w.opt[0]
                    ypt, b_ypt = w.ypt[0]
                    tC, b_tC = w.tC
                    uT, b_uT = w.uT
                    def inproj(col, bank):
                        for k in range(8):
                            P.mm(ps[bank][:], hT[:, k, :], win[:, k, col * 512:(col + 1) * 512], k == 0, k == 7,
                                 reads=[b_hT, b_win], writes=[bps[bank]], inc=(k == 7))

                    def rotary(zb, tbl, outb, b_outb):
                        rA, b_rA = w.rA
                        rB, b_rB = w.rB
                        z3 = ps[zb][:].rearrange("p (h e) -> p h e", h=4)
                        a3 = rA[:].rearrange("p (h e) -> p h e", h=4)
                        b3 = rB[:].rearrange("p (h e) -> p h e", h=4)
                        P.tt("dve", a3, z3, rot[:, tbl, 0, :].unsqueeze(1).to_broadcast([128, 4, 128]), ALU.mult,
                             reads=[bps[zb], b_rot], writes=[b_rA])
                        P.tt("dve", b3[:, :, 0:64], z3[:, :, 64:128], rot[:, tbl, 1, 0:64].unsqueeze(1).to_broadcast([128, 4, 64]),
                             ALU.mult, reads=[bps[zb], b_rot], writes=[b_rB])
                        P.tt("dve", b3[:, :, 64:128], z3[:, :, 0:64], rot[:, tbl, 1, 64:128].unsqueeze(1).to_broadcast([128, 4, 64]),
                             ALU.mult, reads=[bps[zb], b_rot], writes=[b_rB])
                        P.tt("pool", outb[:], rA[:], rB[:], ALU.add, reads=[b_rA, b_rB], writes=[b_outb])

                    inproj(4, 2)
                    P.act(ub[:], ps[2][:], AF.Copy, reads=[bps[2]], writes=[b_ub])
                    P.tt("dve", du[:], ps[2][:], drow[:], ALU.mult, reads=[bps[2], b_drow, b_ub], writes=[b_du])
                    P.dma(STQ, UBs[tsl], ub[:], reads=[b_ub], writes=[b_UBs])
                    for q_ in range(4):
                        qs = slice(q_ * 128, (q_ + 1) * 128)
                        P.mm(ps[0][:, qs], ub[:, qs], ident_bf[:], True, True, reads=[b_ub, b_ident], writes=[bps[0]], inc=(q_ == 3))
                    P.act(uT[:].rearrange("p a b -> p (a b)"), ps[0][:], AF.Copy, reads=[bps[0]], writes=[b_uT])

                    def H0():
                        inproj(0, 2)
                        inproj(1, 1)
                        rotary(2, 0, qr, b_qr)
                        rotary(1, 1, kr, b_kr)

                    def H1():
                        inproj(2, 2)
                        P.act(vb[:], ps[2][:], AF.Copy, reads=[bps[2]], writes=[b_vb])
                        inproj(3, 0)
                        P.act(sg[:, 0:512], ps[0][:], AF.Silu, reads=[bps[0]], writes=[b_sg])
                        inproj(5, 1)
                        P.act(sg[:, 512:1024], ps[1][:], AF.Silu, reads=[bps[1]], writes=[b_sg])
                        P.dma(STQ, SGs[tsl], sg[:], reads=[b_sg], writes=[b_SGs])

                    def R1():
                        for h in range(4):
                            hs = slice(h * 128, (h + 1) * 128)
                            P.mm(ps[0][:, hs], qr[:, hs], ident_bf[:], True, True, reads=[b_qr, b_ident], writes=[bps[0]], inc=(h == 3))
                        for h in range(4):
                            hs = slice(h * 128, (h + 1) * 128)
                            P.mm(ps[1][:, hs], kr[:, hs], ident_bf[:], True, True, reads=[b_kr, b_ident], writes=[bps[1]], inc=(h == 3))
                        P.act(QT[:], ps[0][:], AF.Copy, reads=[bps[0]], writes=[b_QT])
                        P.act(KT[:], ps[1][:], AF.Copy, reads=[bps[1]], writes=[b_KT])
                        p1()
                    def R2():
                        for h in range(4):
                            hs = slice(h * 128, (h + 1) * 128)
                            P.mm(ps[2][:, hs], KT[:, hs], QT[:, hs], True, True, reads=[b_KT, b_QT], writes=[bps[2]], inc=(h == 3))
                        P.tt("dve", STb[:], ps[2][:], dtab[:].rearrange("p h e -> p (h e)"), ALU.mult,
                             reads=[bps[2], b_dtab], writes=[b_STb])
                    def R3():
                        for h in range(4):
                            hs = slice(h * 128, (h + 1) * 128)
                            P.mm(ps[2][:, hs], STb[:, hs], vb[:, hs], True, True, reads=[b_STb, b_vb], writes=[bps[2]], inc=(h == 3))
                        for h in range(4):
                            hs = slice(h * 128, (h + 1) * 128)
                            P.mm(ps[0][:, hs], QT[:, hs], stbf[:, hs], True, True, reads=[b_QT, b_stbf], writes=[bps[0]], inc=(h == 3))
                        P.tt("dve", tC[:].rearrange("p (h e) -> p h e", h=4), ps[0][:].rearrange("p (h e) -> p h e", h=4),
                             gam[:, 0, :].unsqueeze(2).to_broadcast([128, 4, 128]), ALU.mult, reads=[bps[0], b_gam], writes=[b_tC])
                        P.tt("dve", opt[:], tC[:], ps[2][:], ALU.add, reads=[b_tC, bps[2]], writes=[b_opt])
                        P.dma(STQ, OPs[tsl], opt[:], reads=[b_opt], writes=[b_OPs])
                    def R4():
                        ret_state_update(w, kr, b_kr, vb, b_vb, 2)
                    def R5():
                        P.dma(STQ, QTs[n], QT[:], reads=[b_QT], writes=[b_QTs])
                        P.dma(STQ, KRs[tsl], kr[:], reads=[b_kr], writes=[b_KRs])
                        P.dma(STQ, VBs[tsl], vb[:], reads=[b_vb], writes=[b_VBs])
                    def R45():
                        R4()
                        R5()

                    def P23():
                        p2()
                        p3()
                    s5_chunk(w, n, False, hooks=[H0, H1, R1, R2, R3, R45, P23])
                    P.tt("dve", ypt[:], ps[7][:], du[:], ALU.add, reads=[bps[7], b_du], writes=[b_ypt])
                    P.dma(STQ, YPs[tsl], ypt[:], reads=[b_ypt], writes=[b_YPs])

            def passB(s, w):
                OPs, YPs, SGs, QTs, KRs, VBs, UBs = OPs_[s], YPs_[s], SGs_[s], QTs_[s], KRs_[s], VBs_[s], UBs_[s]
                b_OPs, b_YPs, b_SGs, b_QTs, b_KRs, b_VBs, b_UBs = SCRB[s]
                P.memset("dve", stf[:], 0.0, writes=[b_stf])
                P.memset("pool", stbf[:], 0.0, writes=[b_stbf])
                P.memset("dve", cin[0][0][:], 0.0, writes=[cin[0][1]])
                order = list(range(nch - 1, -1, -1)) if KSTOP == 'all' else []

                def loads(ci):
                    n = order[ci]
                    tsl = slice(n * 128, (n + 1) * 128)
                    r = ci % 2
                    P.dma("sp", w.ub[r][0][:], UBs[tsl], reads=[b_UBs], writes=[w.ub[r][1]])
                    P.dma("sp", w.QT[r][0][:], QTs[n], reads=[b_QTs], writes=[w.QT[r][1]])
                    P.dma("sp", w.opt[r][0][:], OPs[tsl], reads=[b_OPs], writes=[w.opt[r][1]])
                    P.dma("sp", w.kr[r][0][:], KRs[tsl], reads=[b_KRs], writes=[w.kr[r][1]])
                    P.dma("sp", w.vb[r][0][:], VBs[tsl], reads=[b_VBs], writes=[w.vb[r][1]])

                def make_tail(ci, n):
                    r = ci % 2
                    tsl = slice(n * 128, (n + 1) * 128)
                    ypt, b_ypt = w.ypt[r]
                    sgb, b_sgb = w.sgb[r]
                    oab, b_oab = w.oab[r]
                    tB, b_tB = w.tB
                    tD, b_tD = w.tD
                    uT2, b_uT2 = w.uT2
                    oT, b_oT = w.oT
                    ygb, b_ygb = w.ygb

                    def T1():
                        P.tt("pool", tB[:], ypt[:], ypt[:], ALU.mult, reads=[b_ypt], writes=[b_tB])
                        P.ts("dve", tB[:], tB[:], 0.044715, 1.0, ALU.mult, ALU.add, reads=[b_tB], writes=[b_tB])
                        P.tt("pool", tB[:], tB[:], ypt[:], ALU.mult, reads=[b_tB, b_ypt], writes=[b_tB])
                        P.act(tB[:], tB[:], AF.Sigmoid, reads=[b_tB], writes=[b_tB], scale=1.5957691216057308)
                        P.tt("dve", tD[:], ypt[:], tB[:], ALU.mult, reads=[b_ypt, b_tB], writes=[b_tD])
                        P.act(ygb[:], tD[:], AF.Copy, reads=[b_tD], writes=[b_ygb])

                    def T2():
                        for q_ in range(4):
                            qs = slice(q_ * 128, (q_ + 1) * 128)
                            P.mm(ps[0][:, qs], ygb[:, qs], ident_bf[:], True, True, reads=[b_ygb, b_ident], writes=[bps[0]], inc=(q_ == 3))
                        P.act(uT2[:].rearrange("p a b -> p (a b)"), ps[0][:], AF.Copy, reads=[bps[0]], writes=[b_uT2])
                        for q_ in range(4):
                            P.mm(ps[1][:], uT2[:, q_, :], wglu[:, q_, :], q_ == 0, q_ == 3, reads=[b_uT2, b_wglu], writes=[bps[1]], inc=(q_ == 3))
                        P.act(tB[:], ps[1][:], AF.Sigmoid, reads=[bps[1]], writes=[b_tB])
                        P.tt("dve", tD[:], tD[:], tB[:], ALU.mult, reads=[b_tD, b_tB], writes=[b_tD])
                        P.tt("pool", oab[:, 512:1024], tD[:], sgb[:], ALU.mult, reads=[b_tD, b_sgb], writes=[b_oab])

                    def T3():
                        for b_ in range(2):
                            for jj in range(4):
                                k = 4 * b_ + jj
                                P.mm(ps[b_][:, jj * 128:(jj + 1) * 128], oab[:, k * 128:(k + 1) * 128], ident_bf[:], True, True,
                                     reads=[b_oab, b_ident], writes=[bps[b_]], inc=(jj == 3))
                        P.act(oT[:, 0:4, :].rearrange("p a b -> p (a b)"), ps[0][:], AF.Copy, reads=[bps[0]], writes=[b_oT])
                        P.act(oT[:, 4:8, :].rearrange("p a b -> p (a b)"), ps[1][:], AF.Copy, reads=[bps[1]], writes=[b_oT])
                        for hh, bk in ((0, 2), (1, 0)):
                            for k in range(8):
                                P.mm(ps[bk][:], oT[:, k, :], wout[:, k, hh * 512:(hh + 1) * 512], k == 0, k == 7,
                                     reads=[b_oT, b_wout], writes=[bps[bk]], inc=(k == 7))

                    def T4():
                        sl = C["cnt"] % 2
                        C["cnt"] += 1
                        xt, bxt = C["xt"][sl], C["bxt"][sl]
                        P.dma("sp", xt[:], src[s, tsl, :], reads=[bsrc[s]], writes=[bxt])
                        post(C, s, (2, 0), xt, bxt, dst, bdst[s], n)
                    return [T1, T2, T3, T4]

                if order:
                    loads(0)
                tail = []
                for ci, n in enumerate(order):
                    tsl = slice(n * 128, (n + 1) * 128)
                    r = ci % 2
                    QT, b_QT = w.QT[r]
                    opt, b_opt = w.opt[r]
                    kr, b_kr = w.kr[r]
                    vb, b_vb = w.vb[r]
                    ub, b_ub = w.ub[r]
                    sga, b_sga = w.sga
                    sgb, b_sgb = w.sgb[r]
                    ypt, b_ypt = w.ypt[r]
                    tA, b_tA = w.tA
                    tC, b_tC = w.tC
                    uT, b_uT = w.uT
                    oab, b_oab = w.oab[r]
                    kf, b_kf = w.kf
                    P.dma("sp", sga[:], SGs[tsl, 0:512], reads=[b_SGs], writes=[b_sga])
                    P.dma("sp", sgb[:], SGs[tsl, 512:1024], reads=[b_SGs], writes=[b_sgb])
                    P.dma("sp", ypt[:], YPs[tsl], reads=[b_YPs], writes=[b_ypt])
                    for q_ in range(4):
                        qs = slice(q_ * 128, (q_ + 1) * 128)
                        P.mm(ps[0][:, qs], ub[:, qs], jmat_bf[:], True, True, reads=[b_ub, b_jmat], writes=[bps[0]], inc=(q_ == 3))
                    P.act(uT[:].rearrange("p a b -> p (a b)"), ps[0][:], AF.Copy, reads=[bps[0]], writes=[b_uT])
                    if ci + 1 < len(order):
                        loads(ci + 1)

                    def RB1():
                        for h in range(4):
                            hs = slice(h * 128, (h + 1) * 128)
                            P.mm(ps[2][:, hs], QT[:, hs], stbf[:, hs], True, True, reads=[b_QT, b_stbf], writes=[bps[2]], inc=(h == 3))
                        P.tt("dve", tC[:].rearrange("p (h e) -> p h e", h=4), ps[2][:].rearrange("p (h e) -> p h e", h=4),
                             gam[:, 1, :].unsqueeze(2).to_broadcast([128, 4, 128]), ALU.mult, reads=[bps[2], b_gam], writes=[b_tC])
                        P.tt("pool", opt[:], opt[:], tC[:], ALU.add, reads=[b_opt, b_tC], writes=[b_opt])
                        P.tt("pool", kf[:].rearrange("p (h e) -> p h e", h=4), kr[:].rearrange("p (h e) -> p h e", h=4),
                             gam[:, 3, :].unsqueeze(2).to_broadcast([128, 4, 128]), ALU.mult,
                             reads=[b_kr, b_gam], writes=[b_kf])
                        for h in range(4):
                            hs = slice(h * 128, (h + 1) * 128)
                            P.mm(ps[1][:, hs], kf[:, hs], vb[:, hs], True, True, reads=[b_kf, b_vb], writes=[bps[1]], inc=(h == 3))
                        P.tt("pool", stf[:].rearrange("p (h e) -> p h e", h=4), stf[:].rearrange("p (h e) -> p h e", h=4),
                             cdec[:].unsqueeze(2).to_broadcast([128, 4, 128]), ALU.mult, reads=[b_stf, b_cdec], writes=[b_stf])

                    def RB2():
                        P.tt("dve", stf[:], stf[:], ps[1][:], ALU.add, reads=[b_stf, bps[1]], writes=[b_stf])
                        P.act(stbf[:], stf[:], AF.Copy, reads=[b_stf], writes=[b_stbf])
                        o3 = opt[:].rearrange("p (h e) -> p h e", h=4)
                        P.T.op("dve", lambda q, o3=o3: q.reduce_sum(out=hn[:, 0:4], in_=o3, axis=mybir.AxisListType.X),
                               reads=[b_opt], writes=[b_hn])
                        P.tt("pool", tA[:], opt[:], opt[:], ALU.mult, reads=[b_opt], writes=[b_tA])
                        P.T.op("dve", lambda q, tA=tA: q.reduce_sum(out=hn[:, 4:8], in_=tA[:].rearrange("p (h e) -> p h e", h=4),
                                                                  axis=mybir.AxisListType.X), reads=[b_tA], writes=[b_hn])
                        P.ts("dve", hn[:, 0:8], hn[:, 0:8], 1.0 / 128.0, None, ALU.mult, None, reads=[b_hn], writes=[b_hn])
                        P.tt("dve", hn[:, 8:12], hn[:, 0:4], hn[:, 0:4], ALU.mult, reads=[b_hn], writes=[b_hn])
                        P.tt("dve", hn[:, 4:8], hn[:, 4:8], hn[:, 8:12], ALU.subtract, reads=[b_hn], writes=[b_hn])
                        P.act(hn[:, 8:12], hn[:, 4:8], AF.Sqrt, reads=[b_hn, b_eps], writes=[b_hn], bias=epsT[:, 0:1])

                    def RB3():
                        o3 = opt[:].rearrange("p (h e) -> p h e", h=4)
                        P.T.op("dve", lambda q: q.reciprocal(out=hn[:, 12:16], in_=hn[:, 8:12]), reads=[b_hn], writes=[b_hn])
                        a3 = tA[:].rearrange("p (h e) -> p h e", h=4)
                        P.tt("pool", a3, o3, hn[:, 0:4].unsqueeze(2).to_broadcast([128, 4, 128]), ALU.subtract,
                             reads=[b_opt, b_hn], writes=[b_tA])
                        P.tt("pool", a3, a3, hn[:, 12:16].unsqueeze(2).to_broadcast([128, 4, 128]), ALU.mult,
                             reads=[b_tA, b_hn], writes=[b_tA])
                        P.tt("pool", oab[:, 0:512], tA[:], sga[:], ALU.mult, reads=[b_tA, b_sga], writes=[b_oab])

                    hooks = [RB1, RB2, RB3] + tail
                    s5_chunk(w, ci, True, hooks=hooks)
                    P.tt("dve", ypt[:], ypt[:], ps[7][:], ALU.add, reads=[b_ypt, bps[7]], writes=[b_ypt])
                    tail = make_tail(ci, n)
                for t_ in tail:
                    t_()

            s5_setup(0)
            with contextlib.ExitStack() as wk:
                w = alloc_work(wk, False)
                for s in range(nseq):
                    passA(s, w)
                T_barrier()
            s5_setup(1)
            with contextlib.ExitStack() as wk:
                w = alloc_work(wk, True)
                for s in range(nseq):
                    passB(s, w)
                T_barrier()

    def T_barrier():
        evs = []
        for e in T.engs.values():
            for k in range(len(e.dsems)):
                evs.append((e.dsems[k], e.dvals[k]))
            if e.sem is not None and e.cnt > 0 and not e.pending:
                evs.append((e.sem, e.cnt))
        for e in T.engs.values():
            for ev in evs:
                T._wait(e, ev)

    def odd_layer(li, src, bsrc, dst, bdst):
        raise NotImplementedError

    P.odd_layer_hook = None
    cur, bcur = x_in, [b_xin] * nseq
    for idx, li in enumerate(layers):
        last = idx == len(layers) - 1
        dstt, bd = (y_out, b_yout) if last else (xs[idx % 2], b_xs[idx % 2])
        if li % 2 == 0:
            even_layer(li, cur, bcur, dstt, bd)
        else:
            ODD_IMPL(P, locals(), li, cur, bcur, dstt, bd)
        cur, bcur = dstt, bd
    T.finish()
    T.replay()
    P.es.close()
    return P


def ODD_IMPL(P, env, li, src, bsrc, dst, bdst):
    nc, T = P.nc, P.T
    P.pool_to_dve = False
    P.pne = 'mix'
    P.stq = os.environ.get('STQO', 'sp')
    E = env
    ps, bps = E["ps"], E["bps"]
    nseq, L = P.nseq, P.L
    ident_bf, b_ident = E["ident_bf"], E["b_ident"]
    j = li // 2
    rows = L // 64
    nblk = L // 256

    def rs(r):
        return min(max(r - 4, 0), rows - 8)

    with contextlib.ExitStack() as st:
        def S(name, shape, dt=F32):
            return st.enter_context(P.sbt(f"{name}_o{li}", list(shape), dt)), Buf(name)
        E["adaln"](li, None)
        C = E["make_common"](st, f"o{li}")
        winc, b_winc = S("winc", [128, 8, 4096], BF16)
        woutc, b_woutc = S("woutc", [128, 8, 1024], BF16)
        Z, b_Z = S("Z", [128, 16, 1024], BF16)
        hT, b_hT = S("hT", [128, 8, 256], BF16)
        KT = [S(f"KT{i}", [128, 8, 256], BF16) for i in range(3)]
        V = [S(f"V{i}", [128, 2, 8, 3, 64], BF16) for i in range(3)]
        QT = [S(f"QT{i}", [128, 8, 256], BF16) for i in range(2)]
        GT = [S(f"GT{i}", [128, 8, 256], BF16) for i in range(2)]
        pT = [S(f"pT{i}", [128, 256], BF16) for i in range(3)]
        og, b_og = S("og", [128, 8, 256], BF16)
        rd2 = [S(f"rd{i}", [128, 256]) for i in range(2)]
        rb2 = [S(f"rb{i}", [128, 256]) for i in range(2)]
        t22 = [S(f"t2{i}", [128, 256]) for i in range(2)]
        selb, b_selb = S("selb", [128, 64], BF16)
        RBh = [S(f"RBh{i}", [128, 256], BF16) for i in range(2)]
        zer, b_zer = S("zer", [128, 256], BF16)
        wsrc = E["w_in_c"][j].rearrange("(k p) n -> p k n", p=128)
        for k in range(8):
            P.dma("pool", winc[:, k, :], wsrc[:, k, :], writes=[b_winc])
        P.dma("pool", woutc[:], E["w_out_c"][j].rearrange("(k p) n -> p k n", p=128), writes=[b_woutc])
        P.dma("pool", selb[:], E["sel_d"], writes=[b_selb])
        for i in range(2):
            P.memset("dve", RBh[i][0][:], 0.0, writes=[RBh[i][1]])
        P.memset("dve", zer[:], 0.0, writes=[b_zer])
        for i in range(3):
            P.memset("pool", V[i][0][:], 1.0, writes=[V[i][1]])
        with contextlib.ExitStack() as s2:
            zm = s2.enter_context(P.sbt(f"zm_o{li}", [128, 1024], F32)); b_zm = Buf("zm")
            zt0_ = s2.enter_context(P.sbt(f"zt0_o{li}", [128, 512], F32))
            zt = [zt0_, zt0_]
            b_zt0_ = Buf("zt0")
            b_zt = [b_zt0_, b_zt0_]
            P.dma("sp", zm[:], E["zmask"], writes=[b_zm])
            for h in range(16):
                for hf in range(2):
                    P.dma("sp", zt[hf][:], E["zg"][j, h][:, hf * 512:(hf + 1) * 512], writes=[b_zt[hf]])
                    P.tt("dve", Z[:, h, hf * 512:(hf + 1) * 512], zt[hf][:], zm[:, hf * 512:(hf + 1) * 512], ALU.add,
                         reads=[b_zt[hf], b_zm], writes=[b_Z])
            E["T_barrier"]()

        def proj(s, b, do_prenorm=True):
            ring = b % 3
            sl = b % 2
            if do_prenorm:
                for t in range(2):
                    E["prenorm"](C, s, src, bsrc[s], 2 * b + t, hT, b_hT, t * 128)
            cnt = 0
            for (col0, kind) in ((0, "q"), (1024, "k"), (3072, "g")):
                for hp2 in range(4):
                    bank = 2 + cnt % 2
                    cnt += 1
                    for hh in range(2):
                        hp = 2 * hp2 + hh
                        for k in range(8):
                            P.mm(ps[bank][:, hh * 256:(hh + 1) * 256], winc[:, k, col0 + hp * 128:col0 + (hp + 1) * 128], hT[:, k, :],
                                 k == 0, k == 7, reads=[b_winc, b_hT], writes=[bps[bank]], inc=(k == 7 and hh == 1))
                    if kind == "q":
                        P.act(QT[sl][0][:, 2 * hp2:2 * hp2 + 2, :].rearrange("p a b -> p (a b)"), ps[bank][:], AF.Copy,
                              reads=[bps[bank]], writes=[QT[sl][1]], scale=0.125)
                    elif kind == "k":
                        P.cp("dve", KT[ring][0][:, 2 * hp2:2 * hp2 + 2, :].rearrange("p a b -> p (a b)"), ps[bank][:],
                             reads=[bps[bank]], writes=[KT[ring][1]])
                    else:
                        P.act(GT[sl][0][:, 2 * hp2:2 * hp2 + 2, :].rearrange("p a b -> p (a b)"), ps[bank][:], AF.Silu,
                              reads=[bps[bank]], writes=[GT[sl][1]])
            for t in range(2):
                for half in range(2):
                    bank = 2 + cnt % 2
                    cnt += 1
                    for k in range(8):
                        P.mm(ps[bank][:], hT[:, k, t * 128:(t + 1) * 128], winc[:, k, 2048 + half * 512:2048 + (half + 1) * 512],
                             k == 0, k == 7, reads=[b_hT, b_winc], writes=[bps[bank]], inc=(k == 7))
                    src4 = ps[bank][:].rearrange("p (a c d) -> p a c d", a=4, c=2)
                    P.cp("dve", V[ring][0][:, t, 4 * half:4 * half + 4, 0:3:2, :], src4, reads=[bps[bank]], writes=[V[ring][1]])

        def attn(s, b, hooks=()):
            hooks = list(hooks)
            R = 4 * b
            sl = b % 2
            lo = rs(R) & ~1
            hi = (rs(R + 3) + 7) & ~1
            tiles = []
            for r0 in range(lo, hi + 1, 2):
                qs = [r for r in range(R, R + 4) if (rs(r) <= r0 + 1 and r0 <= rs(r) + 7)]
                if not qs:
                    continue
                qa, qb = qs[0], qs[-1]
                partial = []
                for r in qs:
                    for a in range(2):
                        if not (rs(r) <= r0 + a <= rs(r) + 7):
                            partial.append((r, a))
                tiles.append((r0, qa, qb, partial))
            tiles.sort(key=lambda t: (0 if (t[1] == R and t[2] == R + 3 and not t[3]) else 1))
            assert tiles[0][1] == R and tiles[0][2] == R + 3 and not tiles[0][3]
            pcount = [0]
            W = [(h, ti) for h in range(16) for ti in range(len(tiles))]
            info = {}
            deferred = []

            def emit_scores(h, ti):
                hp, base = h // 2, 64 * (h % 2)
                r0, qa, qb, partial = tiles[ti]
                kb = r0 // 4
                tt_ = (r0 % 4) // 2
                kring = kb % 3
                c0, c1 = (qa - R) * 64, (qb - R + 1) * 64
                z0, z1 = (qa - r0 + 7) * 64, (qb - r0 + 8) * 64
                bank = 4 + pcount[0] % 2
                pt, b_pt = pT[pcount[0] % 3]
                pcount[0] += 1
                P.mm(ps[bank][:, c0:c1], KT[kring][0][base:base + 64, hp, tt_ * 128:(tt_ + 1) * 128],
                     QT[sl][0][base:base + 64, hp, c0:c1], True, False,
                     reads=[KT[kring][1], QT[sl][1]], writes=[bps[bank]], inc=False)
                P.mm(ps[bank][:, c0:c1], ident_bf[:], Z[:, h, z0:z1], False, True,
                     reads=[b_ident, b_Z], writes=[bps[bank]], inc=True)
                P.act(pt[:, c0:c1], ps[bank][:, c0:c1], AF.Exp, reads=[bps[bank]], writes=[b_pt])
                for (r, a_) in partial:
                    cc = (r - R) * 64
                    P.memset("pool", pt[64 * a_:64 * a_ + 64, cc:cc + 64], 0.0, writes=[b_pt])
                info[(h, ti)] = (pt, b_pt, c0, c1, kring, tt_)

            def emit_pv(h, ti, idx):
                hp = h // 2
                pt, b_pt, c0, c1, kring, tt_ = info.pop((h, ti))
                ob = 6 + h % 2
                va = V[kring][0][:, tt_, hp, 0:2, :] if h % 2 == 0 else V[kring][0][:, tt_, hp, 1:3, :]
                last = ti == len(tiles) - 1
                P.mm(ps[ob][:, c0:c1], va.rearrange("p a b -> p (a b)"), pt[:, c0:c1], ti == 0, last,
                     reads=[V[kring][1], b_pt], writes=[bps[ob]], inc=last)
                if last:
                    par = h % 2
                    dr, orow = (64, 0) if par == 0 else (0, 64)
                    rdp, b_rdp = rd2[par]
                    rbp, b_rbp = rb2[par]
                    t2p, b_t2p = t22[par]
                    rbh, b_rbh = RBh[par]
                    P.T.op("dve", lambda q, dr=dr, ob=ob, rdp=rdp: q.reciprocal(out=rdp[dr:dr + 33, :], in_=ps[ob][dr:dr + 33, 0:256]),
                           reads=[bps[ob]], writes=[b_rdp])
                    P.cp("dve", rbh[dr:dr + 33, :], rdp[dr:dr + 33, :], reads=[b_rdp], writes=[b_rbh])
                    P.tt("dve", rbh[dr:dr + 1, :], rdp[dr:dr + 1, :], rbh[dr:dr + 1, :], ALU.subtract, reads=[b_rdp, b_rbh], writes=[b_rbh])

                    def part2(h=h, hp=hp, par=par, dr=dr, orow=orow, ob=ob, rdp=rdp, b_rdp=b_rdp, rbp=rbp, b_rbp=b_rbp, t2p=t2p, b_t2p=b_t2p,
                              rbh=rbh, b_rbh=b_rbh):
                        P.mm(ps[2 + par][orow:orow + 64, 0:256], selb[dr:dr + 33, 0:64], rbh[dr:dr + 33, :], True, True,
                             reads=[b_selb, b_rbh], writes=[bps[2 + par]])
                        P.act(rbp[orow:orow + 64, :], ps[2 + par][orow:orow + 64, 0:256], AF.Copy, reads=[bps[2 + par]], writes=[b_rbp])
                        P.tt("dve", t2p[orow:orow + 64, :], ps[ob][orow:orow + 64, 0:256], rbp[orow:orow + 64, :], ALU.mult,
                             reads=[bps[ob], b_rbp], writes=[b_t2p])
                        P.tt("dve", og[orow:orow + 64, hp, :], t2p[orow:orow + 64, :], GT[sl][0][orow:orow + 64, hp, :], ALU.mult,
                             reads=[b_t2p, GT[sl][1]], writes=[b_og])
                    deferred.append((idx + min(4, len(tiles) - 1), part2))
                    if h % 2 == 1 and hooks:
                        deferred.append((idx + 1, hooks.pop(0)))
                        deferred.sort(key=lambda d: d[0])

            LA = 1
            for idx in range(len(W) + LA):
                if idx < len(W):
                    emit_scores(*W[idx])
                if idx >= LA:
                    emit_pv(W[idx - LA][0], W[idx - LA][1], idx)
                while deferred and deferred[0][0] <= idx:
                    deferred.pop(0)[1]()
            while deferred:
                deferred.pop(0)[1]()
            for t in range(2):
                n = 2 * b + t
                sx = C["cnt"] % 2
                C["cnt"] += 1
                xt, bxt = C["xt"][sx], C["bxt"][sx]
                P.dma("sp", xt[:], src[s, n * 128:(n + 1) * 128, :], reads=[bsrc[s]], writes=[bxt])
                for hh in range(2):
                    for k in range(8):
                        P.mm(ps[2 + hh][:], og[:, k, t * 128:(t + 1) * 128], woutc[:, k, hh * 512:(hh + 1) * 512], k == 0, k == 7,
                             reads=[b_og, b_woutc], writes=[bps[2 + hh]], inc=(k == 7))
                E["post"](C, s, (2, 3), xt, bxt, dst, bdst[s], n)

        for s in range(nseq):
            proj(s, 0)
            if nblk > 1:
                proj(s, 1)
            for b in range(nblk):
                if b >= 1 and b + 1 < nblk:
                    proj(s, b + 1, do_prenorm=False)
                hk = []
                if b + 2 < nblk:
                    for t in range(2):
                        hk += list(E["prenorm_parts"](C, s, src, bsrc[s], 2 * (b + 2) + t, hT, b_hT, t * 128))
                attn(s, b, hooks=hk)
        E["T_barrier"]()


def _common_inputs(p, L):
    f32 = np.float32
    m = {}
    def T8(a):
        return np.ascontiguousarray(a.reshape(a.shape[0], -1, 128).transpose(0, 2, 1)).astype(f32)
    m["npreT"] = T8(p["norm_pre"])
    m["npostT"] = T8(p["norm_post"])
    m["wmod"] = np.ascontiguousarray(p["w_mod"], dtype=f32)
    m["bmodT"] = T8(p["b_mod"])
    m["w_in_ab"] = np.ascontiguousarray(p["w_in_ab"], dtype=f32)
    m["w_out_ab"] = np.ascontiguousarray(p["w_out_ab"], dtype=f32)
    m["w_glu"] = np.ascontiguousarray(p["ssm_w_glu"], dtype=f32)
    m["ssm_d"] = np.ascontiguousarray(p["ssm_d"], dtype=f32)
    sm3, rep3, bl, cl = [], [], [], []
    for j in range(2):
        a, b, c, d = _s5_layout(p["ssm_a_re"][j], p["ssm_a_im"][j], p["ssm_log_step"][j], p["ssm_b_re"][j],
                                p["ssm_b_im"][j], p["ssm_c_re"][j], p["ssm_c_im"][j])
        sm3.append(a); rep3.append(b); bl.append(c); cl.append(d)
    m["s5_sm3"] = np.stack(sm3).astype(f32)
    m["s5_rep3"] = np.stack(rep3).astype(f32)
    m["s5_bl"] = np.stack(bl).astype(f32)
    m["s5_cl"] = np.stack(cl).astype(f32)
    m["w_in_c"] = np.ascontiguousarray(p["w_in_c"], dtype=f32)
    m["w_out_c"] = np.ascontiguousarray(p["w_out_c"], dtype=f32)
    zs = []
    for j in range(2):
        z, mask = _na_layout(np.asarray(p["na_rel_bias"][j], dtype=f32))
        zs.append(z)
    m["zg"] = np.stack(zs).astype(f32)
    m["zmask"] = mask
    m["ident"] = np.eye(128, dtype=f32)
    m["jmat"] = np.eye(128, dtype=f32)[::-1].copy()
    m["rot"] = _rot_tables(L)
    dt, gam, cd = _ret_consts()
    m["dtab"] = dt
    m["gam"] = np.ascontiguousarray(gam.transpose(0, 1, 2))
    m["cdec"] = cd
    m["jt"] = np.broadcast_to(np.arange(128, dtype=f32)[None, :], (128, 128)).copy()
    sel = np.zeros((128, 64), f32)
    sel[[0, 32, 64, 96], :] = 1.0
    m["sel"] = sel
    return m


_PROG_CACHE = {}


def run_cores(xs_per_core, cs_per_core, params, layers):
    nseq, L, _ = xs_per_core[0].shape
    key = (nseq, L, tuple(layers))
    if key not in _PROG_CACHE:
        _PROG_CACHE[key] = build(nseq, L, list(layers))
    P = _PROG_CACHE[key]
    com = _common_inputs(params, L)
    in_maps = []
    for x, c in zip(xs_per_core, cs_per_core):
        m = dict(com)
        m["x_in"] = np.ascontiguousarray(x, dtype=np.float32)
        m["cT"] = np.ascontiguousarray(c.reshape(nseq, 8, 128).transpose(2, 1, 0), dtype=np.float32)
        in_maps.append(m)
    res = run_bass_kernel_spmd(P.nc, in_maps, core_ids=list(range(len(in_maps))))
    return [np.asarray(r["y_out"]) for r in res.results]


def kernel(**inputs):
    p = {k: np.asarray(v) for k, v in inputs.items()}
    xp, xsamp = p["x_prompt"], p["x_sample"]
    cp, cs = p["c_prompt"], p["c_sample"]
    seqs = [xp[i] for i in range(4)] + [xsamp[i] for i in range(8)]
    cvs = [cp[i] for i in range(4)] + [cs[i] for i in range(8)]
    slots = [(c, 8 + c if c < 4 else c) for c in range(8)]
    xs_pc = [np.stack([seqs[a], seqs[b]]) for a, b in slots]
    cs_pc = [np.stack([cvs[a], cvs[b]]) for a, b in slots]
    outs = run_cores(xs_pc, cs_pc, p, [0, 1, 2, 3])
    res = [None] * 12
    for c, (a, b) in enumerate(slots):
        res[a] = outs[c][0]
        if c < 4:
            res[b] = outs[c][1]
    y_prompt = np.stack(res[0:4]).astype(np.float32)
    y_sample = np.stack(res[4:12]).astype(np.float32)
    return (y_prompt, y_sample)
```

```python
import contextlib
import math
import os
KSTOP = os.environ.get('KSTOP', 'all')
S5E = os.environ.get('S5E', 'dve')
STQ = os.environ.get('STQ', 'sp')
PNE = os.environ.get('PNE', 'act')
import numpy as np
import concourse.bass as bass
import concourse.mybir as mybir
from concourse.bass_utils import run_bass_kernel_spmd

F32 = mybir.dt.float32
BF16 = mybir.dt.bfloat16
I32 = mybir.dt.int32
ALU = mybir.AluOpType
AF = mybir.ActivationFunctionType

D = 1024
EPS = 1e-6
TWO_PI = 2.0 * math.pi


class Buf:
    __slots__ = ("name", "w", "r")

    def __init__(self, name):
        self.name = name
        self.w = []
        self.r = []


class Eng:
    def __init__(self, name):
        self.name = name
        self.ops = []
        self.known = {}
        self.sem = None
        self.cnt = 0
        self.pending = False
        self.dsems = []
        self.dvals = []
        self.dptr = 0
        self.own = set()


EPOCH = 30000
NDSEM = 10


class Tracker:
    def __init__(self, nc):
        self.nc = nc
        self.engs = {n: Eng(n) for n in ("pe", "act", "dve", "pool", "sp")}
        self.sems = []
        for e in self.engs.values():
            if e.name != "sp":
                e.sem = self._newsem(e.name)
                e.own.add(e.sem)
        self.n_ops = 0

    def _newsem(self, nm):
        h = self.nc.alloc_semaphore(name=f"{nm}_{len(self.sems)}")
        self.sems.append(h)
        return len(self.sems) - 1

    def _wait(self, e, ev):
        s, v = ev
        if e.known.get(s, 0) >= v:
            return
        if s in e.own:
            if e.name == "pe":
                return
            if s == e.sem and v > e.cnt:
                return
        e.known[s] = v
        sem = self.sems[s]
        e.ops.append(lambda q, sem=sem, v=v: q.wait_ge(sem, v))

    def _deps(self, e, reads, writes):
        for b in reads:
            for ev in b.w:
                self._wait(e, ev)
        for b in writes:
            for ev in b.w:
                self._wait(e, ev)
            for ev in b.r:
                self._wait(e, ev)

    def _commit(self, ev, reads, writes):
        for b in reads:
            for i, (s0, v0) in enumerate(b.r):
                if s0 == ev[0]:
                    b.r[i] = (s0, max(v0, ev[1]))
                    break
            else:
                b.r.append(ev)
        for b in writes:
            b.w = [ev]
            b.r = []

    def op(self, eng, fn, reads=(), writes=(), inc=True):
        e = self.engs[eng]
        self.n_ops += 1
        self._deps(e, reads, writes)
        if e.cnt >= EPOCH and inc and not e.pending:
            e.sem = self._newsem(e.name)
            e.cnt = 0
            e.own.add(e.sem)
        if inc:
            e.cnt += 1
            sem = self.sems[e.sem]
            e.ops.append(lambda q, fn=fn, sem=sem: fn(q).then_inc(sem, 1))
            ev = (e.sem, e.cnt)
            e.pending = False
        else:
            e.ops.append(lambda q, fn=fn: fn(q))
            ev = (e.sem, e.cnt + 1)
            e.pending = True
        self._commit(ev, reads, writes)

    def dma(self, eng, out, in_, reads=(), writes=()):
        e = self.engs[eng]
        self.n_ops += 1
        self._deps(e, reads, writes)
        if len(e.dsems) < NDSEM:
            e.dsems.append(self._newsem(e.name + "d"))
            e.dvals.append(0)
            k = len(e.dsems) - 1
        else:
            k = e.dptr
            e.dptr = (e.dptr + 1) % NDSEM
            self._wait(e, (e.dsems[k], e.dvals[k]))
            if e.dvals[k] >= EPOCH * 16:
                e.dsems[k] = self._newsem(e.name + "d")
                e.dvals[k] = 0
        e.dvals[k] += 16
        sem = self.sems[e.dsems[k]]
        e.ops.append(lambda q, out=out, in_=in_, sem=sem: q.dma_start(out=out, in_=in_).then_inc(sem, 16))
        ev = (e.dsems[k], e.dvals[k])
        self._commit(ev, reads, writes)
        return ev

    def finish(self):
        sp = self.engs["sp"]
        for e in self.engs.values():
            for k in range(len(e.dsems)):
                self._wait(sp, (e.dsems[k], e.dvals[k]))
            if e.sem is not None and e.cnt > 0:
                self._wait(sp, (e.sem, e.cnt))

    def replay(self):
        nc = self.nc
        E = self.engs
        with nc.Block() as block:
            @block.tensor
            def _(q):
                for f in E["pe"].ops:
                    f(q)

            @block.scalar
            def _(q):
                for f in E["act"].ops:
                    f(q)

            @block.vector
            def _(q):
                for f in E["dve"].ops:
                    f(q)

            @block.gpsimd
            def _(q):
                for f in E["pool"].ops:
                    f(q)

            @block.sync
            def _(q):
                for f in E["sp"].ops:
                    f(q)


RET_H = 4
NA_H = 16
GW = 64


def _ret_consts():
    f32 = np.float32
    h = np.arange(RET_H, dtype=f32)
    log_g = np.log1p(-np.exp2(-5.0 - h)).astype(f32)
    pos = np.arange(128, dtype=f32)
    dt = np.exp(np.abs(pos[:, None] - pos[None, :])[:, None, :] * log_g[None, :, None]).astype(f32)
    gam = np.zeros((128, 4, RET_H), f32)
    gam[:, 0, :] = np.exp(pos[:, None] * log_g[None])
    gam[:, 1, :] = np.exp((127.0 - pos)[:, None] * log_g[None])
    gam[:, 2, :] = np.exp((128.0 - pos)[:, None] * log_g[None])
    gam[:, 3, :] = np.exp((pos + 1.0)[:, None] * log_g[None])
    cdec = np.exp(128.0 * log_g).astype(f32)
    cd = np.broadcast_to(cdec[None, :], (128, RET_H)).copy()
    return dt, gam, cd


def _rot_tables(L):
    f32 = np.float32
    inv = (10000.0 ** (-np.arange(0, 128, 2, dtype=f32) / 128.0)).astype(f32)
    ang = (np.arange(L, dtype=f32)[:, None] * inv[None, :]).astype(f32)
    c = np.cos(ang).astype(f32)
    s = np.sin(ang).astype(f32)
    rq = np.zeros((L, 2, 128), f32)
    rq[:, 0, :64] = c
    rq[:, 0, 64:] = c
    rq[:, 1, :64] = -s
    rq[:, 1, 64:] = s
    rk = (rq * f32(128.0 ** -0.5)).astype(f32)
    return np.stack([rq, rk], axis=1).copy()


def _na_layout(relb):
    a = np.arange(2)[:, None, None, None]
    k = np.arange(64)[None, :, None, None]
    m = np.arange(-7, 9)[None, None, :, None]
    c = np.arange(64)[None, None, None, :]
    dr = np.clip(a - m + 7, 0, 14)
    dc = np.clip(k - c + 15, 0, 30)
    dr_b, dc_b = np.broadcast_arrays(dr, dc)
    z = relb[:, dr_b, dc_b]
    z = z.reshape(16, 128, 16 * 64).astype(np.float32)
    cs = np.clip(np.arange(64) - 8, 0, 48)
    kk = np.arange(64)[:, None]
    valid = (kk >= cs[None, :]) & (kk < cs[None, :] + 16)
    mask = np.where(valid, 0.0, -30000.0).astype(np.float32)
    mask = np.broadcast_to(mask[None, :, None, :], (2, 64, 16, 64)).reshape(128, 1024).copy()
    return z, mask


def _s5_layout(a_re, a_im, ls, b_re, b_im, c_re, c_im):
    f32 = np.float32
    def sm(a):
        return a.reshape(2, 16, 2, 64).transpose(0, 2, 3, 1).reshape(2, 128, 16).astype(f32)
    lsx = np.broadcast_to(ls[:, :, None], (2, 32, 64))
    sm3 = np.stack([sm(a_re), sm(a_im), sm(lsx)], axis=1).copy()
    rep3 = np.stack([a_re.reshape(2, 2048), a_im.reshape(2, 2048), lsx.reshape(2, 2048)], axis=1).astype(f32).copy()
    bl = np.zeros((2, 2, 128, 16, 2, 64), f32)
    cl = np.zeros((2, 2, 128, 16, 2, 16), f32)
    for s in range(16):
        for g1 in range(2):
            g = 2 * s + g1
            r0 = 32 * (s % 4) + 16 * g1
            for ri, b in enumerate((b_re, b_im)):
                bl[:, ri, r0:r0 + 16, s, g1, :] = b[:, g].transpose(0, 2, 1)
            for ri, c in enumerate((c_re, c_im)):
                cl[:, ri, 64 * g1:64 * g1 + 64, s, g1, :] = c[:, g].transpose(0, 2, 1)
    return sm3, rep3, bl.reshape(2, 2, 128, 16 * 128), cl.reshape(2, 2, 128, 16 * 32)


class Prog:
    def __init__(self, nseq, L, layers):
        self.nseq, self.L, self.layers = nseq, L, layers
        self.nch = L // 128
        nc = self.nc = bass.Bass("TRN2", target_bir_lowering=False)
        self.T = Tracker(nc)
        self.es = contextlib.ExitStack()
        self.dram = {}
        self.bufs = {}

    def sbt(self, name, shape, dt=F32):
        self._uid = getattr(self, '_uid', 0) + 1
        return self.nc.sbuf_tensor(f"{name}_u{self._uid}", list(shape), dt)

    def din(self, name, shape, dt=F32):
        t = self.nc.dram_tensor(name, list(shape), dt, kind="ExternalInput").ap()
        self.dram[name] = t
        return t

    def dout(self, name, shape, dt=F32):
        t = self.nc.dram_tensor(name, list(shape), dt, kind="ExternalOutput").ap()
        self.dram[name] = t
        return t

    def dscr(self, name, shape, dt=F32):
        t = self.nc.dram_tensor(name, list(shape), dt, kind="Internal").ap()
        self.dram[name] = t
        return t

    def sb(self, name, shape, dt=F32):
        t = self.es.enter_context(self.sbt(name, list(shape), dt))
        b = Buf(name)
        return t, b

    def ps(self, name):
        t = self.es.enter_context(self.nc.psum_tensor(name, [128, 512], F32))
        return t, Buf(name)

    def mm(self, out, lhsT, rhs, start, stop, reads, writes, inc=True):
        self.T.op("pe", lambda q: q.matmul(out, lhsT=lhsT, rhs=rhs, start=start, stop=stop),
                  reads=reads, writes=writes, inc=inc)

    def act(self, out, in_, func, reads, writes, scale=1.0, bias=None, accum=None):
        kw = {}
        if bias is not None:
            kw["bias"] = bias
        if accum is not None:
            kw["accum_out"] = accum
        self.T.op("act", lambda q: q.activation(out=out, in_=in_, func=func, scale=scale, **kw),
                  reads=reads, writes=writes)

    def tt(self, eng, out, in0, in1, op, reads, writes):
        if eng == "pool" and getattr(self, "pool_to_dve", False):
            eng = "dve"
        self.T.op(eng, lambda q: q.tensor_tensor(out=out, in0=in0, in1=in1, op=op), reads=reads, writes=writes)

    def ts(self, eng, out, in0, s1, s2, op0, op1, reads, writes):
        if s2 is None:
            self.T.op(eng, lambda q: q.tensor_scalar(out=out, in0=in0, scalar1=s1, scalar2=None, op0=op0),
                      reads=reads, writes=writes)
        else:
            self.T.op(eng, lambda q: q.tensor_scalar(out=out, in0=in0, scalar1=s1, scalar2=s2, op0=op0, op1=op1),
                      reads=reads, writes=writes)

    def stt(self, eng, out, in0, scalar, in1, op0, op1, reads, writes):
        self.T.op(eng, lambda q: q.scalar_tensor_tensor(out=out, in0=in0, scalar=scalar, in1=in1, op0=op0, op1=op1),
                  reads=reads, writes=writes)

    def cp(self, eng, out, in_, reads, writes):
        self.T.op(eng, lambda q: q.tensor_copy(out=out, in_=in_), reads=reads, writes=writes)

    def memset(self, eng, ap, val, writes):
        self.T.op(eng, lambda q: q.memset(ap, val), reads=(), writes=writes)

    def dma(self, eng, out, in_, reads=(), writes=()):
        return self.T.dma(eng, out, in_, reads=reads, writes=writes)


def build(nseq, L, layers):
    P = Prog(nseq, L, layers)
    nc, T = P.nc, P.T
    nch = L // 128
    NL = 4
    x_in = P.din("x_in", [nseq, L, D])
    y_out = P.dout("y_out", [nseq, L, D])
    cT = P.din("cT", [128, 8, nseq])
    npreT = P.din("npreT", [NL, 128, 8])
    npostT = P.din("npostT", [NL, 128, 8])
    wmod = P.din("wmod", [NL, D, 3 * D])
    bmodT = P.din("bmodT", [NL, 128, 24])
    w_in_ab = P.din("w_in_ab", [2, D, 3072])
    w_out_ab = P.din("w_out_ab", [2, D, D])
    w_glu = P.din("w_glu", [2, 512, 512])
    ssm_d = P.din("ssm_d", [2, 512])
    s5_sm3 = P.din("s5_sm3", [2, 2, 3, 128, 16])
    s5_rep3 = P.din("s5_rep3", [2, 2, 3, 2048])
    s5_bl = P.din("s5_bl", [2, 2, 2, 128, 2048])
    s5_cl = P.din("s5_cl", [2, 2, 2, 128, 512])
    w_in_c = P.din("w_in_c", [2, D, 4096])
    w_out_c = P.din("w_out_c", [2, D, D])
    zg = P.din("zg", [2, 16, 128, 1024])
    zmask = P.din("zmask", [128, 1024])
    ident_d = P.din("ident", [128, 128])
    jmat_d = P.din("jmat", [128, 128])
    rot_d = P.din("rot", [L, 2, 2, 128])
    dtab_d = P.din("dtab", [128, 4, 128])
    gam_d = P.din("gam", [128, 4, 4])
    cdec_d = P.din("cdec", [128, 4])
    jt_d = P.din("jt", [128, 128])
    sel_d = P.din("sel", [128, 64])
    xs = [P.dscr("xsA", [nseq, L, D]), P.dscr("xsB", [nseq, L, D])]
    OPs_ = P.dscr("OPs", [nseq, L, 512])
    YPs_ = P.dscr("YPs", [nseq, L, 512])
    SGs_ = P.dscr("SGs", [nseq, L, 1024])
    QTs_ = P.dscr("QTs", [nseq, nch, 128, 512], BF16)
    KRs_ = P.dscr("KRs", [nseq, L, 512], BF16)
    VBs_ = P.dscr("VBs", [nseq, L, 512], BF16)
    UBs_ = P.dscr("UBs", [nseq, L, 512], BF16)
    SCRB = [[Buf(f"{n}{i}") for n in "OP YP SG QT KR VB UB".split()] for i in range(nseq)]
    b_xs = [[Buf(f"xs{i}_{s}") for s in range(nseq)] for i in range(2)]
    b_yout = [Buf(f"yout{s}") for s in range(nseq)]
    b_xin = Buf("xin")

    ident_bf, b_ident = P.sb("ident_bf", [128, 128], BF16)
    jmat_bf, b_jmat = P.sb("jmat_bf", [128, 128], BF16)
    identf, b_identf = P.sb("identf", [128, 128])
    epsT, b_eps = P.sb("epsT", [128, 1])
    ss, b_ss = P.sb("ss", [128, 4])
    sd, b_sd = P.sb("sd", [128, 4])
    rstd, b_rstd = P.sb("rstd", [128, 4])
    scT, b_scT = P.sb("scT", [128, 8, nseq])
    cTs, b_cTs = P.sb("cTs", [128, 8, nseq])
    modT, b_modT = P.sb("modT", [128, 24, nseq])
    gsT, b_gsT = P.sb("gsT", [128, 8, nseq])
    ggT, b_ggT = P.sb("ggT", [128, 8, nseq])
    ggrow = [P.sb(f"ggrow{s}", [128, 1024]) for s in range(nseq)]
    vecs, b_vecs = P.sb("vecs", [128, 40])
    psb = [P.ps(f"ps{i}") for i in range(8)]
    ps = [p[0] for p in psb]
    bps = [p[1] for p in psb]

    P.dma("sp", identf[:], ident_d, writes=[b_identf])
    P.dma("pool", ident_bf[:], ident_d, writes=[b_ident])
    P.dma("pool", jmat_bf[:], jmat_d, writes=[b_jmat])
    P.memset("pool", epsT[:], EPS, writes=[b_eps])
    P.dma("sp", cTs[:], cT, writes=[b_cTs])
    P.act(scT[:], cTs[:], AF.Sigmoid, reads=[b_cTs], writes=[b_scT])
    P.tt("dve", scT[:], scT[:], cTs[:], ALU.mult, reads=[b_scT, b_cTs], writes=[b_scT])

    def rstd_from_ss(col, n_feat):
        P.act(sd[:, col:col + 1], ss[:, col:col + 1], AF.Sqrt, reads=[b_ss, b_eps], writes=[b_sd],
              scale=1.0 / n_feat, bias=epsT[:, 0:1])
        P.T.op("dve", lambda q: q.reciprocal(out=rstd[:, col:col + 1], in_=sd[:, col:col + 1]),
               reads=[b_sd], writes=[b_rstd])

    def adaln(li, st_unused):
      with contextlib.ExitStack() as st:
        wm = [st.enter_context(P.sbt(f"wm{j}_{li}", [128, 8, 128], F32)) for j in range(2)]
        bwm = [Buf("wm0"), Buf("wm1")]
        gbl = st.enter_context(P.sbt(f"gbl_{li}", [128, 8, 128], F32))
        b_gbl = Buf("gbl")
        P.dma("sp", vecs[:, 0:8], npreT[li], writes=[b_vecs])
        P.dma("sp", vecs[:, 8:16], npostT[li], writes=[b_vecs])
        P.dma("sp", vecs[:, 16:40], bmodT[li], writes=[b_vecs])
        wsrc = wmod[li].rearrange("(k p) n -> p k n", p=128)
        for j in range(24):
            sl = j % 2
            P.dma("sp", wm[sl][:], wsrc[:, :, j * 128:(j + 1) * 128], writes=[bwm[sl]])
            for k in range(8):
                P.mm(ps[7][:, j * nseq:(j + 1) * nseq], wm[sl][:, k, :], scT[:, k, :], k == 0, k == 7,
                     reads=[bwm[sl], b_scT], writes=[bps[7]], inc=(k == 7))
        psv = ps[7][:, 0:24 * nseq].rearrange("p (j s) -> p j s", s=nseq)
        P.tt("dve", modT[:], psv, vecs[:, 16:40].unsqueeze(2).to_broadcast([128, 24, nseq]), ALU.add,
             reads=[bps[7], b_vecs], writes=[b_modT])
        P.ts("dve", gsT[:], modT[:, 8:16, :], 1.0, None, ALU.add, None, reads=[b_modT], writes=[b_gsT])
        P.tt("dve", gsT[:], gsT[:], vecs[:, 0:8].unsqueeze(2).to_broadcast([128, 8, nseq]), ALU.mult,
             reads=[b_gsT, b_vecs], writes=[b_gsT])
        P.tt("dve", ggT[:], modT[:, 16:24, :], vecs[:, 8:16].unsqueeze(2).to_broadcast([128, 8, nseq]), ALU.mult,
             reads=[b_modT, b_vecs], writes=[b_ggT])
        for s in range(nseq):
            P.cp("dve", gbl[:], ggT[:, :, s:s + 1].to_broadcast([128, 8, 128]), reads=[b_ggT], writes=[b_gbl])
            for c in range(8):
                bk = 5 + c // 4
                P.mm(ps[bk][:, (c % 4) * 128:(c % 4 + 1) * 128], gbl[:, c, :], identf[:], True, True,
                     reads=[b_gbl, b_identf], writes=[bps[bk]], inc=(c % 4 == 3))
            P.cp("dve", ggrow[s][0][:, 0:512], ps[5][:], reads=[bps[5]], writes=[ggrow[s][1]])
            P.act(ggrow[s][0][:, 512:1024], ps[6][:], AF.Copy, reads=[bps[6]], writes=[ggrow[s][1]])

    def make_common(st, tag):
        C = {}
        C["xt"] = [st.enter_context(P.sbt(f"xt{j}_{tag}", [128, 1024], F32)) for j in range(2)]
        C["bxt"] = [Buf("xt0"), Buf("xt1")]
        C["xn"] = st.enter_context(P.sbt(f"xn_{tag}", [128, 1024], BF16))
        C["bxn"] = Buf("xn")
        C["junk"] = st.enter_context(P.sbt(f"junk_{tag}", [128, 1024], BF16))
        C["bjunk"] = Buf("junk")
        C["yt"] = st.enter_context(P.sbt(f"yt_{tag}", [128, 1024], F32))
        C["byt"] = Buf("yt")
        C["cnt"] = 0
        return C

    def prenorm_parts(C, s, src, bsrc, n, hT, bhT, col0):
        sl = C["cnt"] % 2
        C["cnt"] += 1
        xt, bxt = C["xt"][sl], C["bxt"][sl]

        def p1():
            P.dma("sp", xt[:], src[s, n * 128:(n + 1) * 128, :], reads=[bsrc], writes=[bxt])
            P.act(C["yt"][:], xt[:], AF.Square, reads=[bxt], writes=[C["byt"]])
            P.T.op("dve", lambda q, yt_=C["yt"]: q.reduce_sum(out=ss[:, 0:1], in_=yt_[:], axis=mybir.AxisListType.X),
                   reads=[C["byt"]], writes=[b_ss])
            P.act(sd[:, 0:1], ss[:, 0:1], AF.Sqrt, reads=[b_ss, b_eps], writes=[b_sd], scale=1.0 / D, bias=epsT[:, 0:1])

        def p2():
            P.T.op("dve", lambda q: q.reciprocal(out=rstd[:, 0:1], in_=sd[:, 0:1]), reads=[b_sd], writes=[b_rstd])
            P.act(C["xn"][:], xt[:], AF.Copy, reads=[bxt, b_rstd], writes=[C["bxn"]], scale=rstd[:, 0:1])
            for b in range(2):
                for j in range(4):
                    k = 4 * b + j
                    P.mm(ps[b][:, j * 128:(j + 1) * 128], C["xn"][:, k * 128:(k + 1) * 128], ident_bf[:], True, True,
                         reads=[C["bxn"], b_ident], writes=[bps[b]], inc=(j == 3))

        def p3():
            for b in range(2):
                for j in range(4):
                    k = 4 * b + j
                    o = hT[:, k, col0:col0 + 128]
                    i_ = ps[b][:, j * 128:(j + 1) * 128]
                    if b == 0 or P.pne == 'act':
                        P.act(o, i_, AF.Identity, reads=[bps[b], b_gsT, b_modT], writes=[bhT],
                              scale=gsT[:, k, s:s + 1], bias=modT[:, k, s:s + 1])
                    else:
                        P.ts("dve", o, i_, gsT[:, k, s:s + 1], modT[:, k, s:s + 1], ALU.mult, ALU.add,
                             reads=[bps[b], b_gsT, b_modT], writes=[bhT])
        return p1, p2, p3

    def prenorm(C, s, src, bsrc, n, hT, bhT, col0):
        p1, p2, p3 = prenorm_parts(C, s, src, bsrc, n, hT, bhT, col0)
        p1()
        p2()
        p3()

    def post(C, s, ypb, xt, bxt, dst, bdst, n):
        yt, byt = C["yt"], C["byt"]
        for h in range(2):
            P.act(yt[:, h * 512:(h + 1) * 512], ps[ypb[h]][:], AF.Square, reads=[bps[ypb[h]]], writes=[byt])
        P.T.op("dve", lambda q, yt_=yt: q.reduce_sum(out=ss[:, 3:4], in_=yt_[:], axis=mybir.AxisListType.X),
               reads=[byt], writes=[b_ss])
        rstd_from_ss(3, D)
        for h in range(2):
            P.act(yt[:, h * 512:(h + 1) * 512], ps[ypb[h]][:], AF.Copy, reads=[bps[ypb[h]], b_rstd], writes=[byt],
                  scale=rstd[:, 3:4])
        P.tt("dve", yt[:], yt[:], ggrow[s][0][:], ALU.mult, reads=[byt, ggrow[s][1]], writes=[byt])
        P.tt("dve" if getattr(P, "pne", "") == "mix" else "pool", yt[:], yt[:], xt[:], ALU.add, reads=[byt, bxt], writes=[byt])
        P.dma(getattr(P, "stq", "pool"), dst[s, n * 128:(n + 1) * 128, :], yt[:], reads=[byt], writes=[bdst])

    def sincos(st, tag, phi, bphi, shape, out_sin, out_cos, bout):
        tf = st.enter_context(P.sbt(f"sc_tf_{tag}", shape, F32))
        ti = st.enter_context(P.sbt(f"sc_ti_{tag}", shape, I32))
        btf, bti = Buf("tf"), Buf("ti")
        for shift, o in ((0.0, out_sin), (0.5 * math.pi, out_cos)):
            P.ts("dve", tf[:], phi, shift, 1.0 / TWO_PI, ALU.add, ALU.mult, reads=[bphi], writes=[btf])
            P.cp("dve", ti[:], tf[:], reads=[btf], writes=[bti])
            P.cp("dve", tf[:], ti[:], reads=[bti], writes=[btf])
            P.stt("dve", tf[:], tf[:], -TWO_PI, phi, ALU.mult, ALU.add, reads=[btf, bphi], writes=[btf])
            P.ts("dve", tf[:], tf[:], shift, 0.999999, ALU.add, ALU.mult, reads=[btf], writes=[btf])
            P.act(o, tf[:], AF.Sin, reads=[btf], writes=[bout])

    def even_layer(li, src, bsrc, dst, bdst):
        j = li // 2
        P.pool_to_dve = (os.environ.get('P2D', '0') == '1')
        P.pne = 'act'
        P.stq = STQ
        with contextlib.ExitStack() as st:
            def S(name, shape, dt=F32):
                return st.enter_context(P.sbt(f"{name}_e{li}", list(shape), dt)), Buf(name)
            adaln(li, st)
            C = make_common(st, f"e{li}")
            win, b_win = S("win", [128, 8, 3072], BF16)
            wout, b_wout = S("wout", [128, 8, 1024], BF16)
            wglu, b_wglu = S("wglu", [128, 4, 512], BF16)
            drow, b_drow = S("drow", [128, 512])
            dtab, b_dtab = S("dtab", [128, 4, 128])
            gam, b_gam = S("gam", [128, 4, 4])
            cdec, b_cdec = S("cdec", [128, 4])
            jt, b_jt = S("jt", [128, 128])
            wsrc = w_in_ab[j].rearrange("(k p) n -> p k n", p=128)
            for k in range(8):
                P.dma("pool", win[:, k, :], wsrc[:, k, :], writes=[b_win])
            P.dma("pool", wout[:], w_out_ab[j].rearrange("(k p) n -> p k n", p=128), writes=[b_wout])
            P.dma("pool", wglu[:], w_glu[j].rearrange("(k p) n -> p k n", p=128), writes=[b_wglu])
            P.dma("sp", drow[:], ssm_d[j].partition_broadcast(128), writes=[b_drow])
            P.dma("sp", dtab[:], dtab_d, writes=[b_dtab])
            P.dma("sp", gam[:], gam_d, writes=[b_gam])
            P.dma("sp", cdec[:], cdec_d, writes=[b_cdec])
            P.dma("sp", jt[:], jt_d, writes=[b_jt])
            WB = [S(f"WB{r}", [128, 2048], BF16) for r in range(2)]
            WC = [S(f"WC{r}", [128, 512], BF16) for r in range(3)]
            COS, b_COS = S("COS", [128, 16, 128])
            SIN, b_SIN = S("SIN", [128, 16, 128])
            RHO0, b_RHO0 = S("RHO0", [128, 16, 128])
            Gc, b_Gc = S("Gc", [128, 2, 16])
            cin = [S(f"cin{i}", [128, 2, 16]) for i in range(2)]
            stf, b_stf = S("stf", [128, 512])
            stbf, b_stbf = S("stbf", [128, 512], BF16)
            hn, b_hn = S("hn", [128, 16])
            lst, b_lst = S("lst", [128, 8, 16])
            lastb = [S(f"lastb{i}", [128, 2, 16]) for i in range(2)]

            def s5_setup(d):
                with contextlib.ExitStack() as s2:
                    def S2(name, shape, dt=F32):
                        return s2.enter_context(P.sbt(f"{name}_e{li}d{d}", list(shape), dt)), Buf(name)
                    sm, b_sm = S2("sm", [128, 3, 16])
                    P.dma("sp", sm[:], s5_sm3[j, d].rearrange("t p s -> p t s"), writes=[b_sm])
                    dl, b_dl = S2("dl", [128, 16])
                    zr, b_zr = S2("zr", [128, 16])
                    zi, b_zi = S2("zi", [128, 16])
                    rho, b_rho = S2("rho", [128, 16])
                    P.act(dl[:], sm[:, 2, :], AF.Exp, reads=[b_sm], writes=[b_dl])
                    P.tt("dve", zr[:], sm[:, 0, :], dl[:], ALU.mult, reads=[b_sm, b_dl], writes=[b_zr])
                    P.tt("dve", zi[:], sm[:, 1, :], dl[:], ALU.mult, reads=[b_sm, b_dl], writes=[b_zi])
                    P.act(rho[:], zr[:], AF.Exp, reads=[b_zr], writes=[b_rho])
                    for g4 in range(4):
                        with contextlib.ExitStack() as s4:
                            phi = s4.enter_context(P.sbt(f"phi_e{li}d{d}g{g4}", [128, 4, 128], F32))
                            b_phi = Buf("phi")
                            P.tt("dve", phi[:], zi[:, 4 * g4:4 * g4 + 4].unsqueeze(2).to_broadcast([128, 4, 128]),
                                 jt[:].unsqueeze(1).to_broadcast([128, 4, 128]), ALU.mult, reads=[b_zi, b_jt], writes=[b_phi])
                            sincos(s4, f"t{li}{d}{g4}", phi[:], b_phi, [128, 4, 128], SIN[:, 4 * g4:4 * g4 + 4, :],
                                   COS[:, 4 * g4:4 * g4 + 4, :], b_COS)
                            T_barrier()
                    P.cp("dve", RHO0[:], rho[:].unsqueeze(2).to_broadcast([128, 16, 128]), reads=[b_rho], writes=[b_RHO0])
                    P.memset("dve", RHO0[:, :, 0:1], 0.0, writes=[b_RHO0])
                    ph2, b_ph2 = S2("ph2", [128, 16])
                    sn2, b_sn2 = S2("sn2", [128, 2, 16])
                    P.ts("dve", ph2[:], zi[:], 128.0, None, ALU.mult, None, reads=[b_zi], writes=[b_ph2])
                    sincos(s2, f"g{li}{d}", ph2[:], b_ph2, [128, 16], sn2[:, 1, :], sn2[:, 0, :], b_sn2)
                    P.tt("dve", Gc[:], sn2[:], rho[:].unsqueeze(1).to_broadcast([128, 2, 16]), ALU.mult,
                         reads=[b_sn2, b_rho], writes=[b_Gc])
                    T_barrier()
                for cc in range(8):
                  with contextlib.ExitStack() as s3:
                    def S3(name, shape, dt=F32):
                        return s3.enter_context(P.sbt(f"{name}_e{li}d{d}c{cc}", list(shape), dt)), Buf(name)
                    rp, b_rp = S3("rp", [128, 3, 256])
                    P.dma("sp", rp[:], s5_rep3[j, d][:, cc * 256:(cc + 1) * 256].partition_broadcast(128), writes=[b_rp])
                    r_dl, b_r_dl = S3("r_dl", [128, 256])
                    r_zr, b_r_zr = S3("r_zr", [128, 256])
                    r_zi, b_r_zi = S3("r_zi", [128, 256])
                    r_rho, b_r_rho = S3("r_rho", [128, 256])
                    r_sn, b_r_sn = S3("r_sn", [128, 2, 256])
                    P.act(r_dl[:], rp[:, 2, :], AF.Exp, reads=[b_rp], writes=[b_r_dl])
                    P.tt("dve", r_zr[:], rp[:, 0, :], r_dl[:], ALU.mult, reads=[b_rp, b_r_dl], writes=[b_r_zr])
                    P.tt("dve", r_zi[:], rp[:, 1, :], r_dl[:], ALU.mult, reads=[b_rp, b_r_dl], writes=[b_r_zi])
                    P.act(r_rho[:], r_zr[:], AF.Exp, reads=[b_r_zr], writes=[b_r_rho])
                    sincos(s3, f"r{li}{d}{cc}", r_zi[:], b_r_zi, [128, 256], r_sn[:, 1, :], r_sn[:, 0, :], b_r_sn)
                    P.tt("dve", r_sn[:], r_sn[:], r_rho[:].unsqueeze(1).to_broadcast([128, 2, 256]), ALU.mult,
                         reads=[b_r_sn, b_r_rho], writes=[b_r_sn])
                    P.ts("dve", r_sn[:, 0, :], r_sn[:, 0, :], -1.0, None, ALU.add, None, reads=[b_r_sn], writes=[b_r_sn])
                    P.tt("dve", r_dl[:], rp[:, 0, :], rp[:, 0, :], ALU.mult, reads=[b_rp], writes=[b_r_dl])
                    P.tt("dve", r_zr[:], rp[:, 1, :], rp[:, 1, :], ALU.mult, reads=[b_rp], writes=[b_r_zr])
                    P.tt("dve", r_dl[:], r_dl[:], r_zr[:], ALU.add, reads=[b_r_dl, b_r_zr], writes=[b_r_dl])
                    P.T.op("dve", lambda q, r_dl=r_dl: q.reciprocal(out=r_dl[:], in_=r_dl[:]), reads=[b_r_dl], writes=[b_r_dl])
                    P.tt("dve", r_zr[:], r_sn[:, 0, :], rp[:, 0, :], ALU.mult, reads=[b_r_sn, b_rp], writes=[b_r_zr])
                    P.tt("dve", r_rho[:], r_sn[:, 1, :], rp[:, 1, :], ALU.mult, reads=[b_r_sn, b_rp], writes=[b_r_rho])
                    P.tt("dve", r_zr[:], r_zr[:], r_rho[:], ALU.add, reads=[b_r_zr, b_r_rho], writes=[b_r_zr])
                    P.tt("dve", r_zr[:], r_zr[:], r_dl[:], ALU.mult, reads=[b_r_zr, b_r_dl], writes=[b_r_zr])
                    P.tt("dve", r_zi[:], r_sn[:, 1, :], rp[:, 0, :], ALU.mult, reads=[b_r_sn, b_rp], writes=[b_r_zi])
                    P.tt("dve", r_rho[:], r_sn[:, 0, :], rp[:, 1, :], ALU.mult, reads=[b_r_sn, b_rp], writes=[b_r_rho])
                    P.tt("dve", r_zi[:], r_zi[:], r_rho[:], ALU.subtract, reads=[b_r_zi, b_r_rho], writes=[b_r_zi])
                    P.tt("dve", r_zi[:], r_zi[:], r_dl[:], ALU.mult, reads=[b_r_zi, b_r_dl], writes=[b_r_zi])
                    bl, b_bl = S3("bl", [128, 2, 256])
                    P.dma("sp", bl[:], s5_bl[j, d][:, :, cc * 256:(cc + 1) * 256].rearrange("r p n -> p r n"), writes=[b_bl])
                    P.tt("dve", r_dl[:], r_zr[:], bl[:, 0, :], ALU.mult, reads=[b_r_zr, b_bl], writes=[b_r_dl])
                    P.tt("dve", r_rho[:], r_zi[:], bl[:, 1, :], ALU.mult, reads=[b_r_zi, b_bl], writes=[b_r_rho])
                    P.tt("dve", WB[0][0][:, cc * 256:(cc + 1) * 256], r_dl[:], r_rho[:], ALU.subtract, reads=[b_r_dl, b_r_rho], writes=[WB[0][1]])
                    P.tt("dve", r_dl[:], r_zr[:], bl[:, 1, :], ALU.mult, reads=[b_r_zr, b_bl], writes=[b_r_dl])
                    P.tt("dve", r_rho[:], r_zi[:], bl[:, 0, :], ALU.mult, reads=[b_r_zi, b_bl], writes=[b_r_rho])
                    P.tt("dve", WB[1][0][:, cc * 256:(cc + 1) * 256], r_dl[:], r_rho[:], ALU.add, reads=[b_r_dl, b_r_rho], writes=[WB[1][1]])

                    T_barrier()
                with contextlib.ExitStack() as s2:
                    def S2(name, shape, dt=F32):
                        return s2.enter_context(P.sbt(f"{name}_e{li}d{d}x", list(shape), dt)), Buf(name)
                    cl, b_cl = S2("cl", [128, 2, 512])
                    P.dma("sp", cl[:], s5_cl[j, d].rearrange("r p n -> p r n"), writes=[b_cl])
                    P.cp("dve", WC[0][0][:], cl[:, 0, :], reads=[b_cl], writes=[WC[0][1]])
                    P.ts("dve", WC[1][0][:], cl[:, 0, :], -1.0, None, ALU.mult, None, reads=[b_cl], writes=[WC[1][1]])
                    P.ts("dve", WC[2][0][:], cl[:, 1, :], -1.0, None, ALU.mult, None, reads=[b_cl], writes=[WC[2][1]])
                    P.memset("dve", cin[0][0][:], 0.0, writes=[cin[0][1]])
                    T_barrier()

            import types

            def alloc_work(wk, passB):
                def W(name, shape, dt=F32):
                    return wk.enter_context(P.sbt(f"{name}_e{li}", list(shape), dt)), Buf(name)
                w = types.SimpleNamespace()
                w.g = [types.SimpleNamespace() for _ in range(2)]
                tmps = {nm: W(f"{nm}m", [128, 512]) for nm in ("tA", "tB", "tC", "tD")}
                Pk1 = [W(f"Pk_{k}", [128, 512], BF16) for k in range(4)]
                for i, g in enumerate(w.g):
                    for nm in ("wre", "wim", "sre", "sim"):
                        setattr(g, nm, W(f"{nm}{i}", [128, 512]))
                    for nm in ("tA", "tB", "tC", "tD"):
                        setattr(g, nm, tmps[nm])
                    g.Pk = Pk1
                w.uT = W("uT", [128, 4, 128], BF16)
                w.kf = W("kf", [128, 512], BF16)
                w.tC = W("tCr", [128, 512])
                if not passB:
                    w.hT = [W(f"hT{i}", [128, 8, 128], BF16) for i in range(2)]
                    w.rot = [W(f"rot{i}", [128, 2, 2, 128]) for i in range(2)]
                    w.rA = W("rA", [128, 512])
                    w.rB = W("rB", [128, 512])
                    w.qr = W("qr", [128, 512], BF16)
                    w.kr = [W("kr", [128, 512], BF16)]
                    w.vb = [W("vb", [128, 512], BF16)]
                    w.QT = [W("QT", [128, 512], BF16)]
                    w.KT = W("KT", [128, 512], BF16)
                    w.STb = W("STb", [128, 512], BF16)
                    w.sg = [W("sg", [128, 1024])]
                    w.du = W("du", [128, 512])
                    w.ub = [W("ub", [128, 512], BF16)]
                    w.opt = [W("opt", [128, 512])]
                    w.ypt = [W("ypt", [128, 512])]
                else:
                    w.kr = [W(f"kr{i}", [128, 512], BF16) for i in range(2)]
                    w.vb = [W(f"vb{i}", [128, 512], BF16) for i in range(2)]
                    w.QT = [W(f"QT{i}", [128, 512], BF16) for i in range(2)]
                    w.sga = W("sga", [128, 512])
                    w.sgb = [W(f"sgb{i}", [128, 512]) for i in range(2)]
                    w.uT2 = W("uT2", [128, 4, 128], BF16)
                    w.ub = [W(f"ub{i}", [128, 512], BF16) for i in range(2)]
                    w.opt = [W(f"opt{i}", [128, 512]) for i in range(2)]
                    w.ypt = [W(f"ypt{i}", [128, 512]) for i in range(2)]
                    w.oab = [W(f"oab{i}", [128, 1024], BF16) for i in range(2)]
                    w.oT = W("oT", [128, 8, 128], BF16)
                    w.ygb = W("ygb", [128, 512], BF16)
                    w.tA = W("tAh", [128, 512])
                    w.tB = W("tBh", [128, 512])
                    w.tD = W("tDh", [128, 512])
                return w

            def s5_chunk(w, ci, rev, hooks=()):
                cur, nxt = cin[ci % 2], cin[(ci + 1) % 2]
                lb, b_lb = lastb[ci % 2]
                uT, b_uT = w.uT
                hooks = list(hooks)

                def hook():
                    if hooks:
                        hooks.pop(0)()

                def banks(g):
                    return (3, 4) if g % 2 == 0 else (5, 6)

                def stageApe(g):
                    br, bi = banks(g)
                    for t in range(4):
                        s_ = 4 * g + t
                        P.mm(ps[br][:, t * 128:(t + 1) * 128], WB[0][0][:, s_ * 128:(s_ + 1) * 128], uT[:, g, :], True, False,
                             reads=[WB[0][1], b_uT], writes=[bps[br]], inc=False)
                        P.mm(ps[bi][:, t * 128:(t + 1) * 128], WB[1][0][:, s_ * 128:(s_ + 1) * 128], uT[:, g, :], True, False,
                             reads=[WB[1][1], b_uT], writes=[bps[bi]], inc=False)
                    o_r = ps[br][:].rearrange("p (a b) -> p a b", a=4)[:, :, 0:1]
                    o_i = ps[bi][:].rearrange("p (a b) -> p a b", a=4)[:, :, 0:1]
                    P.mm(o_r, identf[:], cur[0][:, 0, 4 * g:4 * g + 4].unsqueeze(2), False, True,
                         reads=[b_identf, cur[1]], writes=[bps[br]], inc=False)
                    P.mm(o_i, identf[:], cur[0][:, 1, 4 * g:4 * g + 4].unsqueeze(2), False, True,
                         reads=[b_identf, cur[1]], writes=[bps[bi]], inc=True)

                def stageAve(g):
                    br, bi = banks(g)
                    G = w.g[g % 2]
                    Cg = COS[:, 4 * g:4 * g + 4, :].rearrange("p a b -> p (a b)")
                    Sg = SIN[:, 4 * g:4 * g + 4, :].rearrange("p a b -> p (a b)")
                    P.tt("dve", G.tA[0][:], ps[br][:], Cg, ALU.mult, reads=[bps[br], b_COS], writes=[G.tA[1]])
                    P.tt("dve", G.tB[0][:], ps[bi][:], Sg, ALU.mult, reads=[bps[bi], b_COS], writes=[G.tB[1]])
                    P.tt(S5E, G.wre[0][:], G.tA[0][:], G.tB[0][:], ALU.add, reads=[G.tA[1], G.tB[1]], writes=[G.wre[1]])
                    P.tt("dve", G.tC[0][:], ps[bi][:], Cg, ALU.mult, reads=[bps[bi], b_COS], writes=[G.tC[1]])
                    P.stt("dve", G.tD[0][:], ps[br][:], -1.0, Sg, ALU.mult, ALU.mult, reads=[bps[br], b_COS], writes=[G.tD[1]])
                    P.tt(S5E, G.wim[0][:], G.tC[0][:], G.tD[0][:], ALU.add, reads=[G.tC[1], G.tD[1]], writes=[G.wim[1]])

                def stageB(g):
                    G = w.g[g % 2]
                    Rg = RHO0[:, 4 * g:4 * g + 4, :].rearrange("p a b -> p (a b)")
                    P.T.op("dve", lambda q, Rg=Rg, o=G.sre[0], i=G.wre[0]: q.tensor_tensor_scan(
                        out=o[:], data0=Rg, data1=i[:], initial=0.0, op0=ALU.mult, op1=ALU.add),
                        reads=[b_RHO0, G.wre[1]], writes=[G.sre[1]])
                    P.T.op("dve", lambda q, Rg=Rg, o=G.sim[0], i=G.wim[0]: q.tensor_tensor_scan(
                        out=o[:], data0=Rg, data1=i[:], initial=0.0, op0=ALU.mult, op1=ALU.add),
                        reads=[b_RHO0, G.wim[1]], writes=[G.sim[1]])
                    sr3 = G.sre[0][:].rearrange("p (a b) -> p a b", a=4)
                    si3 = G.sim[0][:].rearrange("p (a b) -> p a b", a=4)
                    P.act(lb[:, 0, 4 * g:4 * g + 4].unsqueeze(2), sr3[:, :, 127:128], AF.Copy, reads=[G.sre[1]], writes=[b_lb])
                    P.act(lb[:, 1, 4 * g:4 * g + 4].unsqueeze(2), si3[:, :, 127:128], AF.Copy, reads=[G.sim[1]], writes=[b_lb])

                    def pv(ap):
                        v = ap.rearrange("p (a b) -> p a b", a=4)
                        return v[:, :, ::-1] if rev else v
                    C3 = COS[:, 4 * g:4 * g + 4, :]
                    S3 = SIN[:, 4 * g:4 * g + 4, :]
                    e2 = "dve" if rev else S5E
                    P.tt("dve", pv(G.Pk[0][0][:]), sr3, C3, ALU.mult, reads=[G.sre[1], b_COS], writes=[G.Pk[0][1]])
                    P.tt(e2, pv(G.Pk[1][0][:]), si3, S3, ALU.mult, reads=[G.sim[1], b_COS], writes=[G.Pk[1][1]])
                    P.tt("dve", pv(G.Pk[2][0][:]), sr3, S3, ALU.mult, reads=[G.sre[1], b_COS], writes=[G.Pk[2][1]])
                    P.tt(e2, pv(G.Pk[3][0][:]), si3, C3, ALU.mult, reads=[G.sim[1], b_COS], writes=[G.Pk[3][1]])
                    for t in range(4):
                        s_ = 4 * g + t
                        o = ps[7][:, 32 * s_:32 * s_ + 32]
                        wsl = slice(32 * s_, 32 * s_ + 32)
                        tsl = slice(128 * t, 128 * t + 128)
                        Pk = G.Pk
                        P.mm(o, Pk[0][0][:, tsl], WC[0][0][:, wsl], True, False, reads=[Pk[0][1], WC[0][1]], writes=[bps[7]], inc=False)
                        P.mm(o, Pk[1][0][:, tsl], WC[1][0][:, wsl], False, False, reads=[Pk[1][1], WC[1][1]], writes=[bps[7]], inc=False)
                        P.mm(o, Pk[2][0][:, tsl], WC[2][0][:, wsl], False, False, reads=[Pk[2][1], WC[2][1]], writes=[bps[7]], inc=False)
                        P.mm(o, Pk[3][0][:, tsl], WC[2][0][:, wsl], False, True, reads=[Pk[3][1], WC[2][1]], writes=[bps[7]], inc=(t == 3))

                stageApe(0)
                stageAve(0)
                stageApe(1)
                hook()
                stageAve(1)
                hook()
                stageApe(2)
                stageB(0)
                hook()
                stageAve(2)
                hook()
                stageApe(3)
                stageB(1)
                hook()
                stageAve(3)
                hook()
                stageB(2)
                hook()
                stageB(3)
                hook()
                while hooks:
                    hook()
                l4 = lst[:]
                P.tt("pool", l4[:, 0, :], lb[:, 0, :], Gc[:, 0, :], ALU.mult, reads=[b_lb, b_Gc], writes=[b_lst])
                P.tt("pool", l4[:, 1, :], lb[:, 1, :], Gc[:, 1, :], ALU.mult, reads=[b_lb, b_Gc], writes=[b_lst])
                P.tt("pool", l4[:, 2, :], lb[:, 1, :], Gc[:, 0, :], ALU.mult, reads=[b_lb, b_Gc], writes=[b_lst])
                P.tt("pool", l4[:, 3, :], lb[:, 0, :], Gc[:, 1, :], ALU.mult, reads=[b_lb, b_Gc], writes=[b_lst])
                P.tt("pool", nxt[0][:, 0, :], l4[:, 0, :], l4[:, 1, :], ALU.subtract, reads=[b_lst], writes=[nxt[1]])
                P.tt("pool", nxt[0][:, 1, :], l4[:, 2, :], l4[:, 3, :], ALU.add, reads=[b_lst], writes=[nxt[1]])

            def ret_state_update(w, kbuf, b_kbuf, vbt, b_vbt, gcol):
                kf, b_kf = w.kf
                P.tt("pool", kf[:].rearrange("p (h e) -> p h e", h=4), kbuf[:].rearrange("p (h e) -> p h e", h=4),
                     gam[:, gcol, :].unsqueeze(2).to_broadcast([128, 4, 128]), ALU.mult,
                     reads=[b_kbuf, b_gam], writes=[b_kf])
                for h in range(4):
                    hs = slice(h * 128, (h + 1) * 128)
                    P.mm(ps[5][:, hs], kf[:, hs], vbt[:, hs], True, True, reads=[b_kf, b_vbt], writes=[bps[5]], inc=(h == 3))
                P.tt("pool", stf[:].rearrange("p (h e) -> p h e", h=4), stf[:].rearrange("p (h e) -> p h e", h=4),
                     cdec[:].unsqueeze(2).to_broadcast([128, 4, 128]), ALU.mult, reads=[b_stf, b_cdec], writes=[b_stf])
                P.tt("dve", stf[:], stf[:], ps[5][:], ALU.add, reads=[b_stf, bps[5]], writes=[b_stf])
                P.act(stbf[:], stf[:], AF.Copy, reads=[b_stf], writes=[b_stbf])

            def passA(s, w):
                OPs, YPs, SGs, QTs, KRs, VBs, UBs = OPs_[s], YPs_[s], SGs_[s], QTs_[s], KRs_[s], VBs_[s], UBs_[s]
                b_OPs, b_YPs, b_SGs, b_QTs, b_KRs, b_VBs, b_UBs = SCRB[s]
                P.memset("dve", stf[:], 0.0, writes=[b_stf])
                P.memset("pool", stbf[:], 0.0, writes=[b_stbf])
                P.memset("dve", cin[0][0][:], 0.0, writes=[cin[0][1]])
                nA = nch if KSTOP not in ('setup',) else 0
                if nA:
                    prenorm(C, s, src, bsrc[s], 0, w.hT[0][0], w.hT[0][1], 0)
                for n in range(nA):
                    tsl = slice(n * 128, (n + 1) * 128)
                    hT, b_hT = w.hT[n % 2]
                    rot, b_rot = w.rot[n % 2]
                    if n == 0:
                        P.dma("sp", rot[:], rot_d[tsl], writes=[b_rot])
                    if n + 1 < nA:
                        P.dma("sp", w.rot[(n + 1) % 2][0][:], rot_d[(n + 1) * 128:(n + 2) * 128], writes=[w.rot[(n + 1) % 2][1]])
                    if n + 1 < nA:
                        p1, p2, p3 = prenorm_parts(C, s, src, bsrc[s], n + 1, w.hT[(n + 1) % 2][0], w.hT[(n + 1) % 2][1], 0)
                    else:
                        p1 = p2 = p3 = (lambda: None)
                    qr, b_qr = w.qr
                    kr, b_kr = w.kr[0]
                    vb, b_vb = w.vb[0]
                    QT, b_QT = w.QT[0]
                    KT, b_KT = w.KT
                    STb, b_STb = w.STb
                    sg, b_sg = w.sg[0]
                    du, b_du = w.du
                    ub, b_ub = w.ub[0]
                    opt, b_opt = w.opt[0]
                    ypt, b_ypt = w.ypt[0]
                    tC, b_tC = w.tC
                    uT, b_uT = w.uT
                    def inproj(col, bank):
                        for k in range(8):
                            P.mm(ps[bank][:], hT[:, k, :], win[:, k, col * 512:(col + 1) * 512], k == 0, k == 7,
                                 reads=[b_hT, b_win], writes=[bps[bank]], inc=(k == 7))

                    def rotary(zb, tbl, outb, b_outb):
                        rA, b_rA = w.rA
                        rB, b_rB = w.rB
                        z3 = ps[zb][:].rearrange("p (h e) -> p h e", h=4)
                        a3 = rA[:].rearrange("p (h e) -> p h e", h=4)
                        b3 = rB[:].rearrange("p (h e) -> p h e", h=4)
                        P.tt("dve", a3, z3, rot[:, tbl, 0, :].unsqueeze(1).to_broadcast([128, 4, 128]), ALU.mult,
                             reads=[bps[zb], b_rot], writes=[b_rA])
                        P.tt("dve", b3[:, :, 0:64], z3[:, :, 64:128], rot[:, tbl, 1, 0:64].unsqueeze(1).to_broadcast([128, 4, 64]),
                             ALU.mult, reads=[bps[zb], b_rot], writes=[b_rB])
                        P.tt("dve", b3[:, :, 64:128], z3[:, :, 0:64], rot[:, tbl, 1, 64:128].unsqueeze(1).to_broadcast([128, 4, 64]),
                             ALU.mult, reads=[bps[zb], b_rot], writes=[b_rB])
                        P.tt("pool", outb[:], rA[:], rB[:], ALU.add, reads=[b_rA, b_rB], writes=[b_outb])

                    inproj(4, 2)
                    P.act(ub[:], ps[2][:], AF.Copy, reads=[bps[2]], writes=[b_ub])
                    P.tt("dve", du[:], ps[2][:], drow[:], ALU.mult, reads=[bps[2], b_drow, b_ub], writes=[b_du])
                    P.dma(STQ, UBs[tsl], ub[:], reads=[b_ub], writes=[b_UBs])
                    for q_ in range(4):
                        qs = slice(q_ * 128, (q_ + 1) * 128)
                        P.mm(ps[0][:, qs], ub[:, qs], ident_bf[:], True, True, reads=[b_ub, b_ident], writes=[bps[0]], inc=(q_ == 3))
                    P.act(uT[:].rearrange("p a b -> p (a b)"), ps[0][:], AF.Copy, reads=[bps[0]], writes=[b_uT])

                    def H0():
                        inproj(0, 2)
                        inproj(1, 1)
                        rotary(2, 0, qr, b_qr)
                        rotary(1, 1, kr, b_kr)

                    def H1():
                        inproj(2, 2)
                        P.act(vb[:], ps[2][:], AF.Copy, reads=[bps[2]], writes=[b_vb])
                        inproj(3, 0)
                        P.act(sg[:, 0:512], ps[0][:], AF.Silu, reads=[bps[0]], writes=[b_sg])
                        inproj(5, 1)
                        P.act(sg[:, 512:1024], ps[1][:], AF.Silu, reads=[bps[1]], writes=[b_sg])
                        P.dma(STQ, SGs[tsl], sg[:], reads=[b_sg], writes=[b_SGs])

                    def R1():
                        for h in range(4):
                            hs = slice(h * 128, (h + 1) * 128)
                            P.mm(ps[0][:, hs], qr[:, hs], ident_bf[:], True, True, reads=[b_qr, b_ident], writes=[bps[0]], inc=(h == 3))
                        for h in range(4):
                            hs = slice(h * 128, (h + 1) * 128)
                            P.mm(ps[1][:, hs], kr[:, hs], ident_bf[:], True, True, reads=[b_kr, b_ident], writes=[bps[1]], inc=(h == 3))
                        P.act(QT[:], ps[0][:], AF.Copy, reads=[bps[0]], writes=[b_QT])
                        P.act(KT[:], ps[1][:], AF.Copy, reads=[bps[1]], writes=[b_KT])
                        p1()
                    def R2():
                        for h in range(4):
                            hs = slice(h * 128, (h + 1) * 128)
                            P.mm(ps[2][:, hs], KT[:, hs], QT[:, hs], True, True, reads=[b_KT, b_QT], writes=[bps[2]], inc=(h == 3))
                        P.tt("dve", STb[:], ps[2][:], dtab[:].rearrange("p h e -> p (h e)"), ALU.mult,
                             reads=[bps[2], b_dtab], writes=[b_STb])
                    def R3():
                        for h in range(4):
                            hs = slice(h * 128, (h + 1) * 128)
                            P.mm(ps[2][:, hs], STb[:, hs], vb[:, hs], True, True, reads=[b_STb, b_vb], writes=[bps[2]], inc=(h == 3))
                        for h in range(4):
                            hs = slice(h * 128, (h + 1) * 128)
                            P.mm(ps[0][:, hs], QT[:, hs], stbf[:, hs], True, True, reads=[b_QT, b_stbf], writes=[bps[0]], inc=(h == 3))
                        P.tt("dve", tC[:].rearrange("p (h e) -> p h e", h=4), ps[0][:].rearrange("p (h e) -> p h e", h=4),
                             gam[:, 0, :].unsqueeze(2).to_broadcast([128, 4, 128]), ALU.mult, reads=[bps[0], b_gam], writes=[b_tC])
                        P.tt("dve", opt[:], tC[:], ps[2][:], ALU.add, reads=[b_tC, bps[2]], writes=[b_opt])
                        P.dma(STQ, OPs[tsl], opt[:], reads=[b_opt], writes=[b_OPs])
                    def R4():
                        ret_state_update(w, kr, b_kr, vb, b_vb, 2)
                    def R5():
                        P.dma(STQ, QTs[n], QT[:], reads=[b_QT], writes=[b_QTs])
                        P.dma(STQ, KRs[tsl], kr[:], reads=[b_kr], writes=[b_KRs])
                        P.dma(STQ, VBs[tsl], vb[:], reads=[b_vb], writes=[b_VBs])
                    def R45():
                        R4()
                        R5()

                    def P23():
                        p2()
                        p3()
                    s5_chunk(w, n, False, hooks=[H0, H1, R1, R2, R3, R45, P23])
                    P.tt("dve", ypt[:], ps[7][:], du[:], ALU.add, reads=[bps[7], b_du], writes=[b_ypt])
                    P.dma(STQ, YPs[tsl], ypt[:], reads=[b_ypt], writes=[b_YPs])

            def passB(s, w):
                OPs, YPs, SGs, QTs, KRs, VBs, UBs = OPs_[s], YPs_[s], SGs_[s], QTs_[s], KRs_[s], VBs_[s], UBs_[s]
                b_OPs, b_YPs, b_SGs, b_QTs, b_KRs, b_VBs, b_UBs = SCRB[s]
                P.memset("dve", stf[:], 0.0, writes=[b_stf])
                P.memset("pool", stbf[:], 0.0, writes=[b_stbf])
                P.memset("dve", cin[0][0][:], 0.0, writes=[cin[0][1]])
                order = list(range(nch - 1, -1, -1)) if KSTOP == 'all' else []

                def loads(ci):
                    n = order[ci]
                    tsl = slice(n * 128, (n + 1) * 128)
                    r = ci % 2
                    P.dma("sp", w.ub[r][0][:], UBs[tsl], reads=[b_UBs], writes=[w.ub[r][1]])
                    P.dma("sp", w.QT[r][0][:], QTs[n], reads=[b_QTs], writes=[w.QT[r][1]])
                    P.dma("sp", w.opt[r][0][:], OPs[tsl], reads=[b_OPs], writes=[w.opt[r][1]])
                    P.dma("sp", w.kr[r][0][:], KRs[tsl], reads=[b_KRs], writes=[w.kr[r][1]])
                    P.dma("sp", w.vb[r][0][:], VBs[tsl], reads=[b_VBs], writes=[w.vb[r][1]])

                def make_tail(ci, n):
                    r = ci % 2
                    tsl = slice(n * 128, (n + 1) * 128)
                    ypt, b_ypt = w.ypt[r]
                    sgb, b_sgb = w.sgb[r]
                    oab, b_oab = w.oab[r]
                    tB, b_tB = w.tB
                    tD, b_tD = w.tD
                    uT2, b_uT2 = w.uT2
                    oT, b_oT = w.oT
                    ygb, b_ygb = w.ygb

                    def T1():
                        P.tt("pool", tB[:], ypt[:], ypt[:], ALU.mult, reads=[b_ypt], writes=[b_tB])
                        P.ts("dve", tB[:], tB[:], 0.044715, 1.0, ALU.mult, ALU.add, reads=[b_tB], writes=[b_tB])
                        P.tt("pool", tB[:], tB[:], ypt[:], ALU.mult, reads=[b_tB, b_ypt], writes=[b_tB])
                        P.act(tB[:], tB[:], AF.Sigmoid, reads=[b_tB], writes=[b_tB], scale=1.5957691216057308)
                        P.tt("dve", tD[:], ypt[:], tB[:], ALU.mult, reads=[b_ypt, b_tB], writes=[b_tD])
                        P.act(ygb[:], tD[:], AF.Copy, reads=[b_tD], writes=[b_ygb])

                    def T2():
                        for q_ in range(4):
                            qs = slice(q_ * 128, (q_ + 1) * 128)
                            P.mm(ps[0][:, qs], ygb[:, qs], ident_bf[:], True, True, reads=[b_ygb, b_ident], writes=[bps[0]], inc=(q_ == 3))
                        P.act(uT2[:].rearrange("p a b -> p (a b)"), ps[0][:], AF.Copy, reads=[bps[0]], writes=[b_uT2])
                        for q_ in range(4):
                            P.mm(ps[1][:], uT2[:, q_, :], wglu[:, q_, :], q_ == 0, q_ == 3, reads=[b_uT2, b_wglu], writes=[bps[1]], inc=(q_ == 3))
                        P.act(tB[:], ps[1][:], AF.Sigmoid, reads=[bps[1]], writes=[b_tB])
                        P.tt("dve", tD[:], tD[:], tB[:], ALU.mult, reads=[b_tD, b_tB], writes=[b_tD])
                        P.tt("pool", oab[:, 512:1024], tD[:], sgb[:], ALU.mult, reads=[b_tD, b_sgb], writes=[b_oab])

                    def T3():
                        for b_ in range(2):
                            for jj in range(4):
                                k = 4 * b_ + jj
                                P.mm(ps[b_][:, jj * 128:(jj + 1) * 128], oab[:, k * 128:(k + 1) * 128], ident_bf[:], True, True,
                                     reads=[b_oab, b_ident], writes=[bps[b_]], inc=(jj == 3))
                        P.act(oT[:, 0:4, :].rearrange("p a b -> p (a b)"), ps[0][:], AF.Copy, reads=[bps[0]], writes=[b_oT])
                        P.act(oT[:, 4:8, :].rearrange("p a b -> p (a b)"), ps[1][:], AF.Copy, reads=[bps[1]], writes=[b_oT])
                        for hh, bk in ((0, 2), (1, 0)):
                            for k in range(8):
                                P.mm(ps[bk][:], oT[:, k, :], wout[:, k, hh * 512:(hh + 1) * 512], k == 0, k == 7,
                                     reads=[b_oT, b_wout], writes=[bps[bk]], inc=(k == 7))

                    def T4():
                        sl = C["cnt"] % 2
                        C["cnt"] += 1
                        xt, bxt = C["xt"][sl], C["bxt"][sl]
                        P.dma("sp", xt[:], src[s, tsl, :], reads=[bsrc[s]], writes=[bxt])
                        post(C, s, (2, 0), xt, bxt, dst, bdst[s], n)
                    return [T1, T2, T3, T4]

                if order:
                    loads(0)
                tail = []
                for ci, n in enumerate(order):
                    tsl = slice(n * 128, (n + 1) * 128)
                    r = ci % 2
                    QT, b_QT = w.QT[r]
                    opt, b_opt = w.opt[r]
                    kr, b_kr = w.kr[r]
                    vb, b_vb = w.vb[r]
                    ub, b_ub = w.ub[r]
                    sga, b_sga = w.sga
                    sgb, b_sgb = w.sgb[r]
                    ypt, b_ypt = w.ypt[r]
                    tA, b_tA = w.tA
                    tC, b_tC = w.tC
                    uT, b_uT = w.uT
                    oab, b_oab = w.oab[r]
                    kf, b_kf = w.kf
                    P.dma("sp", sga[:], SGs[tsl, 0:512], reads=[b_SGs], writes=[b_sga])
                    P.dma("sp", sgb[:], SGs[tsl, 512:1024], reads=[b_SGs], writes=[b_sgb])
                    P.dma("sp", ypt[:], YPs[tsl], reads=[b_YPs], writes=[b_ypt])
                    for q_ in range(4):
                        qs = slice(q_ * 128, (q_ + 1) * 128)
                        P.mm(ps[0][:, qs], ub[:, qs], jmat_bf[:], True, True, reads=[b_ub, b_jmat], writes=[bps[0]], inc=(q_ == 3))
                    P.act(uT[:].rearrange("p a b -> p (a b)"), ps[0][:], AF.Copy, reads=[bps[0]], writes=[b_uT])
                    if ci + 1 < len(order):
                        loads(ci + 1)

                    def RB1():
                        for h in range(4):
                            hs = slice(h * 128, (h + 1) * 128)
                            P.mm(ps[2][:, hs], QT[:, hs], stbf[:, hs], True, True, reads=[b_QT, b_stbf], writes=[bps[2]], inc=(h == 3))
                        P.tt("dve", tC[:].rearrange("p (h e) -> p h e", h=4), ps[2][:].rearrange("p (h e) -> p h e", h=4),
                             gam[:, 1, :].unsqueeze(2).to_broadcast([128, 4, 128]), ALU.mult, reads=[bps[2], b_gam], writes=[b_tC])
                        P.tt("pool", opt[:], opt[:], tC[:], ALU.add, reads=[b_opt, b_tC], writes=[b_opt])
                        P.tt("pool", kf[:].rearrange("p (h e) -> p h e", h=4), kr[:].rearrange("p (h e) -> p h e", h=4),
                             gam[:, 3, :].unsqueeze(2).to_broadcast([128, 4, 128]), ALU.mult,
                             reads=[b_kr, b_gam], writes=[b_kf])
                        for h in range(4):
                            hs = slice(h * 128, (h + 1) * 128)
                            P.mm(ps[1][:, hs], kf[:, hs], vb[:, hs], True, True, reads=[b_kf, b_vb], writes=[bps[1]], inc=(h == 3))
                        P.tt("pool", stf[:].rearrange("p (h e) -> p h e", h=4), stf[:].rearrange("p (h e) -> p h e", h=4),
                             cdec[:].unsqueeze(2).to_broadcast([128, 4, 128]), ALU.mult, reads=[b_stf, b_cdec], writes=[b_stf])

                    def RB2():
                        P.tt("dve", stf[:], stf[:], ps[1][:], ALU.add, reads=[b_stf, bps[1]], writes=[b_stf])
                        P.act(stbf[:], stf[:], AF.Copy, reads=[b_stf], writes=[b_stbf])
                        o3 = opt[:].rearrange("p (h e) -> p h e", h=4)
                        P.T.op("dve", lambda q, o3=o3: q.reduce_sum(out=hn[:, 0:4], in_=o3, axis=mybir.AxisListType.X),
                               reads=[b_opt], writes=[b_hn])
                        P.tt("pool", tA[:], opt[:], opt[:], ALU.mult, reads=[b_opt], writes=[b_tA])
                        P.T.op("dve", lambda q, tA=tA: q.reduce_sum(out=hn[:, 4:8], in_=tA[:].rearrange("p (h e) -> p h e", h=4),
                                                                  axis=mybir.AxisListType.X), reads=[b_tA], writes=[b_hn])
                        P.ts("dve", hn[:, 0:8], hn[:, 0:8], 1.0 / 128.0, None, ALU.mult, None, reads=[b_hn], writes=[b_hn])
                        P.tt("dve", hn[:, 8:12], hn[:, 0:4], hn[:, 0:4], ALU.mult, reads=[b_hn], writes=[b_hn])
                        P.tt("dve", hn[:, 4:8], hn[:, 4:8], hn[:, 8:12], ALU.subtract, reads=[b_hn], writes=[b_hn])
                        P.act(hn[:, 8:12], hn[:, 4:8], AF.Sqrt, reads=[b_hn, b_eps], writes=[b_hn], bias=epsT[:, 0:1])

                    def RB3():
                        o3 = opt[:].rearrange("p (h e) -> p h e", h=4)
                        P.T.op("dve", lambda q: q.reciprocal(out=hn[:, 12:16], in_=hn[:, 8:12]), reads=[b_hn], writes=[b_hn])
                        a3 = tA[:].rearrange("p (h e) -> p h e", h=4)
                        P.tt("pool", a3, o3, hn[:, 0:4].unsqueeze(2).to_broadcast([128, 4, 128]), ALU.subtract,
                             reads=[b_opt, b_hn], writes=[b_tA])
                        P.tt("pool", a3, a3, hn[:, 12:16].unsqueeze(2).to_broadcast([128, 4, 128]), ALU.mult,
                             reads=[b_tA, b_hn], writes=[b_tA])
                        P.tt("pool", oab[:, 0:512], tA[:], sga[:], ALU.mult, reads=[b_tA, b_sga], writes=[b_oab])

                    hooks = [RB1, RB2, RB3] + tail
                    s5_chunk(w, ci, True, hooks=hooks)
                    P.tt("dve", ypt[:], ypt[:], ps[7][:], ALU.add, reads=[b_ypt, bps[7]], writes=[b_ypt])
                    tail = make_tail(ci, n)
                for t_ in tail:
                    t_()

            s5_setup(0)
            with contextlib.ExitStack() as wk:
                w = alloc_work(wk, False)
                for s in range(nseq):
                    passA(s, w)
                T_barrier()
            s5_setup(1)
            with contextlib.ExitStack() as wk:
                w = alloc_work(wk, True)
                for s in range(nseq):
                    passB(s, w)
                T_barrier()

    def T_barrier():
        evs = []
        for e in T.engs.values():
            for k in range(len(e.dsems)):
                evs.append((e.dsems[k], e.dvals[k]))
            if e.sem is not None and e.cnt > 0 and not e.pending:
                evs.append((e.sem, e.cnt))
        for e in T.engs.values():
            for ev in evs:
                T._wait(e, ev)

    def odd_layer(li, src, bsrc, dst, bdst):
        raise NotImplementedError

    P.odd_layer_hook = None
    cur, bcur = x_in, [b_xin] * nseq
    for idx, li in enumerate(layers):
        last = idx == len(layers) - 1
        dstt, bd = (y_out, b_yout) if last else (xs[idx % 2], b_xs[idx % 2])
        if li % 2 == 0:
            even_layer(li, cur, bcur, dstt, bd)
        else:
            ODD_IMPL(P, locals(), li, cur, bcur, dstt, bd)
        cur, bcur = dstt, bd
    T.finish()
    T.replay()
    P.es.close()
    return P


def ODD_IMPL(P, env, li, src, bsrc, dst, bdst):
    nc, T = P.nc, P.T
    P.pool_to_dve = False
    P.pne = 'mix'
    P.stq = os.environ.get('STQO', 'sp')
    E = env
    ps, bps = E["ps"], E["bps"]
    nseq, L = P.nseq, P.L
    ident_bf, b_ident = E["ident_bf"], E["b_ident"]
    j = li // 2
    rows = L // 64
    nblk = L // 256

    def rs(r):
        return min(max(r - 4, 0), rows - 8)

    with contextlib.ExitStack() as st:
        def S(name, shape, dt=F32):
            return st.enter_context(P.sbt(f"{name}_o{li}", list(shape), dt)), Buf(name)
        E["adaln"](li, None)
        C = E["make_common"](st, f"o{li}")
        winc, b_winc = S("winc", [128, 8, 4096], BF16)
        woutc, b_woutc = S("woutc", [128, 8, 1024], BF16)
        Z, b_Z = S("Z", [128, 16, 1024], BF16)
        hT, b_hT = S("hT", [128, 8, 256], BF16)
        KT = [S(f"KT{i}", [128, 8, 256], BF16) for i in range(3)]
        V = [S(f"V{i}", [128, 2, 8, 3, 64], BF16) for i in range(3)]
        QT = [S(f"QT{i}", [128, 8, 256], BF16) for i in range(2)]
        GT = [S(f"GT{i}", [128, 8, 256], BF16) for i in range(2)]
        pT = [S(f"pT{i}", [128, 256], BF16) for i in range(3)]
        og, b_og = S("og", [128, 8, 256], BF16)
        rd2 = [S(f"rd{i}", [128, 256]) for i in range(2)]
        rb2 = [S(f"rb{i}", [128, 256]) for i in range(2)]
        t22 = [S(f"t2{i}", [128, 256]) for i in range(2)]
        selb, b_selb = S("selb", [128, 64], BF16)
        RBh = [S(f"RBh{i}", [128, 256], BF16) for i in range(2)]
        zer, b_zer = S("zer", [128, 256], BF16)
        wsrc = E["w_in_c"][j].rearrange("(k p) n -> p k n", p=128)
        for k in range(8):
            P.dma("pool", winc[:, k, :], wsrc[:, k, :], writes=[b_winc])
        P.dma("pool", woutc[:], E["w_out_c"][j].rearrange("(k p) n -> p k n", p=128), writes=[b_woutc])
        P.dma("pool", selb[:], E["sel_d"], writes=[b_selb])
        for i in range(2):
            P.memset("dve", RBh[i][0][:], 0.0, writes=[RBh[i][1]])
        P.memset("dve", zer[:], 0.0, writes=[b_zer])
        for i in range(3):
            P.memset("pool", V[i][0][:], 1.0, writes=[V[i][1]])
        with contextlib.ExitStack() as s2:
            zm = s2.enter_context(P.sbt(f"zm_o{li}", [128, 1024], F32)); b_zm = Buf("zm")
            zt0_ = s2.enter_context(P.sbt(f"zt0_o{li}", [128, 512], F32))
            zt = [zt0_, zt0_]
            b_zt0_ = Buf("zt0")
            b_zt = [b_zt0_, b_zt0_]
            P.dma("sp", zm[:], E["zmask"], writes=[b_zm])
            for h in range(16):
                for hf in range(2):
                    P.dma("sp", zt[hf][:], E["zg"][j, h][:, hf * 512:(hf + 1) * 512], writes=[b_zt[hf]])
                    P.tt("dve", Z[:, h, hf * 512:(hf + 1) * 512], zt[hf][:], zm[:, hf * 512:(hf + 1) * 512], ALU.add,
                         reads=[b_zt[hf], b_zm], writes=[b_Z])
            E["T_barrier"]()

        def proj(s, b, do_prenorm=True):
            ring = b % 3
            sl = b % 2
            if do_prenorm:
                for t in range(2):
                    E["prenorm"](C, s, src, bsrc[s], 2 * b + t, hT, b_hT, t * 128)
            cnt = 0
            for (col0, kind) in ((0, "q"), (1024, "k"), (3072, "g")):
                for hp2 in range(4):
                    bank = 2 + cnt % 2
                    cnt += 1
                    for hh in range(2):
                        hp = 2 * hp2 + hh
                        for k in range(8):
                            P.mm(ps[bank][:, hh * 256:(hh + 1) * 256], winc[:, k, col0 + hp * 128:col0 + (hp + 1) * 128], hT[:, k, :],
                                 k == 0, k == 7, reads=[b_winc, b_hT], writes=[bps[bank]], inc=(k == 7 and hh == 1))
                    if kind == "q":
                        P.act(QT[sl][0][:, 2 * hp2:2 * hp2 + 2, :].rearrange("p a b -> p (a b)"), ps[bank][:], AF.Copy,
                              reads=[bps[bank]], writes=[QT[sl][1]], scale=0.125)
                    elif kind == "k":
                        P.cp("dve", KT[ring][0][:, 2 * hp2:2 * hp2 + 2, :].rearrange("p a b -> p (a b)"), ps[bank][:],
                             reads=[bps[bank]], writes=[KT[ring][1]])
                    else:
                        P.act(GT[sl][0][:, 2 * hp2:2 * hp2 + 2, :].rearrange("p a b -> p (a b)"), ps[bank][:], AF.Silu,
                              reads=[bps[bank]], writes=[GT[sl][1]])
            for t in range(2):
                for half in range(2):
                    bank = 2 + cnt % 2
                    cnt += 1
                    for k in range(8):
                        P.mm(ps[bank][:], hT[:, k, t * 128:(t + 1) * 128], winc[:, k, 2048 + half * 512:2048 + (half + 1) * 512],
                             k == 0, k == 7, reads=[b_hT, b_winc], writes=[bps[bank]], inc=(k == 7))
                    src4 = ps[bank][:].rearrange("p (a c d) -> p a c d", a=4, c=2)
                    P.cp("dve", V[ring][0][:, t, 4 * half:4 * half + 4, 0:3:2, :], src4, reads=[bps[bank]], writes=[V[ring][1]])

        def attn(s, b, hooks=()):
            hooks = list(hooks)
            R = 4 * b
            sl = b % 2
            lo = rs(R) & ~1
            hi = (rs(R + 3) + 7) & ~1
            tiles = []
            for r0 in range(lo, hi + 1, 2):
                qs = [r for r in range(R, R + 4) if (rs(r) <= r0 + 1 and r0 <= rs(r) + 7)]
                if not qs:
                    continue
                qa, qb = qs[0], qs[-1]
                partial = []
                for r in qs:
                    for a in range(2):
                        if not (rs(r) <= r0 + a <= rs(r) + 7):
                            partial.append((r, a))
                tiles.append((r0, qa, qb, partial))
            tiles.sort(key=lambda t: (0 if (t[1] == R and t[2] == R + 3 and not t[3]) else 1))
            assert tiles[0][1] == R and tiles[0][2] == R + 3 and not tiles[0][3]
            pcount = [0]
            W = [(h, ti) for h in range(16) for ti in range(len(tiles))]
            info = {}
            deferred = []

            def emit_scores(h, ti):
                hp, base = h // 2, 64 * (h % 2)
                r0, qa, qb, partial = tiles[ti]
                kb = r0 // 4
                tt_ = (r0 % 4) // 2
                kring = kb % 3
                c0, c1 = (qa - R) * 64, (qb - R + 1) * 64
                z0, z1 = (qa - r0 + 7) * 64, (qb - r0 + 8) * 64
                bank = 4 + pcount[0] % 2
                pt, b_pt = pT[pcount[0] % 3]
                pcount[0] += 1
                P.mm(ps[bank][:, c0:c1], KT[kring][0][base:base + 64, hp, tt_ * 128:(tt_ + 1) * 128],
                     QT[sl][0][base:base + 64, hp, c0:c1], True, False,
                     reads=[KT[kring][1], QT[sl][1]], writes=[bps[bank]], inc=False)
                P.mm(ps[bank][:, c0:c1], ident_bf[:], Z[:, h, z0:z1], False, True,
                     reads=[b_ident, b_Z], writes=[bps[bank]], inc=True)
                P.act(pt[:, c0:c1], ps[bank][:, c0:c1], AF.Exp, reads=[bps[bank]], writes=[b_pt])
                for (r, a_) in partial:
                    cc = (r - R) * 64
                    P.memset("pool", pt[64 * a_:64 * a_ + 64, cc:cc + 64], 0.0, writes=[b_pt])
                info[(h, ti)] = (pt, b_pt, c0, c1, kring, tt_)

            def emit_pv(h, ti, idx):
                hp = h // 2
                pt, b_pt, c0, c1, kring, tt_ = info.pop((h, ti))
                ob = 6 + h % 2
                va = V[kring][0][:, tt_, hp, 0:2, :] if h % 2 == 0 else V[kring][0][:, tt_, hp, 1:3, :]
                last = ti == len(tiles) - 1
                P.mm(ps[ob][:, c0:c1], va.rearrange("p a b -> p (a b)"), pt[:, c0:c1], ti == 0, last,
                     reads=[V[kring][1], b_pt], writes=[bps[ob]], inc=last)
                if last:
                    par = h % 2
                    dr, orow = (64, 0) if par == 0 else (0, 64)
                    rdp, b_rdp = rd2[par]
                    rbp, b_rbp = rb2[par]
                    t2p, b_t2p = t22[par]
                    rbh, b_rbh = RBh[par]
                    P.T.op("dve", lambda q, dr=dr, ob=ob, rdp=rdp: q.reciprocal(out=rdp[dr:dr + 33, :], in_=ps[ob][dr:dr + 33, 0:256]),
                           reads=[bps[ob]], writes=[b_rdp])
                    P.cp("dve", rbh[dr:dr + 33, :], rdp[dr:dr + 33, :], reads=[b_rdp], writes=[b_rbh])
                    P.tt("dve", rbh[dr:dr + 1, :], rdp[dr:dr + 1, :], rbh[dr:dr + 1, :], ALU.subtract, reads=[b_rdp, b_rbh], writes=[b_rbh])

                    def part2(h=h, hp=hp, par=par, dr=dr, orow=orow, ob=ob, rdp=rdp, b_rdp=b_rdp, rbp=rbp, b_rbp=b_rbp, t2p=t2p, b_t2p=b_t2p,
                              rbh=rbh, b_rbh=b_rbh):
                        P.mm(ps[2 + par][orow:orow + 64, 0:256], selb[dr:dr + 33, 0:64], rbh[dr:dr + 33, :], True, True,
                             reads=[b_selb, b_rbh], writes=[bps[2 + par]])
                        P.act(rbp[orow:orow + 64, :], ps[2 + par][orow:orow + 64, 0:256], AF.Copy, reads=[bps[2 + par]], writes=[b_rbp])
                        P.tt("dve", t2p[orow:orow + 64, :], ps[ob][orow:orow + 64, 0:256], rbp[orow:orow + 64, :], ALU.mult,
                             reads=[bps[ob], b_rbp], writes=[b_t2p])
                        P.tt("dve", og[orow:orow + 64, hp, :], t2p[orow:orow + 64, :], GT[sl][0][orow:orow + 64, hp, :], ALU.mult,
                             reads=[b_t2p, GT[sl][1]], writes=[b_og])
                    deferred.append((idx + min(4, len(tiles) - 1), part2))
                    if h % 2 == 1 and hooks:
                        deferred.append((idx + 1, hooks.pop(0)))
                        deferred.sort(key=lambda d: d[0])

            LA = 1
            for idx in range(len(W) + LA):
                if idx < len(W):
                    emit_scores(*W[idx])
                if idx >= LA:
                    emit_pv(W[idx - LA][0], W[idx - LA][1], idx)
                while deferred and deferred[0][0] <= idx:
                    deferred.pop(0)[1]()
            while deferred:
                deferred.pop(0)[1]()
            for t in range(2):
                n = 2 * b + t
                sx = C["cnt"] % 2
                C["cnt"] += 1
                xt, bxt = C["xt"][sx], C["bxt"][sx]
                P.dma("sp", xt[:], src[s, n * 128:(n + 1) * 128, :], reads=[bsrc[s]], writes=[bxt])
                for hh in range(2):
                    for k in range(8):
                        P.mm(ps[2 + hh][:], og[:, k, t * 128:(t + 1) * 128], woutc[:, k, hh * 512:(hh + 1) * 512], k == 0, k == 7,
                             reads=[b_og, b_woutc], writes=[bps[2 + hh]], inc=(k == 7))
                E["post"](C, s, (2, 3), xt, bxt, dst, bdst[s], n)

        for s in range(nseq):
            proj(s, 0)
            if nblk > 1:
                proj(s, 1)
            for b in range(nblk):
                if b >= 1 and b + 1 < nblk:
                    proj(s, b + 1, do_prenorm=False)
                hk = []
                if b + 2 < nblk:
                    for t in range(2):
                        hk += list(E["prenorm_parts"](C, s, src, bsrc[s], 2 * (b + 2) + t, hT, b_hT, t * 128))
                attn(s, b, hooks=hk)
        E["T_barrier"]()


def _common_inputs(p, L):
    f32 = np.float32
    m = {}
    def T8(a):
        return np.ascontiguousarray(a.reshape(a.shape[0], -1, 128).transpose(0, 2, 1)).astype(f32)
    m["npreT"] = T8(p["norm_pre"])
    m["npostT"] = T8(p["norm_post"])
    m["wmod"] = np.ascontiguousarray(p["w_mod"], dtype=f32)
    m["bmodT"] = T8(p["b_mod"])
    m["w_in_ab"] = np.ascontiguousarray(p["w_in_ab"], dtype=f32)
    m["w_out_ab"] = np.ascontiguousarray(p["w_out_ab"], dtype=f32)
    m["w_glu"] = np.ascontiguousarray(p["ssm_w_glu"], dtype=f32)
    m["ssm_d"] = np.ascontiguousarray(p["ssm_d"], dtype=f32)
    sm3, rep3, bl, cl = [], [], [], []
    for j in range(2):
        a, b, c, d = _s5_layout(p["ssm_a_re"][j], p["ssm_a_im"][j], p["ssm_log_step"][j], p["ssm_b_re"][j],
                                p["ssm_b_im"][j], p["ssm_c_re"][j], p["ssm_c_im"][j])
        sm3.append(a); rep3.append(b); bl.append(c); cl.append(d)
    m["s5_sm3"] = np.stack(sm3).astype(f32)
    m["s5_rep3"] = np.stack(rep3).astype(f32)
    m["s5_bl"] = np.stack(bl).astype(f32)
    m["s5_cl"] = np.stack(cl).astype(f32)
    m["w_in_c"] = np.ascontiguousarray(p["w_in_c"], dtype=f32)
    m["w_out_c"] = np.ascontiguousarray(p["w_out_c"], dtype=f32)
    zs = []
    for j in range(2):
        z, mask = _na_layout(np.asarray(p["na_rel_bias"][j], dtype=f32))
        zs.append(z)
    m["zg"] = np.stack(zs).astype(f32)
    m["zmask"] = mask
    m["ident"] = np.eye(128, dtype=f32)
    m["jmat"] = np.eye(128, dtype=f32)[::-1].copy()
    m["rot"] = _rot_tables(L)
    dt, gam, cd = _ret_consts()
    m["dtab"] = dt
    m["gam"] = np.ascontiguousarray(gam.transpose(0, 1, 2))
    m["cdec"] = cd
    m["jt"] = np.broadcast_to(np.arange(128, dtype=f32)[None, :], (128, 128)).copy()
    sel = np.zeros((128, 64), f32)
    sel[[0, 32, 64, 96], :] = 1.0
    m["sel"] = sel
    return m


_PROG_CACHE = {}


def run_cores(xs_per_core, cs_per_core, params, layers):
    nseq, L, _ = xs_per_core[0].shape
    key = (nseq, L, tuple(layers))
    if key not in _PROG_CACHE:
        _PROG_CACHE[key] = build(nseq, L, list(layers))
    P = _PROG_CACHE[key]
    com = _common_inputs(params, L)
    in_maps = []
    for x, c in zip(xs_per_core, cs_per_core):
        m = dict(com)
        m["x_in"] = np.ascontiguousarray(x, dtype=np.float32)
        m["cT"] = np.ascontiguousarray(c.reshape(nseq, 8, 128).transpose(2, 1, 0), dtype=np.float32)
        in_maps.append(m)
    res = run_bass_kernel_spmd(P.nc, in_maps, core_ids=list(range(len(in_maps))))
    return [np.asarray(r["y_out"]) for r in res.results]


def kernel(**inputs):
    p = {k: np.asarray(v) for k, v in inputs.items()}
    xp, xsamp = p["x_prompt"], p["x_sample"]
    cp, cs = p["c_prompt"], p["c_sample"]
    seqs = [xp[i] for i in range(4)] + [xsamp[i] for i in range(8)]
    cvs = [cp[i] for i in range(4)] + [cs[i] for i in range(8)]
    slots = [(c, 8 + c if c < 4 else c) for c in range(8)]
    xs_pc = [np.stack([seqs[a], seqs[b]]) for a, b in slots]
    cs_pc = [np.stack([cvs[a], cvs[b]]) for a, b in slots]
    outs = run_cores(xs_pc, cs_pc, p, [0, 1, 2, 3])
    res = [None] * 12
    for c, (a, b) in enumerate(slots):
        res[a] = outs[c][0]
        if c < 4:
            res[b] = outs[c][1]
    y_prompt = np.stack(res[0:4]).astype(np.float32)
    y_sample = np.stack(res[4:12]).astype(np.float32)
    return (y_prompt, y_sample)
```
